# Optimizing a Trainium2 kernel written in Bass

```python
import math
import jax, jax.numpy as jnp
from jax import lax
import numpy as np

D_MODEL = 2048
BATCH = 4
SEQ = 8192
DEPTH = 4

PLE_DIM = 256
CHUNK = 64
CONV_K = 4
EPS = 1e-6
N_EVEN = (DEPTH + 1) // 2
N_ODD = DEPTH // 2

GDN_HEADS = 8
GDN_DK = 128
GDN_DV = 128
GDN_QK = GDN_HEADS * GDN_DK
GDN_V = GDN_HEADS * GDN_DV
SSD_HEADS = 16
SSD_HEADDIM = 64
SSD_INNER = SSD_HEADS * SSD_HEADDIM
SSD_GROUPS = 2
SSD_STATE = 128
RET_HEADS = 8
RET_DK = 128
RET_DV = 128
RET_QK = RET_HEADS * RET_DK
RET_V = RET_HEADS * RET_DV
ROPE_BASE = 10000.0
LRU_WIDTH = 1024
LRU_BLOCKS = 8
LRU_BW = LRU_WIDTH // LRU_BLOCKS
LRU_C = 8.0

EVEN_SPLITS = (3 * GDN_QK if GDN_QK == GDN_V else 2 * GDN_QK + GDN_V, GDN_V, GDN_HEADS, GDN_HEADS,
               SSD_INNER + 2 * SSD_GROUPS * SSD_STATE, SSD_INNER, SSD_HEADS)
EVEN_IN = sum(EVEN_SPLITS)
EVEN_MIX = GDN_V + SSD_INNER
ODD_SPLITS = (RET_QK, RET_QK, RET_V, RET_V, LRU_WIDTH, LRU_WIDTH)
ODD_IN = sum(ODD_SPLITS)
ODD_MIX = RET_V + LRU_WIDTH

kernel_name = 'hybrid_gdn_ssd_retention_rglru_trunk'


def _split(t, sizes):
    idx = np.cumsum(sizes)[:-1].tolist()
    return jnp.split(t, idx, axis=-1)


def rmsnorm(x, g):
    xf = x.astype(jnp.float32)
    y = xf * lax.rsqrt(jnp.mean(xf * xf, axis=-1, keepdims=True) + EPS)
    return (y * g.astype(jnp.float32)).astype(x.dtype)


def causal_dwconv(x, w, b=None):
    K, ch = w.shape
    y = lax.conv_general_dilated(x, w[:, None, :].astype(x.dtype), window_strides=(1,),
                                 padding=[(K - 1, 0)], dimension_numbers=('NWC', 'WIO', 'NWC'),
                                 feature_group_count=ch)
    if b is not None:
        y = y + b.astype(x.dtype)
    return y


def _chunks(t, n, H):
    t = t.reshape((t.shape[0], n, CHUNK, H) + t.shape[3:])
    return jnp.moveaxis(t, 3, 1)


def gated_deltanet(q, k, v, a, b, A_log, dt_bias):
    Bsz, S, H, dk = q.shape
    dv = v.shape[-1]
    n = S // CHUNK
    q = q * lax.rsqrt(jnp.sum(q * q, -1, keepdims=True) + EPS) * (dk ** -0.5)
    k = k * lax.rsqrt(jnp.sum(k * k, -1, keepdims=True) + EPS)
    g = -jnp.exp(A_log) * jax.nn.softplus(a + dt_bias)
    beta = jax.nn.sigmoid(b)
    qc, kc, vc = _chunks(q, n, H), _chunks(k, n, H), _chunks(v, n, H)
    gcum = jnp.cumsum(_chunks(g, n, H), axis=-1)
    betac = _chunks(beta, n, H)[..., None]
    incl = jnp.tril(jnp.ones((CHUNK, CHUNK), dtype=bool))
    strict = jnp.tril(jnp.ones((CHUNK, CHUNK), dtype=bool), k=-1)
    decay = jnp.exp(jnp.where(incl, gcum[..., :, None] - gcum[..., None, :], -jnp.inf))
    kb = kc * betac
    lower = jnp.where(strict, jnp.einsum('bhncd,bhnsd->bhncs', kb, kc) * decay, 0.0)
    lhs = lower + jnp.eye(CHUNK, dtype=lower.dtype)
    rhs = jnp.concatenate([vc * betac, kb * jnp.exp(gcum)[..., None]], axis=-1)
    sol = lax.linalg.triangular_solve(lhs, rhs, left_side=True, lower=True, unit_diagonal=True)
    u, w = sol[..., :dv], sol[..., dv:]
    attn = jnp.einsum('bhncd,bhnsd->bhncs', qc, kc) * decay
    qg = qc * jnp.exp(gcum)[..., None]
    kd = kc * jnp.exp(gcum[..., -1:] - gcum)[..., None]
    gl = jnp.exp(gcum[..., -1])

    def step(state, inp):
        attn_i, u_i, w_i, qg_i, kd_i, gl_i = inp
        v_new = u_i - jnp.einsum('bhcd,bhde->bhce', w_i, state)
        o = jnp.einsum('bhcd,bhde->bhce', qg_i, state) + jnp.einsum('bhcs,bhse->bhce', attn_i, v_new)
        state = state * gl_i[..., None, None] + jnp.einsum('bhcd,bhce->bhde', kd_i, v_new)
        return state, o

    xs = tuple(jnp.moveaxis(t, 2, 0) for t in (attn, u, w, qg, kd, gl))
    s0 = jnp.zeros((Bsz, H, dk, dv), jnp.float32)
    _, o = lax.scan(step, s0, xs)
    return jnp.transpose(o, (1, 0, 3, 2, 4)).reshape(Bsz, S, H, dv)


def ssd(xh, dt, A_log, Bm, Cm):
    Bsz, S, H, P = xh.shape
    G = Bm.shape[2]
    E = H // G
    n = S // CHUNK
    dA = dt * (-jnp.exp(A_log))
    xc = (xh * dt[..., None]).reshape(Bsz, n, CHUNK, G, E, P)
    Bc = Bm.reshape(Bsz, n, CHUNK, G, -1)
    Cc = Cm.reshape(Bsz, n, CHUNK, G, -1)
    cs = jnp.cumsum(dA.reshape(Bsz, n, CHUNK, G, E), axis=2)
    incl = jnp.tril(jnp.ones((CHUNK, CHUNK), dtype=bool))[:, :, None, None]
    lmat = jnp.exp(jnp.where(incl, cs[:, :, :, None] - cs[:, :, None, :], -jnp.inf))
    scores = jnp.einsum('bnlgd,bnsgd->bnlsg', Cc, Bc)
    y_diag = jnp.einsum('bnlsge,bnsgep->bnlgep', scores[..., None] * lmat, xc)

    def step(h, inp):
        C_i, B_i, x_i, cs_i = inp
        y_off = jnp.einsum('blgd,bgepd->blgep', C_i, h) * jnp.exp(cs_i)[..., None]
        dec = jnp.exp(cs_i[:, -1:] - cs_i)
        h = h * jnp.exp(cs_i[:, -1])[..., None, None] + jnp.einsum('blgd,blge,blgep->bgepd', B_i, dec, x_i)
        return h, y_off

    xs = tuple(jnp.moveaxis(t, 1, 0) for t in (Cc, Bc, xc, cs))
    h0 = jnp.zeros((Bsz, G, E, P, Bm.shape[-1]), jnp.float32)
    _, y_off = lax.scan(step, h0, xs)
    y = y_diag + jnp.moveaxis(y_off, 0, 1)
    return y.reshape(Bsz, S, H, P)


def _rotate(x, cos, sin):
    half = x.shape[-1] // 2
    x1, x2 = x[..., :half], x[..., half:]
    return jnp.concatenate([x1 * cos - x2 * sin, x2 * cos + x1 * sin], axis=-1)


def retention(q, k, v, positions):
    Bsz, S, H, dk = q.shape
    dv = v.shape[-1]
    n = S // CHUNK
    inv_freq = ROPE_BASE ** (-jnp.arange(0, dk, 2, dtype=jnp.float32) / dk)
    ang = positions.astype(jnp.float32)[..., None] * inv_freq
    cos, sin = jnp.cos(ang)[:, :, None, :], jnp.sin(ang)[:, :, None, :]
    q = _rotate(q, cos, sin)
    k = _rotate(k, cos, sin) * (dk ** -0.5)
    log_gamma = jnp.log1p(-jnp.exp2(-5.0 - jnp.arange(H, dtype=jnp.float32)))
    idx = jnp.arange(CHUNK, dtype=jnp.float32)
    rel = idx[:, None] - idx[None, :]
    dmat = jnp.where(rel >= 0, jnp.exp(jnp.maximum(rel, 0.0) * log_gamma[:, None, None]), 0.0)
    qc, kc, vc = _chunks(q, n, H), _chunks(k, n, H), _chunks(v, n, H)
    intra = jnp.einsum('bhncs,bhnse->bhnce', jnp.einsum('bhncd,bhnsd->bhncs', qc, kc) * dmat[:, None], vc)
    q_dec = jnp.exp((idx + 1.0)[None, :] * log_gamma[:, None])
    k_dec = jnp.exp((CHUNK - 1.0 - idx)[None, :] * log_gamma[:, None])
    c_dec = jnp.exp(CHUNK * log_gamma)
    qd = qc * q_dec[:, None, :, None]
    kd = kc * k_dec[:, None, :, None]

    def step(state, inp):
        qd_i, kd_i, v_i = inp
        o = jnp.einsum('bhcd,bhde->bhce', qd_i, state)
        state = state * c_dec[:, None, None] + jnp.einsum('bhcd,bhce->bhde', kd_i, v_i)
        return state, o

    xs = tuple(jnp.moveaxis(t, 2, 0) for t in (qd, kd, vc))
    s0 = jnp.zeros((Bsz, H, dk, dv), jnp.float32)
    _, cross = lax.scan(step, s0, xs)
    o = intra + jnp.moveaxis(cross, 0, 2)
    return jnp.transpose(o, (0, 2, 3, 1, 4)).reshape(Bsz, S, H, dv)


def rglru(xc, w_a, b_a, w_x, b_x, lam):
    Bsz, S, W = xc.shape
    xb = xc.reshape(Bsz, S, LRU_BLOCKS, LRU_BW)
    r = jax.nn.sigmoid(jnp.einsum('bsni,nij->bsnj', xb, w_a).reshape(Bsz, S, W) + b_a)
    i = jax.nn.sigmoid(jnp.einsum('bsni,nij->bsnj', xb, w_x).reshape(Bsz, S, W) + b_x)
    log_a = -LRU_C * r * jax.nn.softplus(-lam)
    a = jnp.exp(log_a)
    u = jnp.sqrt(-jnp.expm1(2.0 * log_a)) * (i * xc)

    def combine(left, right):
        a_l, h_l = left
        a_r, h_r = right
        return a_l * a_r, a_r * h_l + h_r

    _, h = lax.associative_scan(combine, (a, u), axis=1)
    return h


def even_layer(h, w_in, w_out, gdn_conv_w, gdn_A_log, gdn_dt_bias, gdn_norm_g,
               ssd_conv_w, ssd_conv_b, ssd_A_log, ssd_dt_bias, ssd_D, ssd_norm_g):
    Bsz, S, _ = h.shape
    proj = (h @ w_in).astype(jnp.float32)
    qkv, z_a, a_a, b_a, xbc, z_b, dt_b = _split(proj, EVEN_SPLITS)
    qkv = jax.nn.silu(causal_dwconv(qkv, gdn_conv_w))
    q, k, v = _split(qkv, (GDN_QK, GDN_QK, GDN_V))
    o_a = gated_deltanet(q.reshape(Bsz, S, GDN_HEADS, GDN_DK), k.reshape(Bsz, S, GDN_HEADS, GDN_DK),
                         v.reshape(Bsz, S, GDN_HEADS, GDN_DV), a_a, b_a, gdn_A_log, gdn_dt_bias)
    o_a = rmsnorm(o_a, gdn_norm_g).reshape(Bsz, S, GDN_V) * jax.nn.silu(z_a)
    xbc = jax.nn.silu(causal_dwconv(xbc, ssd_conv_w, ssd_conv_b))
    xs, Bm, Cm = _split(xbc, (SSD_INNER, SSD_GROUPS * SSD_STATE, SSD_GROUPS * SSD_STATE))
    xs = xs.reshape(Bsz, S, SSD_HEADS, SSD_HEADDIM)
    dt = jax.nn.softplus(dt_b + ssd_dt_bias)
    y = ssd(xs, dt, ssd_A_log, Bm.reshape(Bsz, S, SSD_GROUPS, SSD_STATE), Cm.reshape(Bsz, S, SSD_GROUPS, SSD_STATE))
    y = (y + ssd_D[:, None] * xs).reshape(Bsz, S, SSD_INNER) * jax.nn.silu(z_b)
    y = rmsnorm(y.reshape(Bsz, S, SSD_GROUPS, -1), ssd_norm_g.reshape(SSD_GROUPS, -1)).reshape(Bsz, S, SSD_INNER)
    mixed = jnp.concatenate([o_a, y], axis=-1).astype(h.dtype)
    return mixed @ w_out


def odd_layer(h, positions, w_in, w_out, ret_norm_g, lru_conv_w, lru_conv_b,
              lru_w_a, lru_b_a, lru_w_x, lru_b_x, lru_lambda):
    Bsz, S, _ = h.shape
    proj = (h @ w_in).astype(jnp.float32)
    q, k, v, g_c, x_d, z_d = _split(proj, ODD_SPLITS)
    o_c = retention(q.reshape(Bsz, S, RET_HEADS, RET_DK), k.reshape(Bsz, S, RET_HEADS, RET_DK),
                    v.reshape(Bsz, S, RET_HEADS, RET_DV), positions)
    o_c = rmsnorm(o_c, ret_norm_g.reshape(RET_HEADS, RET_DV)).reshape(Bsz, S, RET_V) * jax.nn.silu(g_c)
    xc = causal_dwconv(x_d, lru_conv_w, lru_conv_b)
    o_d = rglru(xc, lru_w_a, lru_b_a, lru_w_x, lru_b_x, lru_lambda) * jax.nn.silu(z_d)
    mixed = jnp.concatenate([o_c, o_d], axis=-1).astype(h.dtype)
    return mixed @ w_out


def setup_inputs(seed: int = 0) -> dict:
    key = jax.random.key(seed)
    ks = iter(jax.random.split(key, 40))
    f32 = jnp.float32

    def nrm(shape, scale):
        return scale * jax.random.normal(next(ks), shape, f32)

    def gain(shape):
        return 1.0 + 0.02 * jax.random.normal(next(ks), shape, f32)

    def unif(shape, lo, hi):
        return jax.random.uniform(next(ks), shape, f32, lo, hi)

    def dt_bias(shape):
        dt = jnp.exp(unif(shape, math.log(1e-3), math.log(1e-1)))
        return dt + jnp.log(-jnp.expm1(-dt))

    ne, no = N_EVEN, N_ODD
    a0 = unif((no, LRU_WIDTH), 0.9, 0.999)
    sig = a0 ** (1.0 / LRU_C)
    return {
        'x': nrm((BATCH, SEQ, D_MODEL), 1.0),
        'p': nrm((DEPTH, BATCH, SEQ, PLE_DIM), 1.0),
        'positions': jnp.broadcast_to(jnp.arange(SEQ, dtype=jnp.int32), (BATCH, SEQ)),
        'norm_g': gain((DEPTH, D_MODEL)),
        'ple_norm_g': gain((DEPTH, D_MODEL)),
        'w_ple_gate': nrm((DEPTH, D_MODEL, D_MODEL), D_MODEL ** -0.5),
        'w_ple_proj': nrm((DEPTH, PLE_DIM, D_MODEL), PLE_DIM ** -0.5),
        'ev_w_in': nrm((ne, D_MODEL, EVEN_IN), D_MODEL ** -0.5),
        'ev_w_out': nrm((ne, EVEN_MIX, D_MODEL), EVEN_MIX ** -0.5),
        'gdn_conv_w': nrm((ne, CONV_K, 2 * GDN_QK + GDN_V), 0.5),
        'gdn_A_log': jnp.log(unif((ne, GDN_HEADS), 1.0, 16.0)),
        'gdn_dt_bias': dt_bias((ne, GDN_HEADS)),
        'gdn_norm_g': gain((ne, GDN_DV)),
        'ssd_conv_w': nrm((ne, CONV_K, SSD_INNER + 2 * SSD_GROUPS * SSD_STATE), 0.5),
        'ssd_conv_b': nrm((ne, SSD_INNER + 2 * SSD_GROUPS * SSD_STATE), 0.01),
        'ssd_A_log': jnp.log(unif((ne, SSD_HEADS), 1.0, 16.0)),
        'ssd_dt_bias': dt_bias((ne, SSD_HEADS)),
        'ssd_D': gain((ne, SSD_HEADS)),
        'ssd_norm_g': gain((ne, SSD_INNER)),
        'od_w_in': nrm((no, D_MODEL, ODD_IN), D_MODEL ** -0.5),
        'od_w_out': nrm((no, ODD_MIX, D_MODEL), ODD_MIX ** -0.5),
        'ret_norm_g': gain((no, RET_V)),
        'lru_conv_w': nrm((no, CONV_K, LRU_WIDTH), 0.5),
        'lru_conv_b': nrm((no, LRU_WIDTH), 0.01),
        'lru_w_a': nrm((no, LRU_BLOCKS, LRU_BW, LRU_BW), LRU_BW ** -0.5),
        'lru_b_a': nrm((no, LRU_WIDTH), 0.01),
        'lru_w_x': nrm((no, LRU_BLOCKS, LRU_BW, LRU_BW), LRU_BW ** -0.5),
        'lru_b_x': nrm((no, LRU_WIDTH), 0.01),
        'lru_lambda': jnp.log(sig) - jnp.log1p(-sig),
        'final_norm_g': gain((D_MODEL,)),
    }


def reference(x, p, positions, norm_g, ple_norm_g, w_ple_gate, w_ple_proj,
              ev_w_in, ev_w_out, gdn_conv_w, gdn_A_log, gdn_dt_bias, gdn_norm_g,
              ssd_conv_w, ssd_conv_b, ssd_A_log, ssd_dt_bias, ssd_D, ssd_norm_g,
              od_w_in, od_w_out, ret_norm_g, lru_conv_w, lru_conv_b,
              lru_w_a, lru_b_a, lru_w_x, lru_b_x, lru_lambda, final_norm_g):
    for i in range(DEPTH):
        j = i // 2
        hn = rmsnorm(x, norm_g[i])
        if i % 2 == 0:
            mix = even_layer(hn, ev_w_in[j], ev_w_out[j], gdn_conv_w[j], gdn_A_log[j], gdn_dt_bias[j],
                             gdn_norm_g[j], ssd_conv_w[j], ssd_conv_b[j], ssd_A_log[j], ssd_dt_bias[j],
                             ssd_D[j], ssd_norm_g[j])
        else:
            mix = odd_layer(hn, positions, od_w_in[j], od_w_out[j], ret_norm_g[j], lru_conv_w[j],
                            lru_conv_b[j], lru_w_a[j], lru_b_a[j], lru_w_x[j], lru_b_x[j], lru_lambda[j])
        x = x + mix.astype(x.dtype)
        gate = jax.nn.sigmoid(rmsnorm(x, ple_norm_g[i]) @ w_ple_gate[i])
        x = x + gate * (p[i] @ w_ple_proj[i])
    return rmsnorm(x, final_norm_g)
```

```python
import contextlib
import numpy as np
import concourse.bass as bass
import concourse.mybir as mybir
from concourse.bass_utils import run_bass_kernel_spmd

F32 = mybir.dt.float32
BF16 = mybir.dt.bfloat16
I32 = mybir.dt.int32
AF = mybir.ActivationFunctionType
ALU = mybir.AluOpType

D_MODEL = 2048
SEQ = 8192
EPS = 1e-6
NDMA = 16


class _Tok:
    __slots__ = ("w", "rs")

    def __init__(self):
        self.w = None
        self.rs = []


class Prog:
    ENGS = ("pe", "dve", "act", "pool", "sp")

    def __init__(self, nc):
        self.nc = nc
        self.q = {e: [] for e in self.ENGS}
        self.cnt = {e: 0 for e in self.ENGS}
        self.seen = {e: {} for e in self.ENGS}
        self.toks = {}
        self.slot_uses = [0] * NDMA
        self.rr = 0
        self.stack = contextlib.ExitStack()
        self.stacks = [self.stack]
        self.pending = {e: {} for e in self.ENGS}
        self.nt = 0
        self.psum_ids = set()
        self.keep = []

    @contextlib.contextmanager
    def scope(self):
        st = contextlib.ExitStack()
        self.stacks.append(st)
        try:
            yield
        finally:
            self.stacks.pop()
            st.close()
            self.barrier()

    def barrier(self):
        for e in self.ENGS:
            pd = self.pending[e]
            for o in ("pe", "dve", "act", "pool"):
                if self.cnt[o] > 0:
                    pd[o] = self.cnt[o]
            for s in range(NDMA):
                if self.slot_uses[s] > 0:
                    pd[("d", s)] = 16 * self.slot_uses[s]

    def sb(self, shape, dt=F32, name=None):
        self.nt += 1
        t = self.stacks[-1].enter_context(self.nc.sbuf_tensor(name or f"t{self.nt}", list(shape), dt))
        self.keep.append(t)
        return t

    def ps(self, shape, dt=F32, name=None):
        self.nt += 1
        t = self.stacks[-1].enter_context(self.nc.psum_tensor(name or f"p{self.nt}", list(shape), dt))
        self.psum_ids.add(id(t))
        self.keep.append(t)
        return t

    def _tk(self, ref):
        if isinstance(ref, tuple):
            t, k = ref
        else:
            t, k = ref, None
        if id(t) in self.psum_ids:
            k = None
        d = self.toks.setdefault(id(t), {"_": _Tok()})
        return d, k

    def _deps(self, R, W, eng=None):
        need = {}

        def add(ev):
            if ev is not None:
                if need.get(ev[0], 0) < ev[1]:
                    need[ev[0]] = ev[1]

        for ref in R:
            d, k = self._tk(ref)
            add(d["_"].w)
            if k is None:
                for kk, tk in d.items():
                    add(tk.w)
            elif k in d:
                add(d[k].w)
            t_ = ref[0] if isinstance(ref, tuple) else ref
            if id(t_) in self.psum_ids:
                for ev in d["_"].rs:
                    if ev[0] != eng:
                        add(ev)
        for ref in W:
            d, k = self._tk(ref)
            keys = list(d.keys()) if k is None else (["_", k] if k in d else ["_"])
            for kk in keys:
                add(d[kk].w)
                for ev in d[kk].rs:
                    add(ev)
        return need

    def _upd(self, R, W, ev):
        for ref in R:
            d, k = self._tk(ref)
            tk = d["_"] if k is None else d.setdefault(k, _Tok())
            tk.rs.append(ev)
            if len(tk.rs) > 24:
                m = {}
                for e in tk.rs:
                    if m.get(e[0], 0) < e[1]:
                        m[e[0]] = e[1]
                tk.rs = list(m.items())
        for ref in W:
            d, k = self._tk(ref)
            if k is None:
                for kk in d:
                    d[kk].w = ev
                    d[kk].rs = []
            else:
                tk = d.setdefault(k, _Tok())
                tk.w = ev
                tk.rs = []

    def op(self, eng, fn, R=(), W=()):
        need = self._deps(R, W, eng)
        if self.pending[eng]:
            for k, v in self.pending[eng].items():
                if need.get(k, 0) < v:
                    need[k] = v
            self.pending[eng] = {}
        if eng == "sp":
            s = self.rr
            self.rr = (self.rr + 1) % NDMA
            key = ("d", s)
            if self.slot_uses[s] > 0:
                v = 16 * self.slot_uses[s]
                if need.get(key, 0) < v:
                    need[key] = v
            self.slot_uses[s] += 1
            ev = (key, 16 * self.slot_uses[s])
            inc = (key, 16)
        else:
            self.cnt[eng] += 1
            ev = (eng, self.cnt[eng])
            inc = (eng, 1)
        waits = []
        seen = self.seen[eng]
        for k, v in need.items():
            if k == eng and eng == "pe":
                continue
            if seen.get(k, 0) >= v:
                continue
            seen[k] = v
            waits.append((k, v))
        self.q[eng].append((waits, fn, inc))
        self._upd(R, W, ev)

    def mm(self, out, lhsT, rhs, start=True, stop=True, R=(), W=()):
        self.op("pe", lambda e: e.matmul(out, lhsT, rhs, start=start, stop=stop), R, W)

    def tr(self, out, in_, ident, R=(), W=()):
        self.op("pe", lambda e: e.transpose(out, in_, ident), R, W)

    def act(self, out, in_, func, R=(), W=(), bias=None, scale=None, eng="act"):
        kw = {}
        if bias is not None:
            kw["bias"] = bias
        if scale is not None:
            kw["scale"] = scale
        self.op(eng, lambda e: e.activation(out, in_, func, **kw), R, W)

    def tt(self, out, in0, in1, alu, R=(), W=(), eng="dve"):
        self.op(eng, lambda e: e.tensor_tensor(out, in0, in1, alu), R, W)

    def ts(self, out, in0, s1, s2, op0, op1=None, R=(), W=(), eng="dve"):
        if op1 is None:
            self.op(eng, lambda e: e.tensor_scalar(out, in0, s1, None, op0), R, W)
        else:
            self.op(eng, lambda e: e.tensor_scalar(out, in0, s1, s2, op0, op1), R, W)

    def stt(self, out, in0, scalar, in1, op0, op1, R=(), W=()):
        self.op("dve", lambda e: e.scalar_tensor_tensor(out, in0, scalar, in1, op0, op1), R, W)

    def cp(self, out, in_, R=(), W=(), eng="dve"):
        if eng == "act":
            self.op("act", lambda e: e.activation(out, in_, AF.Copy), R, W)
        else:
            self.op(eng, lambda e: e.tensor_copy(out, in_), R, W)

    def recip(self, out, in_, R=(), W=()):
        self.op("dve", lambda e: e.reciprocal(out, in_), R, W)

    def memset(self, ap, val, W=(), eng="pool"):
        self.op(eng, lambda e: e.memset(ap, val), (), W)

    def dma(self, out, in_, R=(), W=()):
        self.op("sp", lambda e: e.dma_start(out=out, in_=in_), R, W)

    def emit(self):
        nc = self.nc
        with contextlib.ExitStack() as es:
            sems = {}
            for e in ("pe", "dve", "act", "pool"):
                sems[e] = es.enter_context(nc.semaphore("s_" + e))
            for s in range(NDMA):
                sems[("d", s)] = es.enter_context(nc.semaphore(f"s_d{s}"))
            block = es.enter_context(nc.Block())

            def run(name):
                def f(eng):
                    for waits, fn, inc in self.q[name]:
                        for k, v in waits:
                            eng.wait_ge(sems[k], v)
                        fn(eng).then_inc(sems[inc[0]], inc[1])
                    if name == "sp":
                        for s in range(NDMA):
                            if self.slot_uses[s] > 0:
                                eng.wait_ge(sems[("d", s)], 16 * self.slot_uses[s])
                return f

            block.tensor(run("pe"))
            block.vector(run("dve"))
            block.scalar(run("act"))
            block.gpsimd(run("pool"))
            block.sync(run("sp"))
        self.stack.close()


def _new_nc():
    return bass.Bass("TRN2", target_bir_lowering=False)


def build_C(T, final, TT=256, ctx=None):
    KT = D_MODEL // 128
    if ctx is None:
        nc = _new_nc()
        mixT = nc.dram_tensor("mixT", [D_MODEL, T], F32, kind="ExternalInput").ap()
        xT = nc.dram_tensor("xT", [D_MODEL, T], F32, kind="ExternalInput").ap()
        pT = nc.dram_tensor("pT", [256, T], F32, kind="ExternalInput").ap()
        wo = nc.dram_tensor("wo", [D_MODEL, D_MODEL], F32, kind="ExternalInput").ap()
        wg = nc.dram_tensor("wg", [D_MODEL, D_MODEL], F32, kind="ExternalInput").ap()
        wp = nc.dram_tensor("wp", [256, D_MODEL], F32, kind="ExternalInput").ap()
        prm = nc.dram_tensor("prm", [128, 32], F32, kind="ExternalInput").ap()
        yT = nc.dram_tensor("yT", [D_MODEL, T], F32, kind="ExternalOutput").ap()
        p = Prog(nc)
        es = contextlib.ExitStack()
        es.enter_context(nc.allow_low_precision("bf16 matmul operands, fp32 accumulation"))
    else:
        nc, p = ctx["nc"], ctx["p"]
        mixT, xT, pT, wo, wg, wp, prm, yT = (ctx[k] for k in ("mixT", "xT", "pT", "wo", "wg", "wp", "prm", "yT"))
    prm_t = p.sb([128, 32])
    p.dma(prm_t[:, :], prm[:, :], W=[prm_t])
    ones_b = p.sb([128, 128], BF16)
    p.memset(ones_b[:, :], 1.0, W=[ones_b])
    eps_t = p.sb([128, 1])
    p.memset(eps_t[:, :], EPS, W=[eps_t])

    wo_b = p.sb([128, KT, D_MODEL], BF16)
    wg_b = p.sb([128, KT, D_MODEL], BF16)
    wp_b = p.sb([128, 2, D_MODEL], BF16)
    wst = [p.sb([128, 4, TT]) for _ in range(3)]
    i = 0
    for (src, dst, nk, gcol) in ((wo, wo_b, KT, None), (wg, wg_b, KT, 0), (wp, wp_b, 2, None)):
        for kt in range(nk):
            for c in range(2):
                st = wst[i % 3]
                i += 1
                p.dma(st[:, :, :], src[kt * 128:(kt + 1) * 128, c * 1024:(c + 1) * 1024].rearrange("p (c t) -> p c t", c=4), W=[st])
                dv = dst[:, kt, c * 1024:(c + 1) * 1024].rearrange("p (c t) -> p c t", c=4)
                if gcol is None:
                    p.cp(dv, st[:, :, :], R=[st], W=[(dst, (kt, c))], eng="act")
                else:
                    p.act(dv, st[:, :, :], AF.Copy, R=[st, prm_t], W=[(dst, (kt, c))],
                          scale=prm_t[:, gcol + kt:gcol + kt + 1])

    NTT = T // TT
    assert TT == 256
    mst = wst
    mb = [p.sb([128, KT, TT], BF16) for _ in range(1)]
    xt = [p.sb([128, KT, TT]) for _ in range(1)]
    xb = [p.sb([128, KT, TT], BF16) for _ in range(1)]
    sq = [p.sb([128, TT], BF16) for _ in range(3)]
    pst = p.sb([128, 2, TT])
    pb = p.sb([128, 2, TT], BF16)
    rs_t = p.sb([128, TT])
    rstd = p.sb([128, TT])
    gt = [p.sb([128, TT]) for _ in range(2)]
    g2 = [p.sb([128, TT]) for _ in range(2)]
    acc = [p.ps([128, 512]) for _ in range(3)]
    acc2 = [p.ps([128, 512]) for _ in range(2)]
    ssq = p.ps([128, 512])
    ci = 0
    for tt in range(NTT):
        tsl = slice(tt * TT, (tt + 1) * TT)
        m_b = mb[0]
        x_t = xt[0]
        x_b = xb[0]
        for kg in range(4):
            st = mst[ci % 3]
            ci += 1
            p.dma(st[:, :, :], mixT[kg * 512:(kg + 1) * 512, tsl].rearrange("(k p) t -> p k t", p=128), W=[st])
            p.cp(m_b[:, kg * 4:(kg + 1) * 4, :], st[:, :, :], R=[st], W=[(m_b, kg)], eng="pool")
        p.dma(x_t[:, :, :], xT[:, tsl].rearrange("(k p) t -> p k t", p=128), W=[x_t])
        p.dma(pst[:, :, :], pT[:, tsl].rearrange("(k p) t -> p k t", p=128), W=[pst])
        p.cp(pb[:, :, :], pst[:, :, :], R=[pst], W=[pb], eng="pool")
        for dc in range(KT):
            a = acc[dc % 3]
            for kt in range(KT):
                p.mm(a[:, 0:TT], wo_b[:, kt, dc * 128:(dc + 1) * 128], m_b[:, kt, :], start=(kt == 0), stop=(kt == KT - 1),
                     R=[wo_b, m_b], W=[a])
            p.tt(x_t[:, dc, :], a[:, 0:TT], x_t[:, dc, :], ALU.add, R=[a, (x_t, dc)], W=[(x_t, dc)])
            s = sq[dc % 3]
            p.act(s[:, :], x_t[:, dc, :], AF.Square, R=[(x_t, dc)], W=[s])
            p.mm(ssq[:, 0:TT], ones_b[:, :], s[:, :], start=(dc == 0), stop=(dc == KT - 1), R=[ones_b, s], W=[ssq])
            p.cp(x_b[:, dc, :], x_t[:, dc, :], R=[(x_t, dc)], W=[(x_b, dc)], eng="pool")
        p.act(rs_t[:, :], ssq[:, 0:TT], AF.Sqrt, R=[ssq, eps_t], W=[rs_t], bias=eps_t[:, 0:1], scale=1.0 / D_MODEL)
        p.recip(rstd[:, :], rs_t[:, :], R=[rs_t], W=[rstd])
        for dc in range(KT):
            a = acc[dc % 3]
            for kt in range(KT):
                p.mm(a[:, 0:TT], wg_b[:, kt, dc * 128:(dc + 1) * 128], x_b[:, kt, :], start=(kt == 0), stop=(kt == KT - 1),
                     R=[wg_b, x_b], W=[a])
            a2 = acc2[dc % 2]
            for kt in range(2):
                p.mm(a2[:, 0:TT], wp_b[:, kt, dc * 128:(dc + 1) * 128], pb[:, kt, :], start=(kt == 0), stop=(kt == 1),
                     R=[wp_b, pb], W=[a2])
            g = gt[dc % 2]
            gg = g2[dc % 2]
            p.tt(g[:, :], a[:, 0:TT], rstd[:, :], ALU.mult, R=[a, rstd], W=[g])
            p.act(gg[:, :], g[:, :], AF.Sigmoid, R=[g], W=[gg])
            p.tt(g[:, :], a2[:, 0:TT], gg[:, :], ALU.mult, R=[a2, gg], W=[g])
            p.tt(x_t[:, dc, :], x_t[:, dc, :], g[:, :], ALU.add, R=[g, (x_t, dc)], W=[(x_t, dc)], eng="pool")
        if final:
            for dc in range(KT):
                s = sq[dc % 3]
                p.act(s[:, :], x_t[:, dc, :], AF.Square, R=[(x_t, dc)], W=[s])
                p.mm(ssq[:, 0:TT], ones_b[:, :], s[:, :], start=(dc == 0), stop=(dc == KT - 1), R=[ones_b, s], W=[ssq])
            p.act(rs_t[:, :], ssq[:, 0:TT], AF.Sqrt, R=[ssq, eps_t], W=[rs_t], bias=eps_t[:, 0:1], scale=1.0 / D_MODEL)
            p.recip(rstd[:, :], rs_t[:, :], R=[rs_t], W=[rstd])
            for dc in range(KT):
                p.stt(x_t[:, dc, :], x_t[:, dc, :], prm_t[:, 16 + dc:17 + dc], rstd[:, :], ALU.mult, ALU.mult,
                      R=[(x_t, dc), prm_t, rstd], W=[(x_t, dc)])
        p.dma(yT[:, tsl].rearrange("(k p) t -> p k t", p=128), x_t[:, :, :], R=[x_t], W=[])
    if ctx is not None:
        return None
    p.emit()
    es.close()
    return nc


def phase_A(p, xT, w_loc, NCOL, projT, prm_t, T, ones_b, eps_t):
    KT = D_MODEL // 128
    TT = 512
    with p.scope():
        wb = p.sb([128, KT, NCOL], BF16)
        wst = [p.sb([128, 512]) for _ in range(3)]
        i = 0
        for kt in range(KT):
            for c0 in range(0, NCOL, 512):
                c1 = min(NCOL, c0 + 512)
                st = wst[i % 3]
                i += 1
                p.dma(st[:, 0:c1 - c0], w_loc[kt * 128:(kt + 1) * 128, c0:c1], W=[st])
                p.act(wb[:, kt, c0:c1], st[:, 0:c1 - c0], AF.Copy, R=[st, prm_t], W=[(wb, (kt, c0))],
                      scale=prm_t[:, kt:kt + 1])
        xst = [p.sb([128, 4, TT]) for _ in range(3)]
        xb = [p.sb([128, KT, TT], BF16) for _ in range(2)]
        sq = [p.sb([128, 4, TT], BF16) for _ in range(2)]
        rs_t = p.sb([128, TT])
        rstd = [p.sb([128, TT]) for _ in range(2)]
        ost = [p.sb([128, TT]) for _ in range(4)]
        acc = [p.ps([128, 512]) for _ in range(4)]
        ssq = p.ps([128, 512])
        ci = 0
        oi = 0
        nct = (NCOL + 127) // 128
        for tt in range(T // TT):
            tsl = slice(tt * TT, (tt + 1) * TT)
            x_b = xb[tt % 2]
            rsd = rstd[tt % 2]
            for kg in range(4):
                st = xst[ci % 3]
                s2 = sq[ci % 2]
                ci += 1
                p.dma(st[:, :, :], xT[kg * 512:(kg + 1) * 512, tsl].rearrange("(k p) t -> p k t", p=128), W=[st])
                p.act(s2[:, :, :], st[:, :, :], AF.Square, R=[st], W=[s2])
                p.cp(x_b[:, kg * 4:(kg + 1) * 4, :], st[:, :, :], R=[st], W=[(x_b, kg)], eng="pool")
                for k in range(4):
                    p.mm(ssq[:, :], ones_b[:, :], s2[:, k, :], start=(kg == 0 and k == 0), stop=(kg == 3 and k == 3),
                         R=[ones_b, s2], W=[ssq])
            p.act(rs_t[:, :], ssq[:, :], AF.Sqrt, R=[ssq, eps_t], W=[rs_t], bias=eps_t[:, 0:1], scale=1.0 / D_MODEL)
            p.recip(rsd[:, :], rs_t[:, :], R=[rs_t], W=[rsd])
            for ct in range(nct):
                m = min(128, NCOL - ct * 128)
                a = acc[ct % 4]
                for kt in range(KT):
                    p.mm(a[0:m, :], wb[:, kt, ct * 128:ct * 128 + m], x_b[:, kt, :], start=(kt == 0), stop=(kt == KT - 1),
                         R=[wb, x_b], W=[a])
                o = ost[oi % 4]
                oi += 1
                p.tt(o[0:m, :], a[0:m, :], rsd[0:m, :], ALU.mult, R=[a, rsd], W=[o])
                p.dma(projT[ct * 128:ct * 128 + m, tsl], o[0:m, :], R=[o])


def _con_layout():
    lay = {}
    off = 0
    for name, w in (("ident", 128), ("ones", 128), ("rotm", 128), ("maskT", 128), ("maskS", 128),
                    ("invf", 128), ("dmT", 512), ("qdec", 512), ("kdec", 4), ("glb", 4),
                    ("selbc4", 4 * 128), ("selbc8", 8 * 128), ("selc", 288), ("cmask", 512), ("elast", 128)):
        lay[name] = (off, w)
        off += w
    return lay, off


def make_consts(g):
    lay, n = _con_layout()
    c = np.zeros((128, n), np.float32)

    def put(name, arr):
        o, w = lay[name]
        c[:arr.shape[0], o:o + w] = arr.reshape(arr.shape[0], -1)

    put("ident", np.eye(128, dtype=np.float32))
    put("ones", np.ones((128, 128), np.float32))
    rot = np.zeros((128, 128), np.float32)
    for d in range(64):
        rot[d + 64, d] = -1.0
        rot[d, d + 64] = 1.0
    put("rotm", rot)
    idx = np.arange(128)
    put("maskT", np.where(idx[None, :] >= idx[:, None], 0.0, -1e30).astype(np.float32))
    put("maskS", np.where(idx[None, :] < idx[:, None], 0.0, -1e30).astype(np.float32))
    invf = (10000.0 ** (-np.arange(0, 128, 2, dtype=np.float32) / np.float32(128))).astype(np.float32)
    put("invf", np.concatenate([invf, invf])[None, :])
    heads = np.arange(4) + 4 * g
    lg = np.log1p(-np.exp2(-5.0 - heads.astype(np.float32))).astype(np.float32)
    sc = np.float32(128 ** -0.5)
    rel = (idx[None, :] - idx[:, None]).astype(np.float32)
    dmT = np.where((rel >= 0)[:, None, :], np.exp(np.maximum(rel, 0.0)[:, None, :] * lg[None, :, None]), 0.0) * sc
    put("dmT", dmT.astype(np.float32))
    qdec = np.exp((idx + 1.0)[None, None, :] * lg[None, :, None]) * np.ones((128, 1, 1))
    put("qdec", qdec.astype(np.float32))
    kdec = np.exp((127.0 - idx)[:, None] * lg[None, :]) * sc
    put("kdec", kdec.astype(np.float32))
    put("glb", (np.exp(128.0 * lg)[None, :] * np.ones((128, 1))).astype(np.float32))
    s4 = np.zeros((4, 4, 128), np.float32)
    for h in range(4):
        s4[h, h, :] = 1.0
    put("selbc4", s4)
    s8 = np.zeros((8, 8, 128), np.float32)
    for h in range(8):
        s8[h, h, :] = 1.0
    put("selbc8", s8)
    sc_ = np.zeros((8, 6, 48), np.float32)
    for h in range(8):
        for q_ in range(6):
            sc_[h, q_, q_ * 8 + h] = 1.0
    put("selc", sc_)
    cm = np.ones((8, 512), np.float32)
    cm[:, 0::128] = 0.0
    put("cmask", cm)
    el = np.zeros((128, 128), np.float32)
    el[127, :] = 1.0
    put("elast", el)
    return c


def _cv(con, lay, name, rows=128):
    o, w = lay[name]
    return con[0:rows, o:o + w]


def build_AB_odd(T, ctx=None):
    NCOL = 3072
    lay, ncon = _con_layout()
    if ctx is None:
        nc = _new_nc()
        xT = nc.dram_tensor("xT", [D_MODEL, T], F32, kind="ExternalInput").ap()
        w_loc = nc.dram_tensor("w", [D_MODEL, NCOL], F32, kind="ExternalInput").ap()
        prm = nc.dram_tensor("prm", [128, 64], F32, kind="ExternalInput").ap()
        con_d = nc.dram_tensor("con", [128, ncon], F32, kind="ExternalInput").ap()
        lruw_d = nc.dram_tensor("lruw", [128, 8 * 128], F32, kind="ExternalInput").ap()
        pos_d = nc.dram_tensor("pos", [1, T], I32, kind="ExternalInput").ap()
        mixT = nc.dram_tensor("mixT", [1024, T], F32, kind="ExternalOutput").ap()
        projT = nc.dram_tensor("projT", [NCOL, T], F32, kind="Internal").ap()
        mixA, mixB = mixT[0:512, :], mixT[512:1024, :]
        p = Prog(nc)
        es = contextlib.ExitStack()
        es.enter_context(nc.allow_low_precision("bf16 matmul operands, fp32 accumulation"))
    else:
        nc, p = ctx["nc"], ctx["p"]
        xT, w_loc, prm, con_d, lruw_d, pos_d, projT, mixA, mixB = (ctx[k] for k in ("xT", "w", "prm", "con", "lruw", "pos", "projT", "mixA", "mixB"))
    prm_t = p.sb([128, 64])
    p.dma(prm_t[:, :], prm[:, :], W=[prm_t])
    con = p.sb([128, ncon])
    p.dma(con[:, :], con_d[:, :], W=[con])
    ones_b = p.sb([128, 128], BF16)
    p.memset(ones_b[:, :], 1.0, W=[ones_b])
    eps_t = p.sb([128, 1])
    p.memset(eps_t[:, :], EPS, W=[eps_t])
    one_t = p.sb([128, 1])
    p.memset(one_t[:, :], 1.0, W=[one_t])
    phase_A(p, xT, w_loc, NCOL, projT, prm_t, T, ones_b, eps_t)

    ident = _cv(con, lay, "ident")
    ones_f = _cv(con, lay, "ones")
    PI = float(np.pi)
    C1 = 6.28125
    C2 = float(2 * np.pi - 6.28125)
    with p.scope():
        lruw = p.sb([128, 8, 128])
        p.dma(lruw[:, :, :], lruw_d.rearrange("p (n j) -> p n j", n=8), W=[lruw])
        sp_t = p.sb([128, 4])
        m8 = p.sb([128, 4])
        m16 = p.sb([128, 4])
        p.act(sp_t[:, :], prm_t[:, 48:52], AF.Exp, R=[prm_t], W=[sp_t], scale=-1.0)
        p.act(sp_t[:, :], sp_t[:, :], AF.Ln, R=[sp_t, one_t], W=[sp_t], bias=one_t[:, 0:1])
        p.ts(m8[:, :], sp_t[:, :], -8.0, None, ALU.mult, R=[sp_t], W=[m8])
        p.ts(m16[:, :], sp_t[:, :], -16.0, None, ALU.mult, R=[sp_t], W=[m16])
        hst = p.sb([128, 4])
        p.memset(hst[:, :], 0.0, W=[hst])
        S4 = p.sb([128, 4, 128])
        p.memset(S4[:, :, :], 0.0, W=[S4])

        q4 = p.sb([128, 4, 512])
        k4 = p.sb([128, 4, 512])
        v4 = p.sb([128, 4, 512])
        g4 = p.sb([128, 4, 512])
        qp4 = p.sb([128, 4, 512])
        kp4 = p.sb([128, 4, 512])
        t14 = p.sb([128, 4, 512])
        o4 = p.sb([128, 4, 512])
        posi = p.sb([1, 512], I32)
        posf = p.sb([1, 512])
        ang = p.sb([128, 512])
        kf = p.sb([128, 512])
        ki = p.sb([128, 512], I32)
        yy = p.sb([128, 512])
        mm_ = p.sb([128, 512])
        sinT = p.sb([128, 512])
        cosT = p.sb([128, 512])
        attnT = p.sb([128, 4, 128])
        qg = p.sb([128, 4, 128])
        kd = p.sb([128, 4, 128])
        Vt = p.sb([128, 4, 128])
        rs1 = p.sb([128, 512])
        rinv = p.sb([128, 512])
        xd = p.sb([128, 515])
        zt = p.sb([128, 512])
        xc = p.sb([128, 512])
        rr = p.sb([128, 512])
        ii = p.sb([128, 512])
        aa = p.sb([128, 512])
        a2 = p.sb([128, 512])
        hh = p.sb([128, 512])
        pA = p.ps([128, 512])
        pB = p.ps([128, 512])
        pG = p.ps([128, 512])
        pK = p.ps([128, 512])
        pV = p.ps([128, 512])
        pO = p.ps([128, 512])
        pD = p.ps([128, 512])
        dmT = _cv(con, lay, "dmT").rearrange("p (h c) -> p h c", h=4)
        qdec = _cv(con, lay, "qdec").rearrange("p (h c) -> p h c", h=4)
        kdec = _cv(con, lay, "kdec")
        glb = _cv(con, lay, "glb")
        rotm = _cv(con, lay, "rotm")
        invf = _cv(con, lay, "invf", 1)
        for blk in range(T // 512):
            bsl = slice(blk * 512, (blk + 1) * 512)
            for (dst, r0) in ((q4, 0), (k4, 512), (v4, 1024), (g4, 1536)):
                p.dma(dst[:, :, :], projT[r0:r0 + 512, bsl].rearrange("(h p) t -> p h t", p=128), W=[dst])
            p.dma(posi[:, :], pos_d[:, bsl], W=[posi])
            p.cp(posf[:, :], posi[:, :], R=[posi], W=[posf])
            p.mm(pA[:, :], invf, posf[0:1, :], R=[con, posf], W=[pA])
            p.cp(ang[:, :], pA[:, :], R=[pA], W=[ang], eng="act")
            p.ts(kf[:, :], ang[:, :], float(1.0 / (2 * np.pi)), None, ALU.mult, R=[ang], W=[kf])
            p.cp(ki[:, :], kf[:, :], R=[kf], W=[ki])
            p.cp(kf[:, :], ki[:, :], R=[ki], W=[kf])
            p.stt(ang[:, :], kf[:, :], -C1, ang[:, :], ALU.mult, ALU.add, R=[kf, ang], W=[ang])
            p.stt(ang[:, :], kf[:, :], -C2, ang[:, :], ALU.mult, ALU.add, R=[kf, ang], W=[ang])
            for (dstT, shift) in ((sinT, 0.0), (cosT, PI / 2)):
                p.ts(yy[:, :], ang[:, :], shift, None, ALU.add, R=[ang], W=[yy])
                p.ts(mm_[:, :], yy[:, :], PI, 2 * PI, ALU.is_gt, ALU.mult, R=[yy], W=[mm_])
                p.tt(yy[:, :], yy[:, :], mm_[:, :], ALU.subtract, R=[yy, mm_], W=[yy])
                p.ts(mm_[:, :], yy[:, :], -PI, 2 * PI, ALU.is_lt, ALU.mult, R=[yy], W=[mm_])
                p.tt(yy[:, :], yy[:, :], mm_[:, :], ALU.add, R=[yy, mm_], W=[yy])
                p.ts(yy[:, :], yy[:, :], -PI, PI, ALU.max, ALU.min, R=[yy], W=[yy])
                p.act(dstT[:, :], yy[:, :], AF.Sin, R=[yy], W=[dstT])
            for (src4, dst4) in ((q4, qp4), (k4, kp4)):
                for h in range(4):
                    pr = pA if h % 2 == 0 else pB
                    p.mm(pr[:, :], rotm, src4[:, h, :], R=[con, src4], W=[pr])
                    p.tt(dst4[:, h, :], pr[:, :], sinT[:, :], ALU.mult, R=[pr, sinT], W=[(dst4, h)])
                    p.tt(t14[:, h, :], src4[:, h, :], cosT[:, :], ALU.mult, R=[src4, cosT], W=[(t14, h)], eng="pool")
                p.tt(dst4[:, :, :], dst4[:, :, :], t14[:, :, :], ALU.add, R=[dst4, t14], W=[dst4], eng="pool")
            for c4 in range(4):
                cs = slice(c4 * 128, (c4 + 1) * 128)
                for h in range(4):
                    hs = slice(h * 128, (h + 1) * 128)
                    p.mm(pG[:, hs], kp4[:, h, cs], qp4[:, h, cs], R=[kp4, qp4], W=[(pG, h)])
                    p.tr(pK[:, hs], kp4[:, h, cs], ident, R=[kp4, con], W=[(pK, h)])
                    p.tr(pV[:, hs], v4[:, h, cs], ident, R=[v4, con], W=[(pV, h)])
                p.tt(attnT[:, :, :], pG[:, :].rearrange("p (h c) -> p h c", h=4), dmT, ALU.mult, R=[pG, con], W=[attnT])
                p.tt(qg[:, :, :], qp4[:, :, cs], qdec, ALU.mult, R=[qp4, con], W=[qg], eng="pool")
                for h in range(4):
                    hs = slice(h * 128, (h + 1) * 128)
                    p.act(kd[:, h, :], pK[:, hs], AF.Copy, R=[(pK, h), con], W=[(kd, h)], scale=kdec[:, h:h + 1])
                p.cp(Vt[:, :, :], pV[:, :].rearrange("p (h c) -> p h c", h=4), R=[pV], W=[Vt], eng="act")
                for h in range(4):
                    hs = slice(h * 128, (h + 1) * 128)
                    p.mm(pO[:, hs], S4[:, h, :], qg[:, h, :], start=True, stop=False, R=[(S4, h), qg], W=[(pO, h)])
                    p.mm(pO[:, hs], Vt[:, h, :], attnT[:, h, :], start=False, stop=True, R=[Vt, attnT], W=[(pO, h)])
                    p.mm(pD[:, hs], kd[:, h, :], Vt[:, h, :], R=[(kd, h), Vt], W=[(pD, h)])
                p.cp(o4[:, :, cs], pO[:, :].rearrange("p (h c) -> p h c", h=4), R=[pO], W=[(o4, c4)], eng="act")
                for h in range(4):
                    hs = slice(h * 128, (h + 1) * 128)
                    p.stt(S4[:, h, :], S4[:, h, :], glb[:, h:h + 1], pD[:, hs], ALU.mult, ALU.add,
                          R=[(S4, h), (pD, h), con], W=[(S4, h)])
            p.act(t14[:, :, :], o4[:, :, :], AF.Square, R=[o4], W=[t14])
            p.act(g4[:, :, :], g4[:, :, :], AF.Silu, R=[g4], W=[g4])
            for h in range(4):
                pr = pA if h % 2 == 0 else pB
                p.mm(pr[:, :], ones_f, t14[:, h, :], R=[con, t14], W=[pr])
                p.act(rs1[:, :], pr[:, :], AF.Sqrt, R=[pr, eps_t], W=[rs1], bias=eps_t[:, 0:1], scale=1.0 / 128)
                p.recip(rinv[:, :], rs1[:, :], R=[rs1], W=[rinv])
                p.stt(o4[:, h, :], o4[:, h, :], prm_t[:, 16 + h:17 + h], rinv[:, :], ALU.mult, ALU.mult,
                      R=[o4, prm_t, rinv], W=[(o4, ("n", h))])
            p.tt(o4[:, :, :], o4[:, :, :], g4[:, :, :], ALU.mult, R=[o4, g4], W=[o4], eng="pool")
            p.dma(mixA[:, bsl].rearrange("(h p) t -> p h t", p=128), o4[:, :, :], R=[o4])
            for ct in range(4):
                r0 = 2048 + ct * 128
                if blk == 0:
                    p.memset(xd[:, 0:3], 0.0, W=[xd])
                    p.dma(xd[:, 3:515], projT[r0:r0 + 128, 0:512], W=[xd])
                else:
                    p.dma(xd[:, :], projT[r0:r0 + 128, blk * 512 - 3:(blk + 1) * 512], W=[xd])
                p.dma(zt[:, :], projT[r0 + 512:r0 + 640, bsl], W=[zt])
                cw = 20 + ct * 4
                p.ts(xc[:, :], xd[:, 0:512], prm_t[:, cw:cw + 1], prm_t[:, 36 + ct:37 + ct], ALU.mult, ALU.add,
                     R=[xd, prm_t], W=[xc])
                for j in range(1, 4):
                    p.stt(xc[:, :], xd[:, j:j + 512], prm_t[:, cw + j:cw + j + 1], xc[:, :], ALU.mult, ALU.add,
                          R=[xd, prm_t, xc], W=[xc])
                p.mm(pA[:, :], lruw[:, ct, :], xc[:, :], R=[lruw, xc], W=[pA])
                p.mm(pB[:, :], lruw[:, 4 + ct, :], xc[:, :], R=[lruw, xc], W=[pB])
                p.act(rr[:, :], pA[:, :], AF.Sigmoid, R=[pA, prm_t], W=[rr], bias=prm_t[:, 40 + ct:41 + ct])
                p.act(ii[:, :], pB[:, :], AF.Sigmoid, R=[pB, prm_t], W=[ii], bias=prm_t[:, 44 + ct:45 + ct])
                p.act(aa[:, :], rr[:, :], AF.Exp, R=[rr, m8], W=[aa], scale=m8[:, ct:ct + 1])
                p.act(a2[:, :], rr[:, :], AF.Exp, R=[rr, m16], W=[a2], scale=m16[:, ct:ct + 1])
                p.act(a2[:, :], a2[:, :], AF.Sqrt, R=[a2, one_t], W=[a2], bias=one_t[:, 0:1], scale=-1.0)
                p.tt(ii[:, :], ii[:, :], a2[:, :], ALU.mult, R=[ii, a2], W=[ii])
                p.tt(ii[:, :], ii[:, :], xc[:, :], ALU.mult, R=[ii, xc], W=[ii], eng="pool")
                p.op("dve", lambda e, ct=ct: e.tensor_tensor_scan(hh[:, :], aa[:, :], ii[:, :], hst[:, ct:ct + 1], ALU.mult, ALU.add),
                     R=[aa, ii, hst], W=[hh])
                p.cp(hst[:, ct:ct + 1], hh[:, 511:512], R=[hh], W=[hst])
                p.act(zt[:, :], zt[:, :], AF.Silu, R=[zt], W=[zt])
                p.tt(hh[:, :], hh[:, :], zt[:, :], ALU.mult, R=[hh, zt], W=[hh], eng="pool")
                p.dma(mixB[ct * 128:(ct + 1) * 128, bsl], hh[:, :], R=[hh])
    if ctx is not None:
        return None
    p.emit()
    es.close()
    return nc


def bcast(ap, n):
    sh = list(ap.shape)
    return ap.unsqueeze(len(sh)).broadcast_to(sh + [n])


def build_AB_even(T, ctx=None):
    NCOL = 3344
    lay, ncon = _con_layout()
    if ctx is None:
        nc = _new_nc()
        xT = nc.dram_tensor("xT", [D_MODEL, T], F32, kind="ExternalInput").ap()
        w_loc = nc.dram_tensor("w", [D_MODEL, NCOL], F32, kind="ExternalInput").ap()
        prm = nc.dram_tensor("prm", [128, 112], F32, kind="ExternalInput").ap()
        hp_d = nc.dram_tensor("hp", [8, 8], F32, kind="ExternalInput").ap()
        dbc_d = nc.dram_tensor("dbc", [128, 512], F32, kind="ExternalInput").ap()
        con_d = nc.dram_tensor("con", [128, ncon], F32, kind="ExternalInput").ap()
        mixT = nc.dram_tensor("mixT", [1024, T], F32, kind="ExternalOutput").ap()
        projT = nc.dram_tensor("projT", [NCOL, T], F32, kind="Internal").ap()
        mixA, mixB = mixT[0:512, :], mixT[512:1024, :]
        p = Prog(nc)
        es = contextlib.ExitStack()
        es.enter_context(nc.allow_low_precision("bf16 matmul operands, fp32 accumulation"))
    else:
        nc, p = ctx["nc"], ctx["p"]
        xT, w_loc, prm, hp_d, dbc_d, con_d, projT, mixA, mixB = (ctx[k] for k in ("xT", "w", "prm", "hp", "dbc", "con", "projT", "mixA", "mixB"))
    prm_t = p.sb([128, 112])
    p.dma(prm_t[:, :], prm[:, :], W=[prm_t])
    con = p.sb([128, ncon])
    p.dma(con[:, :], con_d[:, :], W=[con])
    hp = p.sb([8, 8])
    p.dma(hp[:, :], hp_d[:, :], W=[hp])
    ones_b = p.sb([128, 128], BF16)
    p.memset(ones_b[:, :], 1.0, W=[ones_b])
    eps_t = p.sb([128, 1])
    p.memset(eps_t[:, :], EPS, W=[eps_t])
    one_t = p.sb([128, 1])
    p.memset(one_t[:, :], 1.0, W=[one_t])
    phase_A(p, xT, w_loc, NCOL, projT, prm_t, T, ones_b, eps_t)

    ident = _cv(con, lay, "ident")
    ones_f = _cv(con, lay, "ones")
    maskT = _cv(con, lay, "maskT")
    maskS = _cv(con, lay, "maskS")
    selbc4 = _cv(con, lay, "selbc4", 4).rearrange("p (h c) -> p h c", h=4)
    selbc8 = _cv(con, lay, "selbc8", 8).rearrange("p (h c) -> p h c", h=8)
    selc = _cv(con, lay, "selc", 8).rearrange("p (q c) -> p q c", q=6)
    cmask = _cv(con, lay, "cmask", 8)
    SC = float(128 ** -0.5)

    def v4(ps_t):
        return ps_t[:, :].rearrange("p (h c) -> p h c", h=4)

    with p.scope():
        banks = [p.ps([128, 512]) for _ in range(6)]
        pBig = p.ps([128, 1024])
        bi = [0]

        def bank():
            b = banks[bi[0] % 6]
            bi[0] += 1
            return b

        dbc = p.sb([128, 512])
        p.dma(dbc[:, :], dbc_d[:, :], W=[dbc])
        negA = p.sb([8, 2])
        p.memset(negA[:, :], 0.0, W=[negA])
        p.act(negA[0:4, 0:1], hp[0:4, 0:1], AF.Exp, R=[hp], W=[negA])
        p.act(negA[0:8, 1:2], hp[0:8, 2:3], AF.Exp, R=[hp], W=[negA])
        p.ts(negA[:, :], negA[:, :], -1.0, None, ALU.mult, R=[negA], W=[negA])
        S4 = p.sb([128, 4, 128])
        p.memset(S4[:, :, :], 0.0, W=[S4])
        Hs = p.sb([128, 512])
        p.memset(Hs[:, :], 0.0, W=[Hs])

        qh = p.sb([128, 4, 515])
        kh = p.sb([128, 4, 515])
        vh = p.sb([128, 4, 515])
        z4 = p.sb([128, 4, 512])
        qc = p.sb([128, 4, 512])
        kc = p.sb([128, 4, 512])
        vc = p.sb([128, 4, 512])
        sq4 = p.sb([128, 4, 512])
        o4 = p.sb([128, 4, 512])
        rs1 = p.sb([128, 512])
        rinv = p.sb([128, 512])
        a_r = p.sb([8, 512])
        b_r = p.sb([8, 512])
        g_r = p.sb([8, 512])
        gc_r = p.sb([8, 512])
        be_r = p.sb([8, 512])
        bg_r = p.sb([8, 512])
        colt = p.sb([128, 48])
        ncol = p.sb([128, 8])
        gcbc = p.sb([128, 4, 128])
        egcbc = p.sb([128, 4, 128])
        T1 = p.sb([128, 4, 128])
        DmT = p.sb([128, 4, 128])
        Dm = p.sb([128, 4, 128])
        kdec = p.sb([128, 4])
        kd = p.sb([128, 4, 128])
        Vb = p.sb([128, 4, 128])
        attnT = p.sb([128, 4, 128])
        qg = p.sb([128, 4, 128])
        Pm = [p.sb([128, 4, 128]) for _ in range(2)]
        Ym = [p.sb([128, 4, 128]) for _ in range(2)]
        Am = p.sb([128, 4, 128])
        Xm = p.sb([128, 4, 128])
        Vn = p.sb([128, 4, 128])
        xsh = p.sb([128, 4, 515])
        Bh = p.sb([128, 515])
        Ch = p.sb([128, 515])
        xs4 = p.sb([128, 4, 512])
        Bc = p.sb([128, 512])
        Cc = p.sb([128, 512])
        csb = p.sb([128, 8, 128])
        lmT = p.sb([128, 8, 128])
        ecs = p.sb([128, 8])
        dect = p.sb([128, 8])
        glb8 = p.sb([128, 8])
        xst = p.sb([128, 512])
        xct = p.sb([128, 512])
        xcd = p.sb([128, 512])
        yt = p.sb([128, 512])
        Bt = p.sb([128, 128])

        def conv_tile(dst, src, wcol, bcol):
            if bcol is None:
                p.ts(dst, src[0], prm_t[:, wcol:wcol + 1], None, ALU.mult, R=[src[4], prm_t], W=[src[5]])
            else:
                p.ts(dst, src[0], prm_t[:, wcol:wcol + 1], prm_t[:, bcol:bcol + 1], ALU.mult, ALU.add,
                     R=[src[4], prm_t], W=[src[5]])
            for j in range(1, 4):
                p.stt(dst, src[j], prm_t[:, wcol + j:wcol + j + 1], dst, ALU.mult, ALU.add, R=[src[4], prm_t, src[5]], W=[src[5]])
            p.act(dst, dst, AF.Silu, R=[src[5]], W=[src[5]])

        def load_halo(dst, r0, nrow, blk, multi):
            if multi:
                d0 = dst[:, :, 0:3]
                dr = dst[:, :, 3:515]
                da = dst[:, :, :]
                src = lambda c0, c1: projT[r0:r0 + nrow, c0:c1].rearrange("(k p) t -> p k t", p=128)
            else:
                d0 = dst[:, 0:3]
                dr = dst[:, 3:515]
                da = dst[:, :]
                src = lambda c0, c1: projT[r0:r0 + nrow, c0:c1]
            if blk == 0:
                p.memset(d0, 0.0, W=[dst])
                p.dma(dr, src(0, 512), W=[dst])
            else:
                p.dma(da, src(blk * 512 - 3, (blk + 1) * 512), W=[dst])

        for blk in range(T // 512):
            bsl = slice(blk * 512, (blk + 1) * 512)
            load_halo(qh, 0, 512, blk, True)
            load_halo(kh, 512, 512, blk, True)
            load_halo(vh, 1024, 512, blk, True)
            p.dma(z4[:, :, :], projT[1536:2048, bsl].rearrange("(k p) t -> p k t", p=128), W=[z4])
            p.dma(a_r[0:4, :], projT[3328:3332, bsl], W=[a_r])
            p.dma(b_r[0:4, :], projT[3332:3336, bsl], W=[b_r])
            for sec, (hsrc, dstc) in enumerate(((qh, qc), (kh, kc), (vh, vc))):
                for h in range(4):
                    conv_tile(dstc[:, h, :], [hsrc[:, h, j:j + 512] for j in range(4)] + [hsrc, (dstc, h)],
                              16 + (sec * 4 + h) * 4, None)
            for (cc, scale) in ((qc, SC), (kc, None)):
                p.act(sq4[:, :, :], cc[:, :, :], AF.Square, R=[cc], W=[sq4])
                for h in range(4):
                    b = bank()
                    p.mm(b[:, :], ones_f, sq4[:, h, :], R=[con, sq4], W=[b])
                    p.act(rs1[:, :], b[:, :], AF.Sqrt, R=[b, eps_t], W=[rs1], bias=eps_t[:, 0:1], scale=1.0)
                    p.recip(rinv[:, :], rs1[:, :], R=[rs1], W=[rinv])
                    if scale is None:
                        p.tt(cc[:, h, :], cc[:, h, :], rinv[:, :], ALU.mult, R=[cc, rinv], W=[(cc, h)])
                    else:
                        p.stt(cc[:, h, :], cc[:, h, :], scale, rinv[:, :], ALU.mult, ALU.mult, R=[cc, rinv], W=[(cc, h)])
            p.act(g_r[0:4, :], a_r[0:4, :], AF.Exp, R=[a_r, hp], W=[g_r], bias=hp[0:4, 1:2])
            p.act(g_r[0:4, :], g_r[0:4, :], AF.Ln, R=[g_r, one_t], W=[g_r], bias=one_t[0:4, 0:1])
            p.ts(g_r[0:4, :], g_r[0:4, :], negA[0:4, 0:1], None, ALU.mult, R=[g_r, negA], W=[g_r])
            p.op("dve", lambda e: e.tensor_tensor_scan(gc_r[0:4, :], cmask[0:4, :], g_r[0:4, :], 0.0, ALU.mult, ALU.add),
                 R=[con, g_r], W=[gc_r])
            p.act(be_r[0:4, :], b_r[0:4, :], AF.Sigmoid, R=[b_r], W=[be_r])
            p.act(bg_r[0:4, :], gc_r[0:4, :], AF.Exp, R=[gc_r], W=[bg_r])
            p.tt(bg_r[0:4, :], bg_r[0:4, :], be_r[0:4, :], ALU.mult, R=[bg_r, be_r], W=[bg_r])
            for c4 in range(4):
                cs = slice(c4 * 128, (c4 + 1) * 128)
                b = bank()
                for qi, rows in enumerate((gc_r, be_r, bg_r)):
                    p.mm(b[:, 0:48], rows[0:4, cs], selc[0:4, qi, :], start=(qi == 0), stop=(qi == 2), R=[rows, con], W=[b])
                p.cp(colt[:, :], b[:, 0:48], R=[b], W=[colt], eng="act")
                p.ts(ncol[:, 0:4], colt[:, 16:20], -1.0, None, ALU.mult, R=[colt], W=[ncol])
                b = bank()
                for h in range(4):
                    p.mm(b[:, h * 128:(h + 1) * 128], selbc4[:, h, :], gc_r[0:4, cs], R=[con, gc_r], W=[b])
                p.cp(gcbc[:, :, :], v4(b), R=[b], W=[gcbc], eng="act")
                p.act(egcbc[:, :, :], v4(b), AF.Exp, R=[b], W=[egcbc])
                for h in range(4):
                    p.stt(T1[:, h, :], gcbc[:, h, :], colt[:, h:h + 1], maskT, ALU.subtract, ALU.add, R=[gcbc, colt, con], W=[(T1, h)])
                p.act(DmT[:, :, :], T1[:, :, :], AF.Exp, R=[T1], W=[DmT])
                for h in range(4):
                    p.stt(T1[:, h, :], gcbc[:, h, :], colt[:, h:h + 1], maskS, ALU.subtract, ALU.subtract, R=[gcbc, colt, con], W=[(T1, h)])
                p.act(Dm[:, :, :], T1[:, :, :], AF.Exp, R=[T1], W=[Dm], scale=-1.0)
                p.tt(kdec[:, :], gcbc[:, :, 127], colt[:, 0:4], ALU.subtract, R=[gcbc, colt], W=[kdec])
                p.act(kdec[:, :], kdec[:, :], AF.Exp, R=[kdec], W=[kdec])
                bK = bank()
                bV = bank()
                bG = bank()
                bL = bank()
                for h in range(4):
                    hs = slice(h * 128, (h + 1) * 128)
                    p.tr(bK[:, hs], kc[:, h, cs], ident, R=[kc, con], W=[bK])
                    p.tr(bV[:, hs], vc[:, h, cs], ident, R=[vc, con], W=[bV])
                    p.mm(bG[:, hs], kc[:, h, cs], qc[:, h, cs], R=[kc, qc], W=[bG])
                    p.mm(bL[:, hs], kc[:, h, cs], kc[:, h, cs], R=[kc], W=[bL])
                for h in range(4):
                    hs = slice(h * 128, (h + 1) * 128)
                    p.act(kd[:, h, :], bK[:, hs], AF.Copy, R=[bK, kdec], W=[(kd, h)], scale=kdec[:, h:h + 1])
                    p.act(Vb[:, h, :], bV[:, hs], AF.Copy, R=[bV, colt], W=[(Vb, h)], scale=colt[:, 8 + h:9 + h])
                    p.stt(Pm[0][:, h, :], bL[:, hs], colt[:, 8 + h:9 + h], Dm[:, h, :], ALU.mult, ALU.mult,
                          R=[bL, colt, Dm], W=[(Pm[0], h)])
                p.tt(attnT[:, :, :], v4(bG), DmT[:, :, :], ALU.mult, R=[bG, DmT], W=[attnT])
                p.tt(qg[:, :, :], qc[:, :, cs], egcbc[:, :, :], ALU.mult, R=[qc, egcbc], W=[qg], eng="pool")
                bY = bank()
                for h in range(4):
                    p.tr(bY[:, h * 128:(h + 1) * 128], Pm[0][:, h, :], ident, R=[Pm[0], con], W=[bY])
                p.cp(Ym[0][:, :, :], v4(bY), R=[bY], W=[Ym[0]], eng="act")
                for h in range(4):
                    p.stt(Am[:, h, :], bY[:, h * 128:(h + 1) * 128], -1.0, ident, ALU.mult, ALU.add, R=[con, bY], W=[(Am, h)])
                cur = 0
                for lev in range(6):
                    nxt = 1 - cur
                    bP = bank()
                    for h in range(4):
                        p.mm(bP[:, h * 128:(h + 1) * 128], Ym[cur][:, h, :], Pm[cur][:, h, :], R=[Ym[cur], Pm[cur]], W=[bP])
                    p.cp(Pm[nxt][:, :, :], v4(bP), R=[bP], W=[Pm[nxt]], eng="act")
                    if lev < 5:
                        bY2 = bank()
                        for h in range(4):
                            p.mm(bY2[:, h * 128:(h + 1) * 128], Pm[cur][:, h, :], Ym[cur][:, h, :], R=[Ym[cur], Pm[cur]], W=[bY2])
                        p.cp(Ym[nxt][:, :, :], v4(bY2), R=[bY2], W=[Ym[nxt]])
                    bU = bank()
                    for h in range(4):
                        p.mm(bU[:, h * 128:(h + 1) * 128], Pm[nxt][:, h, :], Am[:, h, :], R=[Pm[nxt], Am], W=[bU])
                    p.tt(Am[:, :, :], v4(bU), Am[:, :, :], ALU.add, R=[Am, bU], W=[Am])
                    cur = nxt
                bKS = bank()
                for h in range(4):
                    p.mm(bKS[:, h * 128:(h + 1) * 128], kc[:, h, cs], S4[:, h, :], R=[kc, (S4, h)], W=[bKS])
                for h in range(4):
                    p.stt(Xm[:, h, :], bKS[:, h * 128:(h + 1) * 128], ncol[:, h:h + 1], Vb[:, h, :], ALU.mult, ALU.add,
                          R=[bKS, ncol, Vb], W=[(Xm, h)])
                bVn = bank()
                for h in range(4):
                    p.mm(bVn[:, h * 128:(h + 1) * 128], Am[:, h, :], Xm[:, h, :], R=[Am, Xm], W=[bVn])
                p.cp(Vn[:, :, :], v4(bVn), R=[bVn], W=[Vn], eng="act")
                bO = bank()
                bD = bank()
                for h in range(4):
                    hs = slice(h * 128, (h + 1) * 128)
                    p.mm(bO[:, hs], S4[:, h, :], qg[:, h, :], start=True, stop=False, R=[(S4, h), qg], W=[bO])
                    p.mm(bO[:, hs], Vn[:, h, :], attnT[:, h, :], start=False, stop=True, R=[Vn, attnT], W=[bO])
                    p.mm(bD[:, hs], kd[:, h, :], Vn[:, h, :], R=[kd, Vn], W=[bD])
                p.cp(o4[:, :, cs], v4(bO), R=[bO], W=[(o4, c4)], eng="act")
                for h in range(4):
                    p.stt(S4[:, h, :], S4[:, h, :], egcbc[:, h, 127:128], bD[:, h * 128:(h + 1) * 128], ALU.mult, ALU.add,
                          R=[(S4, h), bD, egcbc], W=[(S4, h)])
            p.act(sq4[:, :, :], o4[:, :, :], AF.Square, R=[o4], W=[sq4])
            p.act(z4[:, :, :], z4[:, :, :], AF.Silu, R=[z4], W=[z4])
            for h in range(4):
                b = bank()
                p.mm(b[:, :], ones_f, sq4[:, h, :], R=[con, sq4], W=[b])
                p.act(rs1[:, :], b[:, :], AF.Sqrt, R=[b, eps_t], W=[rs1], bias=eps_t[:, 0:1], scale=1.0 / 128)
                p.recip(rinv[:, :], rs1[:, :], R=[rs1], W=[rinv])
                p.stt(o4[:, h, :], o4[:, h, :], prm_t[:, 94:95], rinv[:, :], ALU.mult, ALU.mult,
                      R=[o4, prm_t, rinv], W=[(o4, ("n", h))])
            p.tt(o4[:, :, :], o4[:, :, :], z4[:, :, :], ALU.mult, R=[o4, z4], W=[o4], eng="pool")
            p.dma(mixA[:, bsl].rearrange("(h p) t -> p h t", p=128), o4[:, :, :], R=[o4])

            load_halo(xsh, 2048, 512, blk, True)
            load_halo(Bh, 2560, 128, blk, False)
            load_halo(Ch, 2688, 128, blk, False)
            p.dma(z4[:, :, :], projT[2816:3328, bsl].rearrange("(k p) t -> p k t", p=128), W=[z4])
            p.dma(a_r[0:8, :], projT[3336:3344, bsl], W=[a_r])
            for k in range(4):
                conv_tile(xs4[:, k, :], [xsh[:, k, j:j + 512] for j in range(4)] + [xsh, (xs4, k)], 64 + k * 4, 88 + k)
            conv_tile(Bc[:, :], [Bh[:, j:j + 512] for j in range(4)] + [Bh, Bc], 64 + 16, 88 + 4)
            conv_tile(Cc[:, :], [Ch[:, j:j + 512] for j in range(4)] + [Ch, Cc], 64 + 20, 88 + 5)
            p.act(be_r[0:8, :], a_r[0:8, :], AF.Exp, R=[a_r, hp], W=[be_r], bias=hp[0:8, 3:4])
            p.act(be_r[0:8, :], be_r[0:8, :], AF.Ln, R=[be_r, one_t], W=[be_r], bias=one_t[0:8, 0:1])
            p.ts(g_r[0:8, :], be_r[0:8, :], negA[0:8, 1:2], None, ALU.mult, R=[be_r, negA], W=[g_r])
            p.op("dve", lambda e: e.tensor_tensor_scan(gc_r[0:8, :], cmask[0:8, :], g_r[0:8, :], 0.0, ALU.mult, ALU.add),
                 R=[con, g_r], W=[gc_r])
            for c4 in range(4):
                cs = slice(c4 * 128, (c4 + 1) * 128)
                b = bank()
                p.mm(b[:, 0:48], be_r[0:8, cs], selc[0:8, 3, :], start=True, stop=False, R=[be_r, con], W=[b])
                p.mm(b[:, 0:48], gc_r[0:8, cs], selc[0:8, 4, :], start=False, stop=True, R=[gc_r, con], W=[b])
                p.cp(colt[:, :], b[:, 0:48], R=[b], W=[colt], eng="act")
                p.act(ecs[:, :], colt[:, 32:40], AF.Exp, R=[colt], W=[ecs])
                for h in range(8):
                    p.mm(pBig[:, h * 128:(h + 1) * 128], selbc8[:, h, :], gc_r[0:8, cs], R=[con, gc_r], W=[pBig])
                p.cp(csb[:, :, :], pBig[:, :].rearrange("p (h c) -> p h c", h=8), R=[pBig], W=[csb], eng="act")
                p.tt(dect[:, :], csb[:, :, 127], colt[:, 32:40], ALU.subtract, R=[csb, colt], W=[dect])
                p.act(dect[:, :], dect[:, :], AF.Exp, R=[dect], W=[dect])
                p.act(glb8[:, :], csb[:, :, 127], AF.Exp, R=[csb], W=[glb8])
                for h in range(8):
                    p.stt(lmT[:, h, :], csb[:, h, :], colt[:, 32 + h:33 + h], maskT, ALU.subtract, ALU.add,
                          R=[csb, colt, con], W=[(lmT, h)])
                p.act(lmT[:, :, :], lmT[:, :, :], AF.Exp, R=[lmT], W=[lmT])
                bS = bank()
                p.mm(bS[:, 0:128], Bc[:, cs], Cc[:, cs], R=[Bc, Cc], W=[bS])
                for h in range(8):
                    p.tt(lmT[:, h, :], bS[:, 0:128], lmT[:, h, :], ALU.mult, R=[lmT, bS], W=[(lmT, h)])
                bX = bank()
                for k in range(4):
                    p.tr(bX[:, k * 128:(k + 1) * 128], xs4[:, k, cs], ident, R=[xs4, con], W=[bX])
                p.cp(xst[:, :], bX[:, :], R=[bX], W=[xst], eng="act")
                p.tt(xct[:, :].rearrange("p (h q) -> p h q", h=8), bX[:, :].rearrange("p (h q) -> p h q", h=8),
                     bcast(colt[:, 24:32], 64), ALU.mult, R=[bX, colt], W=[xct])
                bYd = bank()
                for h in range(8):
                    p.mm(bYd[:, h * 64:(h + 1) * 64], lmT[:, h, :], xct[:, h * 64:(h + 1) * 64], R=[lmT, xct], W=[bYd])
                bYo = bank()
                p.mm(bYo[:, :], Cc[:, cs], Hs[:, :], R=[Cc, Hs], W=[bYo])
                p.tt(yt[:, :].rearrange("p (h q) -> p h q", h=8), bYo[:, :].rearrange("p (h q) -> p h q", h=8),
                     bcast(ecs[:, :], 64), ALU.mult, R=[bYo, ecs], W=[yt])
                p.tt(yt[:, :], bYd[:, :], yt[:, :], ALU.add, R=[yt, bYd], W=[yt])
                p.tt(xst[:, :], xst[:, :], dbc[:, :], ALU.mult, R=[xst, dbc], W=[xst], eng="pool")
                p.tt(yt[:, :], yt[:, :], xst[:, :], ALU.add, R=[yt, xst], W=[yt])
                p.tt(xcd[:, :].rearrange("p (h q) -> p h q", h=8), xct[:, :].rearrange("p (h q) -> p h q", h=8),
                     bcast(dect[:, :], 64), ALU.mult, R=[xct, dect], W=[xcd])
                bB = bank()
                p.tr(bB[:, 0:128], Bc[:, cs], ident, R=[Bc, con], W=[bB])
                p.cp(Bt[:, :], bB[:, 0:128], R=[bB], W=[Bt], eng="act")
                bH = bank()
                p.mm(bH[:, :], Bt[:, :], xcd[:, :], R=[Bt, xcd], W=[bH])
                p.tt(Hs[:, :].rearrange("p (h q) -> p h q", h=8), Hs[:, :].rearrange("p (h q) -> p h q", h=8),
                     bcast(glb8[:, :], 64), ALU.mult, R=[Hs, glb8], W=[Hs])
                p.tt(Hs[:, :], bH[:, :], Hs[:, :], ALU.add, R=[Hs, bH], W=[Hs])
                bT = bank()
                for k in range(4):
                    p.tr(bT[:, k * 128:(k + 1) * 128], yt[:, k * 128:(k + 1) * 128], ident, R=[yt, con], W=[bT])
                p.cp(o4[:, :, cs], v4(bT), R=[bT], W=[(o4, c4)], eng="act")
            p.act(z4[:, :, :], z4[:, :, :], AF.Silu, R=[z4], W=[z4])
            p.tt(o4[:, :, :], o4[:, :, :], z4[:, :, :], ALU.mult, R=[o4, z4], W=[o4], eng="pool")
            p.act(sq4[:, :, :], o4[:, :, :], AF.Square, R=[o4], W=[sq4])
            b = bank()
            for k in range(4):
                p.mm(b[:, :], ones_f, sq4[:, k, :], start=(k == 0), stop=(k == 3), R=[con, sq4], W=[b])
            p.act(rs1[:, :], b[:, :], AF.Sqrt, R=[b, eps_t], W=[rs1], bias=eps_t[:, 0:1], scale=1.0 / 512)
            p.recip(rinv[:, :], rs1[:, :], R=[rs1], W=[rinv])
            for k in range(4):
                p.stt(o4[:, k, :], o4[:, k, :], prm_t[:, 96 + k:97 + k], rinv[:, :], ALU.mult, ALU.mult,
                      R=[o4, prm_t, rinv], W=[(o4, ("n", k))])
            p.dma(mixB[:, bsl].rearrange("(h p) t -> p h t", p=128), o4[:, :, :], R=[o4])
    if ctx is not None:
        return None
    p.emit()
    es.close()
    return nc


def pack_even(g, xT, norm_g, w_in, gdn_conv_w, gdn_A_log, gdn_dt_bias, gdn_norm_g,
              ssd_conv_w, ssd_conv_b, ssd_A_log, ssd_dt_bias, ssd_D, ssd_norm_g):
    hq = np.arange(512 * g, 512 * g + 512)
    h4 = np.arange(4 * g, 4 * g + 4)
    h8 = np.arange(8 * g, 8 * g + 8)
    o_za, o_a, o_b, o_xbc = 3072, 4096, 4104, 4112
    o_zb, o_dt = o_xbc + 1536, o_xbc + 1536 + 1024
    cols = np.concatenate([hq, 1024 + hq, 2048 + hq, o_za + hq, o_xbc + hq,
                           o_xbc + 1024 + 128 * g + np.arange(128), o_xbc + 1280 + 128 * g + np.arange(128),
                           o_zb + hq, o_a + h4, o_b + h4, o_dt + h8])
    prm = np.zeros((128, 112), np.float32)
    prm[:, 0:16] = _pk(norm_g, 16)
    gw = np.asarray(gdn_conv_w, np.float32)
    for sec in range(3):
        for h in range(4):
            ch = sec * 1024 + 512 * g + h * 128
            for j in range(4):
                prm[:, 16 + (sec * 4 + h) * 4 + j] = gw[j, ch:ch + 128]
    sw = np.asarray(ssd_conv_w, np.float32)
    sb_ = np.asarray(ssd_conv_b, np.float32)
    starts = [512 * g + k * 128 for k in range(4)] + [1024 + 128 * g, 1280 + 128 * g]
    for i, c0 in enumerate(starts):
        for j in range(4):
            prm[:, 64 + i * 4 + j] = sw[j, c0:c0 + 128]
        prm[:, 88 + i] = sb_[c0:c0 + 128]
    prm[:, 94] = np.asarray(gdn_norm_g, np.float32)
    prm[:, 96:100] = _pk(np.asarray(ssd_norm_g)[hq], 4)
    hp = np.zeros((8, 8), np.float32)
    hp[0:4, 0] = np.asarray(gdn_A_log)[h4]
    hp[0:4, 1] = np.asarray(gdn_dt_bias)[h4]
    hp[0:8, 2] = np.asarray(ssd_A_log)[h8]
    hp[0:8, 3] = np.asarray(ssd_dt_bias)[h8]
    dbc = np.ascontiguousarray(np.broadcast_to(np.repeat(np.asarray(ssd_D, np.float32)[h8], 64)[None, :], (128, 512)))
    return dict(xT=np.ascontiguousarray(xT, dtype=np.float32), w=np.ascontiguousarray(np.asarray(w_in, np.float32)[:, cols]),
                prm=prm, hp=hp, dbc=dbc, con=make_consts(g))


def _pk(v, n):
    return np.ascontiguousarray(np.asarray(v, np.float32).reshape(n, 128).T)


def pack_odd(g, xT, norm_g, w_in, ret_ng, conv_w, conv_b, w_a, b_a, w_x, b_x, lam, pos):
    hq = slice(512 * g, 512 * g + 512)
    cols = np.concatenate([np.arange(0, 1024)[hq], 1024 + np.arange(1024)[hq], 2048 + np.arange(1024)[hq],
                           3072 + np.arange(1024)[hq], 4096 + np.arange(1024)[hq], 5120 + np.arange(1024)[hq]])
    prm = np.zeros((128, 64), np.float32)
    prm[:, 0:16] = _pk(norm_g, 16)
    prm[:, 16:20] = _pk(ret_ng[hq], 4)
    cw = np.asarray(conv_w, np.float32)[:, hq]
    for ct in range(4):
        for j in range(4):
            prm[:, 20 + ct * 4 + j] = cw[j, ct * 128:(ct + 1) * 128]
    prm[:, 36:40] = _pk(np.asarray(conv_b)[hq], 4)
    prm[:, 40:44] = _pk(np.asarray(b_a)[hq], 4)
    prm[:, 44:48] = _pk(np.asarray(b_x)[hq], 4)
    prm[:, 48:52] = _pk(np.asarray(lam)[hq], 4)
    lw = np.concatenate([np.asarray(w_a, np.float32)[4 * g:4 * g + 4], np.asarray(w_x, np.float32)[4 * g:4 * g + 4]], 0)
    lruw = np.ascontiguousarray(lw.transpose(1, 0, 2).reshape(128, 8 * 128))
    return dict(xT=np.ascontiguousarray(xT, dtype=np.float32), w=np.ascontiguousarray(np.asarray(w_in, np.float32)[:, cols]),
                prm=prm, con=make_consts(g), lruw=lruw, pos=np.ascontiguousarray(np.asarray(pos, np.int32).reshape(1, -1)))


def unpack_mix(parts):
    return np.concatenate([parts[0][0:512], parts[1][0:512], parts[0][512:1024], parts[1][512:1024]], 0)


def build_fused(T, depth=4):
    nc = _new_nc()
    lay, ncon = _con_layout()
    dt = lambda name, shape, dty=F32, kind="ExternalInput": nc.dram_tensor(name, list(shape), dty, kind=kind).ap()
    xT = dt("xT", [D_MODEL, T])
    pT = dt("pT", [depth * 256, T])
    pos = dt("pos", [1, T], I32)
    con = [dt(f"con{g}", [128, ncon]) for g in range(2)]
    yT = dt("yT", [D_MODEL, T], kind="ExternalOutput")
    projT = dt("projT", [3344, T], kind="Internal")
    mixF = dt("mixF", [D_MODEL, T], kind="Internal")
    xs = [dt("xA", [D_MODEL, T], kind="Internal"), dt("xB", [D_MODEL, T], kind="Internal")]
    p = Prog(nc)
    es = contextlib.ExitStack()
    es.enter_context(nc.allow_low_precision("bf16 matmul operands, fp32 accumulation"))
    x_cur = xT
    for i in range(depth):
        even = (i % 2 == 0)
        for g in range(2):
            ctx = dict(nc=nc, p=p, xT=x_cur, con=con[g], projT=projT[0:(3344 if even else 3072), :],
                       mixA=mixF[512 * g:512 * g + 512, :], mixB=mixF[1024 + 512 * g:1536 + 512 * g, :],
                       w=dt(f"w{i}_{g}", [D_MODEL, 3344 if even else 3072]),
                       prm=dt(f"prm{i}_{g}", [128, 112 if even else 64]))
            if even:
                ctx["hp"] = dt(f"hp{i}_{g}", [8, 8])
                ctx["dbc"] = dt(f"dbc{i}_{g}", [128, 512])
            else:
                ctx["lruw"] = dt(f"lruw{i}_{g}", [128, 8 * 128])
                ctx["pos"] = pos
            with p.scope():
                (build_AB_even if even else build_AB_odd)(T, ctx=ctx)
        final = (i == depth - 1)
        x_nxt = yT if final else xs[i % 2]
        ctx = dict(nc=nc, p=p, mixT=mixF, xT=x_cur, pT=pT[i * 256:(i + 1) * 256, :], wo=dt(f"wo{i}", [D_MODEL, D_MODEL]),
                   wg=dt(f"wg{i}", [D_MODEL, D_MODEL]), wp=dt(f"wp{i}", [256, D_MODEL]), prm=dt(f"prmC{i}", [128, 32]), yT=x_nxt)
        with p.scope():
            build_C(T, final, ctx=ctx)
        x_cur = x_nxt
    p.emit()
    es.close()
    return nc


_CACHE = {}


def _prog(name, fn):
    if name not in _CACHE:
        _CACHE[name] = fn()
    return _CACHE[name]


def kernel_unfused(x, p, positions, norm_g, ple_norm_g, w_ple_gate, w_ple_proj,
           ev_w_in, ev_w_out, gdn_conv_w, gdn_A_log, gdn_dt_bias, gdn_norm_g,
           ssd_conv_w, ssd_conv_b, ssd_A_log, ssd_dt_bias, ssd_D, ssd_norm_g,
           od_w_in, od_w_out, ret_norm_g, lru_conv_w, lru_conv_b,
           lru_w_a, lru_b_a, lru_w_x, lru_b_x, lru_lambda, final_norm_g):
    f = lambda a: np.asarray(a, dtype=np.float32)
    x = f(x)
    B, S, D = x.shape
    H = S // 2
    cores = list(range(8))
    xT = [np.ascontiguousarray(x[b].T) for b in range(B)]
    depth = int(np.asarray(norm_g).shape[0])
    for i in range(depth):
        j = i // 2
        if i % 2 == 0:
            nc = _prog("even", lambda: build_AB_even(S))
            ins = [pack_even(c % 2, xT[c // 2], f(norm_g)[i], f(ev_w_in)[j], f(gdn_conv_w)[j], f(gdn_A_log)[j],
                             f(gdn_dt_bias)[j], f(gdn_norm_g)[j], f(ssd_conv_w)[j], f(ssd_conv_b)[j], f(ssd_A_log)[j],
                             f(ssd_dt_bias)[j], f(ssd_D)[j], f(ssd_norm_g)[j]) for c in cores]
            w_out = f(ev_w_out)[j]
        else:
            nc = _prog("odd", lambda: build_AB_odd(S))
            ins = [pack_odd(c % 2, xT[c // 2], f(norm_g)[i], f(od_w_in)[j], f(ret_norm_g)[j], f(lru_conv_w)[j],
                            f(lru_conv_b)[j], f(lru_w_a)[j], f(lru_b_a)[j], f(lru_w_x)[j], f(lru_b_x)[j],
                            f(lru_lambda)[j], np.asarray(positions)[c // 2]) for c in cores]
            w_out = f(od_w_out)[j]
        res = run_bass_kernel_spmd(nc, ins, core_ids=cores)
        mix = [unpack_mix([res.results[2 * b]["mixT"], res.results[2 * b + 1]["mixT"]]) for b in range(B)]
        del res, ins
        final = (i == depth - 1)
        ncC = _prog("Cf" if final else "C", lambda: build_C(H, final))
        prm = np.zeros((128, 32), np.float32)
        prm[:, 0:16] = _pk(f(ple_norm_g)[i], 16)
        prm[:, 16:32] = _pk(f(final_norm_g), 16)
        wg = np.ascontiguousarray(f(w_ple_gate)[i])
        wp = np.ascontiguousarray(f(w_ple_proj)[i])
        wo = np.ascontiguousarray(w_out)
        insC = []
        for c in cores:
            b, g = c // 2, c % 2
            sl = slice(g * H, (g + 1) * H)
            insC.append(dict(mixT=np.ascontiguousarray(mix[b][:, sl]), xT=np.ascontiguousarray(xT[b][:, sl]),
                             pT=np.ascontiguousarray(f(p)[i, b, sl, :].T), wo=wo, wg=wg, wp=wp, prm=prm))
        res = run_bass_kernel_spmd(ncC, insC, core_ids=cores)
        xT = [np.concatenate([res.results[2 * b]["yT"], res.results[2 * b + 1]["yT"]], axis=1) for b in range(B)]
        del res, insC, mix
    return np.ascontiguousarray(np.stack([t.T for t in xT], 0)).astype(np.float32)


def kernel(x, p, positions, norm_g, ple_norm_g, w_ple_gate, w_ple_proj,
           ev_w_in, ev_w_out, gdn_conv_w, gdn_A_log, gdn_dt_bias, gdn_norm_g,
           ssd_conv_w, ssd_conv_b, ssd_A_log, ssd_dt_bias, ssd_D, ssd_norm_g,
           od_w_in, od_w_out, ret_norm_g, lru_conv_w, lru_conv_b,
           lru_w_a, lru_b_a, lru_w_x, lru_b_x, lru_lambda, final_norm_g):
    f = lambda a: np.asarray(a, dtype=np.float32)
    x = f(x)
    B, S, D = x.shape
    depth = int(np.asarray(norm_g).shape[0])
    nc = _prog("fused", lambda: build_fused(S, depth))
    shared = {}
    for g in range(2):
        shared[f"con{g}"] = make_consts(g)
    dummy = np.zeros((D, 8), np.float32)
    for i in range(depth):
        j = i // 2
        for g in range(2):
            if i % 2 == 0:
                d = pack_even(g, dummy, f(norm_g)[i], f(ev_w_in)[j], f(gdn_conv_w)[j], f(gdn_A_log)[j],
                              f(gdn_dt_bias)[j], f(gdn_norm_g)[j], f(ssd_conv_w)[j], f(ssd_conv_b)[j], f(ssd_A_log)[j],
                              f(ssd_dt_bias)[j], f(ssd_D)[j], f(ssd_norm_g)[j])
                shared[f"hp{i}_{g}"] = d["hp"]
                shared[f"dbc{i}_{g}"] = d["dbc"]
            else:
                d = pack_odd(g, dummy, f(norm_g)[i], f(od_w_in)[j], f(ret_norm_g)[j], f(lru_conv_w)[j],
                             f(lru_conv_b)[j], f(lru_w_a)[j], f(lru_b_a)[j], f(lru_w_x)[j], f(lru_b_x)[j],
                             f(lru_lambda)[j], np.zeros(8, np.int32))
                shared[f"lruw{i}_{g}"] = d["lruw"]
            shared[f"w{i}_{g}"] = d["w"]
            shared[f"prm{i}_{g}"] = d["prm"]
        prm = np.zeros((128, 32), np.float32)
        prm[:, 0:16] = _pk(f(ple_norm_g)[i], 16)
        prm[:, 16:32] = _pk(f(final_norm_g), 16)
        shared[f"prmC{i}"] = prm
        shared[f"wo{i}"] = np.ascontiguousarray(f(ev_w_out)[j] if i % 2 == 0 else f(od_w_out)[j])
        shared[f"wg{i}"] = np.ascontiguousarray(f(w_ple_gate)[i])
        shared[f"wp{i}"] = np.ascontiguousarray(f(w_ple_proj)[i])
    ins = []
    for c in range(8):
        b = c % B
        d = dict(shared)
        d["xT"] = np.ascontiguousarray(x[b].T)
        d["pT"] = np.ascontiguousarray(np.concatenate([f(p)[i, b].T for i in range(depth)], axis=0))
        d["pos"] = np.ascontiguousarray(np.asarray(positions, np.int32)[b].reshape(1, -1))
        ins.append(d)
    res = run_bass_kernel_spmd(nc, ins, core_ids=list(range(8)))
    return np.ascontiguousarray(np.stack([res.results[b]["yT"].T for b in range(B)], 0)).astype(np.float32)
```

```python
import contextlib
import numpy as np
import concourse.bass as bass
import concourse.mybir as mybir
from concourse.bass_utils import run_bass_kernel_spmd

F32 = mybir.dt.float32
BF16 = mybir.dt.bfloat16
I32 = mybir.dt.int32
AF = mybir.ActivationFunctionType
ALU = mybir.AluOpType

D_MODEL = 2048
SEQ = 8192
EPS = 1e-6
NDMA = 16
SAME_ENGINE_SYNC = True


class _Tok:
    __slots__ = ("w", "rs")

    def __init__(self):
        self.w = None
        self.rs = []


class Prog:
    ENGS = ("pe", "dve", "act", "pool", "sp")

    def __init__(self, nc):
        self.nc = nc
        self.q = {e: [] for e in self.ENGS}
        self.cnt = {e: 0 for e in self.ENGS}
        self.seen = {e: {} for e in self.ENGS}
        self.toks = {}
        self.slot_uses = [0] * NDMA
        self.rr = 0
        self.stack = contextlib.ExitStack()
        self.stacks = [self.stack]
        self.pending = {e: {} for e in self.ENGS}
        self.nt = 0
        self.psum_ids = set()
        self.keep = []

    @contextlib.contextmanager
    def scope(self):
        st = contextlib.ExitStack()
        self.stacks.append(st)
        try:
            yield
        finally:
            self.stacks.pop()
            st.close()
            self.barrier()

    def barrier(self):
        for e in self.ENGS:
            pd = self.pending[e]
            for o in ("pe", "dve", "act", "pool"):
                if self.cnt[o] > 0:
                    pd[o] = self.cnt[o]
            for s in range(NDMA):
                if self.slot_uses[s] > 0:
                    pd[("d", s)] = 16 * self.slot_uses[s]

    def sb(self, shape, dt=F32, name=None):
        self.nt += 1
        t = self.stacks[-1].enter_context(self.nc.sbuf_tensor(name or f"t{self.nt}", list(shape), dt))
        self.keep.append(t)
        return t

    def ps(self, shape, dt=F32, name=None):
        self.nt += 1
        t = self.stacks[-1].enter_context(self.nc.psum_tensor(name or f"p{self.nt}", list(shape), dt))
        self.psum_ids.add(id(t))
        self.keep.append(t)
        return t

    def _tk(self, ref):
        if isinstance(ref, tuple):
            t, k = ref
        else:
            t, k = ref, None
        if id(t) in self.psum_ids:
            k = None
        d = self.toks.setdefault(id(t), {"_": _Tok()})
        return d, k

    def _deps(self, R, W, eng=None):
        need = {}

        def add(ev):
            if ev is not None:
                if need.get(ev[0], 0) < ev[1]:
                    need[ev[0]] = ev[1]

        for ref in R:
            d, k = self._tk(ref)
            add(d["_"].w)
            if k is None:
                for kk, tk in d.items():
                    add(tk.w)
            elif k in d:
                add(d[k].w)
            t_ = ref[0] if isinstance(ref, tuple) else ref
            if id(t_) in self.psum_ids:
                for ev in d["_"].rs:
                    if ev[0] != eng:
                        add(ev)
        for ref in W:
            d, k = self._tk(ref)
            keys = list(d.keys()) if k is None else (["_", k] if k in d else ["_"])
            for kk in keys:
                add(d[kk].w)
                for ev in d[kk].rs:
                    add(ev)
        return need

    def _upd(self, R, W, ev):
        for ref in R:
            d, k = self._tk(ref)
            tk = d["_"] if k is None else d.setdefault(k, _Tok())
            tk.rs.append(ev)
            if len(tk.rs) > 24:
                m = {}
                for e in tk.rs:
                    if m.get(e[0], 0) < e[1]:
                        m[e[0]] = e[1]
                tk.rs = list(m.items())
        for ref in W:
            d, k = self._tk(ref)
            if k is None:
                for kk in d:
                    d[kk].w = ev
                    d[kk].rs = []
            else:
                tk = d.setdefault(k, _Tok())
                tk.w = ev
                tk.rs = []

    def op(self, eng, fn, R=(), W=()):
        need = self._deps(R, W, eng)
        if self.pending[eng]:
            for k, v in self.pending[eng].items():
                if need.get(k, 0) < v:
                    need[k] = v
            self.pending[eng] = {}
        if eng == "sp":
            s = self.rr
            self.rr = (self.rr + 1) % NDMA
            key = ("d", s)
            if self.slot_uses[s] > 0:
                v = 16 * self.slot_uses[s]
                if need.get(key, 0) < v:
                    need[key] = v
            self.slot_uses[s] += 1
            ev = (key, 16 * self.slot_uses[s])
            inc = (key, 16)
        else:
            self.cnt[eng] += 1
            ev = (eng, self.cnt[eng])
            inc = (eng, 1)
        waits = []
        seen = self.seen[eng]
        for k, v in need.items():
            if k == eng and (eng == "pe" or not SAME_ENGINE_SYNC):
                continue
            if seen.get(k, 0) >= v:
                continue
            seen[k] = v
            waits.append((k, v))
        self.q[eng].append((waits, fn, inc))
        self._upd(R, W, ev)

    def mm(self, out, lhsT, rhs, start=True, stop=True, R=(), W=()):
        self.op("pe", lambda e: e.matmul(out, lhsT, rhs, start=start, stop=stop), R, W)

    def tr(self, out, in_, ident, R=(), W=()):
        self.op("pe", lambda e: e.transpose(out, in_, ident), R, W)

    def act(self, out, in_, func, R=(), W=(), bias=None, scale=None, eng="act"):
        kw = {}
        if bias is not None:
            kw["bias"] = bias
        if scale is not None:
            kw["scale"] = scale
        self.op(eng, lambda e: e.activation(out, in_, func, **kw), R, W)

    def tt(self, out, in0, in1, alu, R=(), W=(), eng="dve"):
        self.op(eng, lambda e: e.tensor_tensor(out, in0, in1, alu), R, W)

    def ts(self, out, in0, s1, s2, op0, op1=None, R=(), W=(), eng="dve"):
        if op1 is None:
            self.op(eng, lambda e: e.tensor_scalar(out, in0, s1, None, op0), R, W)
        else:
            self.op(eng, lambda e: e.tensor_scalar(out, in0, s1, s2, op0, op1), R, W)

    def stt(self, out, in0, scalar, in1, op0, op1, R=(), W=()):
        self.op("dve", lambda e: e.scalar_tensor_tensor(out, in0, scalar, in1, op0, op1), R, W)

    def cp(self, out, in_, R=(), W=(), eng="dve"):
        if eng == "act":
            self.op("act", lambda e: e.activation(out, in_, AF.Copy), R, W)
        else:
            self.op(eng, lambda e: e.tensor_copy(out, in_), R, W)

    def recip(self, out, in_, R=(), W=()):
        self.op("dve", lambda e: e.reciprocal(out, in_), R, W)

    def memset(self, ap, val, W=(), eng="pool"):
        self.op(eng, lambda e: e.memset(ap, val), (), W)

    def dma(self, out, in_, R=(), W=()):
        self.op("sp", lambda e: e.dma_start(out=out, in_=in_), R, W)

    def emit(self):
        nc = self.nc
        with contextlib.ExitStack() as es:
            sems = {}
            for e in ("pe", "dve", "act", "pool"):
                sems[e] = es.enter_context(nc.semaphore("s_" + e))
            for s in range(NDMA):
                sems[("d", s)] = es.enter_context(nc.semaphore(f"s_d{s}"))
            block = es.enter_context(nc.Block())

            def run(name):
                def f(eng):
                    for waits, fn, inc in self.q[name]:
                        for k, v in waits:
                            eng.wait_ge(sems[k], v)
                        fn(eng).then_inc(sems[inc[0]], inc[1])
                    if name == "sp":
                        for s in range(NDMA):
                            if self.slot_uses[s] > 0:
                                eng.wait_ge(sems[("d", s)], 16 * self.slot_uses[s])
                return f

            block.tensor(run("pe"))
            block.vector(run("dve"))
            block.scalar(run("act"))
            block.gpsimd(run("pool"))
            block.sync(run("sp"))
        self.stack.close()


def _new_nc():
    return bass.Bass("TRN2", target_bir_lowering=False)


def build_C(T, final, TT=256, ctx=None):
    KT = D_MODEL // 128
    if ctx is None:
        nc = _new_nc()
        mixT = nc.dram_tensor("mixT", [D_MODEL, T], F32, kind="ExternalInput").ap()
        xT = nc.dram_tensor("xT", [D_MODEL, T], F32, kind="ExternalInput").ap()
        pT = nc.dram_tensor("pT", [256, T], F32, kind="ExternalInput").ap()
        wo = nc.dram_tensor("wo", [D_MODEL, D_MODEL], F32, kind="ExternalInput").ap()
        wg = nc.dram_tensor("wg", [D_MODEL, D_MODEL], F32, kind="ExternalInput").ap()
        wp = nc.dram_tensor("wp", [256, D_MODEL], F32, kind="ExternalInput").ap()
        prm = nc.dram_tensor("prm", [128, 32], F32, kind="ExternalInput").ap()
        yT = nc.dram_tensor("yT", [D_MODEL, T], F32, kind="ExternalOutput").ap()
        p = Prog(nc)
        es = contextlib.ExitStack()
        es.enter_context(nc.allow_low_precision("bf16 matmul operands, fp32 accumulation"))
    else:
        nc, p = ctx["nc"], ctx["p"]
        mixT, xT, pT, wo, wg, wp, prm, yT = (ctx[k] for k in ("mixT", "xT", "pT", "wo", "wg", "wp", "prm", "yT"))
    prm_t = p.sb([128, 32])
    p.dma(prm_t[:, :], prm[:, :], W=[prm_t])
    ones_b = p.sb([128, 128], BF16)
    p.memset(ones_b[:, :], 1.0, W=[ones_b])
    eps_t = p.sb([128, 1])
    p.memset(eps_t[:, :], EPS, W=[eps_t])

    wo_b = p.sb([128, KT, D_MODEL], BF16)
    wg_b = p.sb([128, KT, D_MODEL], BF16)
    wp_b = p.sb([128, 2, D_MODEL], BF16)
    wst = [p.sb([128, 4, TT]) for _ in range(3)]
    i = 0
    for (src, dst, nk, gcol) in ((wo, wo_b, KT, None), (wg, wg_b, KT, 0), (wp, wp_b, 2, None)):
        for kt in range(nk):
            for c in range(2):
                st = wst[i % 3]
                i += 1
                p.dma(st[:, :, :], src[kt * 128:(kt + 1) * 128, c * 1024:(c + 1) * 1024].rearrange("p (c t) -> p c t", c=4), W=[st])
                dv = dst[:, kt, c * 1024:(c + 1) * 1024].rearrange("p (c t) -> p c t", c=4)
                if gcol is None:
                    p.cp(dv, st[:, :, :], R=[st], W=[(dst, (kt, c))], eng="act")
                else:
                    p.act(dv, st[:, :, :], AF.Copy, R=[st, prm_t], W=[(dst, (kt, c))],
                          scale=prm_t[:, gcol + kt:gcol + kt + 1])

    NTT = T // TT
    assert TT == 256
    mst = wst
    mb = [p.sb([128, KT, TT], BF16) for _ in range(1)]
    xt = [p.sb([128, KT, TT]) for _ in range(1)]
    xb = [p.sb([128, KT, TT], BF16) for _ in range(1)]
    sq = [p.sb([128, TT], BF16) for _ in range(3)]
    pst = p.sb([128, 2, TT])
    pb = p.sb([128, 2, TT], BF16)
    rs_t = p.sb([128, TT])
    rstd = p.sb([128, TT])
    gt = [p.sb([128, TT]) for _ in range(2)]
    g2 = [p.sb([128, TT]) for _ in range(2)]
    acc = [p.ps([128, 512]) for _ in range(3)]
    acc2 = [p.ps([128, 512]) for _ in range(2)]
    ssq = p.ps([128, 512])
    ci = 0
    for tt in range(NTT):
        tsl = slice(tt * TT, (tt + 1) * TT)
        m_b = mb[0]
        x_t = xt[0]
        x_b = xb[0]
        for kg in range(4):
            st = mst[ci % 3]
            ci += 1
            p.dma(st[:, :, :], mixT[kg * 512:(kg + 1) * 512, tsl].rearrange("(k p) t -> p k t", p=128), W=[st])
            p.cp(m_b[:, kg * 4:(kg + 1) * 4, :], st[:, :, :], R=[st], W=[(m_b, kg)], eng="pool")
        p.dma(x_t[:, :, :], xT[:, tsl].rearrange("(k p) t -> p k t", p=128), W=[x_t])
        p.dma(pst[:, :, :], pT[:, tsl].rearrange("(k p) t -> p k t", p=128), W=[pst])
        p.cp(pb[:, :, :], pst[:, :, :], R=[pst], W=[pb], eng="pool")
        for dc in range(KT):
            a = acc[dc % 3]
            for kt in range(KT):
                p.mm(a[:, 0:TT], wo_b[:, kt, dc * 128:(dc + 1) * 128], m_b[:, kt, :], start=(kt == 0), stop=(kt == KT - 1),
                     R=[wo_b, m_b], W=[a])
            p.tt(x_t[:, dc, :], a[:, 0:TT], x_t[:, dc, :], ALU.add, R=[a, (x_t, dc)], W=[(x_t, dc)])
            s = sq[dc % 3]
            p.act(s[:, :], x_t[:, dc, :], AF.Square, R=[(x_t, dc)], W=[s])
            p.mm(ssq[:, 0:TT], ones_b[:, :], s[:, :], start=(dc == 0), stop=(dc == KT - 1), R=[ones_b, s], W=[ssq])
            p.cp(x_b[:, dc, :], x_t[:, dc, :], R=[(x_t, dc)], W=[(x_b, dc)], eng="pool")
        p.act(rs_t[:, :], ssq[:, 0:TT], AF.Sqrt, R=[ssq, eps_t], W=[rs_t], bias=eps_t[:, 0:1], scale=1.0 / D_MODEL)
        p.recip(rstd[:, :], rs_t[:, :], R=[rs_t], W=[rstd])
        for dc in range(KT):
            a = acc[dc % 3]
            for kt in range(KT):
                p.mm(a[:, 0:TT], wg_b[:, kt, dc * 128:(dc + 1) * 128], x_b[:, kt, :], start=(kt == 0), stop=(kt == KT - 1),
                     R=[wg_b, x_b], W=[a])
            a2 = acc2[dc % 2]
            for kt in range(2):
                p.mm(a2[:, 0:TT], wp_b[:, kt, dc * 128:(dc + 1) * 128], pb[:, kt, :], start=(kt == 0), stop=(kt == 1),
                     R=[wp_b, pb], W=[a2])
            g = gt[dc % 2]
            gg = g2[dc % 2]
            p.tt(g[:, :], a[:, 0:TT], rstd[:, :], ALU.mult, R=[a, rstd], W=[g])
            p.act(gg[:, :], g[:, :], AF.Sigmoid, R=[g], W=[gg])
            p.tt(g[:, :], a2[:, 0:TT], gg[:, :], ALU.mult, R=[a2, gg], W=[g])
            p.tt(x_t[:, dc, :], x_t[:, dc, :], g[:, :], ALU.add, R=[g, (x_t, dc)], W=[(x_t, dc)], eng="pool")
        if final:
            for dc in range(KT):
                s = sq[dc % 3]
                p.act(s[:, :], x_t[:, dc, :], AF.Square, R=[(x_t, dc)], W=[s])
                p.mm(ssq[:, 0:TT], ones_b[:, :], s[:, :], start=(dc == 0), stop=(dc == KT - 1), R=[ones_b, s], W=[ssq])
            p.act(rs_t[:, :], ssq[:, 0:TT], AF.Sqrt, R=[ssq, eps_t], W=[rs_t], bias=eps_t[:, 0:1], scale=1.0 / D_MODEL)
            p.recip(rstd[:, :], rs_t[:, :], R=[rs_t], W=[rstd])
            for dc in range(KT):
                p.stt(x_t[:, dc, :], x_t[:, dc, :], prm_t[:, 16 + dc:17 + dc], rstd[:, :], ALU.mult, ALU.mult,
                      R=[(x_t, dc), prm_t, rstd], W=[(x_t, dc)])
        p.dma(yT[:, tsl].rearrange("(k p) t -> p k t", p=128), x_t[:, :, :], R=[x_t], W=[])
    if ctx is not None:
        return None
    p.emit()
    es.close()
    return nc


def phase_A(p, xT, w_loc, NCOL, projT, prm_t, T, ones_b, eps_t):
    KT = D_MODEL // 128
    TT = 512
    with p.scope():
        wb = p.sb([128, KT, NCOL], BF16)
        wst = [p.sb([128, 512]) for _ in range(3)]
        i = 0
        for kt in range(KT):
            for c0 in range(0, NCOL, 512):
                c1 = min(NCOL, c0 + 512)
                st = wst[i % 3]
                i += 1
                p.dma(st[:, 0:c1 - c0], w_loc[kt * 128:(kt + 1) * 128, c0:c1], W=[st])
                p.act(wb[:, kt, c0:c1], st[:, 0:c1 - c0], AF.Copy, R=[st, prm_t], W=[(wb, (kt, c0))],
                      scale=prm_t[:, kt:kt + 1])
        xst = [p.sb([128, 4, TT]) for _ in range(3)]
        xb = [p.sb([128, KT, TT], BF16) for _ in range(2)]
        sq = [p.sb([128, 4, TT], BF16) for _ in range(2)]
        rs_t = p.sb([128, TT])
        rstd = [p.sb([128, TT]) for _ in range(2)]
        ost = [p.sb([128, TT]) for _ in range(4)]
        acc = [p.ps([128, 512]) for _ in range(4)]
        ssq = p.ps([128, 512])
        ci = 0
        oi = 0
        nct = (NCOL + 127) // 128
        for tt in range(T // TT):
            tsl = slice(tt * TT, (tt + 1) * TT)
            x_b = xb[tt % 2]
            rsd = rstd[tt % 2]
            for kg in range(4):
                st = xst[ci % 3]
                s2 = sq[ci % 2]
                ci += 1
                p.dma(st[:, :, :], xT[kg * 512:(kg + 1) * 512, tsl].rearrange("(k p) t -> p k t", p=128), W=[st])
                p.act(s2[:, :, :], st[:, :, :], AF.Square, R=[st], W=[s2])
                p.cp(x_b[:, kg * 4:(kg + 1) * 4, :], st[:, :, :], R=[st], W=[(x_b, kg)], eng="pool")
                for k in range(4):
                    p.mm(ssq[:, :], ones_b[:, :], s2[:, k, :], start=(kg == 0 and k == 0), stop=(kg == 3 and k == 3),
                         R=[ones_b, s2], W=[ssq])
            p.act(rs_t[:, :], ssq[:, :], AF.Sqrt, R=[ssq, eps_t], W=[rs_t], bias=eps_t[:, 0:1], scale=1.0 / D_MODEL)
            p.recip(rsd[:, :], rs_t[:, :], R=[rs_t], W=[rsd])
            for ct in range(nct):
                m = min(128, NCOL - ct * 128)
                a = acc[ct % 4]
                for kt in range(KT):
                    p.mm(a[0:m, :], wb[:, kt, ct * 128:ct * 128 + m], x_b[:, kt, :], start=(kt == 0), stop=(kt == KT - 1),
                         R=[wb, x_b], W=[a])
                o = ost[oi % 4]
                oi += 1
                p.tt(o[0:m, :], a[0:m, :], rsd[0:m, :], ALU.mult, R=[a, rsd], W=[o])
                p.dma(projT[ct * 128:ct * 128 + m, tsl], o[0:m, :], R=[o])


def _con_layout():
    lay = {}
    off = 0
    for name, w in (("ident", 128), ("ones", 128), ("rotm", 128), ("maskT", 128), ("maskS", 128),
                    ("invf", 128), ("dmT", 512), ("qdec", 512), ("kdec", 4), ("glb", 4),
                    ("selbc4", 4 * 128), ("selbc8", 8 * 128), ("selc", 288), ("cmask", 512), ("elast", 128)):
        lay[name] = (off, w)
        off += w
    return lay, off


def make_consts(g):
    lay, n = _con_layout()
    c = np.zeros((128, n), np.float32)

    def put(name, arr):
        o, w = lay[name]
        c[:arr.shape[0], o:o + w] = arr.reshape(arr.shape[0], -1)

    put("ident", np.eye(128, dtype=np.float32))
    put("ones", np.ones((128, 128), np.float32))
    rot = np.zeros((128, 128), np.float32)
    for d in range(64):
        rot[d + 64, d] = -1.0
        rot[d, d + 64] = 1.0
    put("rotm", rot)
    idx = np.arange(128)
    put("maskT", np.where(idx[None, :] >= idx[:, None], 0.0, -1e30).astype(np.float32))
    put("maskS", np.where(idx[None, :] < idx[:, None], 0.0, -1e30).astype(np.float32))
    invf = (10000.0 ** (-np.arange(0, 128, 2, dtype=np.float32) / np.float32(128))).astype(np.float32)
    put("invf", np.concatenate([invf, invf])[None, :])
    heads = np.arange(4) + 4 * g
    lg = np.log1p(-np.exp2(-5.0 - heads.astype(np.float32))).astype(np.float32)
    sc = np.float32(128 ** -0.5)
    rel = (idx[None, :] - idx[:, None]).astype(np.float32)
    dmT = np.where((rel >= 0)[:, None, :], np.exp(np.maximum(rel, 0.0)[:, None, :] * lg[None, :, None]), 0.0) * sc
    put("dmT", dmT.astype(np.float32))
    qdec = np.exp((idx + 1.0)[None, None, :] * lg[None, :, None]) * np.ones((128, 1, 1))
    put("qdec", qdec.astype(np.float32))
    kdec = np.exp((127.0 - idx)[:, None] * lg[None, :]) * sc
    put("kdec", kdec.astype(np.float32))
    put("glb", (np.exp(128.0 * lg)[None, :] * np.ones((128, 1))).astype(np.float32))
    s4 = np.zeros((4, 4, 128), np.float32)
    for h in range(4):
        s4[h, h, :] = 1.0
    put("selbc4", s4)
    s8 = np.zeros((8, 8, 128), np.float32)
    for h in range(8):
        s8[h, h, :] = 1.0
    put("selbc8", s8)
    sc_ = np.zeros((8, 6, 48), np.float32)
    for h in range(8):
        for q_ in range(6):
            sc_[h, q_, q_ * 8 + h] = 1.0
    put("selc", sc_)
    cm = np.ones((8, 512), np.float32)
    cm[:, 0::128] = 0.0
    put("cmask", cm)
    el = np.zeros((128, 128), np.float32)
    el[127, :] = 1.0
    put("elast", el)
    return c


def _cv(con, lay, name, rows=128):
    o, w = lay[name]
    return con[0:rows, o:o + w]


def build_AB_odd(T, ctx=None):
    NCOL = 3072
    lay, ncon = _con_layout()
    if ctx is None:
        nc = _new_nc()
        xT = nc.dram_tensor("xT", [D_MODEL, T], F32, kind="ExternalInput").ap()
        w_loc = nc.dram_tensor("w", [D_MODEL, NCOL], F32, kind="ExternalInput").ap()
        prm = nc.dram_tensor("prm", [128, 64], F32, kind="ExternalInput").ap()
        con_d = nc.dram_tensor("con", [128, ncon], F32, kind="ExternalInput").ap()
        lruw_d = nc.dram_tensor("lruw", [128, 8 * 128], F32, kind="ExternalInput").ap()
        pos_d = nc.dram_tensor("pos", [1, T], I32, kind="ExternalInput").ap()
        mixT = nc.dram_tensor("mixT", [1024, T], F32, kind="ExternalOutput").ap()
        projT = nc.dram_tensor("projT", [NCOL, T], F32, kind="Internal").ap()
        mixA, mixB = mixT[0:512, :], mixT[512:1024, :]
        p = Prog(nc)
        es = contextlib.ExitStack()
        es.enter_context(nc.allow_low_precision("bf16 matmul operands, fp32 accumulation"))
    else:
        nc, p = ctx["nc"], ctx["p"]
        xT, w_loc, prm, con_d, lruw_d, pos_d, projT, mixA, mixB = (ctx[k] for k in ("xT", "w", "prm", "con", "lruw", "pos", "projT", "mixA", "mixB"))
    prm_t = p.sb([128, 64])
    p.dma(prm_t[:, :], prm[:, :], W=[prm_t])
    con = p.sb([128, ncon])
    p.dma(con[:, :], con_d[:, :], W=[con])
    ones_b = p.sb([128, 128], BF16)
    p.memset(ones_b[:, :], 1.0, W=[ones_b])
    eps_t = p.sb([128, 1])
    p.memset(eps_t[:, :], EPS, W=[eps_t])
    one_t = p.sb([128, 1])
    p.memset(one_t[:, :], 1.0, W=[one_t])
    phase_A(p, xT, w_loc, NCOL, projT, prm_t, T, ones_b, eps_t)

    ident = _cv(con, lay, "ident")
    ones_f = _cv(con, lay, "ones")
    PI = float(np.pi)
    C1 = 6.28125
    C2 = float(2 * np.pi - 6.28125)
    with p.scope():
        lruw = p.sb([128, 8, 128])
        p.dma(lruw[:, :, :], lruw_d.rearrange("p (n j) -> p n j", n=8), W=[lruw])
        sp_t = p.sb([128, 4])
        m8 = p.sb([128, 4])
        m16 = p.sb([128, 4])
        p.act(sp_t[:, :], prm_t[:, 48:52], AF.Exp, R=[prm_t], W=[sp_t], scale=-1.0)
        p.act(sp_t[:, :], sp_t[:, :], AF.Ln, R=[sp_t, one_t], W=[sp_t], bias=one_t[:, 0:1])
        p.ts(m8[:, :], sp_t[:, :], -8.0, None, ALU.mult, R=[sp_t], W=[m8])
        p.ts(m16[:, :], sp_t[:, :], -16.0, None, ALU.mult, R=[sp_t], W=[m16])
        hst = p.sb([128, 4])
        p.memset(hst[:, :], 0.0, W=[hst])
        S4 = p.sb([128, 4, 128])
        p.memset(S4[:, :, :], 0.0, W=[S4])

        q4 = p.sb([128, 4, 512])
        k4 = p.sb([128, 4, 512])
        v4 = p.sb([128, 4, 512])
        g4 = p.sb([128, 4, 512])
        qp4 = p.sb([128, 4, 512])
        kp4 = p.sb([128, 4, 512])
        t14 = p.sb([128, 4, 512])
        o4 = p.sb([128, 4, 512])
        posi = p.sb([1, 512], I32)
        posf = p.sb([1, 512])
        ang = p.sb([128, 512])
        kf = p.sb([128, 512])
        ki = p.sb([128, 512], I32)
        yy = p.sb([128, 512])
        mm_ = p.sb([128, 512])
        sinT = p.sb([128, 512])
        cosT = p.sb([128, 512])
        attnT = p.sb([128, 4, 128])
        qg = p.sb([128, 4, 128])
        kd = p.sb([128, 4, 128])
        Vt = p.sb([128, 4, 128])
        rs1 = p.sb([128, 512])
        rinv = p.sb([128, 512])
        xd_ = [p.sb([128, 515]) for _ in range(2)]
        zt_ = [p.sb([128, 512]) for _ in range(2)]
        xc_ = [p.sb([128, 512]) for _ in range(2)]
        rr_ = [p.sb([128, 512]) for _ in range(2)]
        ii_ = [p.sb([128, 512]) for _ in range(2)]
        aa_ = [p.sb([128, 512]) for _ in range(2)]
        a2_ = [p.sb([128, 512]) for _ in range(2)]
        hh_ = [p.sb([128, 512]) for _ in range(2)]
        pA = p.ps([128, 512])
        pB = p.ps([128, 512])
        pG = p.ps([128, 512])
        pK = p.ps([128, 512])
        pV = p.ps([128, 512])
        pO = p.ps([128, 512])
        pD = p.ps([128, 512])
        dmT = _cv(con, lay, "dmT").rearrange("p (h c) -> p h c", h=4)
        qdec = _cv(con, lay, "qdec").rearrange("p (h c) -> p h c", h=4)
        kdec = _cv(con, lay, "kdec")
        glb = _cv(con, lay, "glb")
        rotm = _cv(con, lay, "rotm")
        invf = _cv(con, lay, "invf", 1)
        def ret_chunk(c4):
                    cs = slice(c4 * 128, (c4 + 1) * 128)
                    for h in range(4):
                        hs = slice(h * 128, (h + 1) * 128)
                        p.mm(pG[:, hs], kp4[:, h, cs], qp4[:, h, cs], R=[kp4, qp4], W=[(pG, h)])
                        p.tr(pK[:, hs], kp4[:, h, cs], ident, R=[kp4, con], W=[(pK, h)])
                        p.tr(pV[:, hs], v4[:, h, cs], ident, R=[v4, con], W=[(pV, h)])
                    yield
                    p.tt(attnT[:, :, :], pG[:, :].rearrange("p (h c) -> p h c", h=4), dmT, ALU.mult, R=[pG, con], W=[attnT])
                    p.tt(qg[:, :, :], qp4[:, :, cs], qdec, ALU.mult, R=[qp4, con], W=[qg], eng="pool")
                    for h in range(4):
                        hs = slice(h * 128, (h + 1) * 128)
                        p.act(kd[:, h, :], pK[:, hs], AF.Copy, R=[(pK, h), con], W=[(kd, h)], scale=kdec[:, h:h + 1])
                    p.cp(Vt[:, :, :], pV[:, :].rearrange("p (h c) -> p h c", h=4), R=[pV], W=[Vt], eng="act")
                    yield
                    for h in range(4):
                        hs = slice(h * 128, (h + 1) * 128)
                        p.mm(pO[:, hs], S4[:, h, :], qg[:, h, :], start=True, stop=False, R=[(S4, h), qg], W=[(pO, h)])
                        p.mm(pO[:, hs], Vt[:, h, :], attnT[:, h, :], start=False, stop=True, R=[Vt, attnT], W=[(pO, h)])
                        p.mm(pD[:, hs], kd[:, h, :], Vt[:, h, :], R=[(kd, h), Vt], W=[(pD, h)])
                    yield
                    p.cp(o4[:, :, cs], pO[:, :].rearrange("p (h c) -> p h c", h=4), R=[pO], W=[(o4, c4)], eng="act")
                    for h in range(4):
                        hs = slice(h * 128, (h + 1) * 128)
                        p.stt(S4[:, h, :], S4[:, h, :], glb[:, h:h + 1], pD[:, hs], ALU.mult, ALU.add,
                              R=[(S4, h), (pD, h), con], W=[(S4, h)])
        def lru_tile(ct):
                    r0 = 2048 + ct * 128
                    if blk == 0:
                        p.memset(xd_[ct % 2][:, 0:3], 0.0, W=[xd_[ct % 2]])
                        p.dma(xd_[ct % 2][:, 3:515], projT[r0:r0 + 128, 0:512], W=[xd_[ct % 2]])
                    else:
                        p.dma(xd_[ct % 2][:, :], projT[r0:r0 + 128, blk * 512 - 3:(blk + 1) * 512], W=[xd_[ct % 2]])
                    p.dma(zt_[ct % 2][:, :], projT[r0 + 512:r0 + 640, bsl], W=[zt_[ct % 2]])
                    cw = 20 + ct * 4
                    p.ts(xc_[ct % 2][:, :], xd_[ct % 2][:, 0:512], prm_t[:, cw:cw + 1], prm_t[:, 36 + ct:37 + ct], ALU.mult, ALU.add,
                         R=[xd_[ct % 2], prm_t], W=[xc_[ct % 2]])
                    for j in range(1, 4):
                        p.stt(xc_[ct % 2][:, :], xd_[ct % 2][:, j:j + 512], prm_t[:, cw + j:cw + j + 1], xc_[ct % 2][:, :], ALU.mult, ALU.add,
                              R=[xd_[ct % 2], prm_t, xc_[ct % 2]], W=[xc_[ct % 2]])
                    yield
                    p.mm(pA[:, :], lruw[:, ct, :], xc_[ct % 2][:, :], R=[lruw, xc_[ct % 2]], W=[pA])
                    p.mm(pB[:, :], lruw[:, 4 + ct, :], xc_[ct % 2][:, :], R=[lruw, xc_[ct % 2]], W=[pB])
                    p.act(rr_[ct % 2][:, :], pA[:, :], AF.Sigmoid, R=[pA, prm_t], W=[rr_[ct % 2]], bias=prm_t[:, 40 + ct:41 + ct])
                    p.act(ii_[ct % 2][:, :], pB[:, :], AF.Sigmoid, R=[pB, prm_t], W=[ii_[ct % 2]], bias=prm_t[:, 44 + ct:45 + ct])
                    yield
                    p.act(aa_[ct % 2][:, :], rr_[ct % 2][:, :], AF.Exp, R=[rr_[ct % 2], m8], W=[aa_[ct % 2]], scale=m8[:, ct:ct + 1])
                    p.act(a2_[ct % 2][:, :], rr_[ct % 2][:, :], AF.Exp, R=[rr_[ct % 2], m16], W=[a2_[ct % 2]], scale=m16[:, ct:ct + 1])
                    p.act(a2_[ct % 2][:, :], a2_[ct % 2][:, :], AF.Sqrt, R=[a2_[ct % 2], one_t], W=[a2_[ct % 2]], bias=one_t[:, 0:1], scale=-1.0)
                    p.tt(ii_[ct % 2][:, :], ii_[ct % 2][:, :], a2_[ct % 2][:, :], ALU.mult, R=[ii_[ct % 2], a2_[ct % 2]], W=[ii_[ct % 2]])
                    p.tt(ii_[ct % 2][:, :], ii_[ct % 2][:, :], xc_[ct % 2][:, :], ALU.mult, R=[ii_[ct % 2], xc_[ct % 2]], W=[ii_[ct % 2]], eng="pool")
                    yield
                    p.op("dve", lambda e, ct=ct: e.tensor_tensor_scan(hh_[ct % 2][:, :], aa_[ct % 2][:, :], ii_[ct % 2][:, :], hst[:, ct:ct + 1], ALU.mult, ALU.add),
                         R=[aa_[ct % 2], ii_[ct % 2], hst], W=[hh_[ct % 2]])
                    p.cp(hst[:, ct:ct + 1], hh_[ct % 2][:, 511:512], R=[hh_[ct % 2]], W=[hst])
                    p.act(zt_[ct % 2][:, :], zt_[ct % 2][:, :], AF.Silu, R=[zt_[ct % 2]], W=[zt_[ct % 2]])
                    p.tt(hh_[ct % 2][:, :], hh_[ct % 2][:, :], zt_[ct % 2][:, :], ALU.mult, R=[hh_[ct % 2], zt_[ct % 2]], W=[hh_[ct % 2]], eng="pool")
                    p.dma(mixB[ct * 128:(ct + 1) * 128, bsl], hh_[ct % 2][:, :], R=[hh_[ct % 2]])

        for blk in range(T // 512):
            bsl = slice(blk * 512, (blk + 1) * 512)
            for (dst, r0) in ((q4, 0), (k4, 512), (v4, 1024), (g4, 1536)):
                p.dma(dst[:, :, :], projT[r0:r0 + 512, bsl].rearrange("(h p) t -> p h t", p=128), W=[dst])
            p.dma(posi[:, :], pos_d[:, bsl], W=[posi])
            p.cp(posf[:, :], posi[:, :], R=[posi], W=[posf])
            p.mm(pA[:, :], invf, posf[0:1, :], R=[con, posf], W=[pA])
            p.cp(ang[:, :], pA[:, :], R=[pA], W=[ang], eng="act")
            p.ts(kf[:, :], ang[:, :], float(1.0 / (2 * np.pi)), None, ALU.mult, R=[ang], W=[kf])
            p.cp(ki[:, :], kf[:, :], R=[kf], W=[ki])
            p.cp(kf[:, :], ki[:, :], R=[ki], W=[kf])
            p.stt(ang[:, :], kf[:, :], -C1, ang[:, :], ALU.mult, ALU.add, R=[kf, ang], W=[ang])
            p.stt(ang[:, :], kf[:, :], -C2, ang[:, :], ALU.mult, ALU.add, R=[kf, ang], W=[ang])
            for (dstT, shift) in ((sinT, 0.0), (cosT, PI / 2)):
                p.ts(yy[:, :], ang[:, :], shift, None, ALU.add, R=[ang], W=[yy])
                p.ts(mm_[:, :], yy[:, :], PI, 2 * PI, ALU.is_gt, ALU.mult, R=[yy], W=[mm_])
                p.tt(yy[:, :], yy[:, :], mm_[:, :], ALU.subtract, R=[yy, mm_], W=[yy])
                p.ts(mm_[:, :], yy[:, :], -PI, 2 * PI, ALU.is_lt, ALU.mult, R=[yy], W=[mm_])
                p.tt(yy[:, :], yy[:, :], mm_[:, :], ALU.add, R=[yy, mm_], W=[yy])
                p.ts(yy[:, :], yy[:, :], -PI, PI, ALU.max, ALU.min, R=[yy], W=[yy])
                p.act(dstT[:, :], yy[:, :], AF.Sin, R=[yy], W=[dstT])
            for (src4, dst4) in ((q4, qp4), (k4, kp4)):
                for h in range(4):
                    pr = pA if h % 2 == 0 else pB
                    p.mm(pr[:, :], rotm, src4[:, h, :], R=[con, src4], W=[pr])
                    p.tt(dst4[:, h, :], pr[:, :], sinT[:, :], ALU.mult, R=[pr, sinT], W=[(dst4, h)])
                    p.tt(t14[:, h, :], src4[:, h, :], cosT[:, :], ALU.mult, R=[src4, cosT], W=[(t14, h)], eng="pool")
                p.tt(dst4[:, :, :], dst4[:, :, :], t14[:, :, :], ALU.add, R=[dst4, t14], W=[dst4], eng="pool")
            for c4 in range(4):
                rg, lg = ret_chunk(c4), lru_tile(c4)
                r_done = l_done = False
                while not (r_done and l_done):
                    if not r_done:
                        try:
                            next(rg)
                        except StopIteration:
                            r_done = True
                    if not l_done:
                        try:
                            next(lg)
                        except StopIteration:
                            l_done = True
            p.act(t14[:, :, :], o4[:, :, :], AF.Square, R=[o4], W=[t14])
            p.act(g4[:, :, :], g4[:, :, :], AF.Silu, R=[g4], W=[g4])
            for h in range(4):
                pr = pA if h % 2 == 0 else pB
                p.mm(pr[:, :], ones_f, t14[:, h, :], R=[con, t14], W=[pr])
                p.act(rs1[:, :], pr[:, :], AF.Sqrt, R=[pr, eps_t], W=[rs1], bias=eps_t[:, 0:1], scale=1.0 / 128)
                p.recip(rinv[:, :], rs1[:, :], R=[rs1], W=[rinv])
                p.stt(o4[:, h, :], o4[:, h, :], prm_t[:, 16 + h:17 + h], rinv[:, :], ALU.mult, ALU.mult,
                      R=[o4, prm_t, rinv], W=[(o4, ("n", h))])
            p.tt(o4[:, :, :], o4[:, :, :], g4[:, :, :], ALU.mult, R=[o4, g4], W=[o4], eng="pool")
            p.dma(mixA[:, bsl].rearrange("(h p) t -> p h t", p=128), o4[:, :, :], R=[o4])
    if ctx is not None:
        return None
    p.emit()
    es.close()
    return nc


def bcast(ap, n):
    sh = list(ap.shape)
    return ap.unsqueeze(len(sh)).broadcast_to(sh + [n])


def build_AB_even(T, ctx=None):
    NCOL = 3344
    lay, ncon = _con_layout()
    if ctx is None:
        nc = _new_nc()
        xT = nc.dram_tensor("xT", [D_MODEL, T], F32, kind="ExternalInput").ap()
        w_loc = nc.dram_tensor("w", [D_MODEL, NCOL], F32, kind="ExternalInput").ap()
        prm = nc.dram_tensor("prm", [128, 112], F32, kind="ExternalInput").ap()
        hp_d = nc.dram_tensor("hp", [8, 8], F32, kind="ExternalInput").ap()
        dbc_d = nc.dram_tensor("dbc", [128, 512], F32, kind="ExternalInput").ap()
        con_d = nc.dram_tensor("con", [128, ncon], F32, kind="ExternalInput").ap()
        mixT = nc.dram_tensor("mixT", [1024, T], F32, kind="ExternalOutput").ap()
        projT = nc.dram_tensor("projT", [NCOL, T], F32, kind="Internal").ap()
        mixA, mixB = mixT[0:512, :], mixT[512:1024, :]
        p = Prog(nc)
        es = contextlib.ExitStack()
        es.enter_context(nc.allow_low_precision("bf16 matmul operands, fp32 accumulation"))
    else:
        nc, p = ctx["nc"], ctx["p"]
        xT, w_loc, prm, hp_d, dbc_d, con_d, projT, mixA, mixB = (ctx[k] for k in ("xT", "w", "prm", "hp", "dbc", "con", "projT", "mixA", "mixB"))
    prm_t = p.sb([128, 112])
    p.dma(prm_t[:, :], prm[:, :], W=[prm_t])
    con = p.sb([128, ncon])
    p.dma(con[:, :], con_d[:, :], W=[con])
    hp = p.sb([8, 8])
    p.dma(hp[:, :], hp_d[:, :], W=[hp])
    ones_b = p.sb([128, 128], BF16)
    p.memset(ones_b[:, :], 1.0, W=[ones_b])
    eps_t = p.sb([128, 1])
    p.memset(eps_t[:, :], EPS, W=[eps_t])
    one_t = p.sb([128, 1])
    p.memset(one_t[:, :], 1.0, W=[one_t])
    phase_A(p, xT, w_loc, NCOL, projT, prm_t, T, ones_b, eps_t)

    ident = _cv(con, lay, "ident")
    ones_f = _cv(con, lay, "ones")
    maskT = _cv(con, lay, "maskT")
    maskS = _cv(con, lay, "maskS")
    selbc4 = _cv(con, lay, "selbc4", 4).rearrange("p (h c) -> p h c", h=4)
    selbc8 = _cv(con, lay, "selbc8", 8).rearrange("p (h c) -> p h c", h=8)
    selc = _cv(con, lay, "selc", 8).rearrange("p (q c) -> p q c", q=6)
    cmask = _cv(con, lay, "cmask", 8)
    SC = float(128 ** -0.5)

    def v4(ps_t):
        return ps_t[:, :].rearrange("p (h c) -> p h c", h=4)

    with p.scope():
        banks = [p.ps([128, 512]) for _ in range(6)]
        pBig = p.ps([128, 1024])
        bi = [0]

        def bank():
            b = banks[bi[0] % 6]
            bi[0] += 1
            return b

        dbc = p.sb([128, 512])
        p.dma(dbc[:, :], dbc_d[:, :], W=[dbc])
        negA = p.sb([8, 2])
        p.memset(negA[:, :], 0.0, W=[negA])
        p.act(negA[0:4, 0:1], hp[0:4, 0:1], AF.Exp, R=[hp], W=[negA])
        p.act(negA[0:8, 1:2], hp[0:8, 2:3], AF.Exp, R=[hp], W=[negA])
        p.ts(negA[:, :], negA[:, :], -1.0, None, ALU.mult, R=[negA], W=[negA])
        S4 = p.sb([128, 4, 128])
        p.memset(S4[:, :, :], 0.0, W=[S4])
        Hs = p.sb([128, 512])
        p.memset(Hs[:, :], 0.0, W=[Hs])

        qh = p.sb([128, 4, 515])
        kh = p.sb([128, 4, 515])
        vh = p.sb([128, 4, 515])
        z4 = p.sb([128, 4, 512])
        qc = p.sb([128, 4, 512])
        kc = p.sb([128, 4, 512])
        vc = p.sb([128, 4, 512])
        sq4 = p.sb([128, 4, 512])
        o4 = p.sb([128, 4, 512])
        rs1 = p.sb([128, 512])
        rinv = p.sb([128, 512])
        a_r = p.sb([8, 512])
        b_r = p.sb([8, 512])
        g_r = p.sb([8, 512])
        gc_r = p.sb([8, 512])
        be_r = p.sb([8, 512])
        bg_r = p.sb([8, 512])
        colt = p.sb([128, 48])
        ncol = p.sb([128, 8])
        gcbc = p.sb([128, 4, 128])
        egcbc = p.sb([128, 4, 128])
        T1 = p.sb([128, 4, 128])
        DmT = p.sb([128, 4, 128])
        Dm = p.sb([128, 4, 128])
        kdec = p.sb([128, 4])
        kd = p.sb([128, 4, 128])
        Vb = p.sb([128, 4, 128])
        attnT = p.sb([128, 4, 128])
        qg = p.sb([128, 4, 128])
        Pm = [p.sb([128, 4, 128]) for _ in range(2)]
        Ym = [p.sb([128, 4, 128]) for _ in range(2)]
        Am = p.sb([128, 4, 128])
        Xm = p.sb([128, 4, 128])
        Vn = p.sb([128, 4, 128])
        rs1s = [rs1, p.sb([128, 512])]
        rinvs = [rinv, p.sb([128, 512])]
        o4s = p.sb([128, 4, 512])
        a_rs = p.sb([8, 512])
        g_rs = p.sb([8, 512])
        gc_rs = p.sb([8, 512])
        be_rs = p.sb([8, 512])
        colts = p.sb([128, 48])
        xsh = p.sb([128, 4, 515])
        Bh = p.sb([128, 515])
        Ch = p.sb([128, 515])
        xs4 = p.sb([128, 4, 512])
        Bc = p.sb([128, 512])
        Cc = p.sb([128, 512])
        csb = p.sb([128, 8, 128])
        lmT = p.sb([128, 8, 128])
        ecs = p.sb([128, 8])
        dect = p.sb([128, 8])
        glb8 = p.sb([128, 8])
        xst = p.sb([128, 512])
        xct = p.sb([128, 512])
        xcd = p.sb([128, 512])
        yt = p.sb([128, 512])
        Bt = p.sb([128, 128])

        def conv_tile(dst, src, wcol, bcol):
            if bcol is None:
                p.ts(dst, src[0], prm_t[:, wcol:wcol + 1], None, ALU.mult, R=[src[4], prm_t], W=[src[5]])
            else:
                p.ts(dst, src[0], prm_t[:, wcol:wcol + 1], prm_t[:, bcol:bcol + 1], ALU.mult, ALU.add,
                     R=[src[4], prm_t], W=[src[5]])
            for j in range(1, 4):
                p.stt(dst, src[j], prm_t[:, wcol + j:wcol + j + 1], dst, ALU.mult, ALU.add, R=[src[4], prm_t, src[5]], W=[src[5]])
            p.act(dst, dst, AF.Silu, R=[src[5]], W=[src[5]])

        def load_halo(dst, r0, nrow, blk, multi):
            if multi:
                d0 = dst[:, :, 0:3]
                dr = dst[:, :, 3:515]
                da = dst[:, :, :]
                src = lambda c0, c1: projT[r0:r0 + nrow, c0:c1].rearrange("(k p) t -> p k t", p=128)
            else:
                d0 = dst[:, 0:3]
                dr = dst[:, 3:515]
                da = dst[:, :]
                src = lambda c0, c1: projT[r0:r0 + nrow, c0:c1]
            if blk == 0:
                p.memset(d0, 0.0, W=[dst])
                p.dma(dr, src(0, 512), W=[dst])
            else:
                p.dma(da, src(blk * 512 - 3, (blk + 1) * 512), W=[dst])

        def conv_tiles(specs):
            for j in range(4):
                for (dst, srcs, stok, dtok, wcol, bcol) in specs:
                    if j == 0:
                        if bcol is None:
                            p.ts(dst, srcs[0], prm_t[:, wcol:wcol + 1], None, ALU.mult, R=[stok, prm_t], W=[dtok])
                        else:
                            p.ts(dst, srcs[0], prm_t[:, wcol:wcol + 1], prm_t[:, bcol:bcol + 1], ALU.mult, ALU.add,
                                 R=[stok, prm_t], W=[dtok])
                    else:
                        p.stt(dst, srcs[j], prm_t[:, wcol + j:wcol + j + 1], dst, ALU.mult, ALU.add, R=[stok, prm_t, dtok], W=[dtok])
            for (dst, srcs, stok, dtok, wcol, bcol) in specs:
                p.act(dst, dst, AF.Silu, R=[dtok], W=[dtok])

        def gdn_chunk(c4):
                    cs = slice(c4 * 128, (c4 + 1) * 128)
                    b = bank()
                    for qi, rows in enumerate((gc_r, be_r, bg_r)):
                        p.mm(b[:, 0:48], rows[0:4, cs], selc[0:4, qi, :], start=(qi == 0), stop=(qi == 2), R=[rows, con], W=[b])
                    p.cp(colt[:, :], b[:, 0:48], R=[b], W=[colt], eng="act")
                    p.ts(ncol[:, 0:4], colt[:, 16:20], -1.0, None, ALU.mult, R=[colt], W=[ncol])
                    b = bank()
                    for h in range(4):
                        p.mm(b[:, h * 128:(h + 1) * 128], selbc4[:, h, :], gc_r[0:4, cs], R=[con, gc_r], W=[b])
                    p.cp(gcbc[:, :, :], v4(b), R=[b], W=[gcbc], eng="act")
                    p.act(egcbc[:, :, :], v4(b), AF.Exp, R=[b], W=[egcbc])
                    for h in range(4):
                        p.stt(T1[:, h, :], gcbc[:, h, :], colt[:, h:h + 1], maskT, ALU.subtract, ALU.add, R=[gcbc, colt, con], W=[(T1, h)])
                    p.act(DmT[:, :, :], T1[:, :, :], AF.Exp, R=[T1], W=[DmT])
                    for h in range(4):
                        p.stt(T1[:, h, :], gcbc[:, h, :], colt[:, h:h + 1], maskS, ALU.subtract, ALU.subtract, R=[gcbc, colt, con], W=[(T1, h)])
                    p.act(Dm[:, :, :], T1[:, :, :], AF.Exp, R=[T1], W=[Dm], scale=-1.0)
                    p.tt(kdec[:, :], gcbc[:, :, 127], colt[:, 0:4], ALU.subtract, R=[gcbc, colt], W=[kdec])
                    p.act(kdec[:, :], kdec[:, :], AF.Exp, R=[kdec], W=[kdec])
                    yield
                    bK = bank()
                    bV = bank()
                    bG = bank()
                    bL = bank()
                    for h in range(4):
                        hs = slice(h * 128, (h + 1) * 128)
                        p.tr(bK[:, hs], kc[:, h, cs], ident, R=[kc, con], W=[bK])
                        p.tr(bV[:, hs], vc[:, h, cs], ident, R=[vc, con], W=[bV])
                        p.mm(bG[:, hs], kc[:, h, cs], qc[:, h, cs], R=[kc, qc], W=[bG])
                        p.mm(bL[:, hs], kc[:, h, cs], kc[:, h, cs], R=[kc], W=[bL])
                    for h in range(4):
                        hs = slice(h * 128, (h + 1) * 128)
                        p.act(kd[:, h, :], bK[:, hs], AF.Copy, R=[bK, kdec], W=[(kd, h)], scale=kdec[:, h:h + 1])
                        p.act(Vb[:, h, :], bV[:, hs], AF.Copy, R=[bV, colt], W=[(Vb, h)], scale=colt[:, 8 + h:9 + h])
                        p.stt(Pm[0][:, h, :], bL[:, hs], colt[:, 8 + h:9 + h], Dm[:, h, :], ALU.mult, ALU.mult,
                              R=[bL, colt, Dm], W=[(Pm[0], h)])
                    p.tt(attnT[:, :, :], v4(bG), DmT[:, :, :], ALU.mult, R=[bG, DmT], W=[attnT])
                    p.tt(qg[:, :, :], qc[:, :, cs], egcbc[:, :, :], ALU.mult, R=[qc, egcbc], W=[qg], eng="pool")
                    yield
                    bY = bank()
                    for h in range(4):
                        p.tr(bY[:, h * 128:(h + 1) * 128], Pm[0][:, h, :], ident, R=[Pm[0], con], W=[bY])
                    p.cp(Ym[0][:, :, :], v4(bY), R=[bY], W=[Ym[0]], eng="act")
                    for h in range(4):
                        p.stt(Am[:, h, :], bY[:, h * 128:(h + 1) * 128], -1.0, ident, ALU.mult, ALU.add, R=[con, bY], W=[(Am, h)])
                    yield
                    cur = 0
                    for lev in range(6):
                        nxt = 1 - cur
                        bP = bank()
                        for h in range(4):
                            p.mm(bP[:, h * 128:(h + 1) * 128], Ym[cur][:, h, :], Pm[cur][:, h, :], R=[Ym[cur], Pm[cur]], W=[bP])
                        p.cp(Pm[nxt][:, :, :], v4(bP), R=[bP], W=[Pm[nxt]], eng="act")
                        if lev < 5:
                            bY2 = bank()
                            for h in range(4):
                                p.mm(bY2[:, h * 128:(h + 1) * 128], Pm[cur][:, h, :], Ym[cur][:, h, :], R=[Ym[cur], Pm[cur]], W=[bY2])
                            p.cp(Ym[nxt][:, :, :], v4(bY2), R=[bY2], W=[Ym[nxt]])
                        yield
                        bU = bank()
                        for h in range(4):
                            p.mm(bU[:, h * 128:(h + 1) * 128], Pm[nxt][:, h, :], Am[:, h, :], R=[Pm[nxt], Am], W=[bU])
                        p.tt(Am[:, :, :], v4(bU), Am[:, :, :], ALU.add, R=[Am, bU], W=[Am])
                        yield
                        cur = nxt
                    yield
                    bKS = bank()
                    for h in range(4):
                        p.mm(bKS[:, h * 128:(h + 1) * 128], kc[:, h, cs], S4[:, h, :], R=[kc, (S4, h)], W=[bKS])
                    for h in range(4):
                        p.stt(Xm[:, h, :], bKS[:, h * 128:(h + 1) * 128], ncol[:, h:h + 1], Vb[:, h, :], ALU.mult, ALU.add,
                              R=[bKS, ncol, Vb], W=[(Xm, h)])
                    yield
                    bVn = bank()
                    for h in range(4):
                        p.mm(bVn[:, h * 128:(h + 1) * 128], Am[:, h, :], Xm[:, h, :], R=[Am, Xm], W=[bVn])
                    p.cp(Vn[:, :, :], v4(bVn), R=[bVn], W=[Vn], eng="act")
                    yield
                    bO = bank()
                    bD = bank()
                    for h in range(4):
                        hs = slice(h * 128, (h + 1) * 128)
                        p.mm(bO[:, hs], S4[:, h, :], qg[:, h, :], start=True, stop=False, R=[(S4, h), qg], W=[bO])
                        p.mm(bO[:, hs], Vn[:, h, :], attnT[:, h, :], start=False, stop=True, R=[Vn, attnT], W=[bO])
                        p.mm(bD[:, hs], kd[:, h, :], Vn[:, h, :], R=[kd, Vn], W=[bD])
                    p.cp(o4[:, :, cs], v4(bO), R=[bO], W=[(o4, c4)], eng="act")
                    for h in range(4):
                        p.stt(S4[:, h, :], S4[:, h, :], egcbc[:, h, 127:128], bD[:, h * 128:(h + 1) * 128], ALU.mult, ALU.add,
                              R=[(S4, h), bD, egcbc], W=[(S4, h)])

        def ssd_chunk(c4):
                    cs = slice(c4 * 128, (c4 + 1) * 128)
                    b = bank()
                    p.mm(b[:, 0:48], be_rs[0:8, cs], selc[0:8, 3, :], start=True, stop=False, R=[be_rs, con], W=[b])
                    p.mm(b[:, 0:48], gc_rs[0:8, cs], selc[0:8, 4, :], start=False, stop=True, R=[gc_rs, con], W=[b])
                    p.cp(colts[:, :], b[:, 0:48], R=[b], W=[colts], eng="act")
                    p.act(ecs[:, :], colts[:, 32:40], AF.Exp, R=[colts], W=[ecs])
                    for h in range(8):
                        p.mm(pBig[:, h * 128:(h + 1) * 128], selbc8[:, h, :], gc_rs[0:8, cs], R=[con, gc_rs], W=[pBig])
                    p.cp(csb[:, :, :], pBig[:, :].rearrange("p (h c) -> p h c", h=8), R=[pBig], W=[csb], eng="act")
                    p.tt(dect[:, :], csb[:, :, 127], colts[:, 32:40], ALU.subtract, R=[csb, colts], W=[dect])
                    p.act(dect[:, :], dect[:, :], AF.Exp, R=[dect], W=[dect])
                    p.act(glb8[:, :], csb[:, :, 127], AF.Exp, R=[csb], W=[glb8])
                    yield
                    for h in range(8):
                        p.stt(lmT[:, h, :], csb[:, h, :], colts[:, 32 + h:33 + h], maskT, ALU.subtract, ALU.add,
                              R=[csb, colts, con], W=[(lmT, h)])
                    p.act(lmT[:, :, :], lmT[:, :, :], AF.Exp, R=[lmT], W=[lmT])
                    yield
                    bS = bank()
                    p.mm(bS[:, 0:128], Bc[:, cs], Cc[:, cs], R=[Bc, Cc], W=[bS])
                    for h in range(8):
                        p.tt(lmT[:, h, :], bS[:, 0:128], lmT[:, h, :], ALU.mult, R=[lmT, bS], W=[(lmT, h)])
                    yield
                    bX = bank()
                    for k in range(4):
                        p.tr(bX[:, k * 128:(k + 1) * 128], xs4[:, k, cs], ident, R=[xs4, con], W=[bX])
                    p.cp(xst[:, :], bX[:, :], R=[bX], W=[xst], eng="act")
                    p.tt(xct[:, :].rearrange("p (h q) -> p h q", h=8), bX[:, :].rearrange("p (h q) -> p h q", h=8),
                         bcast(colts[:, 24:32], 64), ALU.mult, R=[bX, colts], W=[xct])
                    yield
                    bYd = bank()
                    for h in range(8):
                        p.mm(bYd[:, h * 64:(h + 1) * 64], lmT[:, h, :], xct[:, h * 64:(h + 1) * 64], R=[lmT, xct], W=[bYd])
                    bYo = bank()
                    p.mm(bYo[:, :], Cc[:, cs], Hs[:, :], R=[Cc, Hs], W=[bYo])
                    p.tt(yt[:, :].rearrange("p (h q) -> p h q", h=8), bYo[:, :].rearrange("p (h q) -> p h q", h=8),
                         bcast(ecs[:, :], 64), ALU.mult, R=[bYo, ecs], W=[yt])
                    p.tt(yt[:, :], bYd[:, :], yt[:, :], ALU.add, R=[yt, bYd], W=[yt])
                    p.tt(xst[:, :], xst[:, :], dbc[:, :], ALU.mult, R=[xst, dbc], W=[xst], eng="pool")
                    p.tt(yt[:, :], yt[:, :], xst[:, :], ALU.add, R=[yt, xst], W=[yt])
                    yield
                    p.tt(xcd[:, :].rearrange("p (h q) -> p h q", h=8), xct[:, :].rearrange("p (h q) -> p h q", h=8),
                         bcast(dect[:, :], 64), ALU.mult, R=[xct, dect], W=[xcd])
                    bB = bank()
                    p.tr(bB[:, 0:128], Bc[:, cs], ident, R=[Bc, con], W=[bB])
                    p.cp(Bt[:, :], bB[:, 0:128], R=[bB], W=[Bt], eng="act")
                    bH = bank()
                    p.mm(bH[:, :], Bt[:, :], xcd[:, :], R=[Bt, xcd], W=[bH])
                    p.tt(Hs[:, :].rearrange("p (h q) -> p h q", h=8), Hs[:, :].rearrange("p (h q) -> p h q", h=8),
                         bcast(glb8[:, :], 64), ALU.mult, R=[Hs, glb8], W=[Hs])
                    p.tt(Hs[:, :], bH[:, :], Hs[:, :], ALU.add, R=[Hs, bH], W=[Hs])
                    yield
                    bT = bank()
                    for k in range(4):
                        p.tr(bT[:, k * 128:(k + 1) * 128], yt[:, k * 128:(k + 1) * 128], ident, R=[yt, con], W=[bT])
                    p.cp(o4s[:, :, cs], v4(bT), R=[bT], W=[(o4s, c4)], eng="act")

        for blk in range(T // 512):
            bsl = slice(blk * 512, (blk + 1) * 512)
            load_halo(qh, 0, 512, blk, True)
            load_halo(kh, 512, 512, blk, True)
            load_halo(vh, 1024, 512, blk, True)
            p.dma(z4[:, :, :], projT[1536:2048, bsl].rearrange("(k p) t -> p k t", p=128), W=[z4])
            p.dma(a_r[0:4, :], projT[3328:3332, bsl], W=[a_r])
            p.dma(b_r[0:4, :], projT[3332:3336, bsl], W=[b_r])
            load_halo(xsh, 2048, 512, blk, True)
            load_halo(Bh, 2560, 128, blk, False)
            load_halo(Ch, 2688, 128, blk, False)
            p.dma(a_rs[0:8, :], projT[3336:3344, bsl], W=[a_rs])
            specs = []
            for sec, (hsrc, dstc) in enumerate(((qh, qc), (kh, kc), (vh, vc))):
                for h in range(4):
                    specs.append((dstc[:, h, :], [hsrc[:, h, j:j + 512] for j in range(4)], hsrc, (dstc, h), 16 + (sec * 4 + h) * 4, None))
            for k in range(4):
                specs.append((xs4[:, k, :], [xsh[:, k, j:j + 512] for j in range(4)], xsh, (xs4, k), 64 + k * 4, 88 + k))
            specs.append((Bc[:, :], [Bh[:, j:j + 512] for j in range(4)], Bh, Bc, 64 + 16, 88 + 4))
            specs.append((Cc[:, :], [Ch[:, j:j + 512] for j in range(4)], Ch, Cc, 64 + 20, 88 + 5))
            conv_tiles(specs)
            ri = 0
            for (cc, scale) in ((qc, SC), (kc, None)):
                p.act(sq4[:, :, :], cc[:, :, :], AF.Square, R=[cc], W=[sq4])
                for h in range(4):
                    b = bank()
                    r1, r2 = rs1s[ri % 2], rinvs[ri % 2]
                    ri += 1
                    p.mm(b[:, :], ones_f, sq4[:, h, :], R=[con, sq4], W=[b])
                    p.act(r1[:, :], b[:, :], AF.Sqrt, R=[b, eps_t], W=[r1], bias=eps_t[:, 0:1], scale=1.0)
                    p.recip(r2[:, :], r1[:, :], R=[r1], W=[r2])
                    if scale is None:
                        p.tt(cc[:, h, :], cc[:, h, :], r2[:, :], ALU.mult, R=[cc, r2], W=[(cc, h)])
                    else:
                        p.stt(cc[:, h, :], cc[:, h, :], scale, r2[:, :], ALU.mult, ALU.mult, R=[cc, r2], W=[(cc, h)])
            p.act(g_r[0:4, :], a_r[0:4, :], AF.Exp, R=[a_r, hp], W=[g_r], bias=hp[0:4, 1:2])
            p.act(g_r[0:4, :], g_r[0:4, :], AF.Ln, R=[g_r, one_t], W=[g_r], bias=one_t[0:4, 0:1])
            p.ts(g_r[0:4, :], g_r[0:4, :], negA[0:4, 0:1], None, ALU.mult, R=[g_r, negA], W=[g_r])
            p.op("dve", lambda e: e.tensor_tensor_scan(gc_r[0:4, :], cmask[0:4, :], g_r[0:4, :], 0.0, ALU.mult, ALU.add),
                 R=[con, g_r], W=[gc_r])
            p.act(be_r[0:4, :], b_r[0:4, :], AF.Sigmoid, R=[b_r], W=[be_r])
            p.act(bg_r[0:4, :], gc_r[0:4, :], AF.Exp, R=[gc_r], W=[bg_r])
            p.tt(bg_r[0:4, :], bg_r[0:4, :], be_r[0:4, :], ALU.mult, R=[bg_r, be_r], W=[bg_r])
            p.act(be_rs[0:8, :], a_rs[0:8, :], AF.Exp, R=[a_rs, hp], W=[be_rs], bias=hp[0:8, 3:4])
            p.act(be_rs[0:8, :], be_rs[0:8, :], AF.Ln, R=[be_rs, one_t], W=[be_rs], bias=one_t[0:8, 0:1])
            p.ts(g_rs[0:8, :], be_rs[0:8, :], negA[0:8, 1:2], None, ALU.mult, R=[be_rs, negA], W=[g_rs])
            p.op("dve", lambda e: e.tensor_tensor_scan(gc_rs[0:8, :], cmask[0:8, :], g_rs[0:8, :], 0.0, ALU.mult, ALU.add),
                 R=[con, g_rs], W=[gc_rs])
            for c4 in range(4):
                gg, sg = gdn_chunk(c4), ssd_chunk(c4)
                gi = 0
                g_done = s_done = False
                while not (g_done and s_done):
                    if not g_done:
                        try:
                            next(gg)
                        except StopIteration:
                            g_done = True
                    gi += 1
                    if not s_done and (g_done or gi % 2 == 0):
                        try:
                            next(sg)
                        except StopIteration:
                            s_done = True
            p.act(sq4[:, :, :], o4[:, :, :], AF.Square, R=[o4], W=[sq4])
            p.act(z4[:, :, :], z4[:, :, :], AF.Silu, R=[z4], W=[z4])
            for h in range(4):
                b = bank()
                p.mm(b[:, :], ones_f, sq4[:, h, :], R=[con, sq4], W=[b])
                p.act(rs1[:, :], b[:, :], AF.Sqrt, R=[b, eps_t], W=[rs1], bias=eps_t[:, 0:1], scale=1.0 / 128)
                p.recip(rinv[:, :], rs1[:, :], R=[rs1], W=[rinv])
                p.stt(o4[:, h, :], o4[:, h, :], prm_t[:, 94:95], rinv[:, :], ALU.mult, ALU.mult,
                      R=[o4, prm_t, rinv], W=[(o4, ("n", h))])
            p.tt(o4[:, :, :], o4[:, :, :], z4[:, :, :], ALU.mult, R=[o4, z4], W=[o4], eng="pool")
            p.dma(mixA[:, bsl].rearrange("(h p) t -> p h t", p=128), o4[:, :, :], R=[o4])

            p.dma(z4[:, :, :], projT[2816:3328, bsl].rearrange("(k p) t -> p k t", p=128), W=[z4])
            p.act(z4[:, :, :], z4[:, :, :], AF.Silu, R=[z4], W=[z4])
            p.tt(o4s[:, :, :], o4s[:, :, :], z4[:, :, :], ALU.mult, R=[o4s, z4], W=[o4s], eng="pool")
            p.act(sq4[:, :, :], o4s[:, :, :], AF.Square, R=[o4s], W=[sq4])
            b = bank()
            for k in range(4):
                p.mm(b[:, :], ones_f, sq4[:, k, :], start=(k == 0), stop=(k == 3), R=[con, sq4], W=[b])
            p.act(rs1[:, :], b[:, :], AF.Sqrt, R=[b, eps_t], W=[rs1], bias=eps_t[:, 0:1], scale=1.0 / 512)
            p.recip(rinv[:, :], rs1[:, :], R=[rs1], W=[rinv])
            for k in range(4):
                p.stt(o4s[:, k, :], o4s[:, k, :], prm_t[:, 96 + k:97 + k], rinv[:, :], ALU.mult, ALU.mult,
                      R=[o4s, prm_t, rinv], W=[(o4s, ("n", k))])
            p.dma(mixB[:, bsl].rearrange("(h p) t -> p h t", p=128), o4s[:, :, :], R=[o4s])
    if ctx is not None:
        return None
    p.emit()
    es.close()
    return nc


def pack_even(g, xT, norm_g, w_in, gdn_conv_w, gdn_A_log, gdn_dt_bias, gdn_norm_g,
              ssd_conv_w, ssd_conv_b, ssd_A_log, ssd_dt_bias, ssd_D, ssd_norm_g):
    hq = np.arange(512 * g, 512 * g + 512)
    h4 = np.arange(4 * g, 4 * g + 4)
    h8 = np.arange(8 * g, 8 * g + 8)
    o_za, o_a, o_b, o_xbc = 3072, 4096, 4104, 4112
    o_zb, o_dt = o_xbc + 1536, o_xbc + 1536 + 1024
    cols = np.concatenate([hq, 1024 + hq, 2048 + hq, o_za + hq, o_xbc + hq,
                           o_xbc + 1024 + 128 * g + np.arange(128), o_xbc + 1280 + 128 * g + np.arange(128),
                           o_zb + hq, o_a + h4, o_b + h4, o_dt + h8])
    prm = np.zeros((128, 112), np.float32)
    prm[:, 0:16] = _pk(norm_g, 16)
    gw = np.asarray(gdn_conv_w, np.float32)
    for sec in range(3):
        for h in range(4):
            ch = sec * 1024 + 512 * g + h * 128
            for j in range(4):
                prm[:, 16 + (sec * 4 + h) * 4 + j] = gw[j, ch:ch + 128]
    sw = np.asarray(ssd_conv_w, np.float32)
    sb_ = np.asarray(ssd_conv_b, np.float32)
    starts = [512 * g + k * 128 for k in range(4)] + [1024 + 128 * g, 1280 + 128 * g]
    for i, c0 in enumerate(starts):
        for j in range(4):
            prm[:, 64 + i * 4 + j] = sw[j, c0:c0 + 128]
        prm[:, 88 + i] = sb_[c0:c0 + 128]
    prm[:, 94] = np.asarray(gdn_norm_g, np.float32)
    prm[:, 96:100] = _pk(np.asarray(ssd_norm_g)[hq], 4)
    hp = np.zeros((8, 8), np.float32)
    hp[0:4, 0] = np.asarray(gdn_A_log)[h4]
    hp[0:4, 1] = np.asarray(gdn_dt_bias)[h4]
    hp[0:8, 2] = np.asarray(ssd_A_log)[h8]
    hp[0:8, 3] = np.asarray(ssd_dt_bias)[h8]
    dbc = np.ascontiguousarray(np.broadcast_to(np.repeat(np.asarray(ssd_D, np.float32)[h8], 64)[None, :], (128, 512)))
    return dict(xT=np.ascontiguousarray(xT, dtype=np.float32), w=np.ascontiguousarray(np.asarray(w_in, np.float32)[:, cols]),
                prm=prm, hp=hp, dbc=dbc, con=make_consts(g))


def _pk(v, n):
    return np.ascontiguousarray(np.asarray(v, np.float32).reshape(n, 128).T)


def pack_odd(g, xT, norm_g, w_in, ret_ng, conv_w, conv_b, w_a, b_a, w_x, b_x, lam, pos):
    hq = slice(512 * g, 512 * g + 512)
    cols = np.concatenate([np.arange(0, 1024)[hq], 1024 + np.arange(1024)[hq], 2048 + np.arange(1024)[hq],
                           3072 + np.arange(1024)[hq], 4096 + np.arange(1024)[hq], 5120 + np.arange(1024)[hq]])
    prm = np.zeros((128, 64), np.float32)
    prm[:, 0:16] = _pk(norm_g, 16)
    prm[:, 16:20] = _pk(ret_ng[hq], 4)
    cw = np.asarray(conv_w, np.float32)[:, hq]
    for ct in range(4):
        for j in range(4):
            prm[:, 20 + ct * 4 + j] = cw[j, ct * 128:(ct + 1) * 128]
    prm[:, 36:40] = _pk(np.asarray(conv_b)[hq], 4)
    prm[:, 40:44] = _pk(np.asarray(b_a)[hq], 4)
    prm[:, 44:48] = _pk(np.asarray(b_x)[hq], 4)
    prm[:, 48:52] = _pk(np.asarray(lam)[hq], 4)
    lw = np.concatenate([np.asarray(w_a, np.float32)[4 * g:4 * g + 4], np.asarray(w_x, np.float32)[4 * g:4 * g + 4]], 0)
    lruw = np.ascontiguousarray(lw.transpose(1, 0, 2).reshape(128, 8 * 128))
    return dict(xT=np.ascontiguousarray(xT, dtype=np.float32), w=np.ascontiguousarray(np.asarray(w_in, np.float32)[:, cols]),
                prm=prm, con=make_consts(g), lruw=lruw, pos=np.ascontiguousarray(np.asarray(pos, np.int32).reshape(1, -1)))


def unpack_mix(parts):
    return np.concatenate([parts[0][0:512], parts[1][0:512], parts[0][512:1024], parts[1][512:1024]], 0)


def build_fused(T, depth=4):
    nc = _new_nc()
    lay, ncon = _con_layout()
    dt = lambda name, shape, dty=F32, kind="ExternalInput": nc.dram_tensor(name, list(shape), dty, kind=kind).ap()
    xT = dt("xT", [D_MODEL, T])
    pT = dt("pT", [depth * 256, T])
    pos = dt("pos", [1, T], I32)
    con = [dt(f"con{g}", [128, ncon]) for g in range(2)]
    yT = dt("yT", [D_MODEL, T], kind="ExternalOutput")
    projT = dt("projT", [3344, T], kind="Internal")
    mixF = dt("mixF", [D_MODEL, T], kind="Internal")
    xs = [dt("xA", [D_MODEL, T], kind="Internal"), dt("xB", [D_MODEL, T], kind="Internal")]
    p = Prog(nc)
    es = contextlib.ExitStack()
    es.enter_context(nc.allow_low_precision("bf16 matmul operands, fp32 accumulation"))
    x_cur = xT
    for i in range(depth):
        even = (i % 2 == 0)
        for g in range(2):
            ctx = dict(nc=nc, p=p, xT=x_cur, con=con[g], projT=projT[0:(3344 if even else 3072), :],
                       mixA=mixF[512 * g:512 * g + 512, :], mixB=mixF[1024 + 512 * g:1536 + 512 * g, :],
                       w=dt(f"w{i}_{g}", [D_MODEL, 3344 if even else 3072]),
                       prm=dt(f"prm{i}_{g}", [128, 112 if even else 64]))
            if even:
                ctx["hp"] = dt(f"hp{i}_{g}", [8, 8])
                ctx["dbc"] = dt(f"dbc{i}_{g}", [128, 512])
            else:
                ctx["lruw"] = dt(f"lruw{i}_{g}", [128, 8 * 128])
                ctx["pos"] = pos
            with p.scope():
                (build_AB_even if even else build_AB_odd)(T, ctx=ctx)
        final = (i == depth - 1)
        x_nxt = yT if final else xs[i % 2]
        ctx = dict(nc=nc, p=p, mixT=mixF, xT=x_cur, pT=pT[i * 256:(i + 1) * 256, :], wo=dt(f"wo{i}", [D_MODEL, D_MODEL]),
                   wg=dt(f"wg{i}", [D_MODEL, D_MODEL]), wp=dt(f"wp{i}", [256, D_MODEL]), prm=dt(f"prmC{i}", [128, 32]), yT=x_nxt)
        with p.scope():
            build_C(T, final, ctx=ctx)
        x_cur = x_nxt
    p.emit()
    es.close()
    return nc


_CACHE = {}


def _prog(name, fn):
    if name not in _CACHE:
        _CACHE[name] = fn()
    return _CACHE[name]


def kernel_unfused(x, p, positions, norm_g, ple_norm_g, w_ple_gate, w_ple_proj,
           ev_w_in, ev_w_out, gdn_conv_w, gdn_A_log, gdn_dt_bias, gdn_norm_g,
           ssd_conv_w, ssd_conv_b, ssd_A_log, ssd_dt_bias, ssd_D, ssd_norm_g,
           od_w_in, od_w_out, ret_norm_g, lru_conv_w, lru_conv_b,
           lru_w_a, lru_b_a, lru_w_x, lru_b_x, lru_lambda, final_norm_g):
    f = lambda a: np.asarray(a, dtype=np.float32)
    x = f(x)
    B, S, D = x.shape
    H = S // 2
    cores = list(range(8))
    xT = [np.ascontiguousarray(x[b].T) for b in range(B)]
    depth = int(np.asarray(norm_g).shape[0])
    for i in range(depth):
        j = i // 2
        if i % 2 == 0:
            nc = _prog("even", lambda: build_AB_even(S))
            ins = [pack_even(c % 2, xT[c // 2], f(norm_g)[i], f(ev_w_in)[j], f(gdn_conv_w)[j], f(gdn_A_log)[j],
                             f(gdn_dt_bias)[j], f(gdn_norm_g)[j], f(ssd_conv_w)[j], f(ssd_conv_b)[j], f(ssd_A_log)[j],
                             f(ssd_dt_bias)[j], f(ssd_D)[j], f(ssd_norm_g)[j]) for c in cores]
            w_out = f(ev_w_out)[j]
        else:
            nc = _prog("odd", lambda: build_AB_odd(S))
            ins = [pack_odd(c % 2, xT[c // 2], f(norm_g)[i], f(od_w_in)[j], f(ret_norm_g)[j], f(lru_conv_w)[j],
                            f(lru_conv_b)[j], f(lru_w_a)[j], f(lru_b_a)[j], f(lru_w_x)[j], f(lru_b_x)[j],
                            f(lru_lambda)[j], np.asarray(positions)[c // 2]) for c in cores]
            w_out = f(od_w_out)[j]
        res = run_bass_kernel_spmd(nc, ins, core_ids=cores)
        mix = [unpack_mix([res.results[2 * b]["mixT"], res.results[2 * b + 1]["mixT"]]) for b in range(B)]
        del res, ins
        final = (i == depth - 1)
        ncC = _prog("Cf" if final else "C", lambda: build_C(H, final))
        prm = np.zeros((128, 32), np.float32)
        prm[:, 0:16] = _pk(f(ple_norm_g)[i], 16)
        prm[:, 16:32] = _pk(f(final_norm_g), 16)
        wg = np.ascontiguousarray(f(w_ple_gate)[i])
        wp = np.ascontiguousarray(f(w_ple_proj)[i])
        wo = np.ascontiguousarray(w_out)
        insC = []
        for c in cores:
            b, g = c // 2, c % 2
            sl = slice(g * H, (g + 1) * H)
            insC.append(dict(mixT=np.ascontiguousarray(mix[b][:, sl]), xT=np.ascontiguousarray(xT[b][:, sl]),
                             pT=np.ascontiguousarray(f(p)[i, b, sl, :].T), wo=wo, wg=wg, wp=wp, prm=prm))
        res = run_bass_kernel_spmd(ncC, insC, core_ids=cores)
        xT = [np.concatenate([res.results[2 * b]["yT"], res.results[2 * b + 1]["yT"]], axis=1) for b in range(B)]
        del res, insC, mix
    return np.ascontiguousarray(np.stack([t.T for t in xT], 0)).astype(np.float32)


def kernel(x, p, positions, norm_g, ple_norm_g, w_ple_gate, w_ple_proj,
           ev_w_in, ev_w_out, gdn_conv_w, gdn_A_log, gdn_dt_bias, gdn_norm_g,
           ssd_conv_w, ssd_conv_b, ssd_A_log, ssd_dt_bias, ssd_D, ssd_norm_g,
           od_w_in, od_w_out, ret_norm_g, lru_conv_w, lru_conv_b,
           lru_w_a, lru_b_a, lru_w_x, lru_b_x, lru_lambda, final_norm_g):
    f = lambda a: np.asarray(a, dtype=np.float32)
    x = f(x)
    B, S, D = x.shape
    depth = int(np.asarray(norm_g).shape[0])
    nc = _prog("fused", lambda: build_fused(S, depth))
    shared = {}
    for g in range(2):
        shared[f"con{g}"] = make_consts(g)
    dummy = np.zeros((D, 8), np.float32)
    for i in range(depth):
        j = i // 2
        for g in range(2):
            if i % 2 == 0:
                d = pack_even(g, dummy, f(norm_g)[i], f(ev_w_in)[j], f(gdn_conv_w)[j], f(gdn_A_log)[j],
                              f(gdn_dt_bias)[j], f(gdn_norm_g)[j], f(ssd_conv_w)[j], f(ssd_conv_b)[j], f(ssd_A_log)[j],
                              f(ssd_dt_bias)[j], f(ssd_D)[j], f(ssd_norm_g)[j])
                shared[f"hp{i}_{g}"] = d["hp"]
                shared[f"dbc{i}_{g}"] = d["dbc"]
            else:
                d = pack_odd(g, dummy, f(norm_g)[i], f(od_w_in)[j], f(ret_norm_g)[j], f(lru_conv_w)[j],
                             f(lru_conv_b)[j], f(lru_w_a)[j], f(lru_b_a)[j], f(lru_w_x)[j], f(lru_b_x)[j],
                             f(lru_lambda)[j], np.zeros(8, np.int32))
                shared[f"lruw{i}_{g}"] = d["lruw"]
            shared[f"w{i}_{g}"] = d["w"]
            shared[f"prm{i}_{g}"] = d["prm"]
        prm = np.zeros((128, 32), np.float32)
        prm[:, 0:16] = _pk(f(ple_norm_g)[i], 16)
        prm[:, 16:32] = _pk(f(final_norm_g), 16)
        shared[f"prmC{i}"] = prm
        shared[f"wo{i}"] = np.ascontiguousarray(f(ev_w_out)[j] if i % 2 == 0 else f(od_w_out)[j])
        shared[f"wg{i}"] = np.ascontiguousarray(f(w_ple_gate)[i])
        shared[f"wp{i}"] = np.ascontiguousarray(f(w_ple_proj)[i])
    ins = []
    for c in range(8):
        b = c % B
        d = dict(shared)
        d["xT"] = np.ascontiguousarray(x[b].T)
        d["pT"] = np.ascontiguousarray(np.concatenate([f(p)[i, b].T for i in range(depth)], axis=0))
        d["pos"] = np.ascontiguousarray(np.asarray(positions, np.int32)[b].reshape(1, -1))
        ins.append(d)
    res = run_bass_kernel_spmd(nc, ins, core_ids=list(range(8)))
    return np.ascontiguousarray(np.stack([res.results[b]["yT"].T for b in range(B)], 0)).astype(np.float32)
```

```python
import contextlib
import numpy as np
import concourse.bass as bass
import concourse.mybir as mybir
from concourse.bass_utils import run_bass_kernel_spmd

F32 = mybir.dt.float32
F32R = mybir.dt.float32r
FP32R = False
BF16 = mybir.dt.bfloat16
I32 = mybir.dt.int32
AF = mybir.ActivationFunctionType
ALU = mybir.AluOpType

D_MODEL = 2048
SEQ = 8192
EPS = 1e-6
NDMA = 16
SAME_ENGINE_SYNC = True


class _Tok:
    __slots__ = ("w", "rs")

    def __init__(self):
        self.w = None
        self.rs = []


class Prog:
    ENGS = ("pe", "dve", "act", "pool", "sp")

    def __init__(self, nc):
        self.nc = nc
        self.q = {e: [] for e in self.ENGS}
        self.cnt = {e: 0 for e in self.ENGS}
        self.seen = {e: {} for e in self.ENGS}
        self.toks = {}
        self.slot_uses = [0] * NDMA
        self.rr = 0
        self.stack = contextlib.ExitStack()
        self.stacks = [self.stack]
        self.pending = {e: {} for e in self.ENGS}
        self.nt = 0
        self.psum_ids = set()
        self.keep = []

    @contextlib.contextmanager
    def scope(self):
        st = contextlib.ExitStack()
        self.stacks.append(st)
        try:
            yield
        finally:
            self.stacks.pop()
            st.close()
            self.barrier()

    def barrier(self):
        for e in self.ENGS:
            pd = self.pending[e]
            for o in ("pe", "dve", "act", "pool"):
                if self.cnt[o] > 0:
                    pd[o] = self.cnt[o]
            for s in range(NDMA):
                if self.slot_uses[s] > 0:
                    pd[("d", s)] = 16 * self.slot_uses[s]

    def sb(self, shape, dt=F32, name=None):
        self.nt += 1
        t = self.stacks[-1].enter_context(self.nc.sbuf_tensor(name or f"t{self.nt}", list(shape), dt))
        self.keep.append(t)
        return t

    def ps(self, shape, dt=F32, name=None):
        self.nt += 1
        t = self.stacks[-1].enter_context(self.nc.psum_tensor(name or f"p{self.nt}", list(shape), dt))
        self.psum_ids.add(id(t))
        self.keep.append(t)
        return t

    def _tk(self, ref):
        if isinstance(ref, tuple):
            t, k = ref
        else:
            t, k = ref, None
        if id(t) in self.psum_ids:
            k = None
        d = self.toks.setdefault(id(t), {"_": _Tok()})
        return d, k

    def _deps(self, R, W, eng=None):
        need = {}

        def add(ev):
            if ev is not None:
                if need.get(ev[0], 0) < ev[1]:
                    need[ev[0]] = ev[1]

        for ref in R:
            d, k = self._tk(ref)
            add(d["_"].w)
            if k is None:
                for kk, tk in d.items():
                    add(tk.w)
            elif k in d:
                add(d[k].w)
            t_ = ref[0] if isinstance(ref, tuple) else ref
            if id(t_) in self.psum_ids:
                for ev in d["_"].rs:
                    if ev[0] != eng:
                        add(ev)
        for ref in W:
            d, k = self._tk(ref)
            keys = list(d.keys()) if k is None else (["_", k] if k in d else ["_"])
            for kk in keys:
                add(d[kk].w)
                for ev in d[kk].rs:
                    add(ev)
        return need

    def _upd(self, R, W, ev):
        for ref in R:
            d, k = self._tk(ref)
            tk = d["_"] if k is None else d.setdefault(k, _Tok())
            tk.rs.append(ev)
            if len(tk.rs) > 24:
                m = {}
                for e in tk.rs:
                    if m.get(e[0], 0) < e[1]:
                        m[e[0]] = e[1]
                tk.rs = list(m.items())
        for ref in W:
            d, k = self._tk(ref)
            if k is None:
                for kk in d:
                    d[kk].w = ev
                    d[kk].rs = []
            else:
                tk = d.setdefault(k, _Tok())
                tk.w = ev
                tk.rs = []

    def op(self, eng, fn, R=(), W=()):
        need = self._deps(R, W, eng)
        if self.pending[eng]:
            for k, v in self.pending[eng].items():
                if need.get(k, 0) < v:
                    need[k] = v
            self.pending[eng] = {}
        if eng == "sp":
            s = self.rr
            self.rr = (self.rr + 1) % NDMA
            key = ("d", s)
            if self.slot_uses[s] > 0:
                v = 16 * self.slot_uses[s]
                if need.get(key, 0) < v:
                    need[key] = v
            self.slot_uses[s] += 1
            ev = (key, 16 * self.slot_uses[s])
            inc = (key, 16)
        else:
            self.cnt[eng] += 1
            ev = (eng, self.cnt[eng])
            inc = (eng, 1)
        waits = []
        seen = self.seen[eng]
        for k, v in need.items():
            if k == eng and (eng == "pe" or not SAME_ENGINE_SYNC):
                continue
            if seen.get(k, 0) >= v:
                continue
            seen[k] = v
            waits.append((k, v))
        self.q[eng].append((waits, fn, inc))
        self._upd(R, W, ev)

    def mm(self, out, lhsT, rhs, start=True, stop=True, R=(), W=(), exact=False):
        if (FP32R and not exact and lhsT.dtype == F32 and rhs.dtype == F32
                and lhsT.partition_size() == 128 and rhs.partition_size() == 128):
            lhsT = lhsT.bitcast(F32R)
            rhs = rhs.bitcast(F32R)
        self.op("pe", lambda e: e.matmul(out, lhsT, rhs, start=start, stop=stop), R, W)

    def tr(self, out, in_, ident, R=(), W=()):
        self.op("pe", lambda e: e.transpose(out, in_, ident), R, W)

    def act(self, out, in_, func, R=(), W=(), bias=None, scale=None, eng="act"):
        kw = {}
        if bias is not None:
            kw["bias"] = bias
        if scale is not None:
            kw["scale"] = scale
        self.op(eng, lambda e: e.activation(out, in_, func, **kw), R, W)

    def tt(self, out, in0, in1, alu, R=(), W=(), eng="dve"):
        self.op(eng, lambda e: e.tensor_tensor(out, in0, in1, alu), R, W)

    def ts(self, out, in0, s1, s2, op0, op1=None, R=(), W=(), eng="dve"):
        if op1 is None:
            self.op(eng, lambda e: e.tensor_scalar(out, in0, s1, None, op0), R, W)
        else:
            self.op(eng, lambda e: e.tensor_scalar(out, in0, s1, s2, op0, op1), R, W)

    def stt(self, out, in0, scalar, in1, op0, op1, R=(), W=()):
        self.op("dve", lambda e: e.scalar_tensor_tensor(out, in0, scalar, in1, op0, op1), R, W)

    def cp(self, out, in_, R=(), W=(), eng="dve"):
        if eng == "act":
            self.op("act", lambda e: e.activation(out, in_, AF.Copy), R, W)
        else:
            self.op(eng, lambda e: e.tensor_copy(out, in_), R, W)

    def recip(self, out, in_, R=(), W=()):
        self.op("dve", lambda e: e.reciprocal(out, in_), R, W)

    def memset(self, ap, val, W=(), eng="pool"):
        self.op(eng, lambda e: e.memset(ap, val), (), W)

    def dma(self, out, in_, R=(), W=()):
        self.op("sp", lambda e: e.dma_start(out=out, in_=in_), R, W)

    def emit(self):
        nc = self.nc
        with contextlib.ExitStack() as es:
            sems = {}
            for e in ("pe", "dve", "act", "pool"):
                sems[e] = es.enter_context(nc.semaphore("s_" + e))
            for s in range(NDMA):
                sems[("d", s)] = es.enter_context(nc.semaphore(f"s_d{s}"))
            block = es.enter_context(nc.Block())

            def run(name):
                def f(eng):
                    for waits, fn, inc in self.q[name]:
                        for k, v in waits:
                            eng.wait_ge(sems[k], v)
                        fn(eng).then_inc(sems[inc[0]], inc[1])
                    if name == "sp":
                        for s in range(NDMA):
                            if self.slot_uses[s] > 0:
                                eng.wait_ge(sems[("d", s)], 16 * self.slot_uses[s])
                return f

            block.tensor(run("pe"))
            block.vector(run("dve"))
            block.scalar(run("act"))
            block.gpsimd(run("pool"))
            block.sync(run("sp"))
        self.stack.close()


def _new_nc():
    return bass.Bass("TRN2", target_bir_lowering=False)


def build_C(T, final, TT=256, ctx=None):
    KT = D_MODEL // 128
    if ctx is None:
        nc = _new_nc()
        mixT = nc.dram_tensor("mixT", [D_MODEL, T], F32, kind="ExternalInput").ap()
        xT = nc.dram_tensor("xT", [D_MODEL, T], F32, kind="ExternalInput").ap()
        pT = nc.dram_tensor("pT", [256, T], F32, kind="ExternalInput").ap()
        wo = nc.dram_tensor("wo", [D_MODEL, D_MODEL], F32, kind="ExternalInput").ap()
        wg = nc.dram_tensor("wg", [D_MODEL, D_MODEL], F32, kind="ExternalInput").ap()
        wp = nc.dram_tensor("wp", [256, D_MODEL], F32, kind="ExternalInput").ap()
        prm = nc.dram_tensor("prm", [128, 32], F32, kind="ExternalInput").ap()
        yT = nc.dram_tensor("yT", [D_MODEL, T], F32, kind="ExternalOutput").ap()
        p = Prog(nc)
        es = contextlib.ExitStack()
        es.enter_context(nc.allow_low_precision("bf16 matmul operands, fp32 accumulation"))
    else:
        nc, p = ctx["nc"], ctx["p"]
        mixT, xT, pT, wo, wg, wp, prm, yT = (ctx[k] for k in ("mixT", "xT", "pT", "wo", "wg", "wp", "prm", "yT"))
    prm_t = p.sb([128, 32])
    p.dma(prm_t[:, :], prm[:, :], W=[prm_t])
    ones_b = p.sb([128, 128], BF16)
    p.memset(ones_b[:, :], 1.0, W=[ones_b])
    eps_t = p.sb([128, 1])
    p.memset(eps_t[:, :], EPS, W=[eps_t])

    wo_b = p.sb([128, KT, D_MODEL], BF16)
    wg_b = p.sb([128, KT, D_MODEL], BF16)
    wp_b = p.sb([128, 2, D_MODEL], BF16)
    wst = [p.sb([128, 4, TT]) for _ in range(3)]
    i = 0
    for (src, dst, nk, gcol) in ((wo, wo_b, KT, None), (wg, wg_b, KT, 0), (wp, wp_b, 2, None)):
        for kt in range(nk):
            for c in range(2):
                st = wst[i % 3]
                i += 1
                p.dma(st[:, :, :], src[kt * 128:(kt + 1) * 128, c * 1024:(c + 1) * 1024].rearrange("p (c t) -> p c t", c=4), W=[st])
                dv = dst[:, kt, c * 1024:(c + 1) * 1024].rearrange("p (c t) -> p c t", c=4)
                if gcol is None:
                    p.cp(dv, st[:, :, :], R=[st], W=[(dst, (kt, c))], eng="act")
                else:
                    p.act(dv, st[:, :, :], AF.Copy, R=[st, prm_t], W=[(dst, (kt, c))],
                          scale=prm_t[:, gcol + kt:gcol + kt + 1])

    NTT = T // TT
    assert TT == 256
    mst = wst
    mb = [p.sb([128, KT, TT], BF16) for _ in range(1)]
    xt = [p.sb([128, KT, TT]) for _ in range(1)]
    xb = [p.sb([128, KT, TT], BF16) for _ in range(1)]
    sq = [p.sb([128, TT], BF16) for _ in range(3)]
    pst = p.sb([128, 2, TT])
    pb = p.sb([128, 2, TT], BF16)
    rs_t = p.sb([128, TT])
    rstd = p.sb([128, TT])
    gt = [p.sb([128, TT]) for _ in range(2)]
    g2 = [p.sb([128, TT]) for _ in range(2)]
    acc = [p.ps([128, 512]) for _ in range(3)]
    acc2 = [p.ps([128, 512]) for _ in range(2)]
    ssq = p.ps([128, 512])
    ci = 0
    for tt in range(NTT):
        tsl = slice(tt * TT, (tt + 1) * TT)
        m_b = mb[0]
        x_t = xt[0]
        x_b = xb[0]
        for kg in range(4):
            st = mst[ci % 3]
            ci += 1
            p.dma(st[:, :, :], mixT[kg * 512:(kg + 1) * 512, tsl].rearrange("(k p) t -> p k t", p=128), W=[st])
            p.cp(m_b[:, kg * 4:(kg + 1) * 4, :], st[:, :, :], R=[st], W=[(m_b, kg)], eng="pool")
        p.dma(x_t[:, :, :], xT[:, tsl].rearrange("(k p) t -> p k t", p=128), W=[x_t])
        p.dma(pst[:, :, :], pT[:, tsl].rearrange("(k p) t -> p k t", p=128), W=[pst])
        p.cp(pb[:, :, :], pst[:, :, :], R=[pst], W=[pb], eng="pool")
        for dc in range(KT):
            a = acc[dc % 3]
            for kt in range(KT):
                p.mm(a[:, 0:TT], wo_b[:, kt, dc * 128:(dc + 1) * 128], m_b[:, kt, :], start=(kt == 0), stop=(kt == KT - 1),
                     R=[wo_b, m_b], W=[a])
            p.tt(x_t[:, dc, :], a[:, 0:TT], x_t[:, dc, :], ALU.add, R=[a, (x_t, dc)], W=[(x_t, dc)])
            s = sq[dc % 3]
            p.act(s[:, :], x_t[:, dc, :], AF.Square, R=[(x_t, dc)], W=[s])
            p.mm(ssq[:, 0:TT], ones_b[:, :], s[:, :], start=(dc == 0), stop=(dc == KT - 1), R=[ones_b, s], W=[ssq])
            p.cp(x_b[:, dc, :], x_t[:, dc, :], R=[(x_t, dc)], W=[(x_b, dc)], eng="pool")
        p.act(rs_t[:, :], ssq[:, 0:TT], AF.Sqrt, R=[ssq, eps_t], W=[rs_t], bias=eps_t[:, 0:1], scale=1.0 / D_MODEL)
        p.recip(rstd[:, :], rs_t[:, :], R=[rs_t], W=[rstd])
        for dc in range(KT):
            a = acc[dc % 3]
            for kt in range(KT):
                p.mm(a[:, 0:TT], wg_b[:, kt, dc * 128:(dc + 1) * 128], x_b[:, kt, :], start=(kt == 0), stop=(kt == KT - 1),
                     R=[wg_b, x_b], W=[a])
            a2 = acc2[dc % 2]
            for kt in range(2):
                p.mm(a2[:, 0:TT], wp_b[:, kt, dc * 128:(dc + 1) * 128], pb[:, kt, :], start=(kt == 0), stop=(kt == 1),
                     R=[wp_b, pb], W=[a2])
            g = gt[dc % 2]
            gg = g2[dc % 2]
            p.tt(g[:, :], a[:, 0:TT], rstd[:, :], ALU.mult, R=[a, rstd], W=[g])
            p.act(gg[:, :], g[:, :], AF.Sigmoid, R=[g], W=[gg])
            p.tt(g[:, :], a2[:, 0:TT], gg[:, :], ALU.mult, R=[a2, gg], W=[g])
            p.tt(x_t[:, dc, :], x_t[:, dc, :], g[:, :], ALU.add, R=[g, (x_t, dc)], W=[(x_t, dc)], eng="pool")
        if final:
            for dc in range(KT):
                s = sq[dc % 3]
                p.act(s[:, :], x_t[:, dc, :], AF.Square, R=[(x_t, dc)], W=[s])
                p.mm(ssq[:, 0:TT], ones_b[:, :], s[:, :], start=(dc == 0), stop=(dc == KT - 1), R=[ones_b, s], W=[ssq])
            p.act(rs_t[:, :], ssq[:, 0:TT], AF.Sqrt, R=[ssq, eps_t], W=[rs_t], bias=eps_t[:, 0:1], scale=1.0 / D_MODEL)
            p.recip(rstd[:, :], rs_t[:, :], R=[rs_t], W=[rstd])
            for dc in range(KT):
                p.stt(x_t[:, dc, :], x_t[:, dc, :], prm_t[:, 16 + dc:17 + dc], rstd[:, :], ALU.mult, ALU.mult,
                      R=[(x_t, dc), prm_t, rstd], W=[(x_t, dc)])
        p.dma(yT[:, tsl].rearrange("(k p) t -> p k t", p=128), x_t[:, :, :], R=[x_t], W=[])
    if ctx is not None:
        return None
    p.emit()
    es.close()
    return nc


def phase_A(p, xT, w_loc, NCOL, projT, prm_t, T, ones_b, eps_t):
    KT = D_MODEL // 128
    TT = 512
    with p.scope():
        wb = p.sb([128, KT, NCOL], BF16)
        wst = [p.sb([128, 512]) for _ in range(3)]
        i = 0
        for kt in range(KT):
            for c0 in range(0, NCOL, 512):
                c1 = min(NCOL, c0 + 512)
                st = wst[i % 3]
                i += 1
                p.dma(st[:, 0:c1 - c0], w_loc[kt * 128:(kt + 1) * 128, c0:c1], W=[st])
                p.act(wb[:, kt, c0:c1], st[:, 0:c1 - c0], AF.Copy, R=[st, prm_t], W=[(wb, (kt, c0))],
                      scale=prm_t[:, kt:kt + 1])
        xst = [p.sb([128, 4, TT]) for _ in range(3)]
        xb = [p.sb([128, KT, TT], BF16) for _ in range(2)]
        sq = [p.sb([128, 4, TT], BF16) for _ in range(2)]
        rs_t = p.sb([128, TT])
        rstd = [p.sb([128, TT]) for _ in range(2)]
        ost = [p.sb([128, TT]) for _ in range(4)]
        acc = [p.ps([128, 512]) for _ in range(4)]
        ssq = p.ps([128, 512])
        ci = 0
        oi = 0
        nct = (NCOL + 127) // 128
        for tt in range(T // TT):
            tsl = slice(tt * TT, (tt + 1) * TT)
            x_b = xb[tt % 2]
            rsd = rstd[tt % 2]
            for kg in range(4):
                st = xst[ci % 3]
                s2 = sq[ci % 2]
                ci += 1
                p.dma(st[:, :, :], xT[kg * 512:(kg + 1) * 512, tsl].rearrange("(k p) t -> p k t", p=128), W=[st])
                p.act(s2[:, :, :], st[:, :, :], AF.Square, R=[st], W=[s2])
                p.cp(x_b[:, kg * 4:(kg + 1) * 4, :], st[:, :, :], R=[st], W=[(x_b, kg)], eng="pool")
                for k in range(4):
                    p.mm(ssq[:, :], ones_b[:, :], s2[:, k, :], start=(kg == 0 and k == 0), stop=(kg == 3 and k == 3),
                         R=[ones_b, s2], W=[ssq])
            p.act(rs_t[:, :], ssq[:, :], AF.Sqrt, R=[ssq, eps_t], W=[rs_t], bias=eps_t[:, 0:1], scale=1.0 / D_MODEL)
            p.recip(rsd[:, :], rs_t[:, :], R=[rs_t], W=[rsd])
            for ct in range(nct):
                m = min(128, NCOL - ct * 128)
                a = acc[ct % 4]
                for kt in range(KT):
                    p.mm(a[0:m, :], wb[:, kt, ct * 128:ct * 128 + m], x_b[:, kt, :], start=(kt == 0), stop=(kt == KT - 1),
                         R=[wb, x_b], W=[a])
                o = ost[oi % 4]
                oi += 1
                p.tt(o[0:m, :], a[0:m, :], rsd[0:m, :], ALU.mult, R=[a, rsd], W=[o])
                p.dma(projT[ct * 128:ct * 128 + m, tsl], o[0:m, :], R=[o])


def _con_layout():
    lay = {}
    off = 0
    for name, w in (("ident", 128), ("ones", 128), ("rotm", 128), ("maskT", 128), ("maskS", 128),
                    ("invf", 128), ("dmT", 512), ("qdec", 512), ("kdec", 4), ("glb", 4),
                    ("selbc4", 4 * 128), ("selbc8", 8 * 128), ("selc", 288), ("cmask", 512), ("elast", 128)):
        lay[name] = (off, w)
        off += w
    return lay, off


def make_consts(g):
    lay, n = _con_layout()
    c = np.zeros((128, n), np.float32)

    def put(name, arr):
        o, w = lay[name]
        c[:arr.shape[0], o:o + w] = arr.reshape(arr.shape[0], -1)

    put("ident", np.eye(128, dtype=np.float32))
    put("ones", np.ones((128, 128), np.float32))
    rot = np.zeros((128, 128), np.float32)
    for d in range(64):
        rot[d + 64, d] = -1.0
        rot[d, d + 64] = 1.0
    put("rotm", rot)
    idx = np.arange(128)
    put("maskT", np.where(idx[None, :] >= idx[:, None], 0.0, -1e30).astype(np.float32))
    put("maskS", np.where(idx[None, :] < idx[:, None], 0.0, -1e30).astype(np.float32))
    invf = (10000.0 ** (-np.arange(0, 128, 2, dtype=np.float32) / np.float32(128))).astype(np.float32)
    put("invf", np.concatenate([invf, invf])[None, :])
    heads = np.arange(4) + 4 * g
    lg = np.log1p(-np.exp2(-5.0 - heads.astype(np.float32))).astype(np.float32)
    sc = np.float32(128 ** -0.5)
    rel = (idx[None, :] - idx[:, None]).astype(np.float32)
    dmT = np.where((rel >= 0)[:, None, :], np.exp(np.maximum(rel, 0.0)[:, None, :] * lg[None, :, None]), 0.0) * sc
    put("dmT", dmT.astype(np.float32))
    qdec = np.exp((idx + 1.0)[None, None, :] * lg[None, :, None]) * np.ones((128, 1, 1))
    put("qdec", qdec.astype(np.float32))
    kdec = np.exp((127.0 - idx)[:, None] * lg[None, :]) * sc
    put("kdec", kdec.astype(np.float32))
    put("glb", (np.exp(128.0 * lg)[None, :] * np.ones((128, 1))).astype(np.float32))
    s4 = np.zeros((4, 4, 128), np.float32)
    for h in range(4):
        s4[h, h, :] = 1.0
    put("selbc4", s4)
    s8 = np.zeros((8, 8, 128), np.float32)
    for h in range(8):
        s8[h, h, :] = 1.0
    put("selbc8", s8)
    sc_ = np.zeros((8, 6, 48), np.float32)
    for h in range(8):
        for q_ in range(6):
            sc_[h, q_, q_ * 8 + h] = 1.0
    put("selc", sc_)
    cm = np.ones((8, 512), np.float32)
    cm[:, 0::128] = 0.0
    put("cmask", cm)
    el = np.zeros((128, 128), np.float32)
    el[127, :] = 1.0
    put("elast", el)
    return c


def _cv(con, lay, name, rows=128):
    o, w = lay[name]
    return con[0:rows, o:o + w]


def build_AB_odd(T, ctx=None):
    NCOL = 3072
    lay, ncon = _con_layout()
    if ctx is None:
        nc = _new_nc()
        xT = nc.dram_tensor("xT", [D_MODEL, T], F32, kind="ExternalInput").ap()
        w_loc = nc.dram_tensor("w", [D_MODEL, NCOL], F32, kind="ExternalInput").ap()
        prm = nc.dram_tensor("prm", [128, 64], F32, kind="ExternalInput").ap()
        con_d = nc.dram_tensor("con", [128, ncon], F32, kind="ExternalInput").ap()
        lruw_d = nc.dram_tensor("lruw", [128, 8 * 128], F32, kind="ExternalInput").ap()
        pos_d = nc.dram_tensor("pos", [1, T], I32, kind="ExternalInput").ap()
        mixT = nc.dram_tensor("mixT", [1024, T], F32, kind="ExternalOutput").ap()
        projT = nc.dram_tensor("projT", [NCOL, T], F32, kind="Internal").ap()
        mixA, mixB = mixT[0:512, :], mixT[512:1024, :]
        p = Prog(nc)
        es = contextlib.ExitStack()
        es.enter_context(nc.allow_low_precision("bf16 matmul operands, fp32 accumulation"))
    else:
        nc, p = ctx["nc"], ctx["p"]
        xT, w_loc, prm, con_d, lruw_d, pos_d, projT, mixA, mixB = (ctx[k] for k in ("xT", "w", "prm", "con", "lruw", "pos", "projT", "mixA", "mixB"))
    prm_t = p.sb([128, 64])
    p.dma(prm_t[:, :], prm[:, :], W=[prm_t])
    con = p.sb([128, ncon])
    p.dma(con[:, :], con_d[:, :], W=[con])
    ones_b = p.sb([128, 128], BF16)
    p.memset(ones_b[:, :], 1.0, W=[ones_b])
    eps_t = p.sb([128, 1])
    p.memset(eps_t[:, :], EPS, W=[eps_t])
    one_t = p.sb([128, 1])
    p.memset(one_t[:, :], 1.0, W=[one_t])
    phase_A(p, xT, w_loc, NCOL, projT, prm_t, T, ones_b, eps_t)

    ident = _cv(con, lay, "ident")
    ones_f = _cv(con, lay, "ones")
    PI = float(np.pi)
    C1 = 6.28125
    C2 = float(2 * np.pi - 6.28125)
    with p.scope():
        lruw = p.sb([128, 8, 128])
        p.dma(lruw[:, :, :], lruw_d.rearrange("p (n j) -> p n j", n=8), W=[lruw])
        sp_t = p.sb([128, 4])
        m8 = p.sb([128, 4])
        m16 = p.sb([128, 4])
        p.act(sp_t[:, :], prm_t[:, 48:52], AF.Exp, R=[prm_t], W=[sp_t], scale=-1.0)
        p.act(sp_t[:, :], sp_t[:, :], AF.Ln, R=[sp_t, one_t], W=[sp_t], bias=one_t[:, 0:1])
        p.ts(m8[:, :], sp_t[:, :], -8.0, None, ALU.mult, R=[sp_t], W=[m8])
        p.ts(m16[:, :], sp_t[:, :], -16.0, None, ALU.mult, R=[sp_t], W=[m16])
        hst = p.sb([128, 4])
        p.memset(hst[:, :], 0.0, W=[hst])
        S4 = p.sb([128, 4, 128])
        p.memset(S4[:, :, :], 0.0, W=[S4])

        two = lambda shape, dt=F32: [p.sb(shape, dt) for _ in range(2)]
        v4_, qp4_, kp4_, o4_ = (two([128, 4, 512]) for _ in range(4))
        g4s = p.sb([128, 4, 512])
        g4_ = [g4s, g4s]
        q4s = p.sb([128, 4, 512])
        k4s = p.sb([128, 4, 512])
        q4_, k4_ = [q4s, q4s], [k4s, k4s]
        t14 = p.sb([128, 4, 512])
        sq1 = p.sb([128, 512])
        posi = p.sb([1, 512], I32)
        posf = p.sb([1, 512])
        ang = p.sb([128, 512])
        kf = p.sb([128, 512])
        ki = p.sb([128, 512], I32)
        yy = p.sb([128, 512])
        mm_ = p.sb([128, 512])
        sinT = p.sb([128, 512])
        cosT = p.sb([128, 512])
        attnT = p.sb([128, 4, 128])
        qg = p.sb([128, 4, 128])
        kd = p.sb([128, 4, 128])
        Vt = p.sb([128, 4, 128])
        rs1 = p.sb([128, 512])
        rinv = p.sb([128, 512])
        xd_, zt_, xc_, rr_, ii_, aa_, a2_, hh_ = (two([128, 515] if i == 0 else [128, 512]) for i in range(8))
        pA = p.ps([128, 512])
        pB = p.ps([128, 512])
        pG = p.ps([128, 512])
        pK = p.ps([128, 512])
        pV = p.ps([128, 512])
        pO = p.ps([128, 512])
        pD = p.ps([128, 512])
        pN = p.ps([128, 512])
        dmT = _cv(con, lay, "dmT").rearrange("p (h c) -> p h c", h=4)
        qdec = _cv(con, lay, "qdec").rearrange("p (h c) -> p h c", h=4)
        kdec = _cv(con, lay, "kdec")
        glb = _cv(con, lay, "glb")
        rotm = _cv(con, lay, "rotm")
        invf = _cv(con, lay, "invf", 1)
        NB = T // 512

        def prep(blk):
            bsl = slice(blk * 512, (blk + 1) * 512)
            q4, k4, v4, g4, qp4, kp4 = (t[blk % 2] for t in (q4_, k4_, v4_, g4_, qp4_, kp4_))
            for (dst, r0) in ((q4, 0), (k4, 512), (v4, 1024)):
                p.dma(dst[:, :, :], projT[r0:r0 + 512, bsl].rearrange("(h p) t -> p h t", p=128), W=[dst])
            p.dma(posi[:, :], pos_d[:, bsl], W=[posi])
            p.cp(posf[:, :], posi[:, :], R=[posi], W=[posf])
            p.mm(pA[:, :], invf, posf[0:1, :], R=[con, posf], W=[pA])
            p.cp(ang[:, :], pA[:, :], R=[pA], W=[ang], eng="act")
            yield
            p.ts(kf[:, :], ang[:, :], float(1.0 / (2 * np.pi)), None, ALU.mult, R=[ang], W=[kf])
            p.cp(ki[:, :], kf[:, :], R=[kf], W=[ki])
            p.cp(kf[:, :], ki[:, :], R=[ki], W=[kf])
            p.stt(ang[:, :], kf[:, :], -C1, ang[:, :], ALU.mult, ALU.add, R=[kf, ang], W=[ang])
            p.stt(ang[:, :], kf[:, :], -C2, ang[:, :], ALU.mult, ALU.add, R=[kf, ang], W=[ang])
            yield
            for (dstT, shift) in ((sinT, 0.0), (cosT, PI / 2)):
                p.ts(yy[:, :], ang[:, :], shift, None, ALU.add, R=[ang], W=[yy])
                p.ts(mm_[:, :], yy[:, :], PI, 2 * PI, ALU.is_gt, ALU.mult, R=[yy], W=[mm_])
                p.tt(yy[:, :], yy[:, :], mm_[:, :], ALU.subtract, R=[yy, mm_], W=[yy])
                p.ts(mm_[:, :], yy[:, :], -PI, 2 * PI, ALU.is_lt, ALU.mult, R=[yy], W=[mm_])
                p.tt(yy[:, :], yy[:, :], mm_[:, :], ALU.add, R=[yy, mm_], W=[yy])
                p.ts(yy[:, :], yy[:, :], -PI, PI, ALU.max, ALU.min, R=[yy], W=[yy])
                p.act(dstT[:, :], yy[:, :], AF.Sin, R=[yy], W=[dstT])
                yield
            for (src4, dst4) in ((q4, qp4), (k4, kp4)):
                for h in range(4):
                    pr = pA if h % 2 == 0 else pB
                    p.mm(pr[:, :], rotm, src4[:, h, :], R=[con, src4], W=[pr], exact=True)
                    p.tt(dst4[:, h, :], pr[:, :], sinT[:, :], ALU.mult, R=[pr, sinT], W=[(dst4, h)])
                    p.tt(t14[:, h, :], src4[:, h, :], cosT[:, :], ALU.mult, R=[src4, cosT], W=[(t14, h)], eng="pool")
                    yield
                p.tt(dst4[:, :, :], dst4[:, :, :], t14[:, :, :], ALU.add, R=[dst4, t14], W=[dst4], eng="pool")
                yield

        def ret_chunk(blk, c4):
            q4, k4, v4, g4, qp4, kp4, o4 = (t[blk % 2] for t in (q4_, k4_, v4_, g4_, qp4_, kp4_, o4_))
            cs = slice(c4 * 128, (c4 + 1) * 128)
            for h in range(4):
                hs = slice(h * 128, (h + 1) * 128)
                p.mm(pG[:, hs], kp4[:, h, cs], qp4[:, h, cs], R=[kp4, qp4], W=[(pG, h)])
                p.tr(pK[:, hs], kp4[:, h, cs], ident, R=[kp4, con], W=[(pK, h)])
                p.tr(pV[:, hs], v4[:, h, cs], ident, R=[v4, con], W=[(pV, h)])
            yield
            p.tt(attnT[:, :, :], pG[:, :].rearrange("p (h c) -> p h c", h=4), dmT, ALU.mult, R=[pG, con], W=[attnT])
            p.tt(qg[:, :, :], qp4[:, :, cs], qdec, ALU.mult, R=[qp4, con], W=[qg], eng="pool")
            for h in range(4):
                hs = slice(h * 128, (h + 1) * 128)
                p.act(kd[:, h, :], pK[:, hs], AF.Copy, R=[(pK, h), con], W=[(kd, h)], scale=kdec[:, h:h + 1])
            p.cp(Vt[:, :, :], pV[:, :].rearrange("p (h c) -> p h c", h=4), R=[pV], W=[Vt], eng="act")
            yield
            for h in range(4):
                hs = slice(h * 128, (h + 1) * 128)
                p.mm(pO[:, hs], S4[:, h, :], qg[:, h, :], start=True, stop=False, R=[(S4, h), qg], W=[(pO, h)])
                p.mm(pO[:, hs], Vt[:, h, :], attnT[:, h, :], start=False, stop=True, R=[Vt, attnT], W=[(pO, h)])
                p.mm(pD[:, hs], kd[:, h, :], Vt[:, h, :], R=[(kd, h), Vt], W=[(pD, h)])
            yield
            p.cp(o4[:, :, cs], pO[:, :].rearrange("p (h c) -> p h c", h=4), R=[pO], W=[(o4, c4)], eng="act")
            for h in range(4):
                hs = slice(h * 128, (h + 1) * 128)
                p.stt(S4[:, h, :], S4[:, h, :], glb[:, h:h + 1], pD[:, hs], ALU.mult, ALU.add,
                      R=[(S4, h), (pD, h), con], W=[(S4, h)])

        def lru_tile(blk, ct):
            bsl = slice(blk * 512, (blk + 1) * 512)
            xd, zt, xc, rr, ii, aa, a2, hh = (t[ct % 2] for t in (xd_, zt_, xc_, rr_, ii_, aa_, a2_, hh_))
            r0 = 2048 + ct * 128
            if blk == 0:
                p.memset(xd[:, 0:3], 0.0, W=[xd])
                p.dma(xd[:, 3:515], projT[r0:r0 + 128, 0:512], W=[xd])
            else:
                p.dma(xd[:, :], projT[r0:r0 + 128, blk * 512 - 3:(blk + 1) * 512], W=[xd])
            p.dma(zt[:, :], projT[r0 + 512:r0 + 640, bsl], W=[zt])
            cw = 20 + ct * 4
            p.ts(xc[:, :], xd[:, 0:512], prm_t[:, cw:cw + 1], prm_t[:, 36 + ct:37 + ct], ALU.mult, ALU.add,
                 R=[xd, prm_t], W=[xc])
            for j in range(1, 4):
                p.stt(xc[:, :], xd[:, j:j + 512], prm_t[:, cw + j:cw + j + 1], xc[:, :], ALU.mult, ALU.add,
                      R=[xd, prm_t, xc], W=[xc])
            yield
            p.mm(pA[:, :], lruw[:, ct, :], xc[:, :], R=[lruw, xc], W=[pA])
            p.mm(pB[:, :], lruw[:, 4 + ct, :], xc[:, :], R=[lruw, xc], W=[pB])
            p.act(rr[:, :], pA[:, :], AF.Sigmoid, R=[pA, prm_t], W=[rr], bias=prm_t[:, 40 + ct:41 + ct])
            p.act(ii[:, :], pB[:, :], AF.Sigmoid, R=[pB, prm_t], W=[ii], bias=prm_t[:, 44 + ct:45 + ct])
            yield
            p.act(aa[:, :], rr[:, :], AF.Exp, R=[rr, m8], W=[aa], scale=m8[:, ct:ct + 1])
            p.act(a2[:, :], rr[:, :], AF.Exp, R=[rr, m16], W=[a2], scale=m16[:, ct:ct + 1])
            p.act(a2[:, :], a2[:, :], AF.Sqrt, R=[a2, one_t], W=[a2], bias=one_t[:, 0:1], scale=-1.0)
            p.tt(ii[:, :], ii[:, :], a2[:, :], ALU.mult, R=[ii, a2], W=[ii])
            p.tt(ii[:, :], ii[:, :], xc[:, :], ALU.mult, R=[ii, xc], W=[ii], eng="pool")
            yield
            p.op("dve", lambda e: e.tensor_tensor_scan(hh[:, :], aa[:, :], ii[:, :], hst[:, ct:ct + 1], ALU.mult, ALU.add),
                 R=[aa, ii, hst], W=[hh])
            p.cp(hst[:, ct:ct + 1], hh[:, 511:512], R=[hh], W=[hst])
            p.act(zt[:, :], zt[:, :], AF.Silu, R=[zt], W=[zt])
            p.tt(hh[:, :], hh[:, :], zt[:, :], ALU.mult, R=[hh, zt], W=[hh], eng="pool")
            p.dma(mixB[ct * 128:(ct + 1) * 128, bsl], hh[:, :], R=[hh])

        def post(blk):
            bsl = slice(blk * 512, (blk + 1) * 512)
            g4, o4 = g4_[blk % 2], o4_[blk % 2]
            p.dma(g4[:, :, :], projT[1536:2048, bsl].rearrange("(h p) t -> p h t", p=128), W=[g4])
            p.act(g4[:, :, :], g4[:, :, :], AF.Silu, R=[g4], W=[g4])
            yield
            for h in range(4):
                p.act(sq1[:, :], o4[:, h, :], AF.Square, R=[o4], W=[sq1])
                p.mm(pN[:, :], ones_f, sq1[:, :], R=[con, sq1], W=[pN])
                p.act(rs1[:, :], pN[:, :], AF.Sqrt, R=[pN, eps_t], W=[rs1], bias=eps_t[:, 0:1], scale=1.0 / 128)
                p.recip(rinv[:, :], rs1[:, :], R=[rs1], W=[rinv])
                p.stt(o4[:, h, :], o4[:, h, :], prm_t[:, 16 + h:17 + h], rinv[:, :], ALU.mult, ALU.mult,
                      R=[o4, prm_t, rinv], W=[(o4, ("n", h))])
                yield
            p.tt(o4[:, :, :], o4[:, :, :], g4[:, :, :], ALU.mult, R=[o4, g4], W=[o4], eng="pool")
            p.dma(mixA[:, bsl].rearrange("(h p) t -> p h t", p=128), o4[:, :, :], R=[o4])

        def chunks(blk):
            for c4 in range(4):
                gens = [ret_chunk(blk, c4), lru_tile(blk, c4)]
                while gens:
                    for g_ in list(gens):
                        try:
                            next(g_)
                        except StopIteration:
                            gens.remove(g_)
                        yield

        def drain(*gens):
            gens = [g_ for g_ in gens if g_ is not None]
            while gens:
                for g_ in list(gens):
                    try:
                        next(g_)
                    except StopIteration:
                        gens.remove(g_)

        drain(prep(0))
        for blk in range(NB):
            drain(chunks(blk), prep(blk + 1) if blk + 1 < NB else None, post(blk - 1) if blk >= 1 else None)
        drain(post(NB - 1))
    if ctx is not None:
        return None
    p.emit()
    es.close()
    return nc


def bcast(ap, n):
    sh = list(ap.shape)
    return ap.unsqueeze(len(sh)).broadcast_to(sh + [n])


def build_AB_even(T, ctx=None):
    NCOL = 3344
    lay, ncon = _con_layout()
    if ctx is None:
        nc = _new_nc()
        xT = nc.dram_tensor("xT", [D_MODEL, T], F32, kind="ExternalInput").ap()
        w_loc = nc.dram_tensor("w", [D_MODEL, NCOL], F32, kind="ExternalInput").ap()
        prm = nc.dram_tensor("prm", [128, 112], F32, kind="ExternalInput").ap()
        hp_d = nc.dram_tensor("hp", [8, 8], F32, kind="ExternalInput").ap()
        dbc_d = nc.dram_tensor("dbc", [128, 512], F32, kind="ExternalInput").ap()
        con_d = nc.dram_tensor("con", [128, ncon], F32, kind="ExternalInput").ap()
        mixT = nc.dram_tensor("mixT", [1024, T], F32, kind="ExternalOutput").ap()
        projT = nc.dram_tensor("projT", [NCOL, T], F32, kind="Internal").ap()
        mixA, mixB = mixT[0:512, :], mixT[512:1024, :]
        p = Prog(nc)
        es = contextlib.ExitStack()
        es.enter_context(nc.allow_low_precision("bf16 matmul operands, fp32 accumulation"))
    else:
        nc, p = ctx["nc"], ctx["p"]
        xT, w_loc, prm, hp_d, dbc_d, con_d, projT, mixA, mixB = (ctx[k] for k in ("xT", "w", "prm", "hp", "dbc", "con", "projT", "mixA", "mixB"))
    prm_t = p.sb([128, 112])
    p.dma(prm_t[:, :], prm[:, :], W=[prm_t])
    con = p.sb([128, ncon])
    p.dma(con[:, :], con_d[:, :], W=[con])
    hp = p.sb([8, 8])
    p.dma(hp[:, :], hp_d[:, :], W=[hp])
    ones_b = p.sb([128, 128], BF16)
    p.memset(ones_b[:, :], 1.0, W=[ones_b])
    eps_t = p.sb([128, 1])
    p.memset(eps_t[:, :], EPS, W=[eps_t])
    one_t = p.sb([128, 1])
    p.memset(one_t[:, :], 1.0, W=[one_t])
    phase_A(p, xT, w_loc, NCOL, projT, prm_t, T, ones_b, eps_t)

    ident = _cv(con, lay, "ident")
    ones_f = _cv(con, lay, "ones")
    maskT = _cv(con, lay, "maskT")
    maskS = _cv(con, lay, "maskS")
    selbc4 = _cv(con, lay, "selbc4", 4).rearrange("p (h c) -> p h c", h=4)
    selbc8 = _cv(con, lay, "selbc8", 8).rearrange("p (h c) -> p h c", h=8)
    selc = _cv(con, lay, "selc", 8).rearrange("p (q c) -> p q c", q=6)
    cmask = _cv(con, lay, "cmask", 8)
    SC = float(128 ** -0.5)

    def v4(ps_t):
        return ps_t[:, :].rearrange("p (h c) -> p h c", h=4)

    with p.scope():
        banks = [p.ps([128, 512]) for _ in range(6)]
        pBig = p.ps([128, 1024])
        bi = [0]

        def bank():
            b = banks[bi[0] % 6]
            bi[0] += 1
            return b

        dbc = p.sb([128, 512])
        p.dma(dbc[:, :], dbc_d[:, :], W=[dbc])
        negA = p.sb([8, 2])
        p.memset(negA[:, :], 0.0, W=[negA])
        p.act(negA[0:4, 0:1], hp[0:4, 0:1], AF.Exp, R=[hp], W=[negA])
        p.act(negA[0:8, 1:2], hp[0:8, 2:3], AF.Exp, R=[hp], W=[negA])
        p.ts(negA[:, :], negA[:, :], -1.0, None, ALU.mult, R=[negA], W=[negA])
        S4 = p.sb([128, 4, 128])
        p.memset(S4[:, :, :], 0.0, W=[S4])
        Hs = p.sb([128, 512])
        p.memset(Hs[:, :], 0.0, W=[Hs])

        qh = p.sb([128, 4, 515])
        kh = p.sb([128, 4, 515])
        vh = p.sb([128, 4, 515])
        z4 = p.sb([128, 4, 512])
        qc = p.sb([128, 4, 512])
        kc = p.sb([128, 4, 512])
        vc = p.sb([128, 4, 512])
        sq4 = p.sb([128, 4, 512])
        o4 = p.sb([128, 4, 512])
        rs1 = p.sb([128, 512])
        rinv = p.sb([128, 512])
        a_r = p.sb([8, 512])
        b_r = p.sb([8, 512])
        g_r = p.sb([8, 512])
        gc_r = p.sb([8, 512])
        be_r = p.sb([8, 512])
        bg_r = p.sb([8, 512])
        colt = p.sb([128, 48])
        ncol = p.sb([128, 8])
        gcbc = p.sb([128, 4, 128])
        egcbc = p.sb([128, 4, 128])
        T1 = p.sb([128, 4, 128])
        DmT = p.sb([128, 4, 128])
        Dm = p.sb([128, 4, 128])
        kdec = p.sb([128, 4])
        kd = p.sb([128, 4, 128])
        Vb = p.sb([128, 4, 128])
        attnT = p.sb([128, 4, 128])
        qg = p.sb([128, 4, 128])
        Pm = [p.sb([128, 4, 128]) for _ in range(2)]
        Ym = [p.sb([128, 4, 128]) for _ in range(2)]
        Am = p.sb([128, 4, 128])
        Xm = p.sb([128, 4, 128])
        Vn = p.sb([128, 4, 128])
        rs1s = [rs1, p.sb([128, 512])]
        rinvs = [rinv, p.sb([128, 512])]
        o4s = p.sb([128, 4, 512])
        a_rs = p.sb([8, 512])
        g_rs = p.sb([8, 512])
        gc_rs = p.sb([8, 512])
        be_rs = p.sb([8, 512])
        colts = p.sb([128, 48])
        xsh = p.sb([128, 4, 515])
        Bh = p.sb([128, 515])
        Ch = p.sb([128, 515])
        xs4 = p.sb([128, 4, 512])
        Bc = p.sb([128, 512])
        Cc = p.sb([128, 512])
        csb = p.sb([128, 8, 128])
        lmT = p.sb([128, 8, 128])
        ecs = p.sb([128, 8])
        dect = p.sb([128, 8])
        glb8 = p.sb([128, 8])
        xst = p.sb([128, 512])
        xct = p.sb([128, 512])
        xcd = p.sb([128, 512])
        yt = p.sb([128, 512])
        Bt = p.sb([128, 128])

        def conv_tile(dst, src, wcol, bcol):
            if bcol is None:
                p.ts(dst, src[0], prm_t[:, wcol:wcol + 1], None, ALU.mult, R=[src[4], prm_t], W=[src[5]])
            else:
                p.ts(dst, src[0], prm_t[:, wcol:wcol + 1], prm_t[:, bcol:bcol + 1], ALU.mult, ALU.add,
                     R=[src[4], prm_t], W=[src[5]])
            for j in range(1, 4):
                p.stt(dst, src[j], prm_t[:, wcol + j:wcol + j + 1], dst, ALU.mult, ALU.add, R=[src[4], prm_t, src[5]], W=[src[5]])
            p.act(dst, dst, AF.Silu, R=[src[5]], W=[src[5]])

        def load_halo(dst, r0, nrow, blk, multi):
            if multi:
                d0 = dst[:, :, 0:3]
                dr = dst[:, :, 3:515]
                da = dst[:, :, :]
                src = lambda c0, c1: projT[r0:r0 + nrow, c0:c1].rearrange("(k p) t -> p k t", p=128)
            else:
                d0 = dst[:, 0:3]
                dr = dst[:, 3:515]
                da = dst[:, :]
                src = lambda c0, c1: projT[r0:r0 + nrow, c0:c1]
            if blk == 0:
                p.memset(d0, 0.0, W=[dst])
                p.dma(dr, src(0, 512), W=[dst])
            else:
                p.dma(da, src(blk * 512 - 3, (blk + 1) * 512), W=[dst])

        def conv_tiles(specs):
            for j in range(4):
                for (dst, srcs, stok, dtok, wcol, bcol) in specs:
                    if j == 0:
                        if bcol is None:
                            p.ts(dst, srcs[0], prm_t[:, wcol:wcol + 1], None, ALU.mult, R=[stok, prm_t], W=[dtok])
                        else:
                            p.ts(dst, srcs[0], prm_t[:, wcol:wcol + 1], prm_t[:, bcol:bcol + 1], ALU.mult, ALU.add,
                                 R=[stok, prm_t], W=[dtok])
                    else:
                        p.stt(dst, srcs[j], prm_t[:, wcol + j:wcol + j + 1], dst, ALU.mult, ALU.add, R=[stok, prm_t, dtok], W=[dtok])
            for (dst, srcs, stok, dtok, wcol, bcol) in specs:
                p.act(dst, dst, AF.Silu, R=[dtok], W=[dtok])

        def gdn_chunk(c4):
                    cs = slice(c4 * 128, (c4 + 1) * 128)
                    b = bank()
                    for qi, rows in enumerate((gc_r, be_r, bg_r)):
                        p.mm(b[:, 0:48], rows[0:4, cs], selc[0:4, qi, :], start=(qi == 0), stop=(qi == 2), R=[rows, con], W=[b])
                    p.cp(colt[:, :], b[:, 0:48], R=[b], W=[colt], eng="act")
                    p.ts(ncol[:, 0:4], colt[:, 16:20], -1.0, None, ALU.mult, R=[colt], W=[ncol])
                    b = bank()
                    for h in range(4):
                        p.mm(b[:, h * 128:(h + 1) * 128], selbc4[:, h, :], gc_r[0:4, cs], R=[con, gc_r], W=[b])
                    p.cp(gcbc[:, :, :], v4(b), R=[b], W=[gcbc], eng="act")
                    p.act(egcbc[:, :, :], v4(b), AF.Exp, R=[b], W=[egcbc])
                    for h in range(4):
                        p.stt(T1[:, h, :], gcbc[:, h, :], colt[:, h:h + 1], maskT, ALU.subtract, ALU.add, R=[gcbc, colt, con], W=[(T1, h)])
                    p.act(DmT[:, :, :], T1[:, :, :], AF.Exp, R=[T1], W=[DmT])
                    for h in range(4):
                        p.stt(T1[:, h, :], gcbc[:, h, :], colt[:, h:h + 1], maskS, ALU.subtract, ALU.subtract, R=[gcbc, colt, con], W=[(T1, h)])
                    p.act(Dm[:, :, :], T1[:, :, :], AF.Exp, R=[T1], W=[Dm], scale=-1.0)
                    p.tt(kdec[:, :], gcbc[:, :, 127], colt[:, 0:4], ALU.subtract, R=[gcbc, colt], W=[kdec])
                    p.act(kdec[:, :], kdec[:, :], AF.Exp, R=[kdec], W=[kdec])
                    yield
                    bK = bank()
                    bV = bank()
                    bG = bank()
                    bL = bank()
                    for h in range(4):
                        hs = slice(h * 128, (h + 1) * 128)
                        p.tr(bK[:, hs], kc[:, h, cs], ident, R=[kc, con], W=[bK])
                        p.tr(bV[:, hs], vc[:, h, cs], ident, R=[vc, con], W=[bV])
                        p.mm(bG[:, hs], kc[:, h, cs], qc[:, h, cs], R=[kc, qc], W=[bG])
                        p.mm(bL[:, hs], kc[:, h, cs], kc[:, h, cs], R=[kc], W=[bL])
                    for h in range(4):
                        hs = slice(h * 128, (h + 1) * 128)
                        p.act(kd[:, h, :], bK[:, hs], AF.Copy, R=[bK, kdec], W=[(kd, h)], scale=kdec[:, h:h + 1])
                        p.act(Vb[:, h, :], bV[:, hs], AF.Copy, R=[bV, colt], W=[(Vb, h)], scale=colt[:, 8 + h:9 + h])
                        p.stt(Pm[0][:, h, :], bL[:, hs], colt[:, 8 + h:9 + h], Dm[:, h, :], ALU.mult, ALU.mult,
                              R=[bL, colt, Dm], W=[(Pm[0], h)])
                    p.tt(attnT[:, :, :], v4(bG), DmT[:, :, :], ALU.mult, R=[bG, DmT], W=[attnT])
                    p.tt(qg[:, :, :], qc[:, :, cs], egcbc[:, :, :], ALU.mult, R=[qc, egcbc], W=[qg], eng="pool")
                    yield
                    bY = bank()
                    for h in range(4):
                        p.tr(bY[:, h * 128:(h + 1) * 128], Pm[0][:, h, :], ident, R=[Pm[0], con], W=[bY])
                    p.cp(Ym[0][:, :, :], v4(bY), R=[bY], W=[Ym[0]], eng="act")
                    for h in range(4):
                        p.stt(Am[:, h, :], bY[:, h * 128:(h + 1) * 128], -1.0, ident, ALU.mult, ALU.add, R=[con, bY], W=[(Am, h)])
                    yield
                    cur = 0
                    for lev in range(6):
                        nxt = 1 - cur
                        bP = bank()
                        for h in range(4):
                            p.mm(bP[:, h * 128:(h + 1) * 128], Ym[cur][:, h, :], Pm[cur][:, h, :], R=[Ym[cur], Pm[cur]], W=[bP])
                        p.cp(Pm[nxt][:, :, :], v4(bP), R=[bP], W=[Pm[nxt]], eng="act")
                        if lev < 5:
                            bY2 = bank()
                            for h in range(4):
                                p.mm(bY2[:, h * 128:(h + 1) * 128], Pm[cur][:, h, :], Ym[cur][:, h, :], R=[Ym[cur], Pm[cur]], W=[bY2])
                            p.cp(Ym[nxt][:, :, :], v4(bY2), R=[bY2], W=[Ym[nxt]])
                        yield
                        bU = bank()
                        for h in range(4):
                            p.mm(bU[:, h * 128:(h + 1) * 128], Pm[nxt][:, h, :], Am[:, h, :], R=[Pm[nxt], Am], W=[bU])
                        p.tt(Am[:, :, :], v4(bU), Am[:, :, :], ALU.add, R=[Am, bU], W=[Am])
                        yield
                        cur = nxt
                    yield
                    bKS = bank()
                    for h in range(4):
                        p.mm(bKS[:, h * 128:(h + 1) * 128], kc[:, h, cs], S4[:, h, :], R=[kc, (S4, h)], W=[bKS])
                    for h in range(4):
                        p.stt(Xm[:, h, :], bKS[:, h * 128:(h + 1) * 128], ncol[:, h:h + 1], Vb[:, h, :], ALU.mult, ALU.add,
                              R=[bKS, ncol, Vb], W=[(Xm, h)])
                    yield
                    bVn = bank()
                    for h in range(4):
                        p.mm(bVn[:, h * 128:(h + 1) * 128], Am[:, h, :], Xm[:, h, :], R=[Am, Xm], W=[bVn])
                    p.cp(Vn[:, :, :], v4(bVn), R=[bVn], W=[Vn], eng="act")
                    yield
                    bO = bank()
                    bD = bank()
                    for h in range(4):
                        hs = slice(h * 128, (h + 1) * 128)
                        p.mm(bO[:, hs], S4[:, h, :], qg[:, h, :], start=True, stop=False, R=[(S4, h), qg], W=[bO])
                        p.mm(bO[:, hs], Vn[:, h, :], attnT[:, h, :], start=False, stop=True, R=[Vn, attnT], W=[bO])
                        p.mm(bD[:, hs], kd[:, h, :], Vn[:, h, :], R=[kd, Vn], W=[bD])
                    p.cp(o4[:, :, cs], v4(bO), R=[bO], W=[(o4, c4)], eng="act")
                    for h in range(4):
                        p.stt(S4[:, h, :], S4[:, h, :], egcbc[:, h, 127:128], bD[:, h * 128:(h + 1) * 128], ALU.mult, ALU.add,
                              R=[(S4, h), bD, egcbc], W=[(S4, h)])

        def ssd_chunk(c4):
                    cs = slice(c4 * 128, (c4 + 1) * 128)
                    b = bank()
                    p.mm(b[:, 0:48], be_rs[0:8, cs], selc[0:8, 3, :], start=True, stop=False, R=[be_rs, con], W=[b])
                    p.mm(b[:, 0:48], gc_rs[0:8, cs], selc[0:8, 4, :], start=False, stop=True, R=[gc_rs, con], W=[b])
                    p.cp(colts[:, :], b[:, 0:48], R=[b], W=[colts], eng="act")
                    p.act(ecs[:, :], colts[:, 32:40], AF.Exp, R=[colts], W=[ecs])
                    for h in range(8):
                        p.mm(pBig[:, h * 128:(h + 1) * 128], selbc8[:, h, :], gc_rs[0:8, cs], R=[con, gc_rs], W=[pBig])
                    p.cp(csb[:, :, :], pBig[:, :].rearrange("p (h c) -> p h c", h=8), R=[pBig], W=[csb], eng="act")
                    p.tt(dect[:, :], csb[:, :, 127], colts[:, 32:40], ALU.subtract, R=[csb, colts], W=[dect])
                    p.act(dect[:, :], dect[:, :], AF.Exp, R=[dect], W=[dect])
                    p.act(glb8[:, :], csb[:, :, 127], AF.Exp, R=[csb], W=[glb8])
                    yield
                    for h in range(8):
                        p.stt(lmT[:, h, :], csb[:, h, :], colts[:, 32 + h:33 + h], maskT, ALU.subtract, ALU.add,
                              R=[csb, colts, con], W=[(lmT, h)])
                    p.act(lmT[:, :, :], lmT[:, :, :], AF.Exp, R=[lmT], W=[lmT])
                    yield
                    bS = bank()
                    p.mm(bS[:, 0:128], Bc[:, cs], Cc[:, cs], R=[Bc, Cc], W=[bS])
                    for h in range(8):
                        p.tt(lmT[:, h, :], bS[:, 0:128], lmT[:, h, :], ALU.mult, R=[lmT, bS], W=[(lmT, h)])
                    yield
                    bX = bank()
                    for k in range(4):
                        p.tr(bX[:, k * 128:(k + 1) * 128], xs4[:, k, cs], ident, R=[xs4, con], W=[bX])
                    p.cp(xst[:, :], bX[:, :], R=[bX], W=[xst], eng="act")
                    p.tt(xct[:, :].rearrange("p (h q) -> p h q", h=8), bX[:, :].rearrange("p (h q) -> p h q", h=8),
                         bcast(colts[:, 24:32], 64), ALU.mult, R=[bX, colts], W=[xct])
                    yield
                    bYd = bank()
                    for h in range(8):
                        p.mm(bYd[:, h * 64:(h + 1) * 64], lmT[:, h, :], xct[:, h * 64:(h + 1) * 64], R=[lmT, xct], W=[bYd])
                    bYo = bank()
                    p.mm(bYo[:, :], Cc[:, cs], Hs[:, :], R=[Cc, Hs], W=[bYo])
                    p.tt(yt[:, :].rearrange("p (h q) -> p h q", h=8), bYo[:, :].rearrange("p (h q) -> p h q", h=8),
                         bcast(ecs[:, :], 64), ALU.mult, R=[bYo, ecs], W=[yt])
                    p.tt(yt[:, :], bYd[:, :], yt[:, :], ALU.add, R=[yt, bYd], W=[yt])
                    p.tt(xst[:, :], xst[:, :], dbc[:, :], ALU.mult, R=[xst, dbc], W=[xst], eng="pool")
                    p.tt(yt[:, :], yt[:, :], xst[:, :], ALU.add, R=[yt, xst], W=[yt])
                    yield
                    p.tt(xcd[:, :].rearrange("p (h q) -> p h q", h=8), xct[:, :].rearrange("p (h q) -> p h q", h=8),
                         bcast(dect[:, :], 64), ALU.mult, R=[xct, dect], W=[xcd])
                    bB = bank()
                    p.tr(bB[:, 0:128], Bc[:, cs], ident, R=[Bc, con], W=[bB])
                    p.cp(Bt[:, :], bB[:, 0:128], R=[bB], W=[Bt], eng="act")
                    bH = bank()
                    p.mm(bH[:, :], Bt[:, :], xcd[:, :], R=[Bt, xcd], W=[bH])
                    p.tt(Hs[:, :].rearrange("p (h q) -> p h q", h=8), Hs[:, :].rearrange("p (h q) -> p h q", h=8),
                         bcast(glb8[:, :], 64), ALU.mult, R=[Hs, glb8], W=[Hs])
                    p.tt(Hs[:, :], bH[:, :], Hs[:, :], ALU.add, R=[Hs, bH], W=[Hs])
                    yield
                    bT = bank()
                    for k in range(4):
                        p.tr(bT[:, k * 128:(k + 1) * 128], yt[:, k * 128:(k + 1) * 128], ident, R=[yt, con], W=[bT])
                    p.cp(o4s[:, :, cs], v4(bT), R=[bT], W=[(o4s, c4)], eng="act")

        for blk in range(T // 512):
            bsl = slice(blk * 512, (blk + 1) * 512)
            load_halo(qh, 0, 512, blk, True)
            load_halo(kh, 512, 512, blk, True)
            load_halo(vh, 1024, 512, blk, True)
            p.dma(z4[:, :, :], projT[1536:2048, bsl].rearrange("(k p) t -> p k t", p=128), W=[z4])
            p.dma(a_r[0:4, :], projT[3328:3332, bsl], W=[a_r])
            p.dma(b_r[0:4, :], projT[3332:3336, bsl], W=[b_r])
            load_halo(xsh, 2048, 512, blk, True)
            load_halo(Bh, 2560, 128, blk, False)
            load_halo(Ch, 2688, 128, blk, False)
            p.dma(a_rs[0:8, :], projT[3336:3344, bsl], W=[a_rs])
            specs = []
            for sec, (hsrc, dstc) in enumerate(((qh, qc), (kh, kc), (vh, vc))):
                for h in range(4):
                    specs.append((dstc[:, h, :], [hsrc[:, h, j:j + 512] for j in range(4)], hsrc, (dstc, h), 16 + (sec * 4 + h) * 4, None))
            for k in range(4):
                specs.append((xs4[:, k, :], [xsh[:, k, j:j + 512] for j in range(4)], xsh, (xs4, k), 64 + k * 4, 88 + k))
            specs.append((Bc[:, :], [Bh[:, j:j + 512] for j in range(4)], Bh, Bc, 64 + 16, 88 + 4))
            specs.append((Cc[:, :], [Ch[:, j:j + 512] for j in range(4)], Ch, Cc, 64 + 20, 88 + 5))
            conv_tiles(specs)
            ri = 0
            for (cc, scale) in ((qc, SC), (kc, None)):
                p.act(sq4[:, :, :], cc[:, :, :], AF.Square, R=[cc], W=[sq4])
                for h in range(4):
                    b = bank()
                    r1, r2 = rs1s[ri % 2], rinvs[ri % 2]
                    ri += 1
                    p.mm(b[:, :], ones_f, sq4[:, h, :], R=[con, sq4], W=[b])
                    p.act(r1[:, :], b[:, :], AF.Sqrt, R=[b, eps_t], W=[r1], bias=eps_t[:, 0:1], scale=1.0)
                    p.recip(r2[:, :], r1[:, :], R=[r1], W=[r2])
                    if scale is None:
                        p.tt(cc[:, h, :], cc[:, h, :], r2[:, :], ALU.mult, R=[cc, r2], W=[(cc, h)])
                    else:
                        p.stt(cc[:, h, :], cc[:, h, :], scale, r2[:, :], ALU.mult, ALU.mult, R=[cc, r2], W=[(cc, h)])
            p.act(g_r[0:4, :], a_r[0:4, :], AF.Exp, R=[a_r, hp], W=[g_r], bias=hp[0:4, 1:2])
            p.act(g_r[0:4, :], g_r[0:4, :], AF.Ln, R=[g_r, one_t], W=[g_r], bias=one_t[0:4, 0:1])
            p.ts(g_r[0:4, :], g_r[0:4, :], negA[0:4, 0:1], None, ALU.mult, R=[g_r, negA], W=[g_r])
            p.op("dve", lambda e: e.tensor_tensor_scan(gc_r[0:4, :], cmask[0:4, :], g_r[0:4, :], 0.0, ALU.mult, ALU.add),
                 R=[con, g_r], W=[gc_r])
            p.act(be_r[0:4, :], b_r[0:4, :], AF.Sigmoid, R=[b_r], W=[be_r])
            p.act(bg_r[0:4, :], gc_r[0:4, :], AF.Exp, R=[gc_r], W=[bg_r])
            p.tt(bg_r[0:4, :], bg_r[0:4, :], be_r[0:4, :], ALU.mult, R=[bg_r, be_r], W=[bg_r])
            p.act(be_rs[0:8, :], a_rs[0:8, :], AF.Exp, R=[a_rs, hp], W=[be_rs], bias=hp[0:8, 3:4])
            p.act(be_rs[0:8, :], be_rs[0:8, :], AF.Ln, R=[be_rs, one_t], W=[be_rs], bias=one_t[0:8, 0:1])
            p.ts(g_rs[0:8, :], be_rs[0:8, :], negA[0:8, 1:2], None, ALU.mult, R=[be_rs, negA], W=[g_rs])
            p.op("dve", lambda e: e.tensor_tensor_scan(gc_rs[0:8, :], cmask[0:8, :], g_rs[0:8, :], 0.0, ALU.mult, ALU.add),
                 R=[con, g_rs], W=[gc_rs])
            for c4 in range(4):
                gg, sg = gdn_chunk(c4), ssd_chunk(c4)
                gi = 0
                g_done = s_done = False
                while not (g_done and s_done):
                    if not g_done:
                        try:
                            next(gg)
                        except StopIteration:
                            g_done = True
                    gi += 1
                    if not s_done and (g_done or gi % 2 == 0):
                        try:
                            next(sg)
                        except StopIteration:
                            s_done = True
            p.act(sq4[:, :, :], o4[:, :, :], AF.Square, R=[o4], W=[sq4])
            p.act(z4[:, :, :], z4[:, :, :], AF.Silu, R=[z4], W=[z4])
            for h in range(4):
                b = bank()
                p.mm(b[:, :], ones_f, sq4[:, h, :], R=[con, sq4], W=[b])
                p.act(rs1[:, :], b[:, :], AF.Sqrt, R=[b, eps_t], W=[rs1], bias=eps_t[:, 0:1], scale=1.0 / 128)
                p.recip(rinv[:, :], rs1[:, :], R=[rs1], W=[rinv])
                p.stt(o4[:, h, :], o4[:, h, :], prm_t[:, 94:95], rinv[:, :], ALU.mult, ALU.mult,
                      R=[o4, prm_t, rinv], W=[(o4, ("n", h))])
            p.tt(o4[:, :, :], o4[:, :, :], z4[:, :, :], ALU.mult, R=[o4, z4], W=[o4], eng="pool")
            p.dma(mixA[:, bsl].rearrange("(h p) t -> p h t", p=128), o4[:, :, :], R=[o4])

            p.dma(z4[:, :, :], projT[2816:3328, bsl].rearrange("(k p) t -> p k t", p=128), W=[z4])
            p.act(z4[:, :, :], z4[:, :, :], AF.Silu, R=[z4], W=[z4])
            p.tt(o4s[:, :, :], o4s[:, :, :], z4[:, :, :], ALU.mult, R=[o4s, z4], W=[o4s], eng="pool")
            p.act(sq4[:, :, :], o4s[:, :, :], AF.Square, R=[o4s], W=[sq4])
            b = bank()
            for k in range(4):
                p.mm(b[:, :], ones_f, sq4[:, k, :], start=(k == 0), stop=(k == 3), R=[con, sq4], W=[b])
            p.act(rs1[:, :], b[:, :], AF.Sqrt, R=[b, eps_t], W=[rs1], bias=eps_t[:, 0:1], scale=1.0 / 512)
            p.recip(rinv[:, :], rs1[:, :], R=[rs1], W=[rinv])
            for k in range(4):
                p.stt(o4s[:, k, :], o4s[:, k, :], prm_t[:, 96 + k:97 + k], rinv[:, :], ALU.mult, ALU.mult,
                      R=[o4s, prm_t, rinv], W=[(o4s, ("n", k))])
            p.dma(mixB[:, bsl].rearrange("(h p) t -> p h t", p=128), o4s[:, :, :], R=[o4s])
    if ctx is not None:
        return None
    p.emit()
    es.close()
    return nc


def pack_even(g, xT, norm_g, w_in, gdn_conv_w, gdn_A_log, gdn_dt_bias, gdn_norm_g,
              ssd_conv_w, ssd_conv_b, ssd_A_log, ssd_dt_bias, ssd_D, ssd_norm_g):
    hq = np.arange(512 * g, 512 * g + 512)
    h4 = np.arange(4 * g, 4 * g + 4)
    h8 = np.arange(8 * g, 8 * g + 8)
    o_za, o_a, o_b, o_xbc = 3072, 4096, 4104, 4112
    o_zb, o_dt = o_xbc + 1536, o_xbc + 1536 + 1024
    cols = np.concatenate([hq, 1024 + hq, 2048 + hq, o_za + hq, o_xbc + hq,
                           o_xbc + 1024 + 128 * g + np.arange(128), o_xbc + 1280 + 128 * g + np.arange(128),
                           o_zb + hq, o_a + h4, o_b + h4, o_dt + h8])
    prm = np.zeros((128, 112), np.float32)
    prm[:, 0:16] = _pk(norm_g, 16)
    gw = np.asarray(gdn_conv_w, np.float32)
    for sec in range(3):
        for h in range(4):
            ch = sec * 1024 + 512 * g + h * 128
            for j in range(4):
                prm[:, 16 + (sec * 4 + h) * 4 + j] = gw[j, ch:ch + 128]
    sw = np.asarray(ssd_conv_w, np.float32)
    sb_ = np.asarray(ssd_conv_b, np.float32)
    starts = [512 * g + k * 128 for k in range(4)] + [1024 + 128 * g, 1280 + 128 * g]
    for i, c0 in enumerate(starts):
        for j in range(4):
            prm[:, 64 + i * 4 + j] = sw[j, c0:c0 + 128]
        prm[:, 88 + i] = sb_[c0:c0 + 128]
    prm[:, 94] = np.asarray(gdn_norm_g, np.float32)
    prm[:, 96:100] = _pk(np.asarray(ssd_norm_g)[hq], 4)
    hp = np.zeros((8, 8), np.float32)
    hp[0:4, 0] = np.asarray(gdn_A_log)[h4]
    hp[0:4, 1] = np.asarray(gdn_dt_bias)[h4]
    hp[0:8, 2] = np.asarray(ssd_A_log)[h8]
    hp[0:8, 3] = np.asarray(ssd_dt_bias)[h8]
    dbc = np.ascontiguousarray(np.broadcast_to(np.repeat(np.asarray(ssd_D, np.float32)[h8], 64)[None, :], (128, 512)))
    return dict(xT=np.ascontiguousarray(xT, dtype=np.float32), w=np.ascontiguousarray(np.asarray(w_in, np.float32)[:, cols]),
                prm=prm, hp=hp, dbc=dbc, con=make_consts(g))


def _pk(v, n):
    return np.ascontiguousarray(np.asarray(v, np.float32).reshape(n, 128).T)


def pack_odd(g, xT, norm_g, w_in, ret_ng, conv_w, conv_b, w_a, b_a, w_x, b_x, lam, pos):
    hq = slice(512 * g, 512 * g + 512)
    cols = np.concatenate([np.arange(0, 1024)[hq], 1024 + np.arange(1024)[hq], 2048 + np.arange(1024)[hq],
                           3072 + np.arange(1024)[hq], 4096 + np.arange(1024)[hq], 5120 + np.arange(1024)[hq]])
    prm = np.zeros((128, 64), np.float32)
    prm[:, 0:16] = _pk(norm_g, 16)
    prm[:, 16:20] = _pk(ret_ng[hq], 4)
    cw = np.asarray(conv_w, np.float32)[:, hq]
    for ct in range(4):
        for j in range(4):
            prm[:, 20 + ct * 4 + j] = cw[j, ct * 128:(ct + 1) * 128]
    prm[:, 36:40] = _pk(np.asarray(conv_b)[hq], 4)
    prm[:, 40:44] = _pk(np.asarray(b_a)[hq], 4)
    prm[:, 44:48] = _pk(np.asarray(b_x)[hq], 4)
    prm[:, 48:52] = _pk(np.asarray(lam)[hq], 4)
    lw = np.concatenate([np.asarray(w_a, np.float32)[4 * g:4 * g + 4], np.asarray(w_x, np.float32)[4 * g:4 * g + 4]], 0)
    lruw = np.ascontiguousarray(lw.transpose(1, 0, 2).reshape(128, 8 * 128))
    return dict(xT=np.ascontiguousarray(xT, dtype=np.float32), w=np.ascontiguousarray(np.asarray(w_in, np.float32)[:, cols]),
                prm=prm, con=make_consts(g), lruw=lruw, pos=np.ascontiguousarray(np.asarray(pos, np.int32).reshape(1, -1)))


def unpack_mix(parts):
    return np.concatenate([parts[0][0:512], parts[1][0:512], parts[0][512:1024], parts[1][512:1024]], 0)


def build_fused(T, depth=4):
    nc = _new_nc()
    lay, ncon = _con_layout()
    dt = lambda name, shape, dty=F32, kind="ExternalInput": nc.dram_tensor(name, list(shape), dty, kind=kind).ap()
    xT = dt("xT", [D_MODEL, T])
    pT = dt("pT", [depth * 256, T])
    pos = dt("pos", [1, T], I32)
    con = [dt(f"con{g}", [128, ncon]) for g in range(2)]
    yT = dt("yT", [D_MODEL, T], kind="ExternalOutput")
    projT = dt("projT", [3344, T], kind="Internal")
    mixF = dt("mixF", [D_MODEL, T], kind="Internal")
    xs = [dt("xA", [D_MODEL, T], kind="Internal"), dt("xB", [D_MODEL, T], kind="Internal")]
    p = Prog(nc)
    es = contextlib.ExitStack()
    es.enter_context(nc.allow_low_precision("bf16 matmul operands, fp32 accumulation"))
    x_cur = xT
    for i in range(depth):
        even = (i % 2 == 0)
        for g in range(2):
            ctx = dict(nc=nc, p=p, xT=x_cur, con=con[g], projT=projT[0:(3344 if even else 3072), :],
                       mixA=mixF[512 * g:512 * g + 512, :], mixB=mixF[1024 + 512 * g:1536 + 512 * g, :],
                       w=dt(f"w{i}_{g}", [D_MODEL, 3344 if even else 3072]),
                       prm=dt(f"prm{i}_{g}", [128, 112 if even else 64]))
            if even:
                ctx["hp"] = dt(f"hp{i}_{g}", [8, 8])
                ctx["dbc"] = dt(f"dbc{i}_{g}", [128, 512])
            else:
                ctx["lruw"] = dt(f"lruw{i}_{g}", [128, 8 * 128])
                ctx["pos"] = pos
            with p.scope():
                (build_AB_even if even else build_AB_odd)(T, ctx=ctx)
        final = (i == depth - 1)
        x_nxt = yT if final else xs[i % 2]
        ctx = dict(nc=nc, p=p, mixT=mixF, xT=x_cur, pT=pT[i * 256:(i + 1) * 256, :], wo=dt(f"wo{i}", [D_MODEL, D_MODEL]),
                   wg=dt(f"wg{i}", [D_MODEL, D_MODEL]), wp=dt(f"wp{i}", [256, D_MODEL]), prm=dt(f"prmC{i}", [128, 32]), yT=x_nxt)
        with p.scope():
            build_C(T, final, ctx=ctx)
        x_cur = x_nxt
    p.emit()
    es.close()
    return nc


_CACHE = {}


def _prog(name, fn):
    if name not in _CACHE:
        _CACHE[name] = fn()
    return _CACHE[name]


def kernel_unfused(x, p, positions, norm_g, ple_norm_g, w_ple_gate, w_ple_proj,
           ev_w_in, ev_w_out, gdn_conv_w, gdn_A_log, gdn_dt_bias, gdn_norm_g,
           ssd_conv_w, ssd_conv_b, ssd_A_log, ssd_dt_bias, ssd_D, ssd_norm_g,
           od_w_in, od_w_out, ret_norm_g, lru_conv_w, lru_conv_b,
           lru_w_a, lru_b_a, lru_w_x, lru_b_x, lru_lambda, final_norm_g):
    f = lambda a: np.asarray(a, dtype=np.float32)
    x = f(x)
    B, S, D = x.shape
    H = S // 2
    cores = list(range(8))
    xT = [np.ascontiguousarray(x[b].T) for b in range(B)]
    depth = int(np.asarray(norm_g).shape[0])
    for i in range(depth):
        j = i // 2
        if i % 2 == 0:
            nc = _prog("even", lambda: build_AB_even(S))
            ins = [pack_even(c % 2, xT[c // 2], f(norm_g)[i], f(ev_w_in)[j], f(gdn_conv_w)[j], f(gdn_A_log)[j],
                             f(gdn_dt_bias)[j], f(gdn_norm_g)[j], f(ssd_conv_w)[j], f(ssd_conv_b)[j], f(ssd_A_log)[j],
                             f(ssd_dt_bias)[j], f(ssd_D)[j], f(ssd_norm_g)[j]) for c in cores]
            w_out = f(ev_w_out)[j]
        else:
            nc = _prog("odd", lambda: build_AB_odd(S))
            ins = [pack_odd(c % 2, xT[c // 2], f(norm_g)[i], f(od_w_in)[j], f(ret_norm_g)[j], f(lru_conv_w)[j],
                            f(lru_conv_b)[j], f(lru_w_a)[j], f(lru_b_a)[j], f(lru_w_x)[j], f(lru_b_x)[j],
                            f(lru_lambda)[j], np.asarray(positions)[c // 2]) for c in cores]
            w_out = f(od_w_out)[j]
        res = run_bass_kernel_spmd(nc, ins, core_ids=cores)
        mix = [unpack_mix([res.results[2 * b]["mixT"], res.results[2 * b + 1]["mixT"]]) for b in range(B)]
        del res, ins
        final = (i == depth - 1)
        ncC = _prog("Cf" if final else "C", lambda: build_C(H, final))
        prm = np.zeros((128, 32), np.float32)
        prm[:, 0:16] = _pk(f(ple_norm_g)[i], 16)
        prm[:, 16:32] = _pk(f(final_norm_g), 16)
        wg = np.ascontiguousarray(f(w_ple_gate)[i])
        wp = np.ascontiguousarray(f(w_ple_proj)[i])
        wo = np.ascontiguousarray(w_out)
        insC = []
        for c in cores:
            b, g = c // 2, c % 2
            sl = slice(g * H, (g + 1) * H)
            insC.append(dict(mixT=np.ascontiguousarray(mix[b][:, sl]), xT=np.ascontiguousarray(xT[b][:, sl]),
                             pT=np.ascontiguousarray(f(p)[i, b, sl, :].T), wo=wo, wg=wg, wp=wp, prm=prm))
        res = run_bass_kernel_spmd(ncC, insC, core_ids=cores)
        xT = [np.concatenate([res.results[2 * b]["yT"], res.results[2 * b + 1]["yT"]], axis=1) for b in range(B)]
        del res, insC, mix
    return np.ascontiguousarray(np.stack([t.T for t in xT], 0)).astype(np.float32)


def kernel(x, p, positions, norm_g, ple_norm_g, w_ple_gate, w_ple_proj,
           ev_w_in, ev_w_out, gdn_conv_w, gdn_A_log, gdn_dt_bias, gdn_norm_g,
           ssd_conv_w, ssd_conv_b, ssd_A_log, ssd_dt_bias, ssd_D, ssd_norm_g,
           od_w_in, od_w_out, ret_norm_g, lru_conv_w, lru_conv_b,
           lru_w_a, lru_b_a, lru_w_x, lru_b_x, lru_lambda, final_norm_g):
    f = lambda a: np.asarray(a, dtype=np.float32)
    x = f(x)
    B, S, D = x.shape
    depth = int(np.asarray(norm_g).shape[0])
    nc = _prog("fused", lambda: build_fused(S, depth))
    shared = {}
    for g in range(2):
        shared[f"con{g}"] = make_consts(g)
    dummy = np.zeros((D, 8), np.float32)
    for i in range(depth):
        j = i // 2
        for g in range(2):
            if i % 2 == 0:
                d = pack_even(g, dummy, f(norm_g)[i], f(ev_w_in)[j], f(gdn_conv_w)[j], f(gdn_A_log)[j],
                              f(gdn_dt_bias)[j], f(gdn_norm_g)[j], f(ssd_conv_w)[j], f(ssd_conv_b)[j], f(ssd_A_log)[j],
                              f(ssd_dt_bias)[j], f(ssd_D)[j], f(ssd_norm_g)[j])
                shared[f"hp{i}_{g}"] = d["hp"]
                shared[f"dbc{i}_{g}"] = d["dbc"]
            else:
                d = pack_odd(g, dummy, f(norm_g)[i], f(od_w_in)[j], f(ret_norm_g)[j], f(lru_conv_w)[j],
                             f(lru_conv_b)[j], f(lru_w_a)[j], f(lru_b_a)[j], f(lru_w_x)[j], f(lru_b_x)[j],
                             f(lru_lambda)[j], np.zeros(8, np.int32))
                shared[f"lruw{i}_{g}"] = d["lruw"]
            shared[f"w{i}_{g}"] = d["w"]
            shared[f"prm{i}_{g}"] = d["prm"]
        prm = np.zeros((128, 32), np.float32)
        prm[:, 0:16] = _pk(f(ple_norm_g)[i], 16)
        prm[:, 16:32] = _pk(f(final_norm_g), 16)
        shared[f"prmC{i}"] = prm
        shared[f"wo{i}"] = np.ascontiguousarray(f(ev_w_out)[j] if i % 2 == 0 else f(od_w_out)[j])
        shared[f"wg{i}"] = np.ascontiguousarray(f(w_ple_gate)[i])
        shared[f"wp{i}"] = np.ascontiguousarray(f(w_ple_proj)[i])
    ins = []
    for c in range(8):
        b = c % B
        d = dict(shared)
        d["xT"] = np.ascontiguousarray(x[b].T)
        d["pT"] = np.ascontiguousarray(np.concatenate([f(p)[i, b].T for i in range(depth)], axis=0))
        d["pos"] = np.ascontiguousarray(np.asarray(positions, np.int32)[b].reshape(1, -1))
        ins.append(d)
    res = run_bass_kernel_spmd(nc, ins, core_ids=list(range(8)))
    return np.ascontiguousarray(np.stack([res.results[b]["yT"].T for b in range(B)], 0)).astype(np.float32)
```

```python
import contextlib
import numpy as np
import concourse.bass as bass
import concourse.mybir as mybir
from concourse.bass_utils import run_bass_kernel_spmd

F32 = mybir.dt.float32
F32R = mybir.dt.float32r
FP32R = False
BF16 = mybir.dt.bfloat16
I32 = mybir.dt.int32
AF = mybir.ActivationFunctionType
ALU = mybir.AluOpType

D_MODEL = 2048
SEQ = 8192
EPS = 1e-6
NDMA = 24
NDMA_HW = 16
SAME_ENGINE_SYNC = True
STORE_ON_POOL = True


class _Tok:
    __slots__ = ("w", "rs")

    def __init__(self):
        self.w = None
        self.rs = []


class Prog:
    ENGS = ("pe", "dve", "act", "pool", "sp")

    def __init__(self, nc):
        self.nc = nc
        self.q = {e: [] for e in self.ENGS}
        self.cnt = {e: 0 for e in self.ENGS}
        self.seen = {e: {} for e in self.ENGS}
        self.toks = {}
        self.slot_uses = [0] * NDMA
        self.rr = 0
        self.rr2 = 0
        self.stack = contextlib.ExitStack()
        self.stacks = [self.stack]
        self.pending = {e: {} for e in self.ENGS}
        self.nt = 0
        self.psum_ids = set()
        self.keep = []

    @contextlib.contextmanager
    def scope(self):
        st = contextlib.ExitStack()
        self.stacks.append(st)
        try:
            yield
        finally:
            self.stacks.pop()
            st.close()
            self.barrier()

    def barrier(self):
        for e in self.ENGS:
            pd = self.pending[e]
            for o in ("pe", "dve", "act", "pool"):
                if self.cnt[o] > 0:
                    pd[o] = self.cnt[o]
            for s in range(NDMA):
                if self.slot_uses[s] > 0:
                    pd[("d", s)] = 16 * self.slot_uses[s]

    def sb(self, shape, dt=F32, name=None):
        self.nt += 1
        t = self.stacks[-1].enter_context(self.nc.sbuf_tensor(name or f"t{self.nt}", list(shape), dt))
        self.keep.append(t)
        return t

    def ps(self, shape, dt=F32, name=None):
        self.nt += 1
        t = self.stacks[-1].enter_context(self.nc.psum_tensor(name or f"p{self.nt}", list(shape), dt))
        self.psum_ids.add(id(t))
        self.keep.append(t)
        return t

    def _tk(self, ref):
        if isinstance(ref, tuple):
            t, k = ref
        else:
            t, k = ref, None
        if id(t) in self.psum_ids:
            k = None
        d = self.toks.setdefault(id(t), {"_": _Tok()})
        return d, k

    def _deps(self, R, W, eng=None):
        need = {}

        def add(ev):
            if ev is not None:
                if need.get(ev[0], 0) < ev[1]:
                    need[ev[0]] = ev[1]

        for ref in R:
            d, k = self._tk(ref)
            add(d["_"].w)
            if k is None:
                for kk, tk in d.items():
                    add(tk.w)
            elif k in d:
                add(d[k].w)
            t_ = ref[0] if isinstance(ref, tuple) else ref
            if id(t_) in self.psum_ids:
                for ev in d["_"].rs:
                    if ev[0] != eng:
                        add(ev)
        for ref in W:
            d, k = self._tk(ref)
            keys = list(d.keys()) if k is None else (["_", k] if k in d else ["_"])
            for kk in keys:
                add(d[kk].w)
                for ev in d[kk].rs:
                    add(ev)
        return need

    def _upd(self, R, W, ev):
        for ref in R:
            d, k = self._tk(ref)
            tk = d["_"] if k is None else d.setdefault(k, _Tok())
            tk.rs.append(ev)
            if len(tk.rs) > 24:
                m = {}
                for e in tk.rs:
                    if m.get(e[0], 0) < e[1]:
                        m[e[0]] = e[1]
                tk.rs = list(m.items())
        for ref in W:
            d, k = self._tk(ref)
            if k is None:
                for kk in d:
                    d[kk].w = ev
                    d[kk].rs = []
            else:
                tk = d.setdefault(k, _Tok())
                tk.w = ev
                tk.rs = []

    def op(self, eng, fn, R=(), W=(), dma=False):
        need = self._deps(R, W, eng)
        if self.pending[eng]:
            for k, v in self.pending[eng].items():
                if need.get(k, 0) < v:
                    need[k] = v
            self.pending[eng] = {}
        if eng == "sp" or dma:
            if dma:
                s = NDMA_HW + self.rr2
                self.rr2 = (self.rr2 + 1) % (NDMA - NDMA_HW)
            else:
                s = self.rr
                self.rr = (self.rr + 1) % NDMA_HW
            key = ("d", s)
            if self.slot_uses[s] > 0:
                v = 16 * self.slot_uses[s]
                if need.get(key, 0) < v:
                    need[key] = v
            self.slot_uses[s] += 1
            ev = (key, 16 * self.slot_uses[s])
            inc = (key, 16)
        else:
            self.cnt[eng] += 1
            ev = (eng, self.cnt[eng])
            inc = (eng, 1)
        waits = []
        seen = self.seen[eng]
        for k, v in need.items():
            if k == eng and (eng == "pe" or not SAME_ENGINE_SYNC):
                continue
            if seen.get(k, 0) >= v:
                continue
            seen[k] = v
            waits.append((k, v))
        self.q[eng].append((waits, fn, inc))
        self._upd(R, W, ev)

    def mm(self, out, lhsT, rhs, start=True, stop=True, R=(), W=(), exact=False):
        if (FP32R and not exact and lhsT.dtype == F32 and rhs.dtype == F32
                and lhsT.partition_size() == 128 and rhs.partition_size() == 128):
            lhsT = lhsT.bitcast(F32R)
            rhs = rhs.bitcast(F32R)
        self.op("pe", lambda e: e.matmul(out, lhsT, rhs, start=start, stop=stop), R, W)

    def tr(self, out, in_, ident, R=(), W=()):
        self.op("pe", lambda e: e.transpose(out, in_, ident), R, W)

    def act(self, out, in_, func, R=(), W=(), bias=None, scale=None, eng="act"):
        kw = {}
        if bias is not None:
            kw["bias"] = bias
        if scale is not None:
            kw["scale"] = scale
        self.op(eng, lambda e: e.activation(out, in_, func, **kw), R, W)

    def tt(self, out, in0, in1, alu, R=(), W=(), eng="dve"):
        self.op(eng, lambda e: e.tensor_tensor(out, in0, in1, alu), R, W)

    def ts(self, out, in0, s1, s2, op0, op1=None, R=(), W=(), eng="dve"):
        if op1 is None:
            self.op(eng, lambda e: e.tensor_scalar(out, in0, s1, None, op0), R, W)
        else:
            self.op(eng, lambda e: e.tensor_scalar(out, in0, s1, s2, op0, op1), R, W)

    def stt(self, out, in0, scalar, in1, op0, op1, R=(), W=()):
        self.op("dve", lambda e: e.scalar_tensor_tensor(out, in0, scalar, in1, op0, op1), R, W)

    def cp(self, out, in_, R=(), W=(), eng="dve"):
        if eng == "act":
            self.op("act", lambda e: e.activation(out, in_, AF.Copy), R, W)
        else:
            self.op(eng, lambda e: e.tensor_copy(out, in_), R, W)

    def recip(self, out, in_, R=(), W=()):
        self.op("dve", lambda e: e.reciprocal(out, in_), R, W)

    def memset(self, ap, val, W=(), eng="pool"):
        self.op(eng, lambda e: e.memset(ap, val), (), W)

    def dma(self, out, in_, R=(), W=()):
        if STORE_ON_POOL and not W:
            self.op("pool", lambda e: e.dma_start(out=out, in_=in_), R, W, dma=True)
        else:
            self.op("sp", lambda e: e.dma_start(out=out, in_=in_), R, W)

    def emit(self):
        nc = self.nc
        with contextlib.ExitStack() as es:
            sems = {}
            for e in ("pe", "dve", "act", "pool"):
                sems[e] = es.enter_context(nc.semaphore("s_" + e))
            for s in range(NDMA):
                sems[("d", s)] = es.enter_context(nc.semaphore(f"s_d{s}"))
            block = es.enter_context(nc.Block())

            def run(name):
                def f(eng):
                    for waits, fn, inc in self.q[name]:
                        for k, v in waits:
                            eng.wait_ge(sems[k], v)
                        fn(eng).then_inc(sems[inc[0]], inc[1])
                    if name == "sp":
                        for s in range(NDMA):
                            if self.slot_uses[s] > 0:
                                eng.wait_ge(sems[("d", s)], 16 * self.slot_uses[s])
                return f

            block.tensor(run("pe"))
            block.vector(run("dve"))
            block.scalar(run("act"))
            block.gpsimd(run("pool"))
            block.sync(run("sp"))
        self.stack.close()


def _new_nc():
    return bass.Bass("TRN2", target_bir_lowering=False)


def build_C(T, final, TT=256, ctx=None):
    KT = D_MODEL // 128
    if ctx is None:
        nc = _new_nc()
        mixT = nc.dram_tensor("mixT", [D_MODEL, T], F32, kind="ExternalInput").ap()
        xT = nc.dram_tensor("xT", [D_MODEL, T], F32, kind="ExternalInput").ap()
        pT = nc.dram_tensor("pT", [256, T], F32, kind="ExternalInput").ap()
        wo = nc.dram_tensor("wo", [D_MODEL, D_MODEL], F32, kind="ExternalInput").ap()
        wg = nc.dram_tensor("wg", [D_MODEL, D_MODEL], F32, kind="ExternalInput").ap()
        wp = nc.dram_tensor("wp", [256, D_MODEL], F32, kind="ExternalInput").ap()
        prm = nc.dram_tensor("prm", [128, 32], F32, kind="ExternalInput").ap()
        yT = nc.dram_tensor("yT", [D_MODEL, T], F32, kind="ExternalOutput").ap()
        p = Prog(nc)
        es = contextlib.ExitStack()
        es.enter_context(nc.allow_low_precision("bf16 matmul operands, fp32 accumulation"))
    else:
        nc, p = ctx["nc"], ctx["p"]
        mixT, xT, pT, wo, wg, wp, prm, yT = (ctx[k] for k in ("mixT", "xT", "pT", "wo", "wg", "wp", "prm", "yT"))
    prm_t = p.sb([128, 32])
    p.dma(prm_t[:, :], prm[:, :], W=[prm_t])
    ones_b = p.sb([128, 128], BF16)
    p.memset(ones_b[:, :], 1.0, W=[ones_b])
    eps_t = p.sb([128, 1])
    p.memset(eps_t[:, :], EPS, W=[eps_t])

    wo_b = p.sb([128, KT, D_MODEL], BF16)
    wg_b = p.sb([128, KT, D_MODEL], BF16)
    wp_b = p.sb([128, 2, D_MODEL], BF16)
    wst = [p.sb([128, 4, TT]) for _ in range(3)]
    i = 0
    for (src, dst, nk, gcol) in ((wo, wo_b, KT, None), (wg, wg_b, KT, 0), (wp, wp_b, 2, None)):
        for kt in range(nk):
            for c in range(2):
                st = wst[i % 3]
                i += 1
                p.dma(st[:, :, :], src[kt * 128:(kt + 1) * 128, c * 1024:(c + 1) * 1024].rearrange("p (c t) -> p c t", c=4), W=[st])
                dv = dst[:, kt, c * 1024:(c + 1) * 1024].rearrange("p (c t) -> p c t", c=4)
                if gcol is None:
                    p.cp(dv, st[:, :, :], R=[st], W=[(dst, (kt, c))], eng="act")
                else:
                    p.act(dv, st[:, :, :], AF.Copy, R=[st, prm_t], W=[(dst, (kt, c))],
                          scale=prm_t[:, gcol + kt:gcol + kt + 1])

    NTT = T // TT
    assert TT == 256
    mst = wst
    mb = [p.sb([128, KT, TT], BF16) for _ in range(1)]
    xt = [p.sb([128, KT, TT]) for _ in range(2)]
    xb = [p.sb([128, KT, TT], BF16) for _ in range(1)]
    sq = [p.sb([128, TT], BF16) for _ in range(3)]
    pst = p.sb([128, 2, TT])
    pb = p.sb([128, 2, TT], BF16)
    rs_t = p.sb([128, TT])
    rstd = p.sb([128, TT])
    gt = [p.sb([128, TT]) for _ in range(2)]
    g2 = [p.sb([128, TT]) for _ in range(2)]
    acc = [p.ps([128, 512]) for _ in range(3)]
    acc2 = [p.ps([128, 512]) for _ in range(2)]
    ssq = p.ps([128, 512])
    ci = 0
    for tt in range(NTT):
        tsl = slice(tt * TT, (tt + 1) * TT)
        m_b = mb[0]
        x_t = xt[tt % 2]
        x_b = xb[0]
        p.dma(x_t[:, :, :], xT[:, tsl].rearrange("(k p) t -> p k t", p=128), W=[x_t])
        p.dma(pst[:, :, :], pT[:, tsl].rearrange("(k p) t -> p k t", p=128), W=[pst])
        for kg in range(4):
            st = mst[ci % 3]
            ci += 1
            p.dma(st[:, :, :], mixT[kg * 512:(kg + 1) * 512, tsl].rearrange("(k p) t -> p k t", p=128), W=[st])
            p.cp(m_b[:, kg * 4:(kg + 1) * 4, :], st[:, :, :], R=[st], W=[(m_b, kg)], eng="pool")
        p.cp(pb[:, :, :], pst[:, :, :], R=[pst], W=[pb], eng="pool")
        for dc in range(KT):
            a = acc[dc % 3]
            for kt in range(KT):
                p.mm(a[:, 0:TT], wo_b[:, kt, dc * 128:(dc + 1) * 128], m_b[:, kt, :], start=(kt == 0), stop=(kt == KT - 1),
                     R=[wo_b, m_b], W=[a])
            p.tt(x_t[:, dc, :], a[:, 0:TT], x_t[:, dc, :], ALU.add, R=[a, (x_t, dc)], W=[(x_t, dc)])
            s = sq[dc % 3]
            p.act(s[:, :], x_t[:, dc, :], AF.Square, R=[(x_t, dc)], W=[s])
            p.mm(ssq[:, 0:TT], ones_b[:, :], s[:, :], start=(dc == 0), stop=(dc == KT - 1), R=[ones_b, s], W=[ssq])
            p.cp(x_b[:, dc, :], x_t[:, dc, :], R=[(x_t, dc)], W=[(x_b, dc)], eng="pool")
        p.act(rs_t[:, :], ssq[:, 0:TT], AF.Sqrt, R=[ssq, eps_t], W=[rs_t], bias=eps_t[:, 0:1], scale=1.0 / D_MODEL)
        p.recip(rstd[:, :], rs_t[:, :], R=[rs_t], W=[rstd])
        for dc in range(KT):
            a = acc[dc % 3]
            for kt in range(KT):
                p.mm(a[:, 0:TT], wg_b[:, kt, dc * 128:(dc + 1) * 128], x_b[:, kt, :], start=(kt == 0), stop=(kt == KT - 1),
                     R=[wg_b, x_b], W=[a])
            a2 = acc2[dc % 2]
            for kt in range(2):
                p.mm(a2[:, 0:TT], wp_b[:, kt, dc * 128:(dc + 1) * 128], pb[:, kt, :], start=(kt == 0), stop=(kt == 1),
                     R=[wp_b, pb], W=[a2])
            g = gt[dc % 2]
            gg = g2[dc % 2]
            p.tt(g[:, :], a[:, 0:TT], rstd[:, :], ALU.mult, R=[a, rstd], W=[g])
            p.act(gg[:, :], g[:, :], AF.Sigmoid, R=[g], W=[gg])
            p.tt(g[:, :], a2[:, 0:TT], gg[:, :], ALU.mult, R=[a2, gg], W=[g])
            p.tt(x_t[:, dc, :], x_t[:, dc, :], g[:, :], ALU.add, R=[g, (x_t, dc)], W=[(x_t, dc)], eng="pool")
        if final:
            for dc in range(KT):
                s = sq[dc % 3]
                p.act(s[:, :], x_t[:, dc, :], AF.Square, R=[(x_t, dc)], W=[s])
                p.mm(ssq[:, 0:TT], ones_b[:, :], s[:, :], start=(dc == 0), stop=(dc == KT - 1), R=[ones_b, s], W=[ssq])
            p.act(rs_t[:, :], ssq[:, 0:TT], AF.Sqrt, R=[ssq, eps_t], W=[rs_t], bias=eps_t[:, 0:1], scale=1.0 / D_MODEL)
            p.recip(rstd[:, :], rs_t[:, :], R=[rs_t], W=[rstd])
            for dc in range(KT):
                p.stt(x_t[:, dc, :], x_t[:, dc, :], prm_t[:, 16 + dc:17 + dc], rstd[:, :], ALU.mult, ALU.mult,
                      R=[(x_t, dc), prm_t, rstd], W=[(x_t, dc)])
        p.dma(yT[:, tsl].rearrange("(k p) t -> p k t", p=128), x_t[:, :, :], R=[x_t], W=[])
    if ctx is not None:
        return None
    p.emit()
    es.close()
    return nc


def phase_A(p, xT, w_loc, NCOL, projT, prm_t, T, ones_b, eps_t):
    KT = D_MODEL // 128
    TT = 512
    with p.scope():
        wb = p.sb([128, KT, NCOL], BF16)
        wst = [p.sb([128, 512]) for _ in range(3)]
        i = 0
        for kt in range(KT):
            for c0 in range(0, NCOL, 512):
                c1 = min(NCOL, c0 + 512)
                st = wst[i % 3]
                i += 1
                p.dma(st[:, 0:c1 - c0], w_loc[kt * 128:(kt + 1) * 128, c0:c1], W=[st])
                p.act(wb[:, kt, c0:c1], st[:, 0:c1 - c0], AF.Copy, R=[st, prm_t], W=[(wb, (kt, c0))],
                      scale=prm_t[:, kt:kt + 1])
        xst = [p.sb([128, 4, TT]) for _ in range(4)]
        xb = [p.sb([128, KT, TT], BF16) for _ in range(2)]
        sq = [p.sb([128, 4, TT], BF16) for _ in range(2)]
        rs_t = p.sb([128, TT])
        rstd = [p.sb([128, TT]) for _ in range(2)]
        ost = [p.sb([128, TT]) for _ in range(3)]
        acc = [p.ps([128, 512]) for _ in range(4)]
        ssq = p.ps([128, 512])
        oi = 0
        nct = (NCOL + 127) // 128
        NT = T // TT

        def front_loads(tt):
            tsl = slice(tt * TT, (tt + 1) * TT)
            for kg in range(4):
                st = xst[kg]
                p.dma(st[:, :, :], xT[kg * 512:(kg + 1) * 512, tsl].rearrange("(k p) t -> p k t", p=128), W=[st])

        def front_compute(tt):
            x_b = xb[tt % 2]
            rsd = rstd[tt % 2]
            for kg in range(4):
                st = xst[kg]
                s2 = sq[kg % 2]
                p.act(s2[:, :, :], st[:, :, :], AF.Square, R=[st], W=[s2])
                p.cp(x_b[:, kg * 4:(kg + 1) * 4, :], st[:, :, :], R=[st], W=[(x_b, kg)], eng="pool")
                for k in range(4):
                    p.mm(ssq[:, :], ones_b[:, :], s2[:, k, :], start=(kg == 0 and k == 0), stop=(kg == 3 and k == 3),
                         R=[ones_b, s2], W=[ssq])
            p.act(rs_t[:, :], ssq[:, :], AF.Sqrt, R=[ssq, eps_t], W=[rs_t], bias=eps_t[:, 0:1], scale=1.0 / D_MODEL)
            p.recip(rsd[:, :], rs_t[:, :], R=[rs_t], W=[rsd])

        front_loads(0)
        front_compute(0)
        for tt in range(NT):
            tsl = slice(tt * TT, (tt + 1) * TT)
            x_b = xb[tt % 2]
            rsd = rstd[tt % 2]
            for ct in range(nct):
                m = min(128, NCOL - ct * 128)
                a = acc[ct % 4]
                for kt in range(KT):
                    p.mm(a[0:m, :], wb[:, kt, ct * 128:ct * 128 + m], x_b[:, kt, :], start=(kt == 0), stop=(kt == KT - 1),
                         R=[wb, x_b], W=[a])
                o = ost[oi % 3]
                oi += 1
                p.tt(o[0:m, :], a[0:m, :], rsd[0:m, :], ALU.mult, R=[a, rsd], W=[o])
                p.dma(projT[ct * 128:ct * 128 + m, tsl], o[0:m, :], R=[o])
                if tt + 1 < NT and ct == 1:
                    front_loads(tt + 1)
                if tt + 1 < NT and ct == 12:
                    front_compute(tt + 1)


def _con_layout():
    lay = {}
    off = 0
    for name, w in (("ident", 128), ("ones", 128), ("rotm", 128), ("maskT", 128), ("maskS", 128),
                    ("invf", 128), ("dmT", 512), ("qdec", 512), ("kdec", 4), ("glb", 4),
                    ("selbc4", 4 * 128), ("selbc8", 8 * 128), ("selc", 288), ("cmask", 512), ("elast", 128)):
        lay[name] = (off, w)
        off += w
    return lay, off


def make_consts(g):
    lay, n = _con_layout()
    c = np.zeros((128, n), np.float32)

    def put(name, arr):
        o, w = lay[name]
        c[:arr.shape[0], o:o + w] = arr.reshape(arr.shape[0], -1)

    put("ident", np.eye(128, dtype=np.float32))
    put("ones", np.ones((128, 128), np.float32))
    rot = np.zeros((128, 128), np.float32)
    for d in range(64):
        rot[d + 64, d] = -1.0
        rot[d, d + 64] = 1.0
    put("rotm", rot)
    idx = np.arange(128)
    put("maskT", np.where(idx[None, :] >= idx[:, None], 0.0, -1e30).astype(np.float32))
    put("maskS", np.where(idx[None, :] < idx[:, None], 0.0, -1e30).astype(np.float32))
    invf = (10000.0 ** (-np.arange(0, 128, 2, dtype=np.float32) / np.float32(128))).astype(np.float32)
    put("invf", np.concatenate([invf, invf])[None, :])
    heads = np.arange(4) + 4 * g
    lg = np.log1p(-np.exp2(-5.0 - heads.astype(np.float32))).astype(np.float32)
    sc = np.float32(128 ** -0.5)
    rel = (idx[None, :] - idx[:, None]).astype(np.float32)
    dmT = np.where((rel >= 0)[:, None, :], np.exp(np.maximum(rel, 0.0)[:, None, :] * lg[None, :, None]), 0.0) * sc
    put("dmT", dmT.astype(np.float32))
    qdec = np.exp((idx + 1.0)[None, None, :] * lg[None, :, None]) * np.ones((128, 1, 1))
    put("qdec", qdec.astype(np.float32))
    kdec = np.exp((127.0 - idx)[:, None] * lg[None, :]) * sc
    put("kdec", kdec.astype(np.float32))
    put("glb", (np.exp(128.0 * lg)[None, :] * np.ones((128, 1))).astype(np.float32))
    s4 = np.zeros((4, 4, 128), np.float32)
    for h in range(4):
        s4[h, h, :] = 1.0
    put("selbc4", s4)
    s8 = np.zeros((8, 8, 128), np.float32)
    for h in range(8):
        s8[h, h, :] = 1.0
    put("selbc8", s8)
    sc_ = np.zeros((8, 6, 48), np.float32)
    for h in range(8):
        for q_ in range(6):
            sc_[h, q_, q_ * 8 + h] = 1.0
    put("selc", sc_)
    cm = np.ones((8, 512), np.float32)
    cm[:, 0::128] = 0.0
    put("cmask", cm)
    el = np.zeros((128, 128), np.float32)
    el[127, :] = 1.0
    put("elast", el)
    return c


def _cv(con, lay, name, rows=128):
    o, w = lay[name]
    return con[0:rows, o:o + w]


def build_AB_odd(T, ctx=None):
    NCOL = 3072
    lay, ncon = _con_layout()
    if ctx is None:
        nc = _new_nc()
        xT = nc.dram_tensor("xT", [D_MODEL, T], F32, kind="ExternalInput").ap()
        w_loc = nc.dram_tensor("w", [D_MODEL, NCOL], F32, kind="ExternalInput").ap()
        prm = nc.dram_tensor("prm", [128, 64], F32, kind="ExternalInput").ap()
        con_d = nc.dram_tensor("con", [128, ncon], F32, kind="ExternalInput").ap()
        lruw_d = nc.dram_tensor("lruw", [128, 8 * 128], F32, kind="ExternalInput").ap()
        pos_d = nc.dram_tensor("pos", [1, T], I32, kind="ExternalInput").ap()
        mixT = nc.dram_tensor("mixT", [1024, T], F32, kind="ExternalOutput").ap()
        projT = nc.dram_tensor("projT", [NCOL, T], F32, kind="Internal").ap()
        mixA, mixB = mixT[0:512, :], mixT[512:1024, :]
        p = Prog(nc)
        es = contextlib.ExitStack()
        es.enter_context(nc.allow_low_precision("bf16 matmul operands, fp32 accumulation"))
    else:
        nc, p = ctx["nc"], ctx["p"]
        xT, w_loc, prm, con_d, lruw_d, pos_d, projT, mixA, mixB = (ctx[k] for k in ("xT", "w", "prm", "con", "lruw", "pos", "projT", "mixA", "mixB"))
    prm_t = p.sb([128, 64])
    p.dma(prm_t[:, :], prm[:, :], W=[prm_t])
    ones_b = p.sb([128, 128], BF16)
    p.memset(ones_b[:, :], 1.0, W=[ones_b])
    eps_t = p.sb([128, 1])
    p.memset(eps_t[:, :], EPS, W=[eps_t])
    one_t = p.sb([128, 1])
    p.memset(one_t[:, :], 1.0, W=[one_t])
    phase_A(p, xT, w_loc, NCOL, projT, prm_t, T, ones_b, eps_t)
    con = p.sb([128, ncon])
    p.dma(con[:, :], con_d[:, :], W=[con])

    ident = _cv(con, lay, "ident")
    ones_f = _cv(con, lay, "ones")
    PI = float(np.pi)
    C1 = 6.28125
    C2 = float(2 * np.pi - 6.28125)
    with p.scope():
        lruw = p.sb([128, 8, 128])
        p.dma(lruw[:, :, :], lruw_d.rearrange("p (n j) -> p n j", n=8), W=[lruw])
        sp_t = p.sb([128, 4])
        m8 = p.sb([128, 4])
        m16 = p.sb([128, 4])
        p.act(sp_t[:, :], prm_t[:, 48:52], AF.Exp, R=[prm_t], W=[sp_t], scale=-1.0)
        p.act(sp_t[:, :], sp_t[:, :], AF.Ln, R=[sp_t, one_t], W=[sp_t], bias=one_t[:, 0:1])
        p.ts(m8[:, :], sp_t[:, :], -8.0, None, ALU.mult, R=[sp_t], W=[m8])
        p.ts(m16[:, :], sp_t[:, :], -16.0, None, ALU.mult, R=[sp_t], W=[m16])
        hst = p.sb([128, 4])
        p.memset(hst[:, :], 0.0, W=[hst])
        S4 = p.sb([128, 4, 128])
        p.memset(S4[:, :, :], 0.0, W=[S4])

        two = lambda shape, dt=F32: [p.sb(shape, dt) for _ in range(2)]
        v4_, qp4_, kp4_, o4_ = (two([128, 4, 512]) for _ in range(4))
        g4s = p.sb([128, 4, 512])
        g4_ = [g4s, g4s]
        q4s = p.sb([128, 4, 512])
        k4s = p.sb([128, 4, 512])
        q4_, k4_ = [q4s, q4s], [k4s, k4s]
        t14 = p.sb([128, 4, 512])
        sq1 = p.sb([128, 512])
        posi = p.sb([1, 512], I32)
        posf = p.sb([1, 512])
        ang = p.sb([128, 512])
        kf = p.sb([128, 512])
        ki = p.sb([128, 512], I32)
        yy = p.sb([128, 512])
        mm_ = p.sb([128, 512])
        sinT = p.sb([128, 512])
        cosT = p.sb([128, 512])
        attnT = p.sb([128, 4, 128])
        qg = p.sb([128, 4, 128])
        kd = p.sb([128, 4, 128])
        Vt = p.sb([128, 4, 128])
        rs1 = p.sb([128, 512])
        rinv = p.sb([128, 512])
        xd_, zt_, xc_, rr_, ii_, aa_, a2_, hh_ = (two([128, 515] if i == 0 else [128, 512]) for i in range(8))
        pA = p.ps([128, 512])
        pB = p.ps([128, 512])
        pG = p.ps([128, 512])
        pK = p.ps([128, 512])
        pV = p.ps([128, 512])
        pO = p.ps([128, 512])
        pD = p.ps([128, 512])
        pN = p.ps([128, 512])
        dmT = _cv(con, lay, "dmT").rearrange("p (h c) -> p h c", h=4)
        qdec = _cv(con, lay, "qdec").rearrange("p (h c) -> p h c", h=4)
        kdec = _cv(con, lay, "kdec")
        glb = _cv(con, lay, "glb")
        rotm = _cv(con, lay, "rotm")
        invf = _cv(con, lay, "invf", 1)
        NB = T // 512

        def prep(blk):
            bsl = slice(blk * 512, (blk + 1) * 512)
            q4, k4, v4, g4, qp4, kp4 = (t[blk % 2] for t in (q4_, k4_, v4_, g4_, qp4_, kp4_))
            for (dst, r0) in ((q4, 0), (k4, 512), (v4, 1024)):
                p.dma(dst[:, :, :], projT[r0:r0 + 512, bsl].rearrange("(h p) t -> p h t", p=128), W=[dst])
            p.dma(posi[:, :], pos_d[:, bsl], W=[posi])
            p.cp(posf[:, :], posi[:, :], R=[posi], W=[posf])
            p.mm(pA[:, :], invf, posf[0:1, :], R=[con, posf], W=[pA])
            p.cp(ang[:, :], pA[:, :], R=[pA], W=[ang], eng="act")
            yield
            p.ts(kf[:, :], ang[:, :], float(1.0 / (2 * np.pi)), None, ALU.mult, R=[ang], W=[kf])
            p.cp(ki[:, :], kf[:, :], R=[kf], W=[ki])
            p.cp(kf[:, :], ki[:, :], R=[ki], W=[kf])
            p.stt(ang[:, :], kf[:, :], -C1, ang[:, :], ALU.mult, ALU.add, R=[kf, ang], W=[ang])
            p.stt(ang[:, :], kf[:, :], -C2, ang[:, :], ALU.mult, ALU.add, R=[kf, ang], W=[ang])
            yield
            for (dstT, shift) in ((sinT, 0.0), (cosT, PI / 2)):
                p.ts(yy[:, :], ang[:, :], shift, None, ALU.add, R=[ang], W=[yy])
                p.ts(mm_[:, :], yy[:, :], PI, 2 * PI, ALU.is_gt, ALU.mult, R=[yy], W=[mm_])
                p.tt(yy[:, :], yy[:, :], mm_[:, :], ALU.subtract, R=[yy, mm_], W=[yy])
                p.ts(mm_[:, :], yy[:, :], -PI, 2 * PI, ALU.is_lt, ALU.mult, R=[yy], W=[mm_])
                p.tt(yy[:, :], yy[:, :], mm_[:, :], ALU.add, R=[yy, mm_], W=[yy])
                p.ts(yy[:, :], yy[:, :], -PI, PI, ALU.max, ALU.min, R=[yy], W=[yy])
                p.act(dstT[:, :], yy[:, :], AF.Sin, R=[yy], W=[dstT])
                yield
            for (src4, dst4) in ((q4, qp4), (k4, kp4)):
                for h in range(4):
                    pr = pA if h % 2 == 0 else pB
                    p.mm(pr[:, :], rotm, src4[:, h, :], R=[con, src4], W=[pr], exact=True)
                    p.tt(dst4[:, h, :], pr[:, :], sinT[:, :], ALU.mult, R=[pr, sinT], W=[(dst4, h)])
                    p.tt(t14[:, h, :], src4[:, h, :], cosT[:, :], ALU.mult, R=[src4, cosT], W=[(t14, h)], eng="pool")
                    yield
                p.tt(dst4[:, :, :], dst4[:, :, :], t14[:, :, :], ALU.add, R=[dst4, t14], W=[dst4], eng="pool")
                yield

        def ret_chunk(blk, c4):
            q4, k4, v4, g4, qp4, kp4, o4 = (t[blk % 2] for t in (q4_, k4_, v4_, g4_, qp4_, kp4_, o4_))
            cs = slice(c4 * 128, (c4 + 1) * 128)
            for h in range(4):
                hs = slice(h * 128, (h + 1) * 128)
                p.mm(pG[:, hs], kp4[:, h, cs], qp4[:, h, cs], R=[kp4, qp4], W=[(pG, h)])
                p.tr(pK[:, hs], kp4[:, h, cs], ident, R=[kp4, con], W=[(pK, h)])
                p.tr(pV[:, hs], v4[:, h, cs], ident, R=[v4, con], W=[(pV, h)])
            yield
            p.tt(attnT[:, :, :], pG[:, :].rearrange("p (h c) -> p h c", h=4), dmT, ALU.mult, R=[pG, con], W=[attnT])
            p.tt(qg[:, :, :], qp4[:, :, cs], qdec, ALU.mult, R=[qp4, con], W=[qg], eng="pool")
            for h in range(4):
                hs = slice(h * 128, (h + 1) * 128)
                p.act(kd[:, h, :], pK[:, hs], AF.Copy, R=[(pK, h), con], W=[(kd, h)], scale=kdec[:, h:h + 1])
            p.cp(Vt[:, :, :], pV[:, :].rearrange("p (h c) -> p h c", h=4), R=[pV], W=[Vt], eng="act")
            yield
            for h in range(4):
                hs = slice(h * 128, (h + 1) * 128)
                p.mm(pO[:, hs], S4[:, h, :], qg[:, h, :], start=True, stop=False, R=[(S4, h), qg], W=[(pO, h)])
                p.mm(pO[:, hs], Vt[:, h, :], attnT[:, h, :], start=False, stop=True, R=[Vt, attnT], W=[(pO, h)])
                p.mm(pD[:, hs], kd[:, h, :], Vt[:, h, :], R=[(kd, h), Vt], W=[(pD, h)])
            yield
            p.cp(o4[:, :, cs], pO[:, :].rearrange("p (h c) -> p h c", h=4), R=[pO], W=[(o4, c4)], eng="act")
            for h in range(4):
                hs = slice(h * 128, (h + 1) * 128)
                p.stt(S4[:, h, :], S4[:, h, :], glb[:, h:h + 1], pD[:, hs], ALU.mult, ALU.add,
                      R=[(S4, h), (pD, h), con], W=[(S4, h)])

        def lru_tile(blk, ct):
            bsl = slice(blk * 512, (blk + 1) * 512)
            xd, zt, xc, rr, ii, aa, a2, hh = (t[ct % 2] for t in (xd_, zt_, xc_, rr_, ii_, aa_, a2_, hh_))
            r0 = 2048 + ct * 128
            if blk == 0:
                p.memset(xd[:, 0:3], 0.0, W=[xd])
                p.dma(xd[:, 3:515], projT[r0:r0 + 128, 0:512], W=[xd])
            else:
                p.dma(xd[:, :], projT[r0:r0 + 128, blk * 512 - 3:(blk + 1) * 512], W=[xd])
            p.dma(zt[:, :], projT[r0 + 512:r0 + 640, bsl], W=[zt])
            cw = 20 + ct * 4
            p.ts(xc[:, :], xd[:, 0:512], prm_t[:, cw:cw + 1], prm_t[:, 36 + ct:37 + ct], ALU.mult, ALU.add,
                 R=[xd, prm_t], W=[xc])
            for j in range(1, 4):
                p.stt(xc[:, :], xd[:, j:j + 512], prm_t[:, cw + j:cw + j + 1], xc[:, :], ALU.mult, ALU.add,
                      R=[xd, prm_t, xc], W=[xc])
            yield
            p.mm(pA[:, :], lruw[:, ct, :], xc[:, :], R=[lruw, xc], W=[pA])
            p.mm(pB[:, :], lruw[:, 4 + ct, :], xc[:, :], R=[lruw, xc], W=[pB])
            p.act(rr[:, :], pA[:, :], AF.Sigmoid, R=[pA, prm_t], W=[rr], bias=prm_t[:, 40 + ct:41 + ct])
            p.act(ii[:, :], pB[:, :], AF.Sigmoid, R=[pB, prm_t], W=[ii], bias=prm_t[:, 44 + ct:45 + ct])
            yield
            p.act(aa[:, :], rr[:, :], AF.Exp, R=[rr, m8], W=[aa], scale=m8[:, ct:ct + 1])
            p.act(a2[:, :], rr[:, :], AF.Exp, R=[rr, m16], W=[a2], scale=m16[:, ct:ct + 1])
            p.act(a2[:, :], a2[:, :], AF.Sqrt, R=[a2, one_t], W=[a2], bias=one_t[:, 0:1], scale=-1.0)
            p.tt(ii[:, :], ii[:, :], a2[:, :], ALU.mult, R=[ii, a2], W=[ii])
            p.tt(ii[:, :], ii[:, :], xc[:, :], ALU.mult, R=[ii, xc], W=[ii], eng="pool")
            yield
            p.op("dve", lambda e: e.tensor_tensor_scan(hh[:, :], aa[:, :], ii[:, :], hst[:, ct:ct + 1], ALU.mult, ALU.add),
                 R=[aa, ii, hst], W=[hh])
            p.cp(hst[:, ct:ct + 1], hh[:, 511:512], R=[hh], W=[hst])
            p.act(zt[:, :], zt[:, :], AF.Silu, R=[zt], W=[zt])
            p.tt(hh[:, :], hh[:, :], zt[:, :], ALU.mult, R=[hh, zt], W=[hh], eng="pool")
            p.dma(mixB[ct * 128:(ct + 1) * 128, bsl], hh[:, :], R=[hh])

        def post(blk):
            bsl = slice(blk * 512, (blk + 1) * 512)
            g4, o4 = g4_[blk % 2], o4_[blk % 2]
            p.dma(g4[:, :, :], projT[1536:2048, bsl].rearrange("(h p) t -> p h t", p=128), W=[g4])
            p.act(g4[:, :, :], g4[:, :, :], AF.Silu, R=[g4], W=[g4])
            yield
            for h in range(4):
                p.act(sq1[:, :], o4[:, h, :], AF.Square, R=[o4], W=[sq1])
                p.mm(pN[:, :], ones_f, sq1[:, :], R=[con, sq1], W=[pN])
                p.act(rs1[:, :], pN[:, :], AF.Sqrt, R=[pN, eps_t], W=[rs1], bias=eps_t[:, 0:1], scale=1.0 / 128)
                p.recip(rinv[:, :], rs1[:, :], R=[rs1], W=[rinv])
                p.stt(o4[:, h, :], o4[:, h, :], prm_t[:, 16 + h:17 + h], rinv[:, :], ALU.mult, ALU.mult,
                      R=[o4, prm_t, rinv], W=[(o4, ("n", h))])
                yield
            p.tt(o4[:, :, :], o4[:, :, :], g4[:, :, :], ALU.mult, R=[o4, g4], W=[o4], eng="pool")
            p.dma(mixA[:, bsl].rearrange("(h p) t -> p h t", p=128), o4[:, :, :], R=[o4])

        def chunks(blk):
            for c4 in range(4):
                gens = [ret_chunk(blk, c4), lru_tile(blk, c4)]
                while gens:
                    for g_ in list(gens):
                        try:
                            next(g_)
                        except StopIteration:
                            gens.remove(g_)
                        yield

        def drain(*gens):
            gens = [g_ for g_ in gens if g_ is not None]
            while gens:
                for g_ in list(gens):
                    try:
                        next(g_)
                    except StopIteration:
                        gens.remove(g_)

        drain(prep(0))
        for blk in range(NB):
            drain(chunks(blk), prep(blk + 1) if blk + 1 < NB else None, post(blk - 1) if blk >= 1 else None)
        drain(post(NB - 1))
    if ctx is not None:
        return None
    p.emit()
    es.close()
    return nc


def bcast(ap, n):
    sh = list(ap.shape)
    return ap.unsqueeze(len(sh)).broadcast_to(sh + [n])


def build_AB_even(T, ctx=None):
    NCOL = 3344
    lay, ncon = _con_layout()
    if ctx is None:
        nc = _new_nc()
        xT = nc.dram_tensor("xT", [D_MODEL, T], F32, kind="ExternalInput").ap()
        w_loc = nc.dram_tensor("w", [D_MODEL, NCOL], F32, kind="ExternalInput").ap()
        prm = nc.dram_tensor("prm", [128, 112], F32, kind="ExternalInput").ap()
        hp_d = nc.dram_tensor("hp", [8, 8], F32, kind="ExternalInput").ap()
        dbc_d = nc.dram_tensor("dbc", [128, 512], F32, kind="ExternalInput").ap()
        con_d = nc.dram_tensor("con", [128, ncon], F32, kind="ExternalInput").ap()
        mixT = nc.dram_tensor("mixT", [1024, T], F32, kind="ExternalOutput").ap()
        projT = nc.dram_tensor("projT", [NCOL, T], F32, kind="Internal").ap()
        mixA, mixB = mixT[0:512, :], mixT[512:1024, :]
        p = Prog(nc)
        es = contextlib.ExitStack()
        es.enter_context(nc.allow_low_precision("bf16 matmul operands, fp32 accumulation"))
    else:
        nc, p = ctx["nc"], ctx["p"]
        xT, w_loc, prm, hp_d, dbc_d, con_d, projT, mixA, mixB = (ctx[k] for k in ("xT", "w", "prm", "hp", "dbc", "con", "projT", "mixA", "mixB"))
    prm_t = p.sb([128, 112])
    p.dma(prm_t[:, :], prm[:, :], W=[prm_t])
    hp = p.sb([8, 8])
    p.dma(hp[:, :], hp_d[:, :], W=[hp])
    ones_b = p.sb([128, 128], BF16)
    p.memset(ones_b[:, :], 1.0, W=[ones_b])
    eps_t = p.sb([128, 1])
    p.memset(eps_t[:, :], EPS, W=[eps_t])
    one_t = p.sb([128, 1])
    p.memset(one_t[:, :], 1.0, W=[one_t])
    phase_A(p, xT, w_loc, NCOL, projT, prm_t, T, ones_b, eps_t)
    con = p.sb([128, ncon])
    p.dma(con[:, :], con_d[:, :], W=[con])

    ident = _cv(con, lay, "ident")
    ones_f = _cv(con, lay, "ones")
    maskT = _cv(con, lay, "maskT")
    maskS = _cv(con, lay, "maskS")
    selbc4 = _cv(con, lay, "selbc4", 4).rearrange("p (h c) -> p h c", h=4)
    selbc8 = _cv(con, lay, "selbc8", 8).rearrange("p (h c) -> p h c", h=8)
    selc = _cv(con, lay, "selc", 8).rearrange("p (q c) -> p q c", q=6)
    cmask = _cv(con, lay, "cmask", 8)
    SC = float(128 ** -0.5)

    def v4(ps_t):
        return ps_t[:, :].rearrange("p (h c) -> p h c", h=4)

    with p.scope():
        banks = [p.ps([128, 512]) for _ in range(6)]
        pBig = p.ps([128, 1024])
        bi = [0]

        def bank():
            b = banks[bi[0] % 6]
            bi[0] += 1
            return b

        dbc = p.sb([128, 512])
        p.dma(dbc[:, :], dbc_d[:, :], W=[dbc])
        negA = p.sb([8, 2])
        p.memset(negA[:, :], 0.0, W=[negA])
        p.act(negA[0:4, 0:1], hp[0:4, 0:1], AF.Exp, R=[hp], W=[negA])
        p.act(negA[0:8, 1:2], hp[0:8, 2:3], AF.Exp, R=[hp], W=[negA])
        p.ts(negA[:, :], negA[:, :], -1.0, None, ALU.mult, R=[negA], W=[negA])
        S4 = p.sb([128, 4, 128])
        p.memset(S4[:, :, :], 0.0, W=[S4])
        Hs = p.sb([128, 512])
        p.memset(Hs[:, :], 0.0, W=[Hs])

        qh = p.sb([128, 4, 515])
        kh = p.sb([128, 4, 515])
        vh = p.sb([128, 4, 515])
        z4 = p.sb([128, 4, 512])
        qc = p.sb([128, 4, 512])
        kc = p.sb([128, 4, 512])
        vc = p.sb([128, 4, 512])
        sq4 = p.sb([128, 4, 512])
        o4 = p.sb([128, 4, 512])
        rs1 = p.sb([128, 512])
        rinv = p.sb([128, 512])
        a_r = p.sb([8, 512])
        b_r = p.sb([8, 512])
        g_r = p.sb([8, 512])
        gc_r = p.sb([8, 512])
        be_r = p.sb([8, 512])
        bg_r = p.sb([8, 512])
        colt = p.sb([128, 48])
        ncol = p.sb([128, 8])
        gcbc = p.sb([128, 4, 128])
        egcbc = p.sb([128, 4, 128])
        T1 = p.sb([128, 4, 128])
        DmT = p.sb([128, 4, 128])
        Dm = p.sb([128, 4, 128])
        kdec = p.sb([128, 4])
        kd = p.sb([128, 4, 128])
        Vb = p.sb([128, 4, 128])
        attnT = p.sb([128, 4, 128])
        qg = p.sb([128, 4, 128])
        Pm = [p.sb([128, 4, 128]) for _ in range(2)]
        Ym = [p.sb([128, 4, 128]) for _ in range(2)]
        Am = p.sb([128, 4, 128])
        Xm = p.sb([128, 4, 128])
        Vn = p.sb([128, 4, 128])
        rs1s = [rs1, p.sb([128, 512])]
        rinvs = [rinv, p.sb([128, 512])]
        o4s = p.sb([128, 4, 512])
        a_rs = p.sb([8, 512])
        g_rs = p.sb([8, 512])
        gc_rs = p.sb([8, 512])
        be_rs = p.sb([8, 512])
        colts = p.sb([128, 48])
        xsh = p.sb([128, 4, 515])
        Bh = p.sb([128, 515])
        Ch = p.sb([128, 515])
        xs4 = p.sb([128, 4, 512])
        Bc = p.sb([128, 512])
        Cc = p.sb([128, 512])
        csb = p.sb([128, 8, 128])
        lmT = p.sb([128, 8, 128])
        ecs = p.sb([128, 8])
        dect = p.sb([128, 8])
        glb8 = p.sb([128, 8])
        xst = p.sb([128, 512])
        xct = p.sb([128, 512])
        xcd = p.sb([128, 512])
        yt = p.sb([128, 512])
        Bt = p.sb([128, 128])

        def conv_tile(dst, src, wcol, bcol):
            if bcol is None:
                p.ts(dst, src[0], prm_t[:, wcol:wcol + 1], None, ALU.mult, R=[src[4], prm_t], W=[src[5]])
            else:
                p.ts(dst, src[0], prm_t[:, wcol:wcol + 1], prm_t[:, bcol:bcol + 1], ALU.mult, ALU.add,
                     R=[src[4], prm_t], W=[src[5]])
            for j in range(1, 4):
                p.stt(dst, src[j], prm_t[:, wcol + j:wcol + j + 1], dst, ALU.mult, ALU.add, R=[src[4], prm_t, src[5]], W=[src[5]])
            p.act(dst, dst, AF.Silu, R=[src[5]], W=[src[5]])

        def load_halo(dst, r0, nrow, blk, multi):
            if multi:
                d0 = dst[:, :, 0:3]
                dr = dst[:, :, 3:515]
                da = dst[:, :, :]
                src = lambda c0, c1: projT[r0:r0 + nrow, c0:c1].rearrange("(k p) t -> p k t", p=128)
            else:
                d0 = dst[:, 0:3]
                dr = dst[:, 3:515]
                da = dst[:, :]
                src = lambda c0, c1: projT[r0:r0 + nrow, c0:c1]
            if blk == 0:
                p.memset(d0, 0.0, W=[dst])
                p.dma(dr, src(0, 512), W=[dst])
            else:
                p.dma(da, src(blk * 512 - 3, (blk + 1) * 512), W=[dst])

        def conv_tiles(specs):
            for j in range(4):
                for (dst, srcs, stok, dtok, wcol, bcol) in specs:
                    if j == 0:
                        if bcol is None:
                            p.ts(dst, srcs[0], prm_t[:, wcol:wcol + 1], None, ALU.mult, R=[stok, prm_t], W=[dtok])
                        else:
                            p.ts(dst, srcs[0], prm_t[:, wcol:wcol + 1], prm_t[:, bcol:bcol + 1], ALU.mult, ALU.add,
                                 R=[stok, prm_t], W=[dtok])
                    else:
                        p.stt(dst, srcs[j], prm_t[:, wcol + j:wcol + j + 1], dst, ALU.mult, ALU.add, R=[stok, prm_t, dtok], W=[dtok])
            for (dst, srcs, stok, dtok, wcol, bcol) in specs:
                p.act(dst, dst, AF.Silu, R=[dtok], W=[dtok])

        def gdn_chunk(c4):
                    cs = slice(c4 * 128, (c4 + 1) * 128)
                    b = bank()
                    for qi, rows in enumerate((gc_r, be_r, bg_r)):
                        p.mm(b[:, 0:48], rows[0:4, cs], selc[0:4, qi, :], start=(qi == 0), stop=(qi == 2), R=[rows, con], W=[b])
                    p.cp(colt[:, :], b[:, 0:48], R=[b], W=[colt], eng="act")
                    p.ts(ncol[:, 0:4], colt[:, 16:20], -1.0, None, ALU.mult, R=[colt], W=[ncol])
                    b = bank()
                    for h in range(4):
                        p.mm(b[:, h * 128:(h + 1) * 128], selbc4[:, h, :], gc_r[0:4, cs], R=[con, gc_r], W=[b])
                    p.cp(gcbc[:, :, :], v4(b), R=[b], W=[gcbc], eng="act")
                    p.act(egcbc[:, :, :], v4(b), AF.Exp, R=[b], W=[egcbc])
                    for h in range(4):
                        p.stt(T1[:, h, :], gcbc[:, h, :], colt[:, h:h + 1], maskT, ALU.subtract, ALU.add, R=[gcbc, colt, con], W=[(T1, h)])
                    p.act(DmT[:, :, :], T1[:, :, :], AF.Exp, R=[T1], W=[DmT])
                    for h in range(4):
                        p.stt(T1[:, h, :], gcbc[:, h, :], colt[:, h:h + 1], maskS, ALU.subtract, ALU.subtract, R=[gcbc, colt, con], W=[(T1, h)])
                    p.act(Dm[:, :, :], T1[:, :, :], AF.Exp, R=[T1], W=[Dm], scale=-1.0)
                    p.tt(kdec[:, :], gcbc[:, :, 127], colt[:, 0:4], ALU.subtract, R=[gcbc, colt], W=[kdec])
                    p.act(kdec[:, :], kdec[:, :], AF.Exp, R=[kdec], W=[kdec])
                    yield
                    bK = bank()
                    bV = bank()
                    bG = bank()
                    bL = bank()
                    for h in range(4):
                        hs = slice(h * 128, (h + 1) * 128)
                        p.tr(bK[:, hs], kc[:, h, cs], ident, R=[kc, con], W=[bK])
                        p.tr(bV[:, hs], vc[:, h, cs], ident, R=[vc, con], W=[bV])
                        p.mm(bG[:, hs], kc[:, h, cs], qc[:, h, cs], R=[kc, qc], W=[bG])
                        p.mm(bL[:, hs], kc[:, h, cs], kc[:, h, cs], R=[kc], W=[bL])
                    for h in range(4):
                        hs = slice(h * 128, (h + 1) * 128)
                        p.act(kd[:, h, :], bK[:, hs], AF.Copy, R=[bK, kdec], W=[(kd, h)], scale=kdec[:, h:h + 1])
                        p.act(Vb[:, h, :], bV[:, hs], AF.Copy, R=[bV, colt], W=[(Vb, h)], scale=colt[:, 8 + h:9 + h])
                        p.stt(Pm[0][:, h, :], bL[:, hs], colt[:, 8 + h:9 + h], Dm[:, h, :], ALU.mult, ALU.mult,
                              R=[bL, colt, Dm], W=[(Pm[0], h)])
                    p.tt(attnT[:, :, :], v4(bG), DmT[:, :, :], ALU.mult, R=[bG, DmT], W=[attnT])
                    p.tt(qg[:, :, :], qc[:, :, cs], egcbc[:, :, :], ALU.mult, R=[qc, egcbc], W=[qg], eng="pool")
                    yield
                    bY = bank()
                    for h in range(4):
                        p.tr(bY[:, h * 128:(h + 1) * 128], Pm[0][:, h, :], ident, R=[Pm[0], con], W=[bY])
                    p.cp(Ym[0][:, :, :], v4(bY), R=[bY], W=[Ym[0]], eng="act")
                    for h in range(4):
                        p.stt(Am[:, h, :], bY[:, h * 128:(h + 1) * 128], -1.0, ident, ALU.mult, ALU.add, R=[con, bY], W=[(Am, h)])
                    yield
                    cur = 0
                    for lev in range(6):
                        nxt = 1 - cur
                        bP = bank()
                        for h in range(4):
                            p.mm(bP[:, h * 128:(h + 1) * 128], Ym[cur][:, h, :], Pm[cur][:, h, :], R=[Ym[cur], Pm[cur]], W=[bP])
                        p.cp(Pm[nxt][:, :, :], v4(bP), R=[bP], W=[Pm[nxt]], eng="act")
                        if lev < 5:
                            bY2 = bank()
                            for h in range(4):
                                p.mm(bY2[:, h * 128:(h + 1) * 128], Pm[cur][:, h, :], Ym[cur][:, h, :], R=[Ym[cur], Pm[cur]], W=[bY2])
                            p.cp(Ym[nxt][:, :, :], v4(bY2), R=[bY2], W=[Ym[nxt]])
                        yield
                        bU = bank()
                        for h in range(4):
                            p.mm(bU[:, h * 128:(h + 1) * 128], Pm[nxt][:, h, :], Am[:, h, :], R=[Pm[nxt], Am], W=[bU])
                        p.tt(Am[:, :, :], v4(bU), Am[:, :, :], ALU.add, R=[Am, bU], W=[Am])
                        yield
                        cur = nxt
                    yield
                    bKS = bank()
                    for h in range(4):
                        p.mm(bKS[:, h * 128:(h + 1) * 128], kc[:, h, cs], S4[:, h, :], R=[kc, (S4, h)], W=[bKS])
                    for h in range(4):
                        p.stt(Xm[:, h, :], bKS[:, h * 128:(h + 1) * 128], ncol[:, h:h + 1], Vb[:, h, :], ALU.mult, ALU.add,
                              R=[bKS, ncol, Vb], W=[(Xm, h)])
                    yield
                    bVn = bank()
                    for h in range(4):
                        p.mm(bVn[:, h * 128:(h + 1) * 128], Am[:, h, :], Xm[:, h, :], R=[Am, Xm], W=[bVn])
                    p.cp(Vn[:, :, :], v4(bVn), R=[bVn], W=[Vn], eng="act")
                    yield
                    bO = bank()
                    bD = bank()
                    for h in range(4):
                        hs = slice(h * 128, (h + 1) * 128)
                        p.mm(bO[:, hs], S4[:, h, :], qg[:, h, :], start=True, stop=False, R=[(S4, h), qg], W=[bO])
                        p.mm(bO[:, hs], Vn[:, h, :], attnT[:, h, :], start=False, stop=True, R=[Vn, attnT], W=[bO])
                        p.mm(bD[:, hs], kd[:, h, :], Vn[:, h, :], R=[kd, Vn], W=[bD])
                    p.cp(o4[:, :, cs], v4(bO), R=[bO], W=[(o4, c4)], eng="act")
                    for h in range(4):
                        p.stt(S4[:, h, :], S4[:, h, :], egcbc[:, h, 127:128], bD[:, h * 128:(h + 1) * 128], ALU.mult, ALU.add,
                              R=[(S4, h), bD, egcbc], W=[(S4, h)])

        def ssd_chunk(c4):
                    cs = slice(c4 * 128, (c4 + 1) * 128)
                    b = bank()
                    p.mm(b[:, 0:48], be_rs[0:8, cs], selc[0:8, 3, :], start=True, stop=False, R=[be_rs, con], W=[b])
                    p.mm(b[:, 0:48], gc_rs[0:8, cs], selc[0:8, 4, :], start=False, stop=True, R=[gc_rs, con], W=[b])
                    p.cp(colts[:, :], b[:, 0:48], R=[b], W=[colts], eng="act")
                    p.act(ecs[:, :], colts[:, 32:40], AF.Exp, R=[colts], W=[ecs])
                    for h in range(8):
                        p.mm(pBig[:, h * 128:(h + 1) * 128], selbc8[:, h, :], gc_rs[0:8, cs], R=[con, gc_rs], W=[pBig])
                    p.cp(csb[:, :, :], pBig[:, :].rearrange("p (h c) -> p h c", h=8), R=[pBig], W=[csb], eng="act")
                    p.tt(dect[:, :], csb[:, :, 127], colts[:, 32:40], ALU.subtract, R=[csb, colts], W=[dect])
                    p.act(dect[:, :], dect[:, :], AF.Exp, R=[dect], W=[dect])
                    p.act(glb8[:, :], csb[:, :, 127], AF.Exp, R=[csb], W=[glb8])
                    yield
                    for h in range(8):
                        p.stt(lmT[:, h, :], csb[:, h, :], colts[:, 32 + h:33 + h], maskT, ALU.subtract, ALU.add,
                              R=[csb, colts, con], W=[(lmT, h)])
                    p.act(lmT[:, :, :], lmT[:, :, :], AF.Exp, R=[lmT], W=[lmT])
                    yield
                    bS = bank()
                    p.mm(bS[:, 0:128], Bc[:, cs], Cc[:, cs], R=[Bc, Cc], W=[bS])
                    for h in range(8):
                        p.tt(lmT[:, h, :], bS[:, 0:128], lmT[:, h, :], ALU.mult, R=[lmT, bS], W=[(lmT, h)])
                    yield
                    bX = bank()
                    for k in range(4):
                        p.tr(bX[:, k * 128:(k + 1) * 128], xs4[:, k, cs], ident, R=[xs4, con], W=[bX])
                    p.cp(xst[:, :], bX[:, :], R=[bX], W=[xst], eng="act")
                    p.tt(xct[:, :].rearrange("p (h q) -> p h q", h=8), bX[:, :].rearrange("p (h q) -> p h q", h=8),
                         bcast(colts[:, 24:32], 64), ALU.mult, R=[bX, colts], W=[xct])
                    yield
                    bYd = bank()
                    for h in range(8):
                        p.mm(bYd[:, h * 64:(h + 1) * 64], lmT[:, h, :], xct[:, h * 64:(h + 1) * 64], R=[lmT, xct], W=[bYd])
                    bYo = bank()
                    p.mm(bYo[:, :], Cc[:, cs], Hs[:, :], R=[Cc, Hs], W=[bYo])
                    p.tt(yt[:, :].rearrange("p (h q) -> p h q", h=8), bYo[:, :].rearrange("p (h q) -> p h q", h=8),
                         bcast(ecs[:, :], 64), ALU.mult, R=[bYo, ecs], W=[yt])
                    p.tt(yt[:, :], bYd[:, :], yt[:, :], ALU.add, R=[yt, bYd], W=[yt])
                    p.tt(xst[:, :], xst[:, :], dbc[:, :], ALU.mult, R=[xst, dbc], W=[xst], eng="pool")
                    p.tt(yt[:, :], yt[:, :], xst[:, :], ALU.add, R=[yt, xst], W=[yt])
                    yield
                    p.tt(xcd[:, :].rearrange("p (h q) -> p h q", h=8), xct[:, :].rearrange("p (h q) -> p h q", h=8),
                         bcast(dect[:, :], 64), ALU.mult, R=[xct, dect], W=[xcd])
                    bB = bank()
                    p.tr(bB[:, 0:128], Bc[:, cs], ident, R=[Bc, con], W=[bB])
                    p.cp(Bt[:, :], bB[:, 0:128], R=[bB], W=[Bt], eng="act")
                    bH = bank()
                    p.mm(bH[:, :], Bt[:, :], xcd[:, :], R=[Bt, xcd], W=[bH])
                    p.tt(Hs[:, :].rearrange("p (h q) -> p h q", h=8), Hs[:, :].rearrange("p (h q) -> p h q", h=8),
                         bcast(glb8[:, :], 64), ALU.mult, R=[Hs, glb8], W=[Hs])
                    p.tt(Hs[:, :], bH[:, :], Hs[:, :], ALU.add, R=[Hs, bH], W=[Hs])
                    yield
                    bT = bank()
                    for k in range(4):
                        p.tr(bT[:, k * 128:(k + 1) * 128], yt[:, k * 128:(k + 1) * 128], ident, R=[yt, con], W=[bT])
                    p.cp(o4s[:, :, cs], v4(bT), R=[bT], W=[(o4s, c4)], eng="act")

        for blk in range(T // 512):
            bsl = slice(blk * 512, (blk + 1) * 512)
            load_halo(qh, 0, 512, blk, True)
            load_halo(kh, 512, 512, blk, True)
            load_halo(vh, 1024, 512, blk, True)
            p.dma(z4[:, :, :], projT[1536:2048, bsl].rearrange("(k p) t -> p k t", p=128), W=[z4])
            p.dma(a_r[0:4, :], projT[3328:3332, bsl], W=[a_r])
            p.dma(b_r[0:4, :], projT[3332:3336, bsl], W=[b_r])
            load_halo(xsh, 2048, 512, blk, True)
            load_halo(Bh, 2560, 128, blk, False)
            load_halo(Ch, 2688, 128, blk, False)
            p.dma(a_rs[0:8, :], projT[3336:3344, bsl], W=[a_rs])
            specs = []
            for sec, (hsrc, dstc) in enumerate(((qh, qc), (kh, kc), (vh, vc))):
                for h in range(4):
                    specs.append((dstc[:, h, :], [hsrc[:, h, j:j + 512] for j in range(4)], hsrc, (dstc, h), 16 + (sec * 4 + h) * 4, None))
            for k in range(4):
                specs.append((xs4[:, k, :], [xsh[:, k, j:j + 512] for j in range(4)], xsh, (xs4, k), 64 + k * 4, 88 + k))
            specs.append((Bc[:, :], [Bh[:, j:j + 512] for j in range(4)], Bh, Bc, 64 + 16, 88 + 4))
            specs.append((Cc[:, :], [Ch[:, j:j + 512] for j in range(4)], Ch, Cc, 64 + 20, 88 + 5))
            conv_tiles(specs)
            ri = 0
            for (cc, scale) in ((qc, SC), (kc, None)):
                p.act(sq4[:, :, :], cc[:, :, :], AF.Square, R=[cc], W=[sq4])
                for h in range(4):
                    b = bank()
                    r1, r2 = rs1s[ri % 2], rinvs[ri % 2]
                    ri += 1
                    p.mm(b[:, :], ones_f, sq4[:, h, :], R=[con, sq4], W=[b])
                    p.act(r1[:, :], b[:, :], AF.Sqrt, R=[b, eps_t], W=[r1], bias=eps_t[:, 0:1], scale=1.0)
                    p.recip(r2[:, :], r1[:, :], R=[r1], W=[r2])
                    if scale is None:
                        p.tt(cc[:, h, :], cc[:, h, :], r2[:, :], ALU.mult, R=[cc, r2], W=[(cc, h)])
                    else:
                        p.stt(cc[:, h, :], cc[:, h, :], scale, r2[:, :], ALU.mult, ALU.mult, R=[cc, r2], W=[(cc, h)])
            p.act(g_r[0:4, :], a_r[0:4, :], AF.Exp, R=[a_r, hp], W=[g_r], bias=hp[0:4, 1:2])
            p.act(g_r[0:4, :], g_r[0:4, :], AF.Ln, R=[g_r, one_t], W=[g_r], bias=one_t[0:4, 0:1])
            p.ts(g_r[0:4, :], g_r[0:4, :], negA[0:4, 0:1], None, ALU.mult, R=[g_r, negA], W=[g_r])
            p.op("dve", lambda e: e.tensor_tensor_scan(gc_r[0:4, :], cmask[0:4, :], g_r[0:4, :], 0.0, ALU.mult, ALU.add),
                 R=[con, g_r], W=[gc_r])
            p.act(be_r[0:4, :], b_r[0:4, :], AF.Sigmoid, R=[b_r], W=[be_r])
            p.act(bg_r[0:4, :], gc_r[0:4, :], AF.Exp, R=[gc_r], W=[bg_r])
            p.tt(bg_r[0:4, :], bg_r[0:4, :], be_r[0:4, :], ALU.mult, R=[bg_r, be_r], W=[bg_r])
            p.act(be_rs[0:8, :], a_rs[0:8, :], AF.Exp, R=[a_rs, hp], W=[be_rs], bias=hp[0:8, 3:4])
            p.act(be_rs[0:8, :], be_rs[0:8, :], AF.Ln, R=[be_rs, one_t], W=[be_rs], bias=one_t[0:8, 0:1])
            p.ts(g_rs[0:8, :], be_rs[0:8, :], negA[0:8, 1:2], None, ALU.mult, R=[be_rs, negA], W=[g_rs])
            p.op("dve", lambda e: e.tensor_tensor_scan(gc_rs[0:8, :], cmask[0:8, :], g_rs[0:8, :], 0.0, ALU.mult, ALU.add),
                 R=[con, g_rs], W=[gc_rs])
            for c4 in range(4):
                gg, sg = gdn_chunk(c4), ssd_chunk(c4)
                gi = 0
                g_done = s_done = False
                while not (g_done and s_done):
                    if not g_done:
                        try:
                            next(gg)
                        except StopIteration:
                            g_done = True
                    gi += 1
                    if not s_done and (g_done or gi % 2 == 0):
                        try:
                            next(sg)
                        except StopIteration:
                            s_done = True
            p.act(sq4[:, :, :], o4[:, :, :], AF.Square, R=[o4], W=[sq4])
            p.act(z4[:, :, :], z4[:, :, :], AF.Silu, R=[z4], W=[z4])
            for h in range(4):
                b = bank()
                p.mm(b[:, :], ones_f, sq4[:, h, :], R=[con, sq4], W=[b])
                p.act(rs1[:, :], b[:, :], AF.Sqrt, R=[b, eps_t], W=[rs1], bias=eps_t[:, 0:1], scale=1.0 / 128)
                p.recip(rinv[:, :], rs1[:, :], R=[rs1], W=[rinv])
                p.stt(o4[:, h, :], o4[:, h, :], prm_t[:, 94:95], rinv[:, :], ALU.mult, ALU.mult,
                      R=[o4, prm_t, rinv], W=[(o4, ("n", h))])
            p.tt(o4[:, :, :], o4[:, :, :], z4[:, :, :], ALU.mult, R=[o4, z4], W=[o4], eng="pool")
            p.dma(mixA[:, bsl].rearrange("(h p) t -> p h t", p=128), o4[:, :, :], R=[o4])

            p.dma(z4[:, :, :], projT[2816:3328, bsl].rearrange("(k p) t -> p k t", p=128), W=[z4])
            p.act(z4[:, :, :], z4[:, :, :], AF.Silu, R=[z4], W=[z4])
            p.tt(o4s[:, :, :], o4s[:, :, :], z4[:, :, :], ALU.mult, R=[o4s, z4], W=[o4s], eng="pool")
            p.act(sq4[:, :, :], o4s[:, :, :], AF.Square, R=[o4s], W=[sq4])
            b = bank()
            for k in range(4):
                p.mm(b[:, :], ones_f, sq4[:, k, :], start=(k == 0), stop=(k == 3), R=[con, sq4], W=[b])
            p.act(rs1[:, :], b[:, :], AF.Sqrt, R=[b, eps_t], W=[rs1], bias=eps_t[:, 0:1], scale=1.0 / 512)
            p.recip(rinv[:, :], rs1[:, :], R=[rs1], W=[rinv])
            for k in range(4):
                p.stt(o4s[:, k, :], o4s[:, k, :], prm_t[:, 96 + k:97 + k], rinv[:, :], ALU.mult, ALU.mult,
                      R=[o4s, prm_t, rinv], W=[(o4s, ("n", k))])
            p.dma(mixB[:, bsl].rearrange("(h p) t -> p h t", p=128), o4s[:, :, :], R=[o4s])
    if ctx is not None:
        return None
    p.emit()
    es.close()
    return nc


def pack_even(g, xT, norm_g, w_in, gdn_conv_w, gdn_A_log, gdn_dt_bias, gdn_norm_g,
              ssd_conv_w, ssd_conv_b, ssd_A_log, ssd_dt_bias, ssd_D, ssd_norm_g):
    hq = np.arange(512 * g, 512 * g + 512)
    h4 = np.arange(4 * g, 4 * g + 4)
    h8 = np.arange(8 * g, 8 * g + 8)
    o_za, o_a, o_b, o_xbc = 3072, 4096, 4104, 4112
    o_zb, o_dt = o_xbc + 1536, o_xbc + 1536 + 1024
    cols = np.concatenate([hq, 1024 + hq, 2048 + hq, o_za + hq, o_xbc + hq,
                           o_xbc + 1024 + 128 * g + np.arange(128), o_xbc + 1280 + 128 * g + np.arange(128),
                           o_zb + hq, o_a + h4, o_b + h4, o_dt + h8])
    prm = np.zeros((128, 112), np.float32)
    prm[:, 0:16] = _pk(norm_g, 16)
    gw = np.asarray(gdn_conv_w, np.float32)
    for sec in range(3):
        for h in range(4):
            ch = sec * 1024 + 512 * g + h * 128
            for j in range(4):
                prm[:, 16 + (sec * 4 + h) * 4 + j] = gw[j, ch:ch + 128]
    sw = np.asarray(ssd_conv_w, np.float32)
    sb_ = np.asarray(ssd_conv_b, np.float32)
    starts = [512 * g + k * 128 for k in range(4)] + [1024 + 128 * g, 1280 + 128 * g]
    for i, c0 in enumerate(starts):
        for j in range(4):
            prm[:, 64 + i * 4 + j] = sw[j, c0:c0 + 128]
        prm[:, 88 + i] = sb_[c0:c0 + 128]
    prm[:, 94] = np.asarray(gdn_norm_g, np.float32)
    prm[:, 96:100] = _pk(np.asarray(ssd_norm_g)[hq], 4)
    hp = np.zeros((8, 8), np.float32)
    hp[0:4, 0] = np.asarray(gdn_A_log)[h4]
    hp[0:4, 1] = np.asarray(gdn_dt_bias)[h4]
    hp[0:8, 2] = np.asarray(ssd_A_log)[h8]
    hp[0:8, 3] = np.asarray(ssd_dt_bias)[h8]
    dbc = np.ascontiguousarray(np.broadcast_to(np.repeat(np.asarray(ssd_D, np.float32)[h8], 64)[None, :], (128, 512)))
    return dict(xT=np.ascontiguousarray(xT, dtype=np.float32), w=np.ascontiguousarray(np.asarray(w_in, np.float32)[:, cols]),
                prm=prm, hp=hp, dbc=dbc, con=make_consts(g))


def _pk(v, n):
    return np.ascontiguousarray(np.asarray(v, np.float32).reshape(n, 128).T)


def pack_odd(g, xT, norm_g, w_in, ret_ng, conv_w, conv_b, w_a, b_a, w_x, b_x, lam, pos):
    hq = slice(512 * g, 512 * g + 512)
    cols = np.concatenate([np.arange(0, 1024)[hq], 1024 + np.arange(1024)[hq], 2048 + np.arange(1024)[hq],
                           3072 + np.arange(1024)[hq], 4096 + np.arange(1024)[hq], 5120 + np.arange(1024)[hq]])
    prm = np.zeros((128, 64), np.float32)
    prm[:, 0:16] = _pk(norm_g, 16)
    prm[:, 16:20] = _pk(ret_ng[hq], 4)
    cw = np.asarray(conv_w, np.float32)[:, hq]
    for ct in range(4):
        for j in range(4):
            prm[:, 20 + ct * 4 + j] = cw[j, ct * 128:(ct + 1) * 128]
    prm[:, 36:40] = _pk(np.asarray(conv_b)[hq], 4)
    prm[:, 40:44] = _pk(np.asarray(b_a)[hq], 4)
    prm[:, 44:48] = _pk(np.asarray(b_x)[hq], 4)
    prm[:, 48:52] = _pk(np.asarray(lam)[hq], 4)
    lw = np.concatenate([np.asarray(w_a, np.float32)[4 * g:4 * g + 4], np.asarray(w_x, np.float32)[4 * g:4 * g + 4]], 0)
    lruw = np.ascontiguousarray(lw.transpose(1, 0, 2).reshape(128, 8 * 128))
    return dict(xT=np.ascontiguousarray(xT, dtype=np.float32), w=np.ascontiguousarray(np.asarray(w_in, np.float32)[:, cols]),
                prm=prm, con=make_consts(g), lruw=lruw, pos=np.ascontiguousarray(np.asarray(pos, np.int32).reshape(1, -1)))


def unpack_mix(parts):
    return np.concatenate([parts[0][0:512], parts[1][0:512], parts[0][512:1024], parts[1][512:1024]], 0)


def build_fused(T, depth=4):
    nc = _new_nc()
    lay, ncon = _con_layout()
    dt = lambda name, shape, dty=F32, kind="ExternalInput": nc.dram_tensor(name, list(shape), dty, kind=kind).ap()
    xT = dt("xT", [D_MODEL, T])
    pT = dt("pT", [depth * 256, T])
    pos = dt("pos", [1, T], I32)
    con = [dt(f"con{g}", [128, ncon]) for g in range(2)]
    yT = dt("yT", [D_MODEL, T], kind="ExternalOutput")
    projT = dt("projT", [3344, T], kind="Internal")
    mixF = dt("mixF", [D_MODEL, T], kind="Internal")
    xs = [dt("xA", [D_MODEL, T], kind="Internal"), dt("xB", [D_MODEL, T], kind="Internal")]
    p = Prog(nc)
    es = contextlib.ExitStack()
    es.enter_context(nc.allow_low_precision("bf16 matmul operands, fp32 accumulation"))
    x_cur = xT
    for i in range(depth):
        even = (i % 2 == 0)
        for g in range(2):
            ctx = dict(nc=nc, p=p, xT=x_cur, con=con[g], projT=projT[0:(3344 if even else 3072), :],
                       mixA=mixF[512 * g:512 * g + 512, :], mixB=mixF[1024 + 512 * g:1536 + 512 * g, :],
                       w=dt(f"w{i}_{g}", [D_MODEL, 3344 if even else 3072]),
                       prm=dt(f"prm{i}_{g}", [128, 112 if even else 64]))
            if even:
                ctx["hp"] = dt(f"hp{i}_{g}", [8, 8])
                ctx["dbc"] = dt(f"dbc{i}_{g}", [128, 512])
            else:
                ctx["lruw"] = dt(f"lruw{i}_{g}", [128, 8 * 128])
                ctx["pos"] = pos
            with p.scope():
                (build_AB_even if even else build_AB_odd)(T, ctx=ctx)
        final = (i == depth - 1)
        x_nxt = yT if final else xs[i % 2]
        ctx = dict(nc=nc, p=p, mixT=mixF, xT=x_cur, pT=pT[i * 256:(i + 1) * 256, :], wo=dt(f"wo{i}", [D_MODEL, D_MODEL]),
                   wg=dt(f"wg{i}", [D_MODEL, D_MODEL]), wp=dt(f"wp{i}", [256, D_MODEL]), prm=dt(f"prmC{i}", [128, 32]), yT=x_nxt)
        with p.scope():
            build_C(T, final, ctx=ctx)
        x_cur = x_nxt
    p.emit()
    es.close()
    return nc


_CACHE = {}


def _prog(name, fn):
    if name not in _CACHE:
        _CACHE[name] = fn()
    return _CACHE[name]


def kernel_unfused(x, p, positions, norm_g, ple_norm_g, w_ple_gate, w_ple_proj,
           ev_w_in, ev_w_out, gdn_conv_w, gdn_A_log, gdn_dt_bias, gdn_norm_g,
           ssd_conv_w, ssd_conv_b, ssd_A_log, ssd_dt_bias, ssd_D, ssd_norm_g,
           od_w_in, od_w_out, ret_norm_g, lru_conv_w, lru_conv_b,
           lru_w_a, lru_b_a, lru_w_x, lru_b_x, lru_lambda, final_norm_g):
    f = lambda a: np.asarray(a, dtype=np.float32)
    x = f(x)
    B, S, D = x.shape
    H = S // 2
    cores = list(range(8))
    xT = [np.ascontiguousarray(x[b].T) for b in range(B)]
    depth = int(np.asarray(norm_g).shape[0])
    for i in range(depth):
        j = i // 2
        if i % 2 == 0:
            nc = _prog("even", lambda: build_AB_even(S))
            ins = [pack_even(c % 2, xT[c // 2], f(norm_g)[i], f(ev_w_in)[j], f(gdn_conv_w)[j], f(gdn_A_log)[j],
                             f(gdn_dt_bias)[j], f(gdn_norm_g)[j], f(ssd_conv_w)[j], f(ssd_conv_b)[j], f(ssd_A_log)[j],
                             f(ssd_dt_bias)[j], f(ssd_D)[j], f(ssd_norm_g)[j]) for c in cores]
            w_out = f(ev_w_out)[j]
        else:
            nc = _prog("odd", lambda: build_AB_odd(S))
            ins = [pack_odd(c % 2, xT[c // 2], f(norm_g)[i], f(od_w_in)[j], f(ret_norm_g)[j], f(lru_conv_w)[j],
                            f(lru_conv_b)[j], f(lru_w_a)[j], f(lru_b_a)[j], f(lru_w_x)[j], f(lru_b_x)[j],
                            f(lru_lambda)[j], np.asarray(positions)[c // 2]) for c in cores]
            w_out = f(od_w_out)[j]
        res = run_bass_kernel_spmd(nc, ins, core_ids=cores)
        mix = [unpack_mix([res.results[2 * b]["mixT"], res.results[2 * b + 1]["mixT"]]) for b in range(B)]
        del res, ins
        final = (i == depth - 1)
        ncC = _prog("Cf" if final else "C", lambda: build_C(H, final))
        prm = np.zeros((128, 32), np.float32)
        prm[:, 0:16] = _pk(f(ple_norm_g)[i], 16)
        prm[:, 16:32] = _pk(f(final_norm_g), 16)
        wg = np.ascontiguousarray(f(w_ple_gate)[i])
        wp = np.ascontiguousarray(f(w_ple_proj)[i])
        wo = np.ascontiguousarray(w_out)
        insC = []
        for c in cores:
            b, g = c // 2, c % 2
            sl = slice(g * H, (g + 1) * H)
            insC.append(dict(mixT=np.ascontiguousarray(mix[b][:, sl]), xT=np.ascontiguousarray(xT[b][:, sl]),
                             pT=np.ascontiguousarray(f(p)[i, b, sl, :].T), wo=wo, wg=wg, wp=wp, prm=prm))
        res = run_bass_kernel_spmd(ncC, insC, core_ids=cores)
        xT = [np.concatenate([res.results[2 * b]["yT"], res.results[2 * b + 1]["yT"]], axis=1) for b in range(B)]
        del res, insC, mix
    return np.ascontiguousarray(np.stack([t.T for t in xT], 0)).astype(np.float32)


def kernel(x, p, positions, norm_g, ple_norm_g, w_ple_gate, w_ple_proj,
           ev_w_in, ev_w_out, gdn_conv_w, gdn_A_log, gdn_dt_bias, gdn_norm_g,
           ssd_conv_w, ssd_conv_b, ssd_A_log, ssd_dt_bias, ssd_D, ssd_norm_g,
           od_w_in, od_w_out, ret_norm_g, lru_conv_w, lru_conv_b,
           lru_w_a, lru_b_a, lru_w_x, lru_b_x, lru_lambda, final_norm_g):
    f = lambda a: np.asarray(a, dtype=np.float32)
    x = f(x)
    B, S, D = x.shape
    depth = int(np.asarray(norm_g).shape[0])
    nc = _prog("fused", lambda: build_fused(S, depth))
    shared = {}
    for g in range(2):
        shared[f"con{g}"] = make_consts(g)
    dummy = np.zeros((D, 8), np.float32)
    for i in range(depth):
        j = i // 2
        for g in range(2):
            if i % 2 == 0:
                d = pack_even(g, dummy, f(norm_g)[i], f(ev_w_in)[j], f(gdn_conv_w)[j], f(gdn_A_log)[j],
                              f(gdn_dt_bias)[j], f(gdn_norm_g)[j], f(ssd_conv_w)[j], f(ssd_conv_b)[j], f(ssd_A_log)[j],
                              f(ssd_dt_bias)[j], f(ssd_D)[j], f(ssd_norm_g)[j])
                shared[f"hp{i}_{g}"] = d["hp"]
                shared[f"dbc{i}_{g}"] = d["dbc"]
            else:
                d = pack_odd(g, dummy, f(norm_g)[i], f(od_w_in)[j], f(ret_norm_g)[j], f(lru_conv_w)[j],
                             f(lru_conv_b)[j], f(lru_w_a)[j], f(lru_b_a)[j], f(lru_w_x)[j], f(lru_b_x)[j],
                             f(lru_lambda)[j], np.zeros(8, np.int32))
                shared[f"lruw{i}_{g}"] = d["lruw"]
            shared[f"w{i}_{g}"] = d["w"]
            shared[f"prm{i}_{g}"] = d["prm"]
        prm = np.zeros((128, 32), np.float32)
        prm[:, 0:16] = _pk(f(ple_norm_g)[i], 16)
        prm[:, 16:32] = _pk(f(final_norm_g), 16)
        shared[f"prmC{i}"] = prm
        shared[f"wo{i}"] = np.ascontiguousarray(f(ev_w_out)[j] if i % 2 == 0 else f(od_w_out)[j])
        shared[f"wg{i}"] = np.ascontiguousarray(f(w_ple_gate)[i])
        shared[f"wp{i}"] = np.ascontiguousarray(f(w_ple_proj)[i])
    ins = []
    for c in range(8):
        b = c % B
        d = dict(shared)
        d["xT"] = np.ascontiguousarray(x[b].T)
        d["pT"] = np.ascontiguousarray(np.concatenate([f(p)[i, b].T for i in range(depth)], axis=0))
        d["pos"] = np.ascontiguousarray(np.asarray(positions, np.int32)[b].reshape(1, -1))
        ins.append(d)
    res = run_bass_kernel_spmd(nc, ins, core_ids=list(range(8)))
    return np.ascontiguousarray(np.stack([res.results[b]["yT"].T for b in range(B)], 0)).astype(np.float32)
```

```python
import contextlib
import numpy as np
import concourse.bass as bass
import concourse.mybir as mybir
from concourse.bass_utils import run_bass_kernel_spmd

F32 = mybir.dt.float32
F32R = mybir.dt.float32r
FP32R = False
BF16 = mybir.dt.bfloat16
I32 = mybir.dt.int32
AF = mybir.ActivationFunctionType
ALU = mybir.AluOpType

D_MODEL = 2048
SEQ = 8192
EPS = 1e-6
NDMA = 24
NDMA_HW = 16
SAME_ENGINE_SYNC = True
STORE_ON_POOL = True


class _Tok:
    __slots__ = ("w", "rs")

    def __init__(self):
        self.w = None
        self.rs = []


class Prog:
    ENGS = ("pe", "dve", "act", "pool", "sp")

    def __init__(self, nc):
        self.nc = nc
        self.q = {e: [] for e in self.ENGS}
        self.cnt = {e: 0 for e in self.ENGS}
        self.seen = {e: {} for e in self.ENGS}
        self.toks = {}
        self.slot_uses = [0] * NDMA
        self.rr = 0
        self.rr2 = 0
        self.stack = contextlib.ExitStack()
        self.stacks = [self.stack]
        self.pending = {e: {} for e in self.ENGS}
        self.nt = 0
        self.psum_ids = set()
        self.keep = []

    @contextlib.contextmanager
    def scope(self):
        st = contextlib.ExitStack()
        self.stacks.append(st)
        try:
            yield
        finally:
            self.stacks.pop()
            st.close()
            self.barrier()

    def barrier(self):
        for e in self.ENGS:
            pd = self.pending[e]
            for o in ("pe", "dve", "act", "pool"):
                if self.cnt[o] > 0:
                    pd[o] = self.cnt[o]
            for s in range(NDMA):
                if self.slot_uses[s] > 0:
                    pd[("d", s)] = 16 * self.slot_uses[s]

    def sb(self, shape, dt=F32, name=None):
        self.nt += 1
        t = self.stacks[-1].enter_context(self.nc.sbuf_tensor(name or f"t{self.nt}", list(shape), dt))
        self.keep.append(t)
        return t

    def ps(self, shape, dt=F32, name=None):
        self.nt += 1
        t = self.stacks[-1].enter_context(self.nc.psum_tensor(name or f"p{self.nt}", list(shape), dt))
        self.psum_ids.add(id(t))
        self.keep.append(t)
        return t

    def _tk(self, ref):
        if isinstance(ref, tuple):
            t, k = ref
        else:
            t, k = ref, None
        if id(t) in self.psum_ids:
            k = None
        d = self.toks.setdefault(id(t), {"_": _Tok()})
        return d, k

    def _deps(self, R, W, eng=None):
        need = {}

        def add(ev):
            if ev is not None:
                if need.get(ev[0], 0) < ev[1]:
                    need[ev[0]] = ev[1]

        for ref in R:
            d, k = self._tk(ref)
            add(d["_"].w)
            if k is None:
                for kk, tk in d.items():
                    add(tk.w)
            elif k in d:
                add(d[k].w)
            t_ = ref[0] if isinstance(ref, tuple) else ref
            if id(t_) in self.psum_ids:
                for ev in d["_"].rs:
                    if ev[0] != eng:
                        add(ev)
        for ref in W:
            d, k = self._tk(ref)
            keys = list(d.keys()) if k is None else (["_", k] if k in d else ["_"])
            for kk in keys:
                add(d[kk].w)
                for ev in d[kk].rs:
                    add(ev)
        return need

    def _upd(self, R, W, ev):
        for ref in R:
            d, k = self._tk(ref)
            tk = d["_"] if k is None else d.setdefault(k, _Tok())
            tk.rs.append(ev)
            if len(tk.rs) > 24:
                m = {}
                for e in tk.rs:
                    if m.get(e[0], 0) < e[1]:
                        m[e[0]] = e[1]
                tk.rs = list(m.items())
        for ref in W:
            d, k = self._tk(ref)
            if k is None:
                for kk in d:
                    d[kk].w = ev
                    d[kk].rs = []
            else:
                tk = d.setdefault(k, _Tok())
                tk.w = ev
                tk.rs = []

    def op(self, eng, fn, R=(), W=(), dma=False):
        need = self._deps(R, W, eng)
        if self.pending[eng]:
            for k, v in self.pending[eng].items():
                if need.get(k, 0) < v:
                    need[k] = v
            self.pending[eng] = {}
        if eng == "sp" or dma:
            if dma:
                s = NDMA_HW + self.rr2
                self.rr2 = (self.rr2 + 1) % (NDMA - NDMA_HW)
            else:
                s = self.rr
                self.rr = (self.rr + 1) % NDMA_HW
            key = ("d", s)
            if self.slot_uses[s] > 0:
                v = 16 * self.slot_uses[s]
                if need.get(key, 0) < v:
                    need[key] = v
            self.slot_uses[s] += 1
            ev = (key, 16 * self.slot_uses[s])
            inc = (key, 16)
        else:
            self.cnt[eng] += 1
            ev = (eng, self.cnt[eng])
            inc = (eng, 1)
        waits = []
        seen = self.seen[eng]
        for k, v in need.items():
            if k == eng and (eng == "pe" or not SAME_ENGINE_SYNC):
                continue
            if seen.get(k, 0) >= v:
                continue
            seen[k] = v
            waits.append((k, v))
        self.q[eng].append((waits, fn, inc))
        self._upd(R, W, ev)

    def mm(self, out, lhsT, rhs, start=True, stop=True, R=(), W=(), exact=False):
        if (FP32R and not exact and lhsT.dtype == F32 and rhs.dtype == F32
                and lhsT.partition_size() == 128 and rhs.partition_size() == 128):
            lhsT = lhsT.bitcast(F32R)
            rhs = rhs.bitcast(F32R)
        self.op("pe", lambda e: e.matmul(out, lhsT, rhs, start=start, stop=stop), R, W)

    def tr(self, out, in_, ident, R=(), W=()):
        self.op("pe", lambda e: e.transpose(out, in_, ident), R, W)

    def act(self, out, in_, func, R=(), W=(), bias=None, scale=None, eng="act"):
        kw = {}
        if bias is not None:
            kw["bias"] = bias
        if scale is not None:
            kw["scale"] = scale
        self.op(eng, lambda e: e.activation(out, in_, func, **kw), R, W)

    def tt(self, out, in0, in1, alu, R=(), W=(), eng="dve"):
        self.op(eng, lambda e: e.tensor_tensor(out, in0, in1, alu), R, W)

    def ts(self, out, in0, s1, s2, op0, op1=None, R=(), W=(), eng="dve"):
        if op1 is None:
            self.op(eng, lambda e: e.tensor_scalar(out, in0, s1, None, op0), R, W)
        else:
            self.op(eng, lambda e: e.tensor_scalar(out, in0, s1, s2, op0, op1), R, W)

    def stt(self, out, in0, scalar, in1, op0, op1, R=(), W=()):
        self.op("dve", lambda e: e.scalar_tensor_tensor(out, in0, scalar, in1, op0, op1), R, W)

    def cp(self, out, in_, R=(), W=(), eng="dve"):
        if eng == "act":
            self.op("act", lambda e: e.activation(out, in_, AF.Copy), R, W)
        else:
            self.op(eng, lambda e: e.tensor_copy(out, in_), R, W)

    def recip(self, out, in_, R=(), W=()):
        self.op("dve", lambda e: e.reciprocal(out, in_), R, W)

    def memset(self, ap, val, W=(), eng="pool"):
        self.op(eng, lambda e: e.memset(ap, val), (), W)

    def dma(self, out, in_, R=(), W=()):
        if STORE_ON_POOL and not W:
            self.op("pool", lambda e: e.dma_start(out=out, in_=in_), R, W, dma=True)
        else:
            self.op("sp", lambda e: e.dma_start(out=out, in_=in_), R, W)

    def emit(self):
        nc = self.nc
        with contextlib.ExitStack() as es:
            sems = {}
            for e in ("pe", "dve", "act", "pool"):
                sems[e] = es.enter_context(nc.semaphore("s_" + e))
            for s in range(NDMA):
                sems[("d", s)] = es.enter_context(nc.semaphore(f"s_d{s}"))
            block = es.enter_context(nc.Block())

            def run(name):
                def f(eng):
                    for waits, fn, inc in self.q[name]:
                        for k, v in waits:
                            eng.wait_ge(sems[k], v)
                        fn(eng).then_inc(sems[inc[0]], inc[1])
                    if name == "sp":
                        for s in range(NDMA):
                            if self.slot_uses[s] > 0:
                                eng.wait_ge(sems[("d", s)], 16 * self.slot_uses[s])
                return f

            block.tensor(run("pe"))
            block.vector(run("dve"))
            block.scalar(run("act"))
            block.gpsimd(run("pool"))
            block.sync(run("sp"))
        self.stack.close()


def _new_nc():
    return bass.Bass("TRN2", target_bir_lowering=False)


def build_C(T, final, TT=256, ctx=None):
    KT = D_MODEL // 128
    if ctx is None:
        nc = _new_nc()
        mixT = nc.dram_tensor("mixT", [D_MODEL, T], F32, kind="ExternalInput").ap()
        xT = nc.dram_tensor("xT", [D_MODEL, T], F32, kind="ExternalInput").ap()
        pT = nc.dram_tensor("pT", [256, T], F32, kind="ExternalInput").ap()
        wo = nc.dram_tensor("wo", [D_MODEL, D_MODEL], F32, kind="ExternalInput").ap()
        wg = nc.dram_tensor("wg", [D_MODEL, D_MODEL], F32, kind="ExternalInput").ap()
        wp = nc.dram_tensor("wp", [256, D_MODEL], F32, kind="ExternalInput").ap()
        prm = nc.dram_tensor("prm", [128, 32], F32, kind="ExternalInput").ap()
        yT = nc.dram_tensor("yT", [D_MODEL, T], F32, kind="ExternalOutput").ap()
        p = Prog(nc)
        es = contextlib.ExitStack()
        es.enter_context(nc.allow_low_precision("bf16 matmul operands, fp32 accumulation"))
    else:
        nc, p = ctx["nc"], ctx["p"]
        mixT, xT, pT, wo, wg, wp, prm, yT = (ctx[k] for k in ("mixT", "xT", "pT", "wo", "wg", "wp", "prm", "yT"))
    prm_t = p.sb([128, 32])
    p.dma(prm_t[:, :], prm[:, :], W=[prm_t])
    ones_b = p.sb([128, 128], BF16)
    p.memset(ones_b[:, :], 1.0, W=[ones_b])
    eps_t = p.sb([128, 1])
    p.memset(eps_t[:, :], EPS, W=[eps_t])

    wo_b = p.sb([128, KT, D_MODEL], BF16)
    wg_b = p.sb([128, KT, D_MODEL], BF16)
    wp_b = p.sb([128, 2, D_MODEL], BF16)
    wst = [p.sb([128, 4, TT]) for _ in range(3)]
    i = 0
    for (src, dst, nk, gcol) in ((wo, wo_b, KT, None), (wg, wg_b, KT, 0), (wp, wp_b, 2, None)):
        for c in range(2):
            for kt in range(nk):
                st = wst[i % 3]
                i += 1
                p.dma(st[:, :, :], src[kt * 128:(kt + 1) * 128, c * 1024:(c + 1) * 1024].rearrange("p (c t) -> p c t", c=4), W=[st])
                dv = dst[:, kt, c * 1024:(c + 1) * 1024].rearrange("p (c t) -> p c t", c=4)
                if gcol is None:
                    p.cp(dv, st[:, :, :], R=[st], W=[(dst, (kt, c))], eng="act")
                else:
                    p.act(dv, st[:, :, :], AF.Copy, R=[st, prm_t], W=[(dst, (kt, c))],
                          scale=prm_t[:, gcol + kt:gcol + kt + 1])

    NTT = T // TT
    assert TT == 256
    mst = wst
    mb = [p.sb([128, KT, TT], BF16) for _ in range(1)]
    xt = [p.sb([128, KT, TT]) for _ in range(2)]
    xb = [p.sb([128, KT, TT], BF16) for _ in range(1)]
    sq = [p.sb([128, TT], BF16) for _ in range(3)]
    pst = p.sb([128, 2, TT])
    pb = p.sb([128, 2, TT], BF16)
    rs_t = p.sb([128, TT])
    rstd = p.sb([128, TT])
    gt = [p.sb([128, TT]) for _ in range(2)]
    g2 = [p.sb([128, TT]) for _ in range(2)]
    acc = [p.ps([128, 512]) for _ in range(3)]
    acc2 = [p.ps([128, 512]) for _ in range(2)]
    ssq = p.ps([128, 512])
    ci = 0
    for tt in range(NTT):
        tsl = slice(tt * TT, (tt + 1) * TT)
        m_b = mb[0]
        x_t = xt[tt % 2]
        x_b = xb[0]
        p.dma(x_t[:, :, :], xT[:, tsl].rearrange("(k p) t -> p k t", p=128), W=[x_t])
        p.dma(pst[:, :, :], pT[:, tsl].rearrange("(k p) t -> p k t", p=128), W=[pst])
        for kg in range(4):
            st = mst[ci % 3]
            ci += 1
            p.dma(st[:, :, :], mixT[kg * 512:(kg + 1) * 512, tsl].rearrange("(k p) t -> p k t", p=128), W=[st])
            p.cp(m_b[:, kg * 4:(kg + 1) * 4, :], st[:, :, :], R=[st], W=[(m_b, kg)], eng="pool")
        p.cp(pb[:, :, :], pst[:, :, :], R=[pst], W=[pb], eng="pool")
        for dc in range(KT):
            a = acc[dc % 3]
            for kt in range(KT):
                p.mm(a[:, 0:TT], wo_b[:, kt, dc * 128:(dc + 1) * 128], m_b[:, kt, :], start=(kt == 0), stop=(kt == KT - 1),
                     R=[(wo_b, (kt, dc // 8)), m_b], W=[a])
            p.tt(x_t[:, dc, :], a[:, 0:TT], x_t[:, dc, :], ALU.add, R=[a, (x_t, dc)], W=[(x_t, dc)])
            s = sq[dc % 3]
            p.act(s[:, :], x_t[:, dc, :], AF.Square, R=[(x_t, dc)], W=[s])
            p.mm(ssq[:, 0:TT], ones_b[:, :], s[:, :], start=(dc == 0), stop=(dc == KT - 1), R=[ones_b, s], W=[ssq])
            p.cp(x_b[:, dc, :], x_t[:, dc, :], R=[(x_t, dc)], W=[(x_b, dc)], eng="pool")
        p.act(rs_t[:, :], ssq[:, 0:TT], AF.Sqrt, R=[ssq, eps_t], W=[rs_t], bias=eps_t[:, 0:1], scale=1.0 / D_MODEL)
        p.recip(rstd[:, :], rs_t[:, :], R=[rs_t], W=[rstd])
        for dc in range(KT):
            a = acc[dc % 3]
            for kt in range(KT):
                p.mm(a[:, 0:TT], wg_b[:, kt, dc * 128:(dc + 1) * 128], x_b[:, kt, :], start=(kt == 0), stop=(kt == KT - 1),
                     R=[(wg_b, (kt, dc // 8)), x_b], W=[a])
            a2 = acc2[dc % 2]
            for kt in range(2):
                p.mm(a2[:, 0:TT], wp_b[:, kt, dc * 128:(dc + 1) * 128], pb[:, kt, :], start=(kt == 0), stop=(kt == 1),
                     R=[(wp_b, (kt, dc // 8)), pb], W=[a2])
            g = gt[dc % 2]
            gg = g2[dc % 2]
            p.tt(g[:, :], a[:, 0:TT], rstd[:, :], ALU.mult, R=[a, rstd], W=[g])
            p.act(gg[:, :], g[:, :], AF.Sigmoid, R=[g], W=[gg])
            p.tt(g[:, :], a2[:, 0:TT], gg[:, :], ALU.mult, R=[a2, gg], W=[g])
            p.tt(x_t[:, dc, :], x_t[:, dc, :], g[:, :], ALU.add, R=[g, (x_t, dc)], W=[(x_t, dc)], eng="pool")
        if final:
            for dc in range(KT):
                s = sq[dc % 3]
                p.act(s[:, :], x_t[:, dc, :], AF.Square, R=[(x_t, dc)], W=[s])
                p.mm(ssq[:, 0:TT], ones_b[:, :], s[:, :], start=(dc == 0), stop=(dc == KT - 1), R=[ones_b, s], W=[ssq])
            p.act(rs_t[:, :], ssq[:, 0:TT], AF.Sqrt, R=[ssq, eps_t], W=[rs_t], bias=eps_t[:, 0:1], scale=1.0 / D_MODEL)
            p.recip(rstd[:, :], rs_t[:, :], R=[rs_t], W=[rstd])
            for dc in range(KT):
                p.stt(x_t[:, dc, :], x_t[:, dc, :], prm_t[:, 16 + dc:17 + dc], rstd[:, :], ALU.mult, ALU.mult,
                      R=[(x_t, dc), prm_t, rstd], W=[(x_t, dc)])
        p.dma(yT[:, tsl].rearrange("(k p) t -> p k t", p=128), x_t[:, :, :], R=[x_t], W=[])
    if ctx is not None:
        return None
    p.emit()
    es.close()
    return nc


def phase_A(p, xT, w_loc, NCOL, projT, prm_t, T, ones_b, eps_t):
    KT = D_MODEL // 128
    TT = 512
    with p.scope():
        wb = p.sb([128, KT, NCOL], BF16)
        wst = [p.sb([128, 512]) for _ in range(3)]
        i = 0
        for c0 in range(0, NCOL, 512):
            for kt in range(KT):
                c1 = min(NCOL, c0 + 512)
                st = wst[i % 3]
                i += 1
                p.dma(st[:, 0:c1 - c0], w_loc[kt * 128:(kt + 1) * 128, c0:c1], W=[st])
                p.act(wb[:, kt, c0:c1], st[:, 0:c1 - c0], AF.Copy, R=[st, prm_t], W=[(wb, (kt, c0))],
                      scale=prm_t[:, kt:kt + 1])
        xst = [p.sb([128, 4, TT]) for _ in range(4)]
        xb = [p.sb([128, KT, TT], BF16) for _ in range(2)]
        sq = [p.sb([128, 4, TT], BF16) for _ in range(2)]
        rs_t = p.sb([128, TT])
        rstd = [p.sb([128, TT]) for _ in range(2)]
        ost = [p.sb([128, TT]) for _ in range(3)]
        acc = [p.ps([128, 512]) for _ in range(4)]
        ssq = p.ps([128, 512])
        oi = 0
        nct = (NCOL + 127) // 128
        NT = T // TT

        def front_loads(tt):
            tsl = slice(tt * TT, (tt + 1) * TT)
            for kg in range(4):
                st = xst[kg]
                p.dma(st[:, :, :], xT[kg * 512:(kg + 1) * 512, tsl].rearrange("(k p) t -> p k t", p=128), W=[st])

        def front_compute(tt):
            x_b = xb[tt % 2]
            rsd = rstd[tt % 2]
            for kg in range(4):
                st = xst[kg]
                s2 = sq[kg % 2]
                p.act(s2[:, :, :], st[:, :, :], AF.Square, R=[st], W=[s2])
                p.cp(x_b[:, kg * 4:(kg + 1) * 4, :], st[:, :, :], R=[st], W=[(x_b, kg)], eng="pool")
                for k in range(4):
                    p.mm(ssq[:, :], ones_b[:, :], s2[:, k, :], start=(kg == 0 and k == 0), stop=(kg == 3 and k == 3),
                         R=[ones_b, s2], W=[ssq])
            p.act(rs_t[:, :], ssq[:, :], AF.Sqrt, R=[ssq, eps_t], W=[rs_t], bias=eps_t[:, 0:1], scale=1.0 / D_MODEL)
            p.recip(rsd[:, :], rs_t[:, :], R=[rs_t], W=[rsd])

        front_loads(0)
        front_compute(0)
        for tt in range(NT):
            tsl = slice(tt * TT, (tt + 1) * TT)
            x_b = xb[tt % 2]
            rsd = rstd[tt % 2]
            for ct in range(nct):
                m = min(128, NCOL - ct * 128)
                a = acc[ct % 4]
                for kt in range(KT):
                    p.mm(a[0:m, :], wb[:, kt, ct * 128:ct * 128 + m], x_b[:, kt, :], start=(kt == 0), stop=(kt == KT - 1),
                         R=[(wb, (kt, (ct // 4) * 512)), x_b], W=[a])
                o = ost[oi % 3]
                oi += 1
                p.tt(o[0:m, :], a[0:m, :], rsd[0:m, :], ALU.mult, R=[a, rsd], W=[o])
                p.dma(projT[ct * 128:ct * 128 + m, tsl], o[0:m, :], R=[o])
                if tt + 1 < NT and ct == 1:
                    front_loads(tt + 1)
                if tt + 1 < NT and ct == 12:
                    front_compute(tt + 1)


def _con_layout():
    lay = {}
    off = 0
    for name, w in (("ident", 128), ("ones", 128), ("rotm", 128), ("maskT", 128), ("maskS", 128),
                    ("invf", 128), ("dmT", 512), ("qdec", 512), ("kdec", 4), ("glb", 4),
                    ("selbc4", 4 * 128), ("selbc8", 8 * 128), ("selc", 288), ("cmask", 512), ("elast", 128)):
        lay[name] = (off, w)
        off += w
    return lay, off


def make_consts(g):
    lay, n = _con_layout()
    c = np.zeros((128, n), np.float32)

    def put(name, arr):
        o, w = lay[name]
        c[:arr.shape[0], o:o + w] = arr.reshape(arr.shape[0], -1)

    put("ident", np.eye(128, dtype=np.float32))
    put("ones", np.ones((128, 128), np.float32))
    rot = np.zeros((128, 128), np.float32)
    for d in range(64):
        rot[d + 64, d] = -1.0
        rot[d, d + 64] = 1.0
    put("rotm", rot)
    idx = np.arange(128)
    put("maskT", np.where(idx[None, :] >= idx[:, None], 0.0, -1e30).astype(np.float32))
    put("maskS", np.where(idx[None, :] < idx[:, None], 0.0, -1e30).astype(np.float32))
    invf = (10000.0 ** (-np.arange(0, 128, 2, dtype=np.float32) / np.float32(128))).astype(np.float32)
    put("invf", np.concatenate([invf, invf])[None, :])
    heads = np.arange(4) + 4 * g
    lg = np.log1p(-np.exp2(-5.0 - heads.astype(np.float32))).astype(np.float32)
    sc = np.float32(128 ** -0.5)
    rel = (idx[None, :] - idx[:, None]).astype(np.float32)
    dmT = np.where((rel >= 0)[:, None, :], np.exp(np.maximum(rel, 0.0)[:, None, :] * lg[None, :, None]), 0.0) * sc
    put("dmT", dmT.astype(np.float32))
    qdec = np.exp((idx + 1.0)[None, None, :] * lg[None, :, None]) * np.ones((128, 1, 1))
    put("qdec", qdec.astype(np.float32))
    kdec = np.exp((127.0 - idx)[:, None] * lg[None, :]) * sc
    put("kdec", kdec.astype(np.float32))
    put("glb", (np.exp(128.0 * lg)[None, :] * np.ones((128, 1))).astype(np.float32))
    s4 = np.zeros((4, 4, 128), np.float32)
    for h in range(4):
        s4[h, h, :] = 1.0
    put("selbc4", s4)
    s8 = np.zeros((8, 8, 128), np.float32)
    for h in range(8):
        s8[h, h, :] = 1.0
    put("selbc8", s8)
    sc_ = np.zeros((8, 6, 48), np.float32)
    for h in range(8):
        for q_ in range(6):
            sc_[h, q_, q_ * 8 + h] = 1.0
    put("selc", sc_)
    cm = np.ones((8, 512), np.float32)
    cm[:, 0::128] = 0.0
    put("cmask", cm)
    el = np.zeros((128, 128), np.float32)
    el[127, :] = 1.0
    put("elast", el)
    return c


def _cv(con, lay, name, rows=128):
    o, w = lay[name]
    return con[0:rows, o:o + w]


def build_AB_odd(T, ctx=None):
    NCOL = 3072
    lay, ncon = _con_layout()
    if ctx is None:
        nc = _new_nc()
        xT = nc.dram_tensor("xT", [D_MODEL, T], F32, kind="ExternalInput").ap()
        w_loc = nc.dram_tensor("w", [D_MODEL, NCOL], F32, kind="ExternalInput").ap()
        prm = nc.dram_tensor("prm", [128, 64], F32, kind="ExternalInput").ap()
        con_d = nc.dram_tensor("con", [128, ncon], F32, kind="ExternalInput").ap()
        lruw_d = nc.dram_tensor("lruw", [128, 8 * 128], F32, kind="ExternalInput").ap()
        pos_d = nc.dram_tensor("pos", [1, T], I32, kind="ExternalInput").ap()
        mixT = nc.dram_tensor("mixT", [1024, T], F32, kind="ExternalOutput").ap()
        projT = nc.dram_tensor("projT", [NCOL, T], F32, kind="Internal").ap()
        mixA, mixB = mixT[0:512, :], mixT[512:1024, :]
        p = Prog(nc)
        es = contextlib.ExitStack()
        es.enter_context(nc.allow_low_precision("bf16 matmul operands, fp32 accumulation"))
    else:
        nc, p = ctx["nc"], ctx["p"]
        xT, w_loc, prm, con_d, lruw_d, pos_d, projT, mixA, mixB = (ctx[k] for k in ("xT", "w", "prm", "con", "lruw", "pos", "projT", "mixA", "mixB"))
    prm_t = p.sb([128, 64])
    p.dma(prm_t[:, :], prm[:, :], W=[prm_t])
    ones_b = p.sb([128, 128], BF16)
    p.memset(ones_b[:, :], 1.0, W=[ones_b])
    eps_t = p.sb([128, 1])
    p.memset(eps_t[:, :], EPS, W=[eps_t])
    one_t = p.sb([128, 1])
    p.memset(one_t[:, :], 1.0, W=[one_t])
    phase_A(p, xT, w_loc, NCOL, projT, prm_t, T, ones_b, eps_t)
    con = p.sb([128, ncon])
    p.dma(con[:, :], con_d[:, :], W=[con])

    ident = _cv(con, lay, "ident")
    ones_f = _cv(con, lay, "ones")
    PI = float(np.pi)
    C1 = 6.28125
    C2 = float(2 * np.pi - 6.28125)
    with p.scope():
        lruw = p.sb([128, 8, 128])
        p.dma(lruw[:, :, :], lruw_d.rearrange("p (n j) -> p n j", n=8), W=[lruw])
        sp_t = p.sb([128, 4])
        m8 = p.sb([128, 4])
        m16 = p.sb([128, 4])
        p.act(sp_t[:, :], prm_t[:, 48:52], AF.Exp, R=[prm_t], W=[sp_t], scale=-1.0)
        p.act(sp_t[:, :], sp_t[:, :], AF.Ln, R=[sp_t, one_t], W=[sp_t], bias=one_t[:, 0:1])
        p.ts(m8[:, :], sp_t[:, :], -8.0, None, ALU.mult, R=[sp_t], W=[m8])
        p.ts(m16[:, :], sp_t[:, :], -16.0, None, ALU.mult, R=[sp_t], W=[m16])
        hst = p.sb([128, 4])
        p.memset(hst[:, :], 0.0, W=[hst])
        S4 = p.sb([128, 4, 128])
        p.memset(S4[:, :, :], 0.0, W=[S4])

        two = lambda shape, dt=F32: [p.sb(shape, dt) for _ in range(2)]
        v4_, qp4_, kp4_, o4_ = (two([128, 4, 512]) for _ in range(4))
        g4s = p.sb([128, 4, 512])
        g4_ = [g4s, g4s]
        q4s = p.sb([128, 4, 512])
        k4s = p.sb([128, 4, 512])
        q4_, k4_ = [q4s, q4s], [k4s, k4s]
        t14 = p.sb([128, 4, 512])
        sq1 = p.sb([128, 512])
        posi = p.sb([1, 512], I32)
        posf = p.sb([1, 512])
        ang = p.sb([128, 512])
        kf = p.sb([128, 512])
        ki = p.sb([128, 512], I32)
        yy = p.sb([128, 512])
        mm_ = p.sb([128, 512])
        sinT = p.sb([128, 512])
        cosT = p.sb([128, 512])
        attnT = p.sb([128, 4, 128])
        qg = p.sb([128, 4, 128])
        kd = p.sb([128, 4, 128])
        Vt = p.sb([128, 4, 128])
        rs1 = p.sb([128, 512])
        rinv = p.sb([128, 512])
        xd_, zt_, xc_, rr_, ii_, aa_, a2_, hh_ = (two([128, 515] if i == 0 else [128, 512]) for i in range(8))
        pA = p.ps([128, 512])
        pB = p.ps([128, 512])
        pG = p.ps([128, 512])
        pK = p.ps([128, 512])
        pV = p.ps([128, 512])
        pO = p.ps([128, 512])
        pD = p.ps([128, 512])
        pN = p.ps([128, 512])
        dmT = _cv(con, lay, "dmT").rearrange("p (h c) -> p h c", h=4)
        qdec = _cv(con, lay, "qdec").rearrange("p (h c) -> p h c", h=4)
        kdec = _cv(con, lay, "kdec")
        glb = _cv(con, lay, "glb")
        rotm = _cv(con, lay, "rotm")
        invf = _cv(con, lay, "invf", 1)
        NB = T // 512

        def prep(blk):
            bsl = slice(blk * 512, (blk + 1) * 512)
            q4, k4, v4, g4, qp4, kp4 = (t[blk % 2] for t in (q4_, k4_, v4_, g4_, qp4_, kp4_))
            for (dst, r0) in ((q4, 0), (k4, 512), (v4, 1024)):
                p.dma(dst[:, :, :], projT[r0:r0 + 512, bsl].rearrange("(h p) t -> p h t", p=128), W=[dst])
            p.dma(posi[:, :], pos_d[:, bsl], W=[posi])
            p.cp(posf[:, :], posi[:, :], R=[posi], W=[posf])
            p.mm(pA[:, :], invf, posf[0:1, :], R=[con, posf], W=[pA])
            p.cp(ang[:, :], pA[:, :], R=[pA], W=[ang], eng="act")
            yield
            p.ts(kf[:, :], ang[:, :], float(1.0 / (2 * np.pi)), None, ALU.mult, R=[ang], W=[kf])
            p.cp(ki[:, :], kf[:, :], R=[kf], W=[ki])
            p.cp(kf[:, :], ki[:, :], R=[ki], W=[kf])
            p.stt(ang[:, :], kf[:, :], -C1, ang[:, :], ALU.mult, ALU.add, R=[kf, ang], W=[ang])
            p.stt(ang[:, :], kf[:, :], -C2, ang[:, :], ALU.mult, ALU.add, R=[kf, ang], W=[ang])
            yield
            for (dstT, shift) in ((sinT, 0.0), (cosT, PI / 2)):
                p.ts(yy[:, :], ang[:, :], shift, None, ALU.add, R=[ang], W=[yy])
                p.ts(mm_[:, :], yy[:, :], PI, 2 * PI, ALU.is_gt, ALU.mult, R=[yy], W=[mm_])
                p.tt(yy[:, :], yy[:, :], mm_[:, :], ALU.subtract, R=[yy, mm_], W=[yy])
                p.ts(mm_[:, :], yy[:, :], -PI, 2 * PI, ALU.is_lt, ALU.mult, R=[yy], W=[mm_])
                p.tt(yy[:, :], yy[:, :], mm_[:, :], ALU.add, R=[yy, mm_], W=[yy])
                p.ts(yy[:, :], yy[:, :], -PI, PI, ALU.max, ALU.min, R=[yy], W=[yy])
                p.act(dstT[:, :], yy[:, :], AF.Sin, R=[yy], W=[dstT])
                yield
            for (src4, dst4) in ((q4, qp4), (k4, kp4)):
                for h in range(4):
                    pr = pA if h % 2 == 0 else pB
                    p.mm(pr[:, :], rotm, src4[:, h, :], R=[con, src4], W=[pr], exact=True)
                    p.tt(dst4[:, h, :], pr[:, :], sinT[:, :], ALU.mult, R=[pr, sinT], W=[(dst4, h)])
                    p.tt(t14[:, h, :], src4[:, h, :], cosT[:, :], ALU.mult, R=[src4, cosT], W=[(t14, h)], eng="pool")
                    yield
                p.tt(dst4[:, :, :], dst4[:, :, :], t14[:, :, :], ALU.add, R=[dst4, t14], W=[dst4], eng="pool")
                yield

        def ret_chunk(blk, c4):
            q4, k4, v4, g4, qp4, kp4, o4 = (t[blk % 2] for t in (q4_, k4_, v4_, g4_, qp4_, kp4_, o4_))
            cs = slice(c4 * 128, (c4 + 1) * 128)
            for h in range(4):
                hs = slice(h * 128, (h + 1) * 128)
                p.mm(pG[:, hs], kp4[:, h, cs], qp4[:, h, cs], R=[kp4, qp4], W=[(pG, h)])
                p.tr(pK[:, hs], kp4[:, h, cs], ident, R=[kp4, con], W=[(pK, h)])
                p.tr(pV[:, hs], v4[:, h, cs], ident, R=[v4, con], W=[(pV, h)])
            yield
            p.tt(attnT[:, :, :], pG[:, :].rearrange("p (h c) -> p h c", h=4), dmT, ALU.mult, R=[pG, con], W=[attnT])
            p.tt(qg[:, :, :], qp4[:, :, cs], qdec, ALU.mult, R=[qp4, con], W=[qg], eng="pool")
            for h in range(4):
                hs = slice(h * 128, (h + 1) * 128)
                p.act(kd[:, h, :], pK[:, hs], AF.Copy, R=[(pK, h), con], W=[(kd, h)], scale=kdec[:, h:h + 1])
            p.cp(Vt[:, :, :], pV[:, :].rearrange("p (h c) -> p h c", h=4), R=[pV], W=[Vt], eng="act")
            yield
            for h in range(4):
                hs = slice(h * 128, (h + 1) * 128)
                p.mm(pO[:, hs], S4[:, h, :], qg[:, h, :], start=True, stop=False, R=[(S4, h), qg], W=[(pO, h)])
                p.mm(pO[:, hs], Vt[:, h, :], attnT[:, h, :], start=False, stop=True, R=[Vt, attnT], W=[(pO, h)])
                p.mm(pD[:, hs], kd[:, h, :], Vt[:, h, :], R=[(kd, h), Vt], W=[(pD, h)])
            yield
            p.cp(o4[:, :, cs], pO[:, :].rearrange("p (h c) -> p h c", h=4), R=[pO], W=[(o4, c4)], eng="act")
            for h in range(4):
                hs = slice(h * 128, (h + 1) * 128)
                p.stt(S4[:, h, :], S4[:, h, :], glb[:, h:h + 1], pD[:, hs], ALU.mult, ALU.add,
                      R=[(S4, h), (pD, h), con], W=[(S4, h)])

        def lru_tile(blk, ct):
            bsl = slice(blk * 512, (blk + 1) * 512)
            xd, zt, xc, rr, ii, aa, a2, hh = (t[ct % 2] for t in (xd_, zt_, xc_, rr_, ii_, aa_, a2_, hh_))
            r0 = 2048 + ct * 128
            if blk == 0:
                p.memset(xd[:, 0:3], 0.0, W=[xd])
                p.dma(xd[:, 3:515], projT[r0:r0 + 128, 0:512], W=[xd])
            else:
                p.dma(xd[:, :], projT[r0:r0 + 128, blk * 512 - 3:(blk + 1) * 512], W=[xd])
            p.dma(zt[:, :], projT[r0 + 512:r0 + 640, bsl], W=[zt])
            cw = 20 + ct * 4
            p.ts(xc[:, :], xd[:, 0:512], prm_t[:, cw:cw + 1], prm_t[:, 36 + ct:37 + ct], ALU.mult, ALU.add,
                 R=[xd, prm_t], W=[xc])
            for j in range(1, 4):
                p.stt(xc[:, :], xd[:, j:j + 512], prm_t[:, cw + j:cw + j + 1], xc[:, :], ALU.mult, ALU.add,
                      R=[xd, prm_t, xc], W=[xc])
            yield
            p.mm(pA[:, :], lruw[:, ct, :], xc[:, :], R=[lruw, xc], W=[pA])
            p.mm(pB[:, :], lruw[:, 4 + ct, :], xc[:, :], R=[lruw, xc], W=[pB])
            p.act(rr[:, :], pA[:, :], AF.Sigmoid, R=[pA, prm_t], W=[rr], bias=prm_t[:, 40 + ct:41 + ct])
            p.act(ii[:, :], pB[:, :], AF.Sigmoid, R=[pB, prm_t], W=[ii], bias=prm_t[:, 44 + ct:45 + ct])
            yield
            p.act(aa[:, :], rr[:, :], AF.Exp, R=[rr, m8], W=[aa], scale=m8[:, ct:ct + 1])
            p.act(a2[:, :], rr[:, :], AF.Exp, R=[rr, m16], W=[a2], scale=m16[:, ct:ct + 1])
            p.act(a2[:, :], a2[:, :], AF.Sqrt, R=[a2, one_t], W=[a2], bias=one_t[:, 0:1], scale=-1.0)
            p.tt(ii[:, :], ii[:, :], a2[:, :], ALU.mult, R=[ii, a2], W=[ii])
            p.tt(ii[:, :], ii[:, :], xc[:, :], ALU.mult, R=[ii, xc], W=[ii], eng="pool")
            yield
            p.op("dve", lambda e: e.tensor_tensor_scan(hh[:, :], aa[:, :], ii[:, :], hst[:, ct:ct + 1], ALU.mult, ALU.add),
                 R=[aa, ii, hst], W=[hh])
            p.cp(hst[:, ct:ct + 1], hh[:, 511:512], R=[hh], W=[hst])
            p.act(zt[:, :], zt[:, :], AF.Silu, R=[zt], W=[zt])
            p.tt(hh[:, :], hh[:, :], zt[:, :], ALU.mult, R=[hh, zt], W=[hh], eng="pool")
            p.dma(mixB[ct * 128:(ct + 1) * 128, bsl], hh[:, :], R=[hh])

        def post(blk):
            bsl = slice(blk * 512, (blk + 1) * 512)
            g4, o4 = g4_[blk % 2], o4_[blk % 2]
            p.dma(g4[:, :, :], projT[1536:2048, bsl].rearrange("(h p) t -> p h t", p=128), W=[g4])
            p.act(g4[:, :, :], g4[:, :, :], AF.Silu, R=[g4], W=[g4])
            yield
            for h in range(4):
                p.act(sq1[:, :], o4[:, h, :], AF.Square, R=[o4], W=[sq1])
                p.mm(pN[:, :], ones_f, sq1[:, :], R=[con, sq1], W=[pN])
                p.act(rs1[:, :], pN[:, :], AF.Sqrt, R=[pN, eps_t], W=[rs1], bias=eps_t[:, 0:1], scale=1.0 / 128)
                p.recip(rinv[:, :], rs1[:, :], R=[rs1], W=[rinv])
                p.stt(o4[:, h, :], o4[:, h, :], prm_t[:, 16 + h:17 + h], rinv[:, :], ALU.mult, ALU.mult,
                      R=[o4, prm_t, rinv], W=[(o4, ("n", h))])
                yield
            p.tt(o4[:, :, :], o4[:, :, :], g4[:, :, :], ALU.mult, R=[o4, g4], W=[o4], eng="pool")
            p.dma(mixA[:, bsl].rearrange("(h p) t -> p h t", p=128), o4[:, :, :], R=[o4])

        def chunks(blk):
            for c4 in range(4):
                gens = [ret_chunk(blk, c4), lru_tile(blk, c4)]
                while gens:
                    for g_ in list(gens):
                        try:
                            next(g_)
                        except StopIteration:
                            gens.remove(g_)
                        yield

        def drain(*gens):
            gens = [g_ for g_ in gens if g_ is not None]
            while gens:
                for g_ in list(gens):
                    try:
                        next(g_)
                    except StopIteration:
                        gens.remove(g_)

        drain(prep(0))
        for blk in range(NB):
            drain(chunks(blk), prep(blk + 1) if blk + 1 < NB else None, post(blk - 1) if blk >= 1 else None)
        drain(post(NB - 1))
    if ctx is not None:
        return None
    p.emit()
    es.close()
    return nc


def bcast(ap, n):
    sh = list(ap.shape)
    return ap.unsqueeze(len(sh)).broadcast_to(sh + [n])


def build_AB_even(T, ctx=None):
    NCOL = 3344
    lay, ncon = _con_layout()
    if ctx is None:
        nc = _new_nc()
        xT = nc.dram_tensor("xT", [D_MODEL, T], F32, kind="ExternalInput").ap()
        w_loc = nc.dram_tensor("w", [D_MODEL, NCOL], F32, kind="ExternalInput").ap()
        prm = nc.dram_tensor("prm", [128, 112], F32, kind="ExternalInput").ap()
        hp_d = nc.dram_tensor("hp", [8, 8], F32, kind="ExternalInput").ap()
        dbc_d = nc.dram_tensor("dbc", [128, 512], F32, kind="ExternalInput").ap()
        con_d = nc.dram_tensor("con", [128, ncon], F32, kind="ExternalInput").ap()
        mixT = nc.dram_tensor("mixT", [1024, T], F32, kind="ExternalOutput").ap()
        projT = nc.dram_tensor("projT", [NCOL, T], F32, kind="Internal").ap()
        mixA, mixB = mixT[0:512, :], mixT[512:1024, :]
        p = Prog(nc)
        es = contextlib.ExitStack()
        es.enter_context(nc.allow_low_precision("bf16 matmul operands, fp32 accumulation"))
    else:
        nc, p = ctx["nc"], ctx["p"]
        xT, w_loc, prm, hp_d, dbc_d, con_d, projT, mixA, mixB = (ctx[k] for k in ("xT", "w", "prm", "hp", "dbc", "con", "projT", "mixA", "mixB"))
    prm_t = p.sb([128, 112])
    p.dma(prm_t[:, :], prm[:, :], W=[prm_t])
    hp = p.sb([8, 8])
    p.dma(hp[:, :], hp_d[:, :], W=[hp])
    ones_b = p.sb([128, 128], BF16)
    p.memset(ones_b[:, :], 1.0, W=[ones_b])
    eps_t = p.sb([128, 1])
    p.memset(eps_t[:, :], EPS, W=[eps_t])
    one_t = p.sb([128, 1])
    p.memset(one_t[:, :], 1.0, W=[one_t])
    phase_A(p, xT, w_loc, NCOL, projT, prm_t, T, ones_b, eps_t)
    con = p.sb([128, ncon])
    p.dma(con[:, :], con_d[:, :], W=[con])

    ident = _cv(con, lay, "ident")
    ones_f = _cv(con, lay, "ones")
    maskT = _cv(con, lay, "maskT")
    maskS = _cv(con, lay, "maskS")
    selbc4 = _cv(con, lay, "selbc4", 4).rearrange("p (h c) -> p h c", h=4)
    selbc8 = _cv(con, lay, "selbc8", 8).rearrange("p (h c) -> p h c", h=8)
    selc = _cv(con, lay, "selc", 8).rearrange("p (q c) -> p q c", q=6)
    cmask = _cv(con, lay, "cmask", 8)
    SC = float(128 ** -0.5)

    def v4(ps_t):
        return ps_t[:, :].rearrange("p (h c) -> p h c", h=4)

    with p.scope():
        banks = [p.ps([128, 512]) for _ in range(6)]
        pBig = p.ps([128, 1024])
        bi = [0]

        def bank():
            b = banks[bi[0] % 6]
            bi[0] += 1
            return b

        dbc = p.sb([128, 512])
        p.dma(dbc[:, :], dbc_d[:, :], W=[dbc])
        negA = p.sb([8, 2])
        p.memset(negA[:, :], 0.0, W=[negA])
        p.act(negA[0:4, 0:1], hp[0:4, 0:1], AF.Exp, R=[hp], W=[negA])
        p.act(negA[0:8, 1:2], hp[0:8, 2:3], AF.Exp, R=[hp], W=[negA])
        p.ts(negA[:, :], negA[:, :], -1.0, None, ALU.mult, R=[negA], W=[negA])
        S4 = p.sb([128, 4, 128])
        p.memset(S4[:, :, :], 0.0, W=[S4])
        Hs = p.sb([128, 512])
        p.memset(Hs[:, :], 0.0, W=[Hs])

        qh = p.sb([128, 4, 515])
        kh = p.sb([128, 4, 515])
        vh = p.sb([128, 4, 515])
        z4 = p.sb([128, 4, 512])
        qc = p.sb([128, 4, 512])
        kc = p.sb([128, 4, 512])
        vc = p.sb([128, 4, 512])
        sq4 = p.sb([128, 4, 512])
        o4 = p.sb([128, 4, 512])
        rs1 = p.sb([128, 512])
        rinv = p.sb([128, 512])
        a_r = p.sb([8, 512])
        b_r = p.sb([8, 512])
        g_r = p.sb([8, 512])
        gc_r = p.sb([8, 512])
        be_r = p.sb([8, 512])
        bg_r = p.sb([8, 512])
        colt = p.sb([128, 48])
        ncol = p.sb([128, 8])
        gcbc = p.sb([128, 4, 128])
        egcbc = p.sb([128, 4, 128])
        T1 = p.sb([128, 4, 128])
        DmT = p.sb([128, 4, 128])
        Dm = p.sb([128, 4, 128])
        kdec = p.sb([128, 4])
        kd = p.sb([128, 4, 128])
        Vb = p.sb([128, 4, 128])
        attnT = p.sb([128, 4, 128])
        qg = p.sb([128, 4, 128])
        Pm = [p.sb([128, 4, 128]) for _ in range(2)]
        Ym = [p.sb([128, 4, 128]) for _ in range(2)]
        Am = p.sb([128, 4, 128])
        Xm = p.sb([128, 4, 128])
        Vn = p.sb([128, 4, 128])
        rs1s = [rs1, p.sb([128, 512])]
        rinvs = [rinv, p.sb([128, 512])]
        o4s = p.sb([128, 4, 512])
        a_rs = p.sb([8, 512])
        g_rs = p.sb([8, 512])
        gc_rs = p.sb([8, 512])
        be_rs = p.sb([8, 512])
        colts = p.sb([128, 48])
        xsh = p.sb([128, 4, 515])
        Bh = p.sb([128, 515])
        Ch = p.sb([128, 515])
        xs4 = p.sb([128, 4, 512])
        Bc = p.sb([128, 512])
        Cc = p.sb([128, 512])
        csb = p.sb([128, 8, 128])
        lmT = p.sb([128, 8, 128])
        ecs = p.sb([128, 8])
        dect = p.sb([128, 8])
        glb8 = p.sb([128, 8])
        xst = p.sb([128, 512])
        xct = p.sb([128, 512])
        xcd = p.sb([128, 512])
        yt = p.sb([128, 512])
        Bt = p.sb([128, 128])

        def conv_tile(dst, src, wcol, bcol):
            if bcol is None:
                p.ts(dst, src[0], prm_t[:, wcol:wcol + 1], None, ALU.mult, R=[src[4], prm_t], W=[src[5]])
            else:
                p.ts(dst, src[0], prm_t[:, wcol:wcol + 1], prm_t[:, bcol:bcol + 1], ALU.mult, ALU.add,
                     R=[src[4], prm_t], W=[src[5]])
            for j in range(1, 4):
                p.stt(dst, src[j], prm_t[:, wcol + j:wcol + j + 1], dst, ALU.mult, ALU.add, R=[src[4], prm_t, src[5]], W=[src[5]])
            p.act(dst, dst, AF.Silu, R=[src[5]], W=[src[5]])

        def load_halo(dst, r0, nrow, blk, multi):
            if multi:
                d0 = dst[:, :, 0:3]
                dr = dst[:, :, 3:515]
                da = dst[:, :, :]
                src = lambda c0, c1: projT[r0:r0 + nrow, c0:c1].rearrange("(k p) t -> p k t", p=128)
            else:
                d0 = dst[:, 0:3]
                dr = dst[:, 3:515]
                da = dst[:, :]
                src = lambda c0, c1: projT[r0:r0 + nrow, c0:c1]
            if blk == 0:
                p.memset(d0, 0.0, W=[dst])
                p.dma(dr, src(0, 512), W=[dst])
            else:
                p.dma(da, src(blk * 512 - 3, (blk + 1) * 512), W=[dst])

        def conv_tiles(specs):
            for j in range(4):
                for (dst, srcs, stok, dtok, wcol, bcol) in specs:
                    if j == 0:
                        if bcol is None:
                            p.ts(dst, srcs[0], prm_t[:, wcol:wcol + 1], None, ALU.mult, R=[stok, prm_t], W=[dtok])
                        else:
                            p.ts(dst, srcs[0], prm_t[:, wcol:wcol + 1], prm_t[:, bcol:bcol + 1], ALU.mult, ALU.add,
                                 R=[stok, prm_t], W=[dtok])
                    else:
                        p.stt(dst, srcs[j], prm_t[:, wcol + j:wcol + j + 1], dst, ALU.mult, ALU.add, R=[stok, prm_t, dtok], W=[dtok])
            for (dst, srcs, stok, dtok, wcol, bcol) in specs:
                p.act(dst, dst, AF.Silu, R=[dtok], W=[dtok])

        def gdn_chunk(c4):
                    cs = slice(c4 * 128, (c4 + 1) * 128)
                    b = bank()
                    for qi, rows in enumerate((gc_r, be_r, bg_r)):
                        p.mm(b[:, 0:48], rows[0:4, cs], selc[0:4, qi, :], start=(qi == 0), stop=(qi == 2), R=[rows, con], W=[b])
                    p.cp(colt[:, :], b[:, 0:48], R=[b], W=[colt], eng="act")
                    p.ts(ncol[:, 0:4], colt[:, 16:20], -1.0, None, ALU.mult, R=[colt], W=[ncol])
                    b = bank()
                    for h in range(4):
                        p.mm(b[:, h * 128:(h + 1) * 128], selbc4[:, h, :], gc_r[0:4, cs], R=[con, gc_r], W=[b])
                    p.cp(gcbc[:, :, :], v4(b), R=[b], W=[gcbc], eng="act")
                    p.act(egcbc[:, :, :], v4(b), AF.Exp, R=[b], W=[egcbc])
                    for h in range(4):
                        p.stt(T1[:, h, :], gcbc[:, h, :], colt[:, h:h + 1], maskT, ALU.subtract, ALU.add, R=[gcbc, colt, con], W=[(T1, h)])
                    p.act(DmT[:, :, :], T1[:, :, :], AF.Exp, R=[T1], W=[DmT])
                    for h in range(4):
                        p.stt(T1[:, h, :], gcbc[:, h, :], colt[:, h:h + 1], maskS, ALU.subtract, ALU.subtract, R=[gcbc, colt, con], W=[(T1, h)])
                    p.act(Dm[:, :, :], T1[:, :, :], AF.Exp, R=[T1], W=[Dm], scale=-1.0)
                    p.tt(kdec[:, :], gcbc[:, :, 127], colt[:, 0:4], ALU.subtract, R=[gcbc, colt], W=[kdec])
                    p.act(kdec[:, :], kdec[:, :], AF.Exp, R=[kdec], W=[kdec])
                    yield
                    bK = bank()
                    bV = bank()
                    bG = bank()
                    bL = bank()
                    for h in range(4):
                        hs = slice(h * 128, (h + 1) * 128)
                        p.tr(bK[:, hs], kc[:, h, cs], ident, R=[kc, con], W=[bK])
                        p.tr(bV[:, hs], vc[:, h, cs], ident, R=[vc, con], W=[bV])
                        p.mm(bG[:, hs], kc[:, h, cs], qc[:, h, cs], R=[kc, qc], W=[bG])
                        p.mm(bL[:, hs], kc[:, h, cs], kc[:, h, cs], R=[kc], W=[bL])
                    for h in range(4):
                        hs = slice(h * 128, (h + 1) * 128)
                        p.act(kd[:, h, :], bK[:, hs], AF.Copy, R=[bK, kdec], W=[(kd, h)], scale=kdec[:, h:h + 1])
                        p.act(Vb[:, h, :], bV[:, hs], AF.Copy, R=[bV, colt], W=[(Vb, h)], scale=colt[:, 8 + h:9 + h])
                        p.stt(Pm[0][:, h, :], bL[:, hs], colt[:, 8 + h:9 + h], Dm[:, h, :], ALU.mult, ALU.mult,
                              R=[bL, colt, Dm], W=[(Pm[0], h)])
                    p.tt(attnT[:, :, :], v4(bG), DmT[:, :, :], ALU.mult, R=[bG, DmT], W=[attnT])
                    p.tt(qg[:, :, :], qc[:, :, cs], egcbc[:, :, :], ALU.mult, R=[qc, egcbc], W=[qg], eng="pool")
                    yield
                    bY = bank()
                    for h in range(4):
                        p.tr(bY[:, h * 128:(h + 1) * 128], Pm[0][:, h, :], ident, R=[Pm[0], con], W=[bY])
                    p.cp(Ym[0][:, :, :], v4(bY), R=[bY], W=[Ym[0]], eng="act")
                    for h in range(4):
                        p.stt(Am[:, h, :], bY[:, h * 128:(h + 1) * 128], -1.0, ident, ALU.mult, ALU.add, R=[con, bY], W=[(Am, h)])
                    yield
                    cur = 0
                    for lev in range(6):
                        nxt = 1 - cur
                        bP = bank()
                        for h in range(4):
                            p.mm(bP[:, h * 128:(h + 1) * 128], Ym[cur][:, h, :], Pm[cur][:, h, :], R=[Ym[cur], Pm[cur]], W=[bP])
                        p.cp(Pm[nxt][:, :, :], v4(bP), R=[bP], W=[Pm[nxt]], eng="act")
                        if lev < 5:
                            bY2 = bank()
                            for h in range(4):
                                p.mm(bY2[:, h * 128:(h + 1) * 128], Pm[cur][:, h, :], Ym[cur][:, h, :], R=[Ym[cur], Pm[cur]], W=[bY2])
                            p.cp(Ym[nxt][:, :, :], v4(bY2), R=[bY2], W=[Ym[nxt]])
                        yield
                        bU = bank()
                        for h in range(4):
                            p.mm(bU[:, h * 128:(h + 1) * 128], Pm[nxt][:, h, :], Am[:, h, :], R=[Pm[nxt], Am], W=[bU])
                        p.tt(Am[:, :, :], v4(bU), Am[:, :, :], ALU.add, R=[Am, bU], W=[Am])
                        yield
                        cur = nxt
                    yield
                    bKS = bank()
                    for h in range(4):
                        p.mm(bKS[:, h * 128:(h + 1) * 128], kc[:, h, cs], S4[:, h, :], R=[kc, (S4, h)], W=[bKS])
                    for h in range(4):
                        p.stt(Xm[:, h, :], bKS[:, h * 128:(h + 1) * 128], ncol[:, h:h + 1], Vb[:, h, :], ALU.mult, ALU.add,
                              R=[bKS, ncol, Vb], W=[(Xm, h)])
                    yield
                    bVn = bank()
                    for h in range(4):
                        p.mm(bVn[:, h * 128:(h + 1) * 128], Am[:, h, :], Xm[:, h, :], R=[Am, Xm], W=[bVn])
                    p.cp(Vn[:, :, :], v4(bVn), R=[bVn], W=[Vn], eng="act")
                    yield
                    bO = bank()
                    bD = bank()
                    for h in range(4):
                        hs = slice(h * 128, (h + 1) * 128)
                        p.mm(bO[:, hs], S4[:, h, :], qg[:, h, :], start=True, stop=False, R=[(S4, h), qg], W=[bO])
                        p.mm(bO[:, hs], Vn[:, h, :], attnT[:, h, :], start=False, stop=True, R=[Vn, attnT], W=[bO])
                        p.mm(bD[:, hs], kd[:, h, :], Vn[:, h, :], R=[kd, Vn], W=[bD])
                    p.cp(o4[:, :, cs], v4(bO), R=[bO], W=[(o4, c4)], eng="act")
                    for h in range(4):
                        p.stt(S4[:, h, :], S4[:, h, :], egcbc[:, h, 127:128], bD[:, h * 128:(h + 1) * 128], ALU.mult, ALU.add,
                              R=[(S4, h), bD, egcbc], W=[(S4, h)])

        def ssd_chunk(c4):
                    cs = slice(c4 * 128, (c4 + 1) * 128)
                    b = bank()
                    p.mm(b[:, 0:48], be_rs[0:8, cs], selc[0:8, 3, :], start=True, stop=False, R=[be_rs, con], W=[b])
                    p.mm(b[:, 0:48], gc_rs[0:8, cs], selc[0:8, 4, :], start=False, stop=True, R=[gc_rs, con], W=[b])
                    p.cp(colts[:, :], b[:, 0:48], R=[b], W=[colts], eng="act")
                    p.act(ecs[:, :], colts[:, 32:40], AF.Exp, R=[colts], W=[ecs])
                    for h in range(8):
                        p.mm(pBig[:, h * 128:(h + 1) * 128], selbc8[:, h, :], gc_rs[0:8, cs], R=[con, gc_rs], W=[pBig])
                    p.cp(csb[:, :, :], pBig[:, :].rearrange("p (h c) -> p h c", h=8), R=[pBig], W=[csb], eng="act")
                    p.tt(dect[:, :], csb[:, :, 127], colts[:, 32:40], ALU.subtract, R=[csb, colts], W=[dect])
                    p.act(dect[:, :], dect[:, :], AF.Exp, R=[dect], W=[dect])
                    p.act(glb8[:, :], csb[:, :, 127], AF.Exp, R=[csb], W=[glb8])
                    yield
                    for h in range(8):
                        p.stt(lmT[:, h, :], csb[:, h, :], colts[:, 32 + h:33 + h], maskT, ALU.subtract, ALU.add,
                              R=[csb, colts, con], W=[(lmT, h)])
                    p.act(lmT[:, :, :], lmT[:, :, :], AF.Exp, R=[lmT], W=[lmT])
                    yield
                    bS = bank()
                    p.mm(bS[:, 0:128], Bc[:, cs], Cc[:, cs], R=[Bc, Cc], W=[bS])
                    for h in range(8):
                        p.tt(lmT[:, h, :], bS[:, 0:128], lmT[:, h, :], ALU.mult, R=[lmT, bS], W=[(lmT, h)])
                    yield
                    bX = bank()
                    for k in range(4):
                        p.tr(bX[:, k * 128:(k + 1) * 128], xs4[:, k, cs], ident, R=[xs4, con], W=[bX])
                    p.cp(xst[:, :], bX[:, :], R=[bX], W=[xst], eng="act")
                    p.tt(xct[:, :].rearrange("p (h q) -> p h q", h=8), bX[:, :].rearrange("p (h q) -> p h q", h=8),
                         bcast(colts[:, 24:32], 64), ALU.mult, R=[bX, colts], W=[xct])
                    yield
                    bYd = bank()
                    for h in range(8):
                        p.mm(bYd[:, h * 64:(h + 1) * 64], lmT[:, h, :], xct[:, h * 64:(h + 1) * 64], R=[lmT, xct], W=[bYd])
                    bYo = bank()
                    p.mm(bYo[:, :], Cc[:, cs], Hs[:, :], R=[Cc, Hs], W=[bYo])
                    p.tt(yt[:, :].rearrange("p (h q) -> p h q", h=8), bYo[:, :].rearrange("p (h q) -> p h q", h=8),
                         bcast(ecs[:, :], 64), ALU.mult, R=[bYo, ecs], W=[yt])
                    p.tt(yt[:, :], bYd[:, :], yt[:, :], ALU.add, R=[yt, bYd], W=[yt])
                    p.tt(xst[:, :], xst[:, :], dbc[:, :], ALU.mult, R=[xst, dbc], W=[xst], eng="pool")
                    p.tt(yt[:, :], yt[:, :], xst[:, :], ALU.add, R=[yt, xst], W=[yt])
                    yield
                    p.tt(xcd[:, :].rearrange("p (h q) -> p h q", h=8), xct[:, :].rearrange("p (h q) -> p h q", h=8),
                         bcast(dect[:, :], 64), ALU.mult, R=[xct, dect], W=[xcd])
                    bB = bank()
                    p.tr(bB[:, 0:128], Bc[:, cs], ident, R=[Bc, con], W=[bB])
                    p.cp(Bt[:, :], bB[:, 0:128], R=[bB], W=[Bt], eng="act")
                    bH = bank()
                    p.mm(bH[:, :], Bt[:, :], xcd[:, :], R=[Bt, xcd], W=[bH])
                    p.tt(Hs[:, :].rearrange("p (h q) -> p h q", h=8), Hs[:, :].rearrange("p (h q) -> p h q", h=8),
                         bcast(glb8[:, :], 64), ALU.mult, R=[Hs, glb8], W=[Hs])
                    p.tt(Hs[:, :], bH[:, :], Hs[:, :], ALU.add, R=[Hs, bH], W=[Hs])
                    yield
                    bT = bank()
                    for k in range(4):
                        p.tr(bT[:, k * 128:(k + 1) * 128], yt[:, k * 128:(k + 1) * 128], ident, R=[yt, con], W=[bT])
                    p.cp(o4s[:, :, cs], v4(bT), R=[bT], W=[(o4s, c4)], eng="act")

        def loads(blk):
            bsl = slice(blk * 512, (blk + 1) * 512)
            load_halo(qh, 0, 512, blk, True)
            load_halo(kh, 512, 512, blk, True)
            load_halo(vh, 1024, 512, blk, True)
            p.dma(a_r[0:4, :], projT[3328:3332, bsl], W=[a_r])
            p.dma(b_r[0:4, :], projT[3332:3336, bsl], W=[b_r])
            load_halo(xsh, 2048, 512, blk, True)
            load_halo(Bh, 2560, 128, blk, False)
            load_halo(Ch, 2688, 128, blk, False)
            p.dma(a_rs[0:8, :], projT[3336:3344, bsl], W=[a_rs])

        for blk in range(T // 512):
            bsl = slice(blk * 512, (blk + 1) * 512)
            if blk == 0:
                loads(0)
            p.dma(z4[:, :, :], projT[1536:2048, bsl].rearrange("(k p) t -> p k t", p=128), W=[z4])
            specs = []
            for sec, (hsrc, dstc) in enumerate(((qh, qc), (kh, kc), (vh, vc))):
                for h in range(4):
                    specs.append((dstc[:, h, :], [hsrc[:, h, j:j + 512] for j in range(4)], hsrc, (dstc, h), 16 + (sec * 4 + h) * 4, None))
            for k in range(4):
                specs.append((xs4[:, k, :], [xsh[:, k, j:j + 512] for j in range(4)], xsh, (xs4, k), 64 + k * 4, 88 + k))
            specs.append((Bc[:, :], [Bh[:, j:j + 512] for j in range(4)], Bh, Bc, 64 + 16, 88 + 4))
            specs.append((Cc[:, :], [Ch[:, j:j + 512] for j in range(4)], Ch, Cc, 64 + 20, 88 + 5))
            conv_tiles(specs)
            ri = 0
            for (cc, scale) in ((qc, SC), (kc, None)):
                p.act(sq4[:, :, :], cc[:, :, :], AF.Square, R=[cc], W=[sq4])
                for h in range(4):
                    b = bank()
                    r1, r2 = rs1s[ri % 2], rinvs[ri % 2]
                    ri += 1
                    p.mm(b[:, :], ones_f, sq4[:, h, :], R=[con, sq4], W=[b])
                    p.act(r1[:, :], b[:, :], AF.Sqrt, R=[b, eps_t], W=[r1], bias=eps_t[:, 0:1], scale=1.0)
                    p.recip(r2[:, :], r1[:, :], R=[r1], W=[r2])
                    if scale is None:
                        p.tt(cc[:, h, :], cc[:, h, :], r2[:, :], ALU.mult, R=[cc, r2], W=[(cc, h)])
                    else:
                        p.stt(cc[:, h, :], cc[:, h, :], scale, r2[:, :], ALU.mult, ALU.mult, R=[cc, r2], W=[(cc, h)])
            p.act(g_r[0:4, :], a_r[0:4, :], AF.Exp, R=[a_r, hp], W=[g_r], bias=hp[0:4, 1:2])
            p.act(g_r[0:4, :], g_r[0:4, :], AF.Ln, R=[g_r, one_t], W=[g_r], bias=one_t[0:4, 0:1])
            p.ts(g_r[0:4, :], g_r[0:4, :], negA[0:4, 0:1], None, ALU.mult, R=[g_r, negA], W=[g_r])
            p.op("dve", lambda e: e.tensor_tensor_scan(gc_r[0:4, :], cmask[0:4, :], g_r[0:4, :], 0.0, ALU.mult, ALU.add),
                 R=[con, g_r], W=[gc_r])
            p.act(be_r[0:4, :], b_r[0:4, :], AF.Sigmoid, R=[b_r], W=[be_r])
            p.act(bg_r[0:4, :], gc_r[0:4, :], AF.Exp, R=[gc_r], W=[bg_r])
            p.tt(bg_r[0:4, :], bg_r[0:4, :], be_r[0:4, :], ALU.mult, R=[bg_r, be_r], W=[bg_r])
            p.act(be_rs[0:8, :], a_rs[0:8, :], AF.Exp, R=[a_rs, hp], W=[be_rs], bias=hp[0:8, 3:4])
            p.act(be_rs[0:8, :], be_rs[0:8, :], AF.Ln, R=[be_rs, one_t], W=[be_rs], bias=one_t[0:8, 0:1])
            p.ts(g_rs[0:8, :], be_rs[0:8, :], negA[0:8, 1:2], None, ALU.mult, R=[be_rs, negA], W=[g_rs])
            p.op("dve", lambda e: e.tensor_tensor_scan(gc_rs[0:8, :], cmask[0:8, :], g_rs[0:8, :], 0.0, ALU.mult, ALU.add),
                 R=[con, g_rs], W=[gc_rs])
            if blk + 1 < T // 512:
                loads(blk + 1)
            for c4 in range(4):
                gg, sg = gdn_chunk(c4), ssd_chunk(c4)
                gi = 0
                g_done = s_done = False
                while not (g_done and s_done):
                    if not g_done:
                        try:
                            next(gg)
                        except StopIteration:
                            g_done = True
                    gi += 1
                    if not s_done and (g_done or gi % 2 == 0):
                        try:
                            next(sg)
                        except StopIteration:
                            s_done = True
            p.act(sq4[:, :, :], o4[:, :, :], AF.Square, R=[o4], W=[sq4])
            p.act(z4[:, :, :], z4[:, :, :], AF.Silu, R=[z4], W=[z4])
            for h in range(4):
                b = bank()
                p.mm(b[:, :], ones_f, sq4[:, h, :], R=[con, sq4], W=[b])
                p.act(rs1[:, :], b[:, :], AF.Sqrt, R=[b, eps_t], W=[rs1], bias=eps_t[:, 0:1], scale=1.0 / 128)
                p.recip(rinv[:, :], rs1[:, :], R=[rs1], W=[rinv])
                p.stt(o4[:, h, :], o4[:, h, :], prm_t[:, 94:95], rinv[:, :], ALU.mult, ALU.mult,
                      R=[o4, prm_t, rinv], W=[(o4, ("n", h))])
            p.tt(o4[:, :, :], o4[:, :, :], z4[:, :, :], ALU.mult, R=[o4, z4], W=[o4], eng="pool")
            p.dma(mixA[:, bsl].rearrange("(h p) t -> p h t", p=128), o4[:, :, :], R=[o4])

            p.dma(z4[:, :, :], projT[2816:3328, bsl].rearrange("(k p) t -> p k t", p=128), W=[z4])
            p.act(z4[:, :, :], z4[:, :, :], AF.Silu, R=[z4], W=[z4])
            p.tt(o4s[:, :, :], o4s[:, :, :], z4[:, :, :], ALU.mult, R=[o4s, z4], W=[o4s], eng="pool")
            p.act(sq4[:, :, :], o4s[:, :, :], AF.Square, R=[o4s], W=[sq4])
            b = bank()
            for k in range(4):
                p.mm(b[:, :], ones_f, sq4[:, k, :], start=(k == 0), stop=(k == 3), R=[con, sq4], W=[b])
            p.act(rs1[:, :], b[:, :], AF.Sqrt, R=[b, eps_t], W=[rs1], bias=eps_t[:, 0:1], scale=1.0 / 512)
            p.recip(rinv[:, :], rs1[:, :], R=[rs1], W=[rinv])
            for k in range(4):
                p.stt(o4s[:, k, :], o4s[:, k, :], prm_t[:, 96 + k:97 + k], rinv[:, :], ALU.mult, ALU.mult,
                      R=[o4s, prm_t, rinv], W=[(o4s, ("n", k))])
            p.dma(mixB[:, bsl].rearrange("(h p) t -> p h t", p=128), o4s[:, :, :], R=[o4s])
    if ctx is not None:
        return None
    p.emit()
    es.close()
    return nc


def pack_even(g, xT, norm_g, w_in, gdn_conv_w, gdn_A_log, gdn_dt_bias, gdn_norm_g,
              ssd_conv_w, ssd_conv_b, ssd_A_log, ssd_dt_bias, ssd_D, ssd_norm_g):
    hq = np.arange(512 * g, 512 * g + 512)
    h4 = np.arange(4 * g, 4 * g + 4)
    h8 = np.arange(8 * g, 8 * g + 8)
    o_za, o_a, o_b, o_xbc = 3072, 4096, 4104, 4112
    o_zb, o_dt = o_xbc + 1536, o_xbc + 1536 + 1024
    cols = np.concatenate([hq, 1024 + hq, 2048 + hq, o_za + hq, o_xbc + hq,
                           o_xbc + 1024 + 128 * g + np.arange(128), o_xbc + 1280 + 128 * g + np.arange(128),
                           o_zb + hq, o_a + h4, o_b + h4, o_dt + h8])
    prm = np.zeros((128, 112), np.float32)
    prm[:, 0:16] = _pk(norm_g, 16)
    gw = np.asarray(gdn_conv_w, np.float32)
    for sec in range(3):
        for h in range(4):
            ch = sec * 1024 + 512 * g + h * 128
            for j in range(4):
                prm[:, 16 + (sec * 4 + h) * 4 + j] = gw[j, ch:ch + 128]
    sw = np.asarray(ssd_conv_w, np.float32)
    sb_ = np.asarray(ssd_conv_b, np.float32)
    starts = [512 * g + k * 128 for k in range(4)] + [1024 + 128 * g, 1280 + 128 * g]
    for i, c0 in enumerate(starts):
        for j in range(4):
            prm[:, 64 + i * 4 + j] = sw[j, c0:c0 + 128]
        prm[:, 88 + i] = sb_[c0:c0 + 128]
    prm[:, 94] = np.asarray(gdn_norm_g, np.float32)
    prm[:, 96:100] = _pk(np.asarray(ssd_norm_g)[hq], 4)
    hp = np.zeros((8, 8), np.float32)
    hp[0:4, 0] = np.asarray(gdn_A_log)[h4]
    hp[0:4, 1] = np.asarray(gdn_dt_bias)[h4]
    hp[0:8, 2] = np.asarray(ssd_A_log)[h8]
    hp[0:8, 3] = np.asarray(ssd_dt_bias)[h8]
    dbc = np.ascontiguousarray(np.broadcast_to(np.repeat(np.asarray(ssd_D, np.float32)[h8], 64)[None, :], (128, 512)))
    return dict(xT=np.ascontiguousarray(xT, dtype=np.float32), w=np.ascontiguousarray(np.asarray(w_in, np.float32)[:, cols]),
                prm=prm, hp=hp, dbc=dbc, con=make_consts(g))


def _pk(v, n):
    return np.ascontiguousarray(np.asarray(v, np.float32).reshape(n, 128).T)


def pack_odd(g, xT, norm_g, w_in, ret_ng, conv_w, conv_b, w_a, b_a, w_x, b_x, lam, pos):
    hq = slice(512 * g, 512 * g + 512)
    cols = np.concatenate([np.arange(0, 1024)[hq], 1024 + np.arange(1024)[hq], 2048 + np.arange(1024)[hq],
                           3072 + np.arange(1024)[hq], 4096 + np.arange(1024)[hq], 5120 + np.arange(1024)[hq]])
    prm = np.zeros((128, 64), np.float32)
    prm[:, 0:16] = _pk(norm_g, 16)
    prm[:, 16:20] = _pk(ret_ng[hq], 4)
    cw = np.asarray(conv_w, np.float32)[:, hq]
    for ct in range(4):
        for j in range(4):
            prm[:, 20 + ct * 4 + j] = cw[j, ct * 128:(ct + 1) * 128]
    prm[:, 36:40] = _pk(np.asarray(conv_b)[hq], 4)
    prm[:, 40:44] = _pk(np.asarray(b_a)[hq], 4)
    prm[:, 44:48] = _pk(np.asarray(b_x)[hq], 4)
    prm[:, 48:52] = _pk(np.asarray(lam)[hq], 4)
    lw = np.concatenate([np.asarray(w_a, np.float32)[4 * g:4 * g + 4], np.asarray(w_x, np.float32)[4 * g:4 * g + 4]], 0)
    lruw = np.ascontiguousarray(lw.transpose(1, 0, 2).reshape(128, 8 * 128))
    return dict(xT=np.ascontiguousarray(xT, dtype=np.float32), w=np.ascontiguousarray(np.asarray(w_in, np.float32)[:, cols]),
                prm=prm, con=make_consts(g), lruw=lruw, pos=np.ascontiguousarray(np.asarray(pos, np.int32).reshape(1, -1)))


def unpack_mix(parts):
    return np.concatenate([parts[0][0:512], parts[1][0:512], parts[0][512:1024], parts[1][512:1024]], 0)


def build_fused(T, depth=4):
    nc = _new_nc()
    lay, ncon = _con_layout()
    dt = lambda name, shape, dty=F32, kind="ExternalInput": nc.dram_tensor(name, list(shape), dty, kind=kind).ap()
    xT = dt("xT", [D_MODEL, T])
    pT = dt("pT", [depth * 256, T])
    pos = dt("pos", [1, T], I32)
    con = [dt(f"con{g}", [128, ncon]) for g in range(2)]
    yT = dt("yT", [D_MODEL, T], kind="ExternalOutput")
    projT = dt("projT", [3344, T], kind="Internal")
    mixF = dt("mixF", [D_MODEL, T], kind="Internal")
    xs = [dt("xA", [D_MODEL, T], kind="Internal"), dt("xB", [D_MODEL, T], kind="Internal")]
    p = Prog(nc)
    es = contextlib.ExitStack()
    es.enter_context(nc.allow_low_precision("bf16 matmul operands, fp32 accumulation"))
    x_cur = xT
    for i in range(depth):
        even = (i % 2 == 0)
        for g in range(2):
            ctx = dict(nc=nc, p=p, xT=x_cur, con=con[g], projT=projT[0:(3344 if even else 3072), :],
                       mixA=mixF[512 * g:512 * g + 512, :], mixB=mixF[1024 + 512 * g:1536 + 512 * g, :],
                       w=dt(f"w{i}_{g}", [D_MODEL, 3344 if even else 3072]),
                       prm=dt(f"prm{i}_{g}", [128, 112 if even else 64]))
            if even:
                ctx["hp"] = dt(f"hp{i}_{g}", [8, 8])
                ctx["dbc"] = dt(f"dbc{i}_{g}", [128, 512])
            else:
                ctx["lruw"] = dt(f"lruw{i}_{g}", [128, 8 * 128])
                ctx["pos"] = pos
            with p.scope():
                (build_AB_even if even else build_AB_odd)(T, ctx=ctx)
        final = (i == depth - 1)
        x_nxt = yT if final else xs[i % 2]
        ctx = dict(nc=nc, p=p, mixT=mixF, xT=x_cur, pT=pT[i * 256:(i + 1) * 256, :], wo=dt(f"wo{i}", [D_MODEL, D_MODEL]),
                   wg=dt(f"wg{i}", [D_MODEL, D_MODEL]), wp=dt(f"wp{i}", [256, D_MODEL]), prm=dt(f"prmC{i}", [128, 32]), yT=x_nxt)
        with p.scope():
            build_C(T, final, ctx=ctx)
        x_cur = x_nxt
    p.emit()
    es.close()
    return nc


_CACHE = {}


def _prog(name, fn):
    if name not in _CACHE:
        _CACHE[name] = fn()
    return _CACHE[name]


def kernel_unfused(x, p, positions, norm_g, ple_norm_g, w_ple_gate, w_ple_proj,
           ev_w_in, ev_w_out, gdn_conv_w, gdn_A_log, gdn_dt_bias, gdn_norm_g,
           ssd_conv_w, ssd_conv_b, ssd_A_log, ssd_dt_bias, ssd_D, ssd_norm_g,
           od_w_in, od_w_out, ret_norm_g, lru_conv_w, lru_conv_b,
           lru_w_a, lru_b_a, lru_w_x, lru_b_x, lru_lambda, final_norm_g):
    f = lambda a: np.asarray(a, dtype=np.float32)
    x = f(x)
    B, S, D = x.shape
    H = S // 2
    cores = list(range(8))
    xT = [np.ascontiguousarray(x[b].T) for b in range(B)]
    depth = int(np.asarray(norm_g).shape[0])
    for i in range(depth):
        j = i // 2
        if i % 2 == 0:
            nc = _prog("even", lambda: build_AB_even(S))
            ins = [pack_even(c % 2, xT[c // 2], f(norm_g)[i], f(ev_w_in)[j], f(gdn_conv_w)[j], f(gdn_A_log)[j],
                             f(gdn_dt_bias)[j], f(gdn_norm_g)[j], f(ssd_conv_w)[j], f(ssd_conv_b)[j], f(ssd_A_log)[j],
                             f(ssd_dt_bias)[j], f(ssd_D)[j], f(ssd_norm_g)[j]) for c in cores]
            w_out = f(ev_w_out)[j]
        else:
            nc = _prog("odd", lambda: build_AB_odd(S))
            ins = [pack_odd(c % 2, xT[c // 2], f(norm_g)[i], f(od_w_in)[j], f(ret_norm_g)[j], f(lru_conv_w)[j],
                            f(lru_conv_b)[j], f(lru_w_a)[j], f(lru_b_a)[j], f(lru_w_x)[j], f(lru_b_x)[j],
                            f(lru_lambda)[j], np.asarray(positions)[c // 2]) for c in cores]
            w_out = f(od_w_out)[j]
        res = run_bass_kernel_spmd(nc, ins, core_ids=cores)
        mix = [unpack_mix([res.results[2 * b]["mixT"], res.results[2 * b + 1]["mixT"]]) for b in range(B)]
        del res, ins
        final = (i == depth - 1)
        ncC = _prog("Cf" if final else "C", lambda: build_C(H, final))
        prm = np.zeros((128, 32), np.float32)
        prm[:, 0:16] = _pk(f(ple_norm_g)[i], 16)
        prm[:, 16:32] = _pk(f(final_norm_g), 16)
        wg = np.ascontiguousarray(f(w_ple_gate)[i])
        wp = np.ascontiguousarray(f(w_ple_proj)[i])
        wo = np.ascontiguousarray(w_out)
        insC = []
        for c in cores:
            b, g = c // 2, c % 2
            sl = slice(g * H, (g + 1) * H)
            insC.append(dict(mixT=np.ascontiguousarray(mix[b][:, sl]), xT=np.ascontiguousarray(xT[b][:, sl]),
                             pT=np.ascontiguousarray(f(p)[i, b, sl, :].T), wo=wo, wg=wg, wp=wp, prm=prm))
        res = run_bass_kernel_spmd(ncC, insC, core_ids=cores)
        xT = [np.concatenate([res.results[2 * b]["yT"], res.results[2 * b + 1]["yT"]], axis=1) for b in range(B)]
        del res, insC, mix
    return np.ascontiguousarray(np.stack([t.T for t in xT], 0)).astype(np.float32)


def kernel_fused(x, p, positions, norm_g, ple_norm_g, w_ple_gate, w_ple_proj,
           ev_w_in, ev_w_out, gdn_conv_w, gdn_A_log, gdn_dt_bias, gdn_norm_g,
           ssd_conv_w, ssd_conv_b, ssd_A_log, ssd_dt_bias, ssd_D, ssd_norm_g,
           od_w_in, od_w_out, ret_norm_g, lru_conv_w, lru_conv_b,
           lru_w_a, lru_b_a, lru_w_x, lru_b_x, lru_lambda, final_norm_g):
    f = lambda a: np.asarray(a, dtype=np.float32)
    x = f(x)
    B, S, D = x.shape
    depth = int(np.asarray(norm_g).shape[0])
    nc = _prog("fused", lambda: build_fused(S, depth))
    shared = {}
    for g in range(2):
        shared[f"con{g}"] = make_consts(g)
    dummy = np.zeros((D, 8), np.float32)
    for i in range(depth):
        j = i // 2
        for g in range(2):
            if i % 2 == 0:
                d = pack_even(g, dummy, f(norm_g)[i], f(ev_w_in)[j], f(gdn_conv_w)[j], f(gdn_A_log)[j],
                              f(gdn_dt_bias)[j], f(gdn_norm_g)[j], f(ssd_conv_w)[j], f(ssd_conv_b)[j], f(ssd_A_log)[j],
                              f(ssd_dt_bias)[j], f(ssd_D)[j], f(ssd_norm_g)[j])
                shared[f"hp{i}_{g}"] = d["hp"]
                shared[f"dbc{i}_{g}"] = d["dbc"]
            else:
                d = pack_odd(g, dummy, f(norm_g)[i], f(od_w_in)[j], f(ret_norm_g)[j], f(lru_conv_w)[j],
                             f(lru_conv_b)[j], f(lru_w_a)[j], f(lru_b_a)[j], f(lru_w_x)[j], f(lru_b_x)[j],
                             f(lru_lambda)[j], np.zeros(8, np.int32))
                shared[f"lruw{i}_{g}"] = d["lruw"]
            shared[f"w{i}_{g}"] = d["w"]
            shared[f"prm{i}_{g}"] = d["prm"]
        prm = np.zeros((128, 32), np.float32)
        prm[:, 0:16] = _pk(f(ple_norm_g)[i], 16)
        prm[:, 16:32] = _pk(f(final_norm_g), 16)
        shared[f"prmC{i}"] = prm
        shared[f"wo{i}"] = np.ascontiguousarray(f(ev_w_out)[j] if i % 2 == 0 else f(od_w_out)[j])
        shared[f"wg{i}"] = np.ascontiguousarray(f(w_ple_gate)[i])
        shared[f"wp{i}"] = np.ascontiguousarray(f(w_ple_proj)[i])
    ins = []
    for c in range(8):
        b = c % B
        d = dict(shared)
        d["xT"] = np.ascontiguousarray(x[b].T)
        d["pT"] = np.ascontiguousarray(np.concatenate([f(p)[i, b].T for i in range(depth)], axis=0))
        d["pos"] = np.ascontiguousarray(np.asarray(positions, np.int32)[b].reshape(1, -1))
        ins.append(d)
    res = run_bass_kernel_spmd(nc, ins, core_ids=list(range(8)))
    return np.ascontiguousarray(np.stack([res.results[b]["yT"].T for b in range(B)], 0)).astype(np.float32)


kernel = kernel_unfused
```
